# Optimizing a Trainium2 kernel written in Bass

```python
import math
import jax, jax.numpy as jnp
from jax import lax
import numpy as np

D_MODEL = 1024
BATCH = 16
SEQ = 2048
DEPTH = 4

MIX_WIDTH = 2 * D_MODEL
SSD_WIDTH = MIX_WIDTH // 2
SSD_HEAD_DIM = 64
SSD_HEADS = SSD_WIDTH // SSD_HEAD_DIM
SSD_GROUPS = 2
SSD_HEADS_PER_GROUP = SSD_HEADS // SSD_GROUPS
SSD_STATE = 128
SSD_CONV = 4
SSD_CHUNK = 128
SSD_CONV_DIM = SSD_WIDTH + 2 * SSD_GROUPS * SSD_STATE
HGRN_WIDTH = MIX_WIDTH // 4
HGRN_HEAD_DIM = 128
HGRN_HEADS = HGRN_WIDTH // HGRN_HEAD_DIM
HGRN_CHUNK = 64
S5_WIDTH = MIX_WIDTH - SSD_WIDTH - HGRN_WIDTH
S5_GROUP_SIZE = 16
S5_GROUPS = S5_WIDTH // S5_GROUP_SIZE
S5_STATE = 64
S5_MIN_NEG = 1e-4

IN_COLS = SSD_WIDTH + SSD_CONV_DIM + SSD_HEADS + 4 * HGRN_WIDTH + 2 * S5_WIDTH
EPS = 1e-6

kernel_name = "hybrid_ssd_hgrn2_s5_parallel_heads"


def _in_proj_splits():
    widths = [SSD_WIDTH, SSD_CONV_DIM, SSD_HEADS, HGRN_WIDTH, HGRN_WIDTH,
              HGRN_WIDTH, HGRN_WIDTH, S5_WIDTH, S5_WIDTH]
    pts, acc = [], 0
    for w in widths[:-1]:
        acc += w
        pts.append(acc)
    return pts


def rms_norm(x, w):
    xf = x.astype(jnp.float32)
    y = xf * lax.rsqrt(jnp.mean(xf * xf, axis=-1, keepdims=True) + EPS)
    return (y * w.astype(jnp.float32)).astype(x.dtype)


def grouped_rms_norm(x, w, n_groups):
    shp = x.shape
    xf = x.astype(jnp.float32).reshape(shp[:-1] + (n_groups, shp[-1] // n_groups))
    y = xf * lax.rsqrt(jnp.mean(xf * xf, axis=-1, keepdims=True) + EPS)
    return (y.reshape(shp) * w.astype(jnp.float32)).astype(x.dtype)


def causal_depthwise_conv(x, w, b):
    out = lax.conv_general_dilated(
        x, w[:, None, :].astype(x.dtype), window_strides=(1,),
        padding=[(SSD_CONV - 1, 0)], dimension_numbers=("NWC", "WIO", "NWC"),
        feature_group_count=x.shape[-1])
    return out + b.astype(x.dtype)


def ssd_mixer(z, xbc, dt_raw, conv_w, conv_b, dt_bias, a_log, d_skip, norm_w):
    bsz, seqlen, _ = xbc.shape
    nc = seqlen // SSD_CHUNK
    G, R, P, N, T = SSD_GROUPS, SSD_HEADS_PER_GROUP, SSD_HEAD_DIM, SSD_STATE, SSD_CHUNK
    xbc = jax.nn.silu(causal_depthwise_conv(xbc, conv_w, conv_b))
    xs, b_in, c_in = jnp.split(xbc, [SSD_WIDTH, SSD_WIDTH + G * N], axis=-1)
    xs = xs.reshape(bsz, nc, T, G, R, P)
    b_in = b_in.reshape(bsz, nc, T, G, N)
    c_in = c_in.reshape(bsz, nc, T, G, N)
    dt = jax.nn.softplus(dt_raw.astype(jnp.float32) + dt_bias.astype(jnp.float32))
    dt = dt.reshape(bsz, nc, T, G, R)
    a = -jnp.exp(a_log.astype(jnp.float32)).reshape(G, R)
    a_cum = jnp.cumsum(dt * a, axis=2)
    mask = jnp.tril(jnp.ones((T, T), dtype=bool))[:, :, None, None]
    seg = a_cum[:, :, :, None] - a_cum[:, :, None, :]
    decay = jnp.exp(jnp.where(mask, seg, -jnp.inf))
    x_dt = xs * dt[..., None]
    scores = jnp.einsum("bclgn,bcsgn->bclsg", c_in, b_in)
    y_diag = jnp.einsum("bclsg,bclsgr,bcsgrp->bclgrp", scores, decay, x_dt)
    decay_to_end = jnp.exp(a_cum[:, :, -1:] - a_cum)
    chunk_states = jnp.einsum("bclgn,bclgr,bclgrp->bcgrpn", b_in, decay_to_end, x_dt)
    chunk_decay = jnp.exp(a_cum[:, :, -1])

    def step(h, inp):
        dec, st = inp
        return dec[..., None, None] * h + st, h

    h0 = jnp.zeros((bsz, G, R, P, N), chunk_states.dtype)
    _, prev = lax.scan(step, h0, (jnp.moveaxis(chunk_decay, 1, 0),
                                  jnp.moveaxis(chunk_states, 1, 0)))
    prev = jnp.moveaxis(prev, 0, 1)
    y_off = jnp.einsum("bclgn,bcgrpn,bclgr->bclgrp", c_in, prev, jnp.exp(a_cum))
    y = y_diag + y_off + d_skip.reshape(G, R)[:, :, None] * xs
    y = y.reshape(bsz, seqlen, SSD_WIDTH)
    return grouped_rms_norm(y * jax.nn.silu(z), norm_w, SSD_GROUPS)


def hgrn2_mixer(q, f_raw, i_in, g, lb, norm_w):
    bsz, seqlen, _ = q.shape
    nc = seqlen // HGRN_CHUNK
    H, Dh, T = HGRN_HEADS, HGRN_HEAD_DIM, HGRN_CHUNK
    q = jax.nn.silu(q)
    lbf = lb.astype(jnp.float32)
    log_f = jnp.logaddexp(jnp.log(lbf), jnp.log1p(-lbf) + jax.nn.log_sigmoid(f_raw.astype(jnp.float32)))
    k = -jnp.expm1(log_f)

    def to_chunks(t):
        return t.reshape(bsz, nc, T, H, Dh).transpose(1, 0, 3, 2, 4)

    mask = jnp.tril(jnp.ones((T, T), dtype=bool))[:, :, None]

    def step(state, inp):
        qc, kc, vc, lfc = inp
        cum = jnp.cumsum(lfc, axis=2)
        decay = jnp.exp(jnp.where(mask, cum[:, :, :, None, :] - cum[:, :, None, :, :], -jnp.inf))
        attn = jnp.einsum("bhlk,bhsk,bhlsk->bhls", qc, kc, decay)
        o = (jnp.einsum("bhls,bhsv->bhlv", attn, vc)
             + jnp.einsum("bhlk,bhkv->bhlv", qc * jnp.exp(cum), state))
        last = cum[:, :, -1:, :]
        state = (jnp.exp(last[:, :, 0, :])[..., None] * state
                 + jnp.einsum("bhsk,bhsv->bhkv", kc * jnp.exp(last - cum), vc))
        return state, o

    state0 = jnp.zeros((bsz, H, Dh, Dh), jnp.float32)
    _, o = lax.scan(step, state0, (to_chunks(q), to_chunks(k), to_chunks(i_in), to_chunks(log_f)))
    o = o.transpose(1, 0, 3, 2, 4).reshape(bsz, seqlen, HGRN_WIDTH)
    return grouped_rms_norm(o, norm_w, HGRN_HEADS) * jax.nn.silu(g)


def s5_mixer(u, gate, lam_re, lam_im, log_step, b_re, b_im, c_re, c_im, d_skip, glu_w, glu_b, norm_w):
    bsz, seqlen, _ = u.shape
    G, P, Hg = S5_GROUPS, S5_STATE, S5_GROUP_SIZE
    uf = u.astype(jnp.float32)
    lr = jnp.minimum(lam_re.astype(jnp.float32), -S5_MIN_NEG)
    li = lam_im.astype(jnp.float32)
    step = jnp.exp(log_step.astype(jnp.float32))[:, None]
    mag = jnp.exp(lr * step)
    ang = li * step
    ab_re, ab_im = mag * jnp.cos(ang), mag * jnp.sin(ang)
    den = lr * lr + li * li
    num_re, num_im = ab_re - 1.0, ab_im
    co_re = (num_re * lr + num_im * li) / den
    co_im = (num_im * lr - num_re * li) / den
    bb_re = co_re[..., None] * b_re - co_im[..., None] * b_im
    bb_im = co_re[..., None] * b_im + co_im[..., None] * b_re
    ug = uf.reshape(bsz, seqlen, G, Hg)
    bu_re = jnp.einsum("blgh,gph->lbgp", ug, bb_re)
    bu_im = jnp.einsum("blgh,gph->lbgp", ug, bb_im)
    a_re = jnp.broadcast_to(ab_re, (seqlen, 1, G, P))
    a_im = jnp.broadcast_to(ab_im, (seqlen, 1, G, P))

    def combine(e_i, e_j):
        ar_i, ai_i, br_i, bi_i = e_i
        ar_j, ai_j, br_j, bi_j = e_j
        return (ar_j * ar_i - ai_j * ai_i,
                ar_j * ai_i + ai_j * ar_i,
                ar_j * br_i - ai_j * bi_i + br_j,
                ar_j * bi_i + ai_j * br_i + bi_j)

    _, _, s_re, s_im = lax.associative_scan(combine, (a_re, a_im, bu_re, bu_im), axis=0)
    y = jnp.einsum("lbgp,ghp->blgh", s_re, c_re) - jnp.einsum("lbgp,ghp->blgh", s_im, c_im)
    y = y.reshape(bsz, seqlen, S5_WIDTH) + d_skip * uf
    y = jax.nn.gelu(y)
    hg = y @ glu_w + glu_b
    val, gt = jnp.split(hg, 2, axis=-1)
    y = val * jax.nn.sigmoid(gt) * jax.nn.silu(gate)
    return rms_norm(y, norm_w)


def setup_inputs(seed: int = 0) -> dict:
    key = jax.random.key(seed)
    ks = jax.random.split(key, 24)
    f32 = jnp.float32
    nrm = lambda k, shp, s: s * jax.random.normal(k, shp, f32)
    dt0 = jnp.exp(jax.random.uniform(ks[6], (DEPTH, SSD_HEADS), f32, math.log(1e-3), math.log(1e-1)))
    lam_im0 = math.pi * jnp.arange(S5_STATE, dtype=f32)
    return {
        "x": jax.random.normal(ks[0], (BATCH, SEQ, D_MODEL), f32),
        "pre_norm_w": 1.0 + nrm(ks[1], (DEPTH, D_MODEL), 0.02),
        "post_norm_w": 1.0 + nrm(ks[2], (DEPTH, D_MODEL), 0.02),
        "w_in": nrm(ks[3], (DEPTH, D_MODEL, IN_COLS), D_MODEL ** -0.5),
        "w_out": nrm(ks[4], (DEPTH, MIX_WIDTH, D_MODEL), MIX_WIDTH ** -0.5),
        "ssd_conv_w": nrm(ks[5], (DEPTH, SSD_CONV, SSD_CONV_DIM), SSD_CONV ** -0.5),
        "ssd_conv_b": nrm(ks[7], (DEPTH, SSD_CONV_DIM), 0.02),
        "ssd_dt_bias": dt0 + jnp.log(-jnp.expm1(-dt0)),
        "ssd_a_log": jnp.log(jax.random.uniform(ks[8], (DEPTH, SSD_HEADS), f32, 1.0, 16.0)),
        "ssd_d": 1.0 + nrm(ks[9], (DEPTH, SSD_HEADS), 0.02),
        "ssd_norm_w": 1.0 + nrm(ks[10], (DEPTH, SSD_WIDTH), 0.02),
        "hgrn_lower_bounds": nrm(ks[11], (DEPTH, HGRN_WIDTH), 0.1),
        "hgrn_norm_w": 1.0 + nrm(ks[12], (DEPTH, HGRN_WIDTH), 0.02),
        "s5_lambda_re": -0.5 + nrm(ks[13], (DEPTH, S5_GROUPS, S5_STATE), 0.01),
        "s5_lambda_im": lam_im0 + nrm(ks[14], (DEPTH, S5_GROUPS, S5_STATE), 0.01),
        "s5_log_step": jax.random.uniform(ks[15], (DEPTH, S5_GROUPS), f32, math.log(1e-3), math.log(1e-1)),
        "s5_b_re": nrm(ks[16], (DEPTH, S5_GROUPS, S5_STATE, S5_GROUP_SIZE), (2 * S5_GROUP_SIZE) ** -0.5),
        "s5_b_im": nrm(ks[17], (DEPTH, S5_GROUPS, S5_STATE, S5_GROUP_SIZE), (2 * S5_GROUP_SIZE) ** -0.5),
        "s5_c_re": nrm(ks[18], (DEPTH, S5_GROUPS, S5_GROUP_SIZE, S5_STATE), S5_STATE ** -0.5),
        "s5_c_im": nrm(ks[19], (DEPTH, S5_GROUPS, S5_GROUP_SIZE, S5_STATE), S5_STATE ** -0.5),
        "s5_d": jax.random.normal(ks[20], (DEPTH, S5_WIDTH), f32),
        "s5_glu_w": nrm(ks[21], (DEPTH, S5_WIDTH, 2 * S5_WIDTH), S5_WIDTH ** -0.5),
        "s5_glu_b": nrm(ks[22], (DEPTH, 2 * S5_WIDTH), 0.02),
        "s5_norm_w": 1.0 + nrm(ks[23], (DEPTH, S5_WIDTH), 0.02),
    }


def reference(x, pre_norm_w, post_norm_w, w_in, w_out, ssd_conv_w, ssd_conv_b, ssd_dt_bias,
              ssd_a_log, ssd_d, ssd_norm_w, hgrn_lower_bounds, hgrn_norm_w, s5_lambda_re,
              s5_lambda_im, s5_log_step, s5_b_re, s5_b_im, s5_c_re, s5_c_im, s5_d, s5_glu_w,
              s5_glu_b, s5_norm_w):
    lb_all = jnp.cumsum(jax.nn.softmax(hgrn_lower_bounds.astype(jnp.float32), axis=0), axis=0)
    lb_all = lb_all - lb_all[0:1]
    splits = _in_proj_splits()
    for l in range(DEPTH):
        h = rms_norm(x, pre_norm_w[l])
        proj = h @ w_in[l]
        z, xbc, dt_raw, q, f_raw, i_in, g, u, s5_gate = jnp.split(proj, splits, axis=-1)
        y_ssd = ssd_mixer(z, xbc, dt_raw, ssd_conv_w[l], ssd_conv_b[l], ssd_dt_bias[l],
                          ssd_a_log[l], ssd_d[l], ssd_norm_w[l])
        y_hgrn = hgrn2_mixer(q, f_raw, i_in, g, lb_all[l], hgrn_norm_w[l])
        y_s5 = s5_mixer(u, s5_gate, s5_lambda_re[l], s5_lambda_im[l], s5_log_step[l],
                        s5_b_re[l], s5_b_im[l], s5_c_re[l], s5_c_im[l], s5_d[l],
                        s5_glu_w[l], s5_glu_b[l], s5_norm_w[l])
        mix = jnp.concatenate([y_ssd, y_hgrn, y_s5], axis=-1).astype(x.dtype)
        x = x + rms_norm(mix @ w_out[l], post_norm_w[l])
    return x
```

```python
import math
from contextlib import ExitStack

import numpy as np
import concourse.bass as bass
import concourse.mybir as mybir
from concourse.bass_utils import run_bass_kernel_spmd

F32 = mybir.dt.float32
BF16 = mybir.dt.bfloat16
AF = mybir.ActivationFunctionType
ALU = mybir.AluOpType
AX = mybir.AxisListType

D_MODEL = 1024
IN_COLS = 5648
EPS = 1e-6
NL_FULL = 4
SEQ_FULL = 2048
NCORES = 8
T = 128

C_Z, C_XBC, C_DT, C_Q, C_F, C_I, C_G, C_U, C_SG = 0, 1024, 2560, 2576, 3088, 3600, 4112, 4624, 5136
N_TMA = 1024 + 16 + 512 + 512
N_FMA = 1536 + 512 + 512
N_B = 1024
GELU_C1 = 0.7978845608028654
GELU_C2 = 0.044715 * GELU_C1


class Sched:
    def __init__(self, nc):
        self.nc = nc
        self.eng = {"pe": nc.tensor, "act": nc.scalar, "dve": nc.vector, "pool": nc.gpsimd, "sp": nc.sync}
        self.sem = {}
        self.cnt = {}
        self.waited = {}
        self.lastw = {}
        self.readers = {}
        self.pending = {}
        self.ninst = 0
        self.gen = {}
        for k in ("pe", "act", "dve", "pool"):
            self._mk(k)

    def _mk(self, k):
        self.sem[k] = self.nc.alloc_semaphore("s_" + k)
        self.cnt[k] = 0
        self.pending[k] = False

    def _phys(self, names, writing):
        out = []
        for b in names:
            if "#" in b:
                p, g = b.split("#")
                if writing:
                    if self.gen.get(p) != g and int(g) > int(self.gen.get(p, "-1")):
                        self.gen[p] = g
                assert self.gen.get(p) == g, f"stale PSUM bank use {b} (current gen {self.gen.get(p)})"
                b = p
            out.append(b)
        return out

    def _deps(self, reads, writes):
        reads[:] = self._phys(reads, False)
        writes[:] = self._phys(writes, True)
        deps = {}
        raw = {}
        for b in reads:
            lw = self.lastw.get(b)
            if lw:
                deps[lw[0]] = max(deps.get(lw[0], 0), lw[1])
                raw[lw[0]] = max(raw.get(lw[0], 0), lw[1])
        for b in writes:
            lw = self.lastw.get(b)
            if lw:
                deps[lw[0]] = max(deps.get(lw[0], 0), lw[1])
            for e, i in self.readers.get(b, {}).items():
                deps[e] = max(deps.get(e, 0), i)
        return deps, raw

    def _emit_waits(self, issuer, me, deps, raw):
        w = self.waited.setdefault(issuer, {})
        for src, idx in deps.items():
            if src == me:
                if me == "pe":
                    continue
            if idx > w.get(src, 0):
                self.eng[issuer].wait_ge(self.sem[src], idx)
                w[src] = idx
                self.ninst += 1

    def op(self, e, fn, reads=(), writes=(), inc=1):
        reads, writes = list(reads), list(writes)
        deps, raw = self._deps(reads, writes)
        self._emit_waits(e, e, deps, raw)
        ins = fn(self.eng[e])
        self.ninst += 1
        if inc:
            ins.then_inc(self.sem[e], 1)
            self.cnt[e] += 1
            idx = self.cnt[e]
            self.pending[e] = False
        else:
            idx = self.cnt[e] + 1
            self.pending[e] = True
        for b in reads:
            self.readers.setdefault(b, {})[e] = max(self.readers.get(b, {}).get(e, 0), idx)
        for b in writes:
            self.lastw[b] = (e, idx)
            self.readers[b] = {}
        return ins

    def dma(self, q, slot, out, in_, reads=(), writes=(), **kw):
        if slot not in self.sem:
            self._mk(slot)
        reads, writes = list(reads), list(writes)
        deps, raw = self._deps(reads, writes)
        self._emit_waits(q, None, deps, raw)
        ins = self.eng[q].dma_start(out=out, in_=in_, **kw)
        ins.then_inc(self.sem[slot], 16)
        self.ninst += 1
        self.cnt[slot] += 16
        idx = self.cnt[slot]
        for b in reads:
            self.readers.setdefault(b, {})[slot] = idx
        for b in writes:
            self.lastw[b] = (slot, idx)
            self.readers[b] = {}

    def barrier(self):
        for e in ("pe", "act", "dve", "pool", "sp"):
            w = self.waited.setdefault(e, {})
            for src, c in self.cnt.items():
                if c == 0 or (src == e and e == "pe"):
                    continue
                if c > w.get(src, 0):
                    self.eng[e].wait_ge(self.sem[src], c)
                    w[src] = c
                    self.ninst += 1

    def finish(self, q, bufs):
        deps, raw = self._deps(list(bufs), [])
        self._emit_waits(q, None, deps, raw)
        for k, v in self.pending.items():
            assert not v, k


def host_consts():
    c = {}
    c["ident"] = np.eye(128, dtype=np.float32)
    tl = np.arange(128)
    c["utri"] = (tl[:, None] <= tl[None, :]).astype(np.float32)
    c["negm"] = np.where(tl[None, :] >= tl[:, None], 0.0, -30000.0).astype(np.float32)
    blk = (tl[:, None] // 64) == (tl[None, :] // 64)
    c["m64"] = ((tl[None, :] >= tl[:, None]) & blk).astype(np.float32)
    sc = np.ones((128, 512), np.float32)
    sc[:, 0::64] = 0.0
    c["scan0"] = sc
    band = np.zeros((8, 128, 240), np.float32)
    for a in range(8):
        for k in range(16 * a, 16 * a + 16):
            band[a, k, (k % 16) + 112] = 1.0
    c["band"] = band.transpose(1, 0, 2).reshape(128, 8 * 240).copy()
    c["m8"] = ((tl[None, :] // 16) >= (tl[:, None] // 16)).astype(np.float32)
    c["ones"] = np.ones((128, 128), np.float32)
    return c


def build(NL, S, L, dbg=None, layers=None):
    NCH = L // T
    nc = bass.Bass("TRN2", target_bir_lowering=False)
    es = ExitStack()

    def din(name, shape, dt=F32):
        return nc.dram_tensor(name, list(shape), dt, kind="ExternalInput").ap()

    x_in = din("x", [S, L, D_MODEL])
    w_tma = din("w_tma", [NL, D_MODEL, N_TMA])
    w_fma = din("w_fma", [NL, D_MODEL, N_FMA])
    w_b = din("w_b", [NL, D_MODEL, N_B])
    w_out = din("w_out", [NL, 2048, D_MODEL])
    prew = din("prew", [NL, 128, 8])
    postw = din("postw", [NL, 1, D_MODEL])
    mixnw = din("mixnw", [NL, 128, 16])
    convw = din("convw", [NL, 128, 48])
    convb_pp = din("convb_pp", [NL, 128, 12])
    convb_row = din("convb_row", [NL, 1, 1536])
    dtb = din("dtb", [NL, 1, 16])
    alog = din("alog", [NL, 1, 16])
    ssdd = din("ssdd", [NL, 1, 16])
    hlb = din("hlb", [128, 4 * NL])
    lamre = din("lamre", [NL, 64, 32])
    lamim = din("lamim", [NL, 64, 32])
    lstep = din("lstep", [NL, 1, 32])
    bre = din("bre", [NL, 64, 512])
    bim = din("bim", [NL, 64, 512])
    cre = din("cre", [NL, 64, 512])
    cim = din("cim", [NL, 64, 512])
    s5d = din("s5d", [NL, 128, 32])
    gluw = din("gluw", [NL, 512, 1024])
    glub = din("glub", [NL, 1, 1024])
    consts = {k: din("c_" + k, v.shape) for k, v in host_consts().items()}

    out = nc.dram_tensor("out", [S, L, D_MODEL], F32, kind="ExternalOutput").ap()
    hT_d = nc.dram_tensor("hT_scr", [S * NCH, 128, 8 * 128], BF16, kind="Internal").ap()
    mixT_d = nc.dram_tensor("mixT_scr", [S * NCH, 128, 12 * 128], BF16, kind="Internal").ap()
    dbg_out = None
    if dbg:
        dbg_out = {k: nc.dram_tensor("dbg_" + k, list(shp), F32, kind="ExternalOutput").ap() for k, shp in dbg.items() if not k.startswith("_")}

    def sb(name, shape, dt=F32):
        return es.enter_context(nc.sbuf_tensor(name, list(shape), dt))

    ident_b = sb("ident_b", [128, 128], BF16)
    ident_f = sb("ident_f", [128, 128])
    utri = sb("utri", [128, 128])
    ones_f = sb("ones_f", [128, 128])
    ones_b = sb("ones_b", [128, 128], BF16)
    negm_b = sb("negm_b", [128, 128], BF16)
    m64 = sb("m64", [128, 128], BF16)
    scan0 = sb("scan0", [128, 512])
    band = sb("band", [128, 8, 240], BF16)
    m8 = sb("m8", [128, 128])
    hlb_t = sb("hlb_t", [128, 4, NL])
    lb_all = sb("lb_all", [128, 4, NL])
    neghalf = sb("neghalf", [128, 16])

    WREG = 38912
    wreg = sb("wreg", [128, WREG], BF16)
    wtma = wreg[:, 0:8 * N_TMA].rearrange("p (k n) -> p k n", k=8)
    wfma = wreg[:, 8 * N_TMA:8 * (N_TMA + N_FMA)].rearrange("p (k n) -> p k n", k=8)
    o = 0
    wb_v = wreg[:, o:o + 8 * N_B].rearrange("p (k n) -> p k n", k=8); o += 8 * N_B
    wout_v = wreg[:, o:o + 16 * 1024].rearrange("p (k n) -> p k n", k=16); o += 16 * 1024
    glu_v = wreg[:, o:o + 4 * 1024].rearrange("p (k n) -> p k n", k=4); o += 4 * 1024
    wsim_v = wreg[:, o:o + 32 * 64].rearrange("p (g n) -> p g n", g=32); o += 32 * 64
    wore_v = wreg[0:64, o:o + 32 * 128].rearrange("p (g n) -> p g n", g=32); o += 32 * 128
    woim_v = wreg[0:64, o:o + 32 * 128].rearrange("p (g n) -> p g n", g=32); o += 32 * 128
    assert o <= WREG, (o, WREG)

    REG2 = 48 * 128 * 2
    reg2 = sb("reg2", [128, 48 * 128], BF16)
    dconv = reg2[:].rearrange("p (k j n) -> p k j n", k=4, j=12)
    tz_v = reg2[:, 0:4096].rearrange("p (g n) -> p g n", g=32)
    wsre_v = reg2[:, 4096:6144].rearrange("p (g n) -> p g n", g=32)

    prew_t = sb("prew_t", [128, 8])
    postw_bc = sb("postw_bc", [128, 1024])
    mixnw_t = sb("mixnw_t", [128, 16])
    convw_t = sb("convw_t", [128, 48])
    convb_t = sb("convb_t", [128, 12])
    convb_r = sb("convb_r", [1, 1536], BF16)
    dtb_bc = sb("dtb_bc", [128, 16])
    a_bc = sb("a_bc", [128, 16])
    d_bc = sb("d_bc", [128, 16])
    lb_t = sb("lb_t", [128, 4])

    xt = [sb(f"xt{i}", [128, 1024]) for i in range(2)]
    sq_junk = sb("sq_junk", [128, 1024], BF16)
    st8 = sb("st8", [128, 8])
    xn = sb("xn", [128, 1024], BF16)
    hT = sb("hT", [128, 8, 128], BF16)
    zs = sb("zs", [128, 1024], BF16)
    vtok = sb("vtok", [128, 512], BF16)
    gs = sb("gs", [128, 512], BF16)
    dt_t = sb("dt_t", [128, 16])
    dtmp = sb("dtmp", [128, 16])
    XT = [sb(f"XT{s}", [128, 12, 131], BF16) for s in range(S)]
    qs = sb("qs", [128, 512])
    ef = sb("ef", [128, 512])
    xs = sb("xs", [128, 1024])
    btok = sb("btok", [128, 256], BF16)
    bct = sb("bct", [128, 4, 128], BF16)
    dtA = sb("dtA", [128, 16])
    acum = sb("acum", [128, 16])
    nacum = sb("nacum", [128, 16])
    eacum = sb("eacum", [128, 16])
    alast = sb("alast", [128, 16])
    dte = sb("dte", [128, 16])
    cdec = sb("cdec", [128, 16])
    xdt = sb("xdt", [128, 1024], BF16)
    xw = sb("xw", [128, 1024], BF16)
    xsd = sb("xsd", [128, 1024], BF16)
    scT = sb("scT", [128, 2, 128], BF16)
    dec = sb("dec", [128, 8, 128], BF16)
    MT = sb("MT", [128, 8, 128], BF16)
    hst = [sb(f"hst{s}", [128, 1024]) for s in range(S)]
    hbf = [sb(f"hbf{s}", [128, 1024], BF16) for s in range(S)]
    ytmp = sb("ytmp", [128, 512])
    yg = sb("yg", [128, 512])
    mix = sb("mix", [128, 2048], BF16)
    mixT = sb("mixT", [128, 16, 128], BF16)
    L1 = sb("L1", [128, 512])
    L2 = sb("L2", [128, 512])
    cum = sb("cum", [128, 512])
    ecum = sb("ecum", [128, 512])
    ecend = sb("ecend", [128, 8])
    qTa = sb("qTa", [128, 4, 128], BF16)
    qTb = sb("qTb", [128, 4, 128], BF16)
    kT = sb("kT", [128, 4, 128], BF16)
    kend = sb("kend", [128, 4, 128], BF16)
    kendT = sb("kendT", [128, 4, 128], BF16)
    attn = sb("attn", [128, 4, 128], BF16)
    Sst = [sb(f"Sst{s}", [128, 4, 128]) for s in range(S)]
    Sbf = [sb(f"Sbf{s}", [128, 4, 128], BF16) for s in range(S)]
    otmp = sb("otmp", [128, 512])

    g32 = sb("g32", [64, 40, 32])
    LA = sb("LA", [64, 2, 32])
    LB = sb("LB", [64, 2, 32])
    v4 = lambda t_: t_[0:64, 0:512].rearrange("p (g n) -> p g n", g=4)
    Zre, Zim, Yre, Yim, T1, T2 = v4(qs), v4(ef), v4(ytmp), v4(yg), v4(otmp), v4(xs)
    s5d_t = sb("s5d_t", [128, 32])
    glub_r = sb("glub_r", [1, 1024], BF16)
    S2 = [sb(f"S2_{s}", [64, 2, 32]) for s in range(S)]
    rt = sb("rt", [64, 2, 32])
    ru = sb("ru", [64, 2, 32])
    Hb = sb("Hb", [64, 2, 32, 16], BF16)

    ps = [es.enter_context(nc.psum_tensor(f"ps{i}", [128, 512], F32)) for i in range(8)]
    psn = [f"ps{i}" for i in range(8)]
    ps_rr = [0]

    def nps():
        i = ps_rr[0] % 8
        ps_rr[0] += 1
        return ps[i], f"{psn[i]}#{ps_rr[0]}"

    sch = Sched(nc)
    blk = es.enter_context(nc.Block())
    op = sch.op

    def act(out_, in_, func, reads, writes, **kw):
        return op("act", lambda e: e.activation(out=out_, in_=in_, func=func, **kw), reads, writes)

    def tt(eng, out_, a, b, o_, reads, writes):
        return op(eng, lambda e: e.tensor_tensor(out=out_, in0=a, in1=b, op=o_), reads, writes)

    def ts(eng, out_, a, s1, s2, o0, o1, reads, writes):
        return op(eng, lambda e: e.tensor_scalar(out=out_, in0=a, scalar1=s1, scalar2=s2, op0=o0, op1=o1), reads, writes)

    def cp(eng, out_, in_, reads, writes):
        if eng == "act":
            return act(out_, in_, AF.Copy, reads, writes)
        return op(eng, lambda e: e.tensor_copy(out=out_, in_=in_), reads, writes)

    def mm(out_, lhsT, rhs, start, stop, reads, writes, inc=None):
        return op("pe", lambda e: e.matmul(out_, lhsT, rhs, start=start, stop=stop), reads, writes,
                  inc=(1 if stop else 0) if inc is None else inc)

    def rsqrt(out_, in_, scale, reads, writes, n):
        ts("dve", out_, in_, scale, EPS, ALU.mult, ALU.add, reads, writes)
        op("pool", lambda e: e.tensor_tensor(out=out_, in0=out_, in1=neghalf[:, 0:n], op=ALU.pow), list(writes) + ["neghalf"], writes)

    def ld(q, slot, out_, in_, w):
        sch.dma(q, "d_" + w[0], out_, in_, (), w)

    ld("sp", "dc", ident_f[:], consts["ident"], ["ident_f"])
    ld("sp", "dc", utri[:], consts["utri"], ["utri"])
    ld("sp", "dc", ones_f[:], consts["ones"], ["ones_f"])
    ld("sp", "dc", scan0[:], consts["scan0"], ["scan0"])
    ld("sp", "dc", m8[:], consts["m8"], ["m8"])
    ld("sp", "dc", hlb_t[:].rearrange("p h l -> p (h l)"), hlb, ["hlb_t"])
    ld("pool", "dc2", ident_b[:], consts["ident"], ["ident_b"])
    ld("pool", "dc2", ones_b[:], consts["ones"], ["ones_b"])
    ld("pool", "dc2", negm_b[:], consts["negm"], ["negm_b"])
    ld("pool", "dc2", m64[:], consts["m64"], ["m64"])
    ld("pool", "dc2", band[:].rearrange("p a x -> p (a x)"), consts["band"], ["band"])
    op("dve", lambda e: e.memset(neghalf[:], -0.5), (), ["neghalf"])
    op("dve", lambda e: e.memset(qTa[:], 0.0), (), ["qTa"])
    op("dve", lambda e: e.memset(qTb[:], 0.0), (), ["qTb"])
    act(hlb_t[:], hlb_t[:], AF.Exp, ["hlb_t"], ["hlb_t"])
    op("dve", lambda e: e.tensor_reduce(out=st8[:, 0:4], in_=hlb_t[:], axis=AX.X, op=ALU.add), ["hlb_t"], ["st8"])
    op("dve", lambda e: e.reciprocal(out=st8[:, 0:4], in_=st8[:, 0:4]), ["st8"], ["st8"])
    tt("dve", hlb_t[:], hlb_t[:], st8[:, 0:4].unsqueeze(2).broadcast_to([128, 4, NL]), ALU.mult, ["hlb_t", "st8"], ["hlb_t"])
    op("dve", lambda e: e.memset(lb_all[:], 0.0), (), ["lb_all"])
    for l in range(1, NL):
        tt("dve", lb_all[:, :, l], lb_all[:, :, l - 1], hlb_t[:, :, l], ALU.add, ["lb_all", "hlb_t"], ["lb_all"])

    chunks = [(s, c) for s in range(S) for c in range(NCH)]

    layers = list(range(NL)) if layers is None else list(layers)
    for layer in layers:
        xsrc = x_in if layer == layers[0] else out
        for k in range(8):
            sch.dma("pool", "wa_t", wtma[:, k, :], w_tma[layer, k * 128:(k + 1) * 128, :], (), ["wtma"])
            sch.dma("pool", "wa_f", wfma[:, k, :], w_fma[layer, k * 128:(k + 1) * 128, :], (), ["wfma"])
        ld("sp", "dp", prew_t[:], prew[layer], ["prew_t"])
        ld("sp", "dp", convw_t[:], convw[layer], ["convw_t"])
        ld("sp", "dp", convb_t[:], convb_pp[layer], ["convb_t"])
        ld("pool", "dp2", convb_r[:], convb_row[layer], ["convb_r"])
        ld("sp", "dp", dtb_bc[:], dtb[layer].partition_broadcast(128), ["dtb_bc"])
        ld("sp", "dp", a_bc[:], alog[layer].partition_broadcast(128), ["a_bc"])
        ld("sp", "dp", d_bc[:], ssdd[layer].partition_broadcast(128), ["d_bc"])
        act(a_bc[:], a_bc[:], AF.Exp, ["a_bc"], ["a_bc"])
        ts("dve", a_bc[:], a_bc[:], -1.0, None, ALU.mult, ALU.bypass, ["a_bc"], ["a_bc"])
        cp("dve", lb_t[:], lb_all[:, :, layer], ["lb_all"], ["lb_t"])
        for k in range(8):
            ts("dve", wtma[:, k, :], wtma[:, k, :], prew_t[:, k:k + 1], None, ALU.mult, ALU.bypass, ["wtma", "prew_t"], ["wtma"])
            ts("pool", wfma[:, k, :], wfma[:, k, :], prew_t[:, k:k + 1], None, ALU.mult, ALU.bypass, ["wfma", "prew_t"], ["wfma"])
        for k in range(4):
            for j in range(12):
                ts("dve", dconv[:, k, j, :], ident_b[:], convw_t[:, k * 12 + j:k * 12 + j + 1], None, ALU.mult, ALU.bypass,
                   ["ident_b", "convw_t"], ["dconv"])
        for s in range(S):
            op("dve", lambda e: e.memset(XT[s][:, :, 0:3], 0.0), (), [f"XT{s}"])
            op("dve", lambda e: e.memset(hst[s][:], 0.0), (), [f"hst{s}"])
            op("pool", lambda e: e.memset(hbf[s][:], 0.0), (), [f"hbf{s}"])
            op("dve", lambda e: e.memset(Sst[s][:], 0.0), (), [f"Sst{s}"])
            op("pool", lambda e: e.memset(Sbf[s][:], 0.0), (), [f"Sbf{s}"])

        xrd0 = ["out_d"] if layer != layers[0] else []
        sch.dma("sp", "x0", xt[0][:], xsrc[chunks[0][0], chunks[0][1] * T:(chunks[0][1] + 1) * T, :], xrd0, ["xt0"])
        for ci, (s, c) in enumerate(chunks):
            xb = xt[ci % 2]
            xbn = f"xt{ci % 2}"
            if ci + 1 < len(chunks):
                s2, c2 = chunks[ci + 1]
                sch.dma("sp", f"x{(ci + 1) % 2}", xt[(ci + 1) % 2][:], xsrc[s2, c2 * T:(c2 + 1) * T, :], xrd0, [f"xt{(ci + 1) % 2}"])
            XTs, XTn = XT[s], f"XT{s}"
            hs, hsn, hb, hbn = hst[s], f"hst{s}", hbf[s], f"hbf{s}"
            Ss, Ssn, Sb, Sbn = Sst[s], f"Sst{s}", Sbf[s], f"Sbf{s}"
            act(sq_junk[:], xb[:], AF.Square, [xbn], ["sq_junk", "st8"], accum_out=st8[:, 0:1])
            rsqrt(st8[:, 1:2], st8[:, 0:1], 1.0 / D_MODEL, ["st8"], ["st8"], 1)
            ts("dve", xn[:], xb[:], st8[:, 1:2], None, ALU.mult, ALU.bypass, [xbn, "st8"], ["xn"])
            p0, p0n = nps()
            p0b = p0[:].bitcast(BF16)
            for k in range(8):
                op("pe", lambda e: e.transpose(p0b[:, k * 128:(k + 1) * 128], xn[:, k * 128:(k + 1) * 128], ident_b[:]),
                   ["xn", "ident_b"], [p0n], inc=1 if k == 7 else 0)
            cp("act", hT[:].rearrange("p k t -> p (k t)"), p0b, [p0n], ["hT"])
            sch.dma("sp", "hts", hT_d[ci], hT[:].rearrange("p k t -> p (k t)"), ["hT"], ["hT_d"])
            def tm_slab(c0, n):
                pz, pzn = nps()
                for k in range(8):
                    mm(pz[:, 0:n], hT[:, k, :], wtma[:, k, c0:c0 + n], k == 0, k == 7, ["hT", "wtma"], [pzn])
                return pz, pzn
            for h2 in range(2):
                pz, pzn = tm_slab(h2 * 512, 512)
                act(zs[:, h2 * 512:(h2 + 1) * 512], pz[:], AF.Silu, [pzn], ["zs"])
            pz, pzn = tm_slab(1040 + 512, 512)
            act(gs[:], pz[:], AF.Silu, [pzn], ["gs"])
            pz, pzn = tm_slab(1040, 512)
            cp("act", vtok[:], pz[:], [pzn], ["vtok"])
            pd, pdn = tm_slab(1024, 16)
            tt("dve", dtmp[:], pd[:, 0:16], dtb_bc[:], ALU.add, [pdn, "dtb_bc"], ["dtmp"])
            def fm_group(j0, nj):
                pf, pfn = nps()
                for jj in range(nj):
                    for k in range(8):
                        mm(pf[:, jj * 128:(jj + 1) * 128], wfma[:, k, (j0 + jj) * 128:(j0 + jj + 1) * 128], hT[:, k, :],
                           k == 0, k == 7, ["hT", "wfma"], [pfn], inc=1 if (k == 7 and jj == nj - 1) else 0)
                return pf, pfn
            for g3 in range(3):
                pf, pfn = fm_group(g3 * 4, 4)
                cp("act" if g3 != 1 else "dve", XTs[:, g3 * 4:(g3 + 1) * 4, 3:131], pf[:].rearrange("p (j t) -> p j t", j=4), [pfn], [XTn])
            pf, pfn = fm_group(12, 4)
            act(qs[:], pf[:], AF.Silu, [pfn], ["qs"])
            pfq, pfqn = fm_group(16, 4)
            pc = [nps() for _ in range(3)]
            for j in range(10):
                pcj, pcjn = pc[j // 4]
                o_ = pcj[:, (j % 4) * 128:(j % 4 + 1) * 128]
                mm(o_, ones_b[0:1, :], convb_r[0:1, j * 128:(j + 1) * 128], True, False, ["ones_b", "convb_r"], [pcjn])
                for k in range(4):
                    last = (k == 3)
                    mm(o_, XTs[:, j, k:k + 128], dconv[:, k, j, :], False, last, [XTn, "dconv"], [pcjn],
                       inc=1 if (last and (j % 4 == 3 or j == 9)) else 0)
            act(xs[:, 0:512], pc[0][0][:], AF.Silu, [pc[0][1]], ["xs"])
            act(xs[:, 512:1024], pc[1][0][:], AF.Silu, [pc[1][1]], ["xs"])
            act(btok[:], pc[2][0][:, 0:256], AF.Silu, [pc[2][1]], ["btok"])
            pb, pbn = nps()
            for jj in range(4):
                j = 8 + jj
                for k in range(4):
                    mm(pb[:, jj * 128:(jj + 1) * 128], dconv[:, k, j, :], XTs[:, j, k:k + 128], k == 0, k == 3, [XTn, "dconv"], [pbn],
                       inc=1 if (k == 3 and jj == 3) else 0)
            for jj in range(4):
                act(bct[:, jj, :], pb[:, jj * 128:(jj + 1) * 128], AF.Silu, [pbn, "convb_t"], ["bct"], bias=convb_t[:, 8 + jj:9 + jj])
            cp("pool", XTs[:, :, 0:3], XTs[:, :, 128:131], [XTn], [XTn])
            act(dtmp[:], dtmp[:], AF.Exp, ["dtmp"], ["dtmp"])
            act(dt_t[:], dtmp[:], AF.Ln, ["dtmp"], ["dt_t"], bias=1.0)
            act(ef[:], pfq[:], AF.Exp, [pfqn], ["ef"], scale=-1.0)
            tt("dve", dtA[:], dt_t[:], a_bc[:], ALU.mult, ["dt_t", "a_bc"], ["dtA"])
            pa, pan = nps()
            mm(pa[:, 0:16], utri[:], dtA[:], True, True, ["utri", "dtA"], [pan], inc=0)
            mm(pa[:, 16:32], ones_f[:], dtA[:], True, True, ["ones_f", "dtA"], [pan])
            cp("dve", acum[:], pa[:, 0:16], [pan], ["acum"])
            ts("dve", nacum[:], pa[:, 0:16], -1.0, None, ALU.mult, ALU.bypass, [pan], ["nacum"])
            act(eacum[:], pa[:, 0:16], AF.Exp, [pan], ["eacum"])
            act(cdec[:], pa[:, 16:32], AF.Exp, [pan], ["cdec"])
            tt("dve", dte[:], pa[:, 16:32], acum[:], ALU.subtract, [pan, "acum"], ["dte"])
            act(dte[:], dte[:], AF.Exp, ["dte"], ["dte"])
            tt("dve", dte[:], dte[:], dt_t[:], ALU.mult, ["dte", "dt_t"], ["dte"])
            xs3 = xs[:].rearrange("p (r q) -> p r q", r=16)
            tt("dve", xdt[:].rearrange("p (r q) -> p r q", r=16), xs3, dt_t[:].unsqueeze(2).broadcast_to([128, 16, 64]), ALU.mult,
               ["xs", "dt_t"], ["xdt"])
            tt("dve", xw[:].rearrange("p (r q) -> p r q", r=16), xs3, dte[:].unsqueeze(2).broadcast_to([128, 16, 64]), ALU.mult,
               ["xs", "dte"], ["xw"])
            tt("pool", xsd[:].rearrange("p (r q) -> p r q", r=16), xs3, d_bc[:].unsqueeze(2).broadcast_to([128, 16, 64]), ALU.mult,
               ["xs", "d_bc"], ["xsd"])
            for g in range(2):
                psc, pscn = nps()
                mm(psc[:, 0:128], bct[:, g, :], bct[:, 2 + g, :], True, True, ["bct"], [pscn])
                cp("dve", scT[:, g, :], psc[:, 0:128], [pscn], ["scT"])
                pab = [nps(), nps()]
                for r in range(8):
                    pq, pqn = pab[r // 4]
                    o_ = pq[:, (r % 4) * 128:(r % 4 + 1) * 128]
                    hh = g * 8 + r
                    mm(o_, dtA[:, hh:hh + 1].broadcast_to([128, 128]), utri[:], True, False, ["dtA", "utri"], [pqn])
                    mm(o_, ident_b[:], negm_b[:], False, True, ["ident_b", "negm_b"], [pqn], inc=1 if r % 4 == 3 else 0)
                for r in range(8):
                    pq, pqn = pab[r // 4]
                    hh = g * 8 + r
                    act(dec[:, r, :], pq[:, (r % 4) * 128:(r % 4 + 1) * 128], AF.Exp, [pqn, "nacum"], ["dec"], bias=nacum[:, hh:hh + 1])
                tt("dve", MT[:], dec[:], scT[:, g:g + 1, :].broadcast_to([128, 8, 128]), ALU.mult, ["dec", "scT"], ["MT"])
                py, pyn = nps()
                mm(py[:], ident_b[:], xsd[:, g * 512:(g + 1) * 512], True, False, ["ident_b", "xsd"], [pyn])
                for r in range(8):
                    hh = g * 8 + r
                    mm(py[:, r * 64:(r + 1) * 64], MT[:, r, :], xdt[:, hh * 64:(hh + 1) * 64], False, r == 7, ["MT", "xdt"], [pyn])
                po, pon = nps()
                mm(po[:], bct[:, 2 + g, :], hb[:, g * 512:(g + 1) * 512], True, True, ["bct", hbn], [pon])
                tt("dve", ytmp[:].rearrange("p (r q) -> p r q", r=8), po[:].rearrange("p (r q) -> p r q", r=8),
                   eacum[:, g * 8:(g + 1) * 8].unsqueeze(2).broadcast_to([128, 8, 64]), ALU.mult, [pon, "eacum"], ["ytmp"])
                tt("dve", ytmp[:], ytmp[:], py[:], ALU.add, ["ytmp", pyn], ["ytmp"])
                tt("dve", yg[:], ytmp[:], zs[:, g * 512:(g + 1) * 512], ALU.mult, ["ytmp", "zs"], ["yg"])
                act(sq_junk[:, 0:512], yg[:], AF.Square, ["yg"], ["sq_junk", "st8"], accum_out=st8[:, 2:3])
                rsqrt(st8[:, 3:4], st8[:, 2:3], 1.0 / 512, ["st8"], ["st8"], 1)
                ts("dve", mix[:, g * 512:(g + 1) * 512], yg[:], st8[:, 3:4], None, ALU.mult, ALU.bypass, ["yg", "st8"], ["mix"])
                ph, phn = nps()
                mm(ph[:], btok[:, g * 128:(g + 1) * 128], xw[:, g * 512:(g + 1) * 512], True, True, ["btok", "xw"], [phn])
                hv = hs[:, g * 512:(g + 1) * 512]
                tt("dve", hv.rearrange("p (r q) -> p r q", r=8), hv.rearrange("p (r q) -> p r q", r=8),
                   cdec[:, g * 8:(g + 1) * 8].unsqueeze(2).broadcast_to([128, 8, 64]), ALU.mult, [hsn, "cdec"], [hsn])
                tt("dve", hv, hv, ph[:], ALU.add, [hsn, phn], [hsn])
                cp("pool", hb[:, g * 512:(g + 1) * 512], hv, [hsn], [hbn])
            act(L2[:], ef[:], AF.Ln, ["ef"], ["L2"], bias=1.0)
            for h in range(4):
                act(L1[:, h * 128:(h + 1) * 128], ef[:, h * 128:(h + 1) * 128], AF.Ln, ["ef", "lb_t"], ["L1"], bias=1.0, scale=lb_t[:, h:h + 1])
            tt("dve", L1[:], L1[:], L2[:], ALU.subtract, ["L1", "L2"], ["L1"])
            act(L2[:], L1[:], AF.Exp, ["L1"], ["L2"])
            ts("dve", L2[:], L2[:], -1.0, 1.0, ALU.mult, ALU.add, ["L2"], ["L2"])
            op("dve", lambda e: e.tensor_tensor_scan(out=cum[:], data0=scan0[:], data1=L1[:], initial=0.0, op0=ALU.mult, op1=ALU.add),
               ["scan0", "L1"], ["cum"])
            cum4 = cum[:].rearrange("p (j t) -> p j t", j=8)
            act(ecend[:], cum4[:, :, 63], AF.Exp, ["cum"], ["ecend"])
            act(ecum[:], cum[:], AF.Exp, ["cum"], ["ecum"])
            q4 = qs[:].rearrange("p (h t) -> p h t", h=4)
            e4 = ecum[:].rearrange("p (h t) -> p h t", h=4)
            tt("dve", qTa[:, :, 0:64], q4[:, :, 0:64], e4[:, :, 0:64], ALU.mult, ["qs", "ecum"], ["qTa"])
            tt("dve", qTb[:, :, 64:128], q4[:, :, 64:128], e4[:, :, 64:128], ALU.mult, ["qs", "ecum"], ["qTb"])
            act(ecum[:], cum[:], AF.Exp, ["cum", "qTa", "qTb"], ["ecum"], scale=-1.0)
            tt("dve", ecum[:], L2[:], ecum[:], ALU.mult, ["L2", "ecum"], ["ecum"])
            cp("pool", kT[:].rearrange("p h t -> p (h t)"), ecum[:], ["ecum"], ["kT"])
            tt("dve", kend[:].rearrange("p h (b t) -> p (h b) t", b=2), ecum[:].rearrange("p (j t) -> p j t", j=8),
               ecend[:].unsqueeze(2).broadcast_to([128, 8, 64]), ALU.mult, ["ecum", "ecend"], ["kend"])
            pk, pkn = nps()
            pkb = pk[:].bitcast(BF16)
            for h in range(4):
                op("pe", lambda e: e.transpose(pkb[:, h * 128:(h + 1) * 128], kend[:, h, :], ident_b[:]), ["kend", "ident_b"], [pkn],
                   inc=1 if h == 3 else 0)
            cp("act", kendT[:].rearrange("p h t -> p (h t)"), pkb[:, 0:512], [pkn], ["kendT"])
            pat, patn = nps()
            for h in range(4):
                mm(pat[:, h * 128:(h + 1) * 128], kT[:, h, :], qTa[:, h, :], True, False, ["kT", "qTa"], [patn])
                mm(pat[:, h * 128:(h + 1) * 128], kT[:, h, :], qTb[:, h, :], False, True, ["kT", "qTb"], [patn], inc=1 if h == 3 else 0)
            tt("dve", attn[:], pat[:].rearrange("p (h t) -> p h t", h=4), m64[:].unsqueeze(1).broadcast_to([128, 4, 128]), ALU.mult,
               [patn, "m64"], ["attn"])
            pho, phon = nps()
            for h in range(4):
                o_ = pho[:, h * 128:(h + 1) * 128]
                mm(o_, attn[:, h, :], vtok[:, h * 128:(h + 1) * 128], h == 0, False, ["attn", "vtok"], [phon])
                mm(o_, qTa[:, h, :], Sb[:, h, :], False, False, ["qTa", Sbn], [phon])
            for b2 in range(2):
                pst, pstn = nps()
                for h in range(4):
                    mm(pst[:, h * 128:(h + 1) * 128], kendT[b2 * 64:(b2 + 1) * 64, h, :], vtok[b2 * 64:(b2 + 1) * 64, h * 128:(h + 1) * 128],
                       True, True, ["kendT", "vtok"], [pstn], inc=1 if h == 3 else 0)
                ec = ecend[:].rearrange("p (h b) -> p h b", b=2)[:, :, b2:b2 + 1].broadcast_to([128, 4, 128])
                tt("dve", Ss[:], Ss[:], ec, ALU.mult, [Ssn, "ecend"], [Ssn])
                tt("dve", Ss[:].rearrange("p h v -> p (h v)"), Ss[:].rearrange("p h v -> p (h v)"), pst[:], ALU.add, [Ssn, pstn], [Ssn])
                cp("pool", Sb[:], Ss[:], [Ssn], [Sbn])
                if b2 == 0:
                    for h in range(4):
                        mm(pho[:, h * 128:(h + 1) * 128], qTb[:, h, :], Sb[:, h, :], False, h == 3, ["qTb", Sbn], [phon], inc=1 if h == 3 else 0)
            for h in range(4):
                act(sq_junk[:, 0:128], pho[:, h * 128:(h + 1) * 128], AF.Square, [phon], ["sq_junk", "st8"], accum_out=st8[:, 4 + h:5 + h])
            rsqrt(st8[:, 4:8], st8[:, 4:8], 1.0 / 128, ["st8"], ["st8"], 4)
            tt("dve", otmp[:].rearrange("p (h v) -> p h v", h=4), pho[:].rearrange("p (h v) -> p h v", h=4),
               st8[:, 4:8].unsqueeze(2).broadcast_to([128, 4, 128]), ALU.mult, [phon, "st8"], ["otmp"])
            tt("dve", mix[:, 1024:1536], otmp[:], gs[:], ALU.mult, ["otmp", "gs"], ["mix"])
            pm1, pm1n = nps()
            pm2, pm2n = nps()
            pm1b, pm2b = pm1[:].bitcast(BF16), pm2[:].bitcast(BF16)
            for j in range(12):
                dst, dn = (pm1b, pm1n) if j < 8 else (pm2b, pm2n)
                jj = j % 8
                op("pe", lambda e: e.transpose(dst[:, jj * 128:(jj + 1) * 128], mix[:, j * 128:(j + 1) * 128], ident_b[:]),
                   ["mix", "ident_b"], [dn], inc=1 if j in (7, 11) else 0)
            mT2 = mixT[:].rearrange("p k t -> p (k t)")
            cp("act", mT2[:, 0:1024], pm1b, [pm1n], ["mixT"])
            cp("dve", mT2[:, 1024:1536], pm2b[:, 0:512], [pm2n], ["mixT"])
            sch.dma("sp", "mts", mixT_d[ci], mT2[:, 0:1536], ["mixT"], ["mixT_d"])
            if dbg and "mixA" in dbg and layer == 0:
                sch.dma("pool", "dbg", dbg_out["mixA"][ci], mix[:, 0:1536], ["mix"], ["dbgo"])

        if dbg and dbg.get("_stopA"):
            break

        sch.barrier()
        for k in range(8):
            sch.dma("pool", "wb", wb_v[:, k, :], w_b[layer, k * 128:(k + 1) * 128, :], (), ["wb"])
        for k in range(16):
            sch.dma("pool", "wo", wout_v[:, k, :], w_out[layer, k * 128:(k + 1) * 128, :], (), ["wout"])
        for k in range(4):
            sch.dma("pool", "wg", glu_v[:, k, :], gluw[layer, k * 128:(k + 1) * 128, :], (), ["glu"])
        ld("sp", "dp", mixnw_t[:], mixnw[layer], ["mixnw_t"])
        ld("sp", "dp", postw_bc[:], postw[layer].partition_broadcast(128), ["postw_bc"])
        ld("sp", "dp", s5d_t[:], s5d[layer], ["s5d_t"])
        ld("pool", "dp", glub_r[:], glub[layer], ["glub_r"])
        ld("sp", "dp", g32[:, 0, :], lamre[layer], ["g_lr"])
        ld("sp", "dp", g32[:, 1, :], lamim[layer], ["g_li"])
        ld("sp", "dp", g32[:, 2, :], lstep[layer].partition_broadcast(64), ["g_st"])
        ld("sp", "dp", L1[0:64, :], bre[layer], ["L1"])
        ld("sp", "dp", L2[0:64, :], bim[layer], ["L2"])
        ld("sp", "dp", cum[0:64, :], cre[layer], ["cum"])
        ld("sp", "dp", ecum[0:64, :], cim[layer], ["ecum"])
        for k in range(8):
            ts("dve" if k % 2 else "pool", wb_v[:, k, :], wb_v[:, k, :], prew_t[:, k:k + 1], None, ALU.mult, ALU.bypass, ["wb", "prew_t"], ["wb"])
        for k in range(16):
            ts("dve" if k % 2 else "pool", wout_v[:, k, :], wout_v[:, k, :], mixnw_t[:, k:k + 1], None, ALU.mult, ALU.bypass,
               ["wout", "mixnw_t"], ["wout"])
        ts("pool", glu_v[:].rearrange("p k n -> p (k n)"), glu_v[:].rearrange("p k n -> p (k n)"), 0.5, None, ALU.mult, ALU.bypass, ["glu"], ["glu"])
        GN = ["g32"]

        def gq(i):
            return g32[:, i, :]

        def gmul(o_, a, b):
            tt("dve", gq(o_), gq(a), gq(b), ALU.mult, GN + ["g_lr", "g_li", "g_st"], GN)

        def gadd(o_, a, b, o2=ALU.add):
            tt("dve", gq(o_), gq(a), gq(b), o2, GN, GN)

        def gts(o_, a, m_, a_):
            ts("dve", gq(o_), gq(a), m_, a_, ALU.mult, ALU.add, GN + ["g_lr", "g_li", "g_st"], GN)

        I_LR, I_LI, I_ST, I_X, I_ANG, I_MAG, I_MAGI, I_C, I_S, I_T1, I_T2, I_LRE, I_LIM, I_IRE, I_IIM, I_CRE, I_CIM, I_8RE, I_8IM, I_Y, I_P, I_NX, I_A16, I_DEN = range(24)
        act(gq(I_ST), gq(I_ST), AF.Exp, ["g_st"], GN + ["g_st"])
        ts("dve", gq(I_LR), gq(I_LR), -1e-4, None, ALU.min, ALU.bypass, ["g_lr"], GN + ["g_lr"])
        gmul(I_X, I_LR, I_ST)
        gmul(I_ANG, I_LI, I_ST)
        gts(I_NX, I_X, -1.0, 0.0)

        def expser(o_, xi):
            gts(o_, xi, 1.0 / 6, 1.0)
            for kf in (5, 4, 3, 2, 1):
                gmul(o_, o_, xi)
                gts(o_, o_, 1.0 / kf, 1.0)
        expser(I_MAG, I_X)
        expser(I_MAGI, I_NX)
        gts(I_A16, I_ANG, 1.0 / 16, 0.0)
        gmul(I_Y, I_A16, I_A16)
        sc_ = [1.0, -1.0 / 6, 1.0 / 120, -1.0 / 5040, 1.0 / 362880, -1.0 / 39916800, 1.0 / 6227020800]
        cc_ = [1.0, -0.5, 1.0 / 24, -1.0 / 720, 1.0 / 40320, -1.0 / 3628800, 1.0 / 479001600, -1.0 / 87178291200]
        gts(I_P, I_Y, sc_[6], sc_[5])
        for kf in (4, 3, 2, 1, 0):
            gmul(I_P, I_P, I_Y)
            gts(I_P, I_P, 1.0, sc_[kf])
        gmul(I_S, I_P, I_A16)
        gts(I_P, I_Y, cc_[7], cc_[6])
        for kf in (5, 4, 3, 2, 1, 0):
            gmul(I_P, I_P, I_Y)
            gts(I_P, I_P, 1.0, cc_[kf])
        gts(I_C, I_P, 1.0, 0.0)

        def csq(re, im):
            gmul(I_T1, re, re)
            gmul(I_T2, im, im)
            gmul(im, re, im)
            gts(im, im, 2.0, 0.0)
            gadd(re, I_T1, I_T2, ALU.subtract)
        for _ in range(4):
            csq(I_C, I_S)
        gmul(I_LRE, I_MAG, I_C)
        gmul(I_LIM, I_MAG, I_S)
        gmul(I_IRE, I_MAGI, I_C)
        gmul(I_IIM, I_MAGI, I_S)
        gts(I_IIM, I_IIM, -1.0, 0.0)
        gts(I_P, I_LRE, 1.0, -1.0)
        gmul(I_T1, I_LR, I_LR)
        gmul(I_T2, I_LI, I_LI)
        gadd(I_DEN, I_T1, I_T2)
        op("dve", lambda e: e.reciprocal(out=gq(I_DEN), in_=gq(I_DEN)), GN, GN)
        gmul(I_T1, I_P, I_LR)
        gmul(I_T2, I_LIM, I_LI)
        gadd(I_CRE, I_T1, I_T2)
        gmul(I_CRE, I_CRE, I_DEN)
        gmul(I_T1, I_LIM, I_LR)
        gmul(I_T2, I_P, I_LI)
        gadd(I_CIM, I_T1, I_T2, ALU.subtract)
        gmul(I_CIM, I_CIM, I_DEN)
        gts(I_8RE, I_LRE, 1.0, 0.0)
        gts(I_8IM, I_LIM, 1.0, 0.0)
        for _ in range(3):
            csq(I_8RE, I_8IM)
        cp("dve", LA[:, 0, :], gq(I_8RE), GN, ["LA"])
        cp("dve", LA[:, 1, :], gq(I_8RE), GN, ["LA"])
        ts("dve", LB[:, 0, :], gq(I_8IM), -1.0, None, ALU.mult, ALU.bypass, GN, ["LB"])
        cp("dve", LB[:, 1, :], gq(I_8IM), GN, ["LB"])

        def cmul(eng, ore, oim, are, aim, xre, xim, n, rd, wr):
            ab = lambda a_: a_.unsqueeze(2).broadcast_to([64, 4, n])
            tv1, tv2 = T1[:, :, 0:n], T2[:, :, 0:n]
            tt(eng, tv1, xre, ab(are), ALU.mult, rd + GN, ["otmp"])
            tt(eng, tv2, xim, ab(aim), ALU.mult, rd + GN, ["xs"])
            tt(eng, ore, tv1, tv2, ALU.subtract, ["otmp", "xs"], wr)
            tt(eng, tv1, xim, ab(are), ALU.mult, rd + GN, ["otmp"])
            tt(eng, tv2, xre, ab(aim), ALU.mult, rd + GN, ["xs"])
            tt(eng, oim, tv1, tv2, ALU.add, ["otmp", "xs"], wr)

        for gb in range(8):
            g0 = gb * 4
            sl = slice(g0, g0 + 4)
            Z4r = Zre[:].rearrange("p g (s h) -> p g s h", s=8)
            Z4i = Zim[:].rearrange("p g (s h) -> p g s h", s=8)
            Y4r = Yre[:].rearrange("p g (s h) -> p g s h", s=8)
            Y4i = Yim[:].rearrange("p g (s h) -> p g s h", s=8)
            Bv = lambda tile_: tile_[0:64, g0 * 16:(g0 + 4) * 16].rearrange("p (g h) -> p g h", g=4)
            cmul("dve", Z4r[:, :, 7, :], Z4i[:, :, 7, :], gq(I_CRE)[:, sl], gq(I_CIM)[:, sl], Bv(L1), Bv(L2), 16, ["L1", "L2"], ["qs", "ef"])
            for s8 in range(6, -1, -1):
                cmul("dve", Z4r[:, :, s8, :], Z4i[:, :, s8, :], gq(I_LRE)[:, sl], gq(I_LIM)[:, sl], Z4r[:, :, s8 + 1, :], Z4i[:, :, s8 + 1, :], 16,
                     ["qs", "ef"], ["qs", "ef"])
            cp("dve", Y4r[:, :, 7, :], Bv(cum), ["cum"], ["ytmp"])
            cp("dve", Y4i[:, :, 7, :], Bv(ecum), ["ecum"], ["yg"])
            for l8 in range(6, -1, -1):
                cmul("dve", Y4r[:, :, l8, :], Y4i[:, :, l8, :], gq(I_IRE)[:, sl], gq(I_IIM)[:, sl], Y4r[:, :, l8 + 1, :], Y4i[:, :, l8 + 1, :], 16,
                     ["ytmp", "yg"], ["ytmp", "yg"])
            ts("dve", T1[:], Yim[:], -1.0, None, ALU.mult, ALU.bypass, ["yg"], ["otmp"])
            for hb_ in range(1):
                pt_, ptn = nps()
                for gi in range(4):
                    gg = gi
                    mm(pt_[:, gi * 128:(gi + 1) * 128], Zre[:, gg, :], Yre[:, gg, :], True, False, ["qs", "ytmp"], [ptn])
                    mm(pt_[:, gi * 128:(gi + 1) * 128], Zim[:, gg, :], T1[:, gg, :], False, True, ["ef", "otmp"], [ptn], inc=1 if gi == 3 else 0)
                tt("dve", tz_v[:, g0 + hb_ * 4:g0 + hb_ * 4 + 4, :], pt_[:].rearrange("p (g n) -> p g n", g=4),
                   m8[:].unsqueeze(1).broadcast_to([128, 4, 128]), ALU.mult, [ptn, "m8"], ["tz"])
            for (Zt, Zn, dst, dn) in ((Zre, "qs", wsre_v, "wsre"), (Zim, "ef", wsim_v, "wsim")):
                pw_, pwn = nps()
                for gi in range(4):
                    mm(pw_[:, gi * 64:(gi + 1) * 64], Zt[:, gi, :], ident_f[0:64, 0:64], True, True, [Zn, "ident_f"], [pwn], inc=1 if gi == 3 else 0)
                cp("act", dst[:, sl, :], pw_[:, 0:256].rearrange("p (g n) -> p g n", g=4), [pwn], [dn])
            a8 = lambda i: gq(i)[:, sl].unsqueeze(2).broadcast_to([64, 4, 128])
            tt("dve", T1[:], Yre[:], a8(I_8RE), ALU.mult, ["ytmp"] + GN, ["otmp"])
            tt("dve", T2[:], Yim[:], a8(I_8IM), ALU.mult, ["yg"] + GN, ["xs"])
            tt("dve", wore_v[:, sl, :], T1[:], T2[:], ALU.subtract, ["otmp", "xs"], ["wore"])
            tt("dve", T1[:], Yim[:], a8(I_8RE), ALU.mult, ["yg"] + GN, ["otmp"])
            tt("dve", T2[:], Yre[:], a8(I_8IM), ALU.mult, ["ytmp"] + GN, ["xs"])
            tt("dve", T1[:], T1[:], T2[:], ALU.add, ["otmp", "xs"], ["otmp"])
            ts("dve", woim_v[:, sl, :], T1[:], -1.0, None, ALU.mult, ALU.bypass, ["otmp"], ["woim"])
        for s in range(S):
            op("dve", lambda e: e.memset(S2[s][:], 0.0), (), [f"S2_{s}"])

        ublk, gyb, uT, gyT = attn, kend, kT, kendT
        for ci, (s, c) in enumerate(chunks):
            xb = xt[ci % 2]
            xbn = f"xt{ci % 2}"
            xrd = ["out_d"] if layer != layers[0] else []
            sch.dma("sp", f"x{ci % 2}", xb[:], xsrc[s, c * T:(c + 1) * T, :], xrd, [xbn])
            sch.dma("sp", "htl", hT[:].rearrange("p k t -> p (k t)"), hT_d[ci], ["hT_d"], ["hT"])
            mT2 = mixT[:].rearrange("p k t -> p (k t)")
            sch.dma("sp", "mtl", mT2[:, 0:1536], mixT_d[ci], ["mixT_d"], ["mixT"])
            S2s, S2n = S2[s], f"S2_{s}"
            pu, pun = nps()
            for j in range(4):
                for k in range(8):
                    mm(pu[:, j * 128:(j + 1) * 128], wb_v[:, k, j * 128:(j + 1) * 128], hT[:, k, :], k == 0, k == 7, ["hT", "wb"], [pun],
                       inc=1 if (k == 7 and j == 3) else 0)
            cp("act", uT[:].rearrange("p j t -> p (j t)"), pu[:], [pun], ["kT"])
            pg_, pgn = nps()
            for k in range(8):
                mm(pg_[:], hT[:, k, :], wb_v[:, k, 512:1024], k == 0, k == 7, ["hT", "wb"], [pgn])
            act(otmp[:], pg_[:], AF.Silu, [pgn], ["otmp"])
            pb_, pbn_ = nps()
            for g in range(32):
                j, g8 = divmod(g, 8)
                for s8 in range(8):
                    rhs = uT[:, j, :].rearrange("p (b s) -> p s b", s=8)[:, s8, :]
                    mm(pb_[:, g * 16:(g + 1) * 16], band[:, g8, 112 - 16 * s8:240 - 16 * s8], rhs, g == 0 and s8 == 0, g == 31 and s8 == 7,
                       ["band", "kT"], [pbn_])
            cp("act", ublk[:].rearrange("p a b -> p (a b)"), pb_[:], [pbn_], ["attn"])
            ub3 = ublk[:].rearrange("p a b -> p (a b)").rearrange("p (g b) -> p g b", g=32)
            w2 = xs[0:64, :].rearrange("p (r g b) -> p r g b", r=2, g=32)
            for ri, (wsv, wsn) in enumerate(((wsre_v, "wsre"), (wsim_v, "wsim"))):
                pw_, pwn = nps()
                for g in range(32):
                    mm(pw_[0:64, g * 16:(g + 1) * 16], wsv[:, g, :], ub3[:, g, :], g == 0, g == 31, [wsn, "attn"], [pwn])
                cp("act", w2[:, ri, :, :], pw_[0:64, :].rearrange("p (g b) -> p g b", g=32), [pwn], ["xs"])
            for b in range(16):
                cp("pool", Hb[:, :, :, b], S2s[:], [S2n], ["Hb"])
                tt("dve", rt[:], S2s[:], LA[:], ALU.mult, [S2n, "LA"], ["rt"])
                tt("dve", ru[:, 0, :], S2s[:, 1, :], LB[:, 0, :], ALU.mult, [S2n, "LB"], ["ru"])
                tt("dve", ru[:, 1, :], S2s[:, 0, :], LB[:, 1, :], ALU.mult, [S2n, "LB"], ["ru"])
                tt("dve", rt[:], rt[:], ru[:], ALU.add, ["rt", "ru"], ["rt"])
                tt("dve", S2s[:], rt[:], w2[:, :, :, b], ALU.add, ["rt", "xs"], [S2n])
            py_, pyn_ = nps()
            for g in range(32):
                o_ = py_[:, g * 16:(g + 1) * 16]
                mm(o_, tz_v[:, g, :], ub3[:, g, :], g == 0, False, ["tz", "attn"], [pyn_])
                mm(o_, wore_v[:, g, :], Hb[:, 0, g, :], False, False, ["wore", "Hb"], [pyn_])
                mm(o_, woim_v[:, g, :], Hb[:, 1, g, :], False, g == 31, ["woim", "Hb"], [pyn_])
            tt("dve", ytmp[:].rearrange("p (g b) -> p g b", g=32), ub3, s5d_t[:].unsqueeze(2).broadcast_to([128, 32, 16]), ALU.mult,
               ["attn", "s5d_t"], ["ytmp"])
            tt("dve", ytmp[:], ytmp[:], py_[:], ALU.add, ["ytmp", pyn_], ["ytmp"])
            act(yg[:], ytmp[:], AF.Square, ["ytmp"], ["yg"])
            ts("dve", yg[:], yg[:], GELU_C2, GELU_C1, ALU.mult, ALU.add, ["yg"], ["yg"])
            tt("dve", yg[:], yg[:], ytmp[:], ALU.mult, ["yg", "ytmp"], ["yg"])
            act(yg[:], yg[:], AF.Tanh, ["yg"], ["yg"])
            op("dve", lambda e: e.scalar_tensor_tensor(out=gyb[:].rearrange("p a b -> p (a b)"), in0=yg[:], scalar=1.0, in1=ytmp[:],
                                                        op0=ALU.add, op1=ALU.mult), ["yg", "ytmp"], ["kend"])
            gy3 = gyb[:].rearrange("p a b -> p (a b)").rearrange("p (g b) -> p g b", g=32)
            pq_, pqn_ = nps()
            first = True
            for j in range(4):
                for l8 in range(8):
                    o_ = pq_[:, j * 128:(j + 1) * 128].rearrange("p (b l) -> p l b", l=8)[:, l8, :]
                    for g8 in range(8):
                        last = (j == 3 and l8 == 7 and g8 == 7)
                        mm(o_, band[:, l8, 112 - 16 * g8:240 - 16 * g8], gy3[:, j * 8 + g8, :], first, last, ["band", "kend"], [pqn_])
                        first = False
            cp("act", gyT[:].rearrange("p j t -> p (j t)"), pq_[:], [pqn_], ["kendT"])
            pv = [nps(), nps()]
            for hf in range(2):
                pp, ppn = pv[hf]
                mm(pp[:], ones_b[0:1, :], glub_r[0:1, hf * 512:(hf + 1) * 512], True, False, ["ones_b", "glub_r"], [ppn])
                for j in range(4):
                    mm(pp[:], gyT[:, j, :], glu_v[:, j, hf * 512:(hf + 1) * 512], False, j == 3, ["kendT", "glu"], [ppn])
            act(L1[:], pv[1][0][:], AF.Tanh, [pv[1][1]], ["L1"], scale=0.5)
            ts("dve", L1[:], L1[:], 0.5, 0.5, ALU.mult, ALU.add, ["L1"], ["L1"])
            tt("dve", L2[:], pv[0][0][:], L1[:], ALU.mult, [pv[0][1], "L1"], ["L2"])
            tt("dve", L2[:], L2[:], otmp[:], ALU.mult, ["L2", "otmp"], ["L2"])
            act(sq_junk[:, 0:512], L2[:], AF.Square, ["L2"], ["sq_junk", "st8"], accum_out=st8[:, 2:3])
            rsqrt(st8[:, 3:4], st8[:, 2:3], 1.0 / 512, ["st8"], ["st8"], 1)
            ts("dve", mix[:, 1536:2048], L2[:], st8[:, 3:4], None, ALU.mult, ALU.bypass, ["L2", "st8"], ["mix"])
            pm_, pmn_ = nps()
            pmb = pm_[:].bitcast(BF16)
            for j in range(4):
                op("pe", lambda e: e.transpose(pmb[:, j * 128:(j + 1) * 128], mix[:, 1536 + j * 128:1536 + (j + 1) * 128], ident_b[:]),
                   ["mix", "ident_b"], [pmn_], inc=1 if j == 3 else 0)
            cp("act", mT2[:, 1536:2048], pmb[:, 0:512], [pmn_], ["mixT"])
            po_ = [nps(), nps()]
            for n2 in range(2):
                pp, ppn = po_[n2]
                for kk in range(16):
                    mm(pp[:], mixT[:, kk, :], wout_v[:, kk, n2 * 512:(n2 + 1) * 512], kk == 0, kk == 15, ["mixT", "wout"], [ppn])
            for n2 in range(2):
                act(sq_junk[:, n2 * 512:(n2 + 1) * 512], po_[n2][0][:], AF.Square, [po_[n2][1]], ["sq_junk", "st8"], accum_out=st8[:, 4 + n2:5 + n2])
            tt("dve", st8[:, 6:7], st8[:, 4:5], st8[:, 5:6], ALU.add, ["st8"], ["st8"])
            rsqrt(st8[:, 7:8], st8[:, 6:7], 1.0 / D_MODEL, ["st8"], ["st8"], 1)
            for n2 in range(2):
                op("dve", lambda e: e.scalar_tensor_tensor(out=yg[:], in0=po_[n2][0][:], scalar=st8[:, 7:8], in1=postw_bc[:, n2 * 512:(n2 + 1) * 512],
                                                            op0=ALU.mult, op1=ALU.mult), [po_[n2][1], "st8", "postw_bc"], ["yg"])
                tt("dve", xb[:, n2 * 512:(n2 + 1) * 512], xb[:, n2 * 512:(n2 + 1) * 512], yg[:], ALU.add, [xbn, "yg"], [xbn])
            sch.dma("sp", "ost", out[s, c * T:(c + 1) * T, :], xb[:], [xbn], ["out_d"])
        sch.barrier()

    sch.finish("sp", ["mixT_d", "hT_d", "dbgo", "out_d"])
    es.close()
    return nc, sch


def prep_shared(inp, NL):
    f = lambda a: np.ascontiguousarray(np.asarray(a, dtype=np.float32))
    w_in = np.asarray(inp["w_in"], dtype=np.float32)[:NL]
    sh = {}
    sh["w_tma"] = f(np.concatenate([w_in[:, :, C_Z:C_Z + 1024], w_in[:, :, C_DT:C_DT + 16], w_in[:, :, C_I:C_I + 512],
                                    w_in[:, :, C_G:C_G + 512]], axis=2))
    sh["w_fma"] = f(np.concatenate([w_in[:, :, C_XBC:C_XBC + 1536], w_in[:, :, C_Q:C_Q + 512], w_in[:, :, C_F:C_F + 512]], axis=2))
    sh["w_b"] = f(w_in[:, :, C_U:C_U + 1024])
    sh["w_out"] = f(np.asarray(inp["w_out"])[:NL])
    sh["prew"] = f(np.asarray(inp["pre_norm_w"])[:NL].reshape(NL, 8, 128).transpose(0, 2, 1))
    sh["postw"] = f(np.asarray(inp["post_norm_w"])[:NL].reshape(NL, 1, D_MODEL))
    mixnw = np.concatenate([np.asarray(inp["ssd_norm_w"])[:NL], np.asarray(inp["hgrn_norm_w"])[:NL], np.asarray(inp["s5_norm_w"])[:NL]], axis=1)
    sh["mixnw"] = f(mixnw.reshape(NL, 16, 128).transpose(0, 2, 1))
    cw = np.asarray(inp["ssd_conv_w"])[:NL]
    sh["convw"] = f(cw.reshape(NL, 4, 12, 128).transpose(0, 3, 1, 2).reshape(NL, 128, 48))
    cb = np.asarray(inp["ssd_conv_b"])[:NL]
    sh["convb_pp"] = f(cb.reshape(NL, 12, 128).transpose(0, 2, 1))
    sh["convb_row"] = f(cb.reshape(NL, 1, 1536))
    sh["dtb"] = f(np.asarray(inp["ssd_dt_bias"])[:NL].reshape(NL, 1, 16))
    sh["alog"] = f(np.asarray(inp["ssd_a_log"])[:NL].reshape(NL, 1, 16))
    sh["ssdd"] = f(np.asarray(inp["ssd_d"])[:NL].reshape(NL, 1, 16))
    hl = np.asarray(inp["hgrn_lower_bounds"])[:NL]
    sh["hlb"] = f(hl.reshape(NL, 4, 128).transpose(2, 1, 0).reshape(128, 4 * NL))
    sh["lamre"] = f(np.asarray(inp["s5_lambda_re"])[:NL].transpose(0, 2, 1))
    sh["lamim"] = f(np.asarray(inp["s5_lambda_im"])[:NL].transpose(0, 2, 1))
    sh["lstep"] = f(np.asarray(inp["s5_log_step"])[:NL].reshape(NL, 1, 32))
    sh["bre"] = f(np.asarray(inp["s5_b_re"])[:NL].transpose(0, 2, 1, 3).reshape(NL, 64, 512))
    sh["bim"] = f(np.asarray(inp["s5_b_im"])[:NL].transpose(0, 2, 1, 3).reshape(NL, 64, 512))
    sh["cre"] = f(np.asarray(inp["s5_c_re"])[:NL].transpose(0, 3, 1, 2).reshape(NL, 64, 512))
    sh["cim"] = f(np.asarray(inp["s5_c_im"])[:NL].transpose(0, 3, 1, 2).reshape(NL, 64, 512))
    d5 = np.asarray(inp["s5_d"])[:NL].reshape(NL, 32, 16)
    sh["s5d"] = f(np.broadcast_to(d5.transpose(0, 2, 1)[:, None, :, :], (NL, 8, 16, 32)).reshape(NL, 128, 32))
    sh["gluw"] = f(np.asarray(inp["s5_glu_w"])[:NL])
    sh["glub"] = f(np.asarray(inp["s5_glu_b"])[:NL].reshape(NL, 1, 1024))
    for k, v in host_consts().items():
        sh["c_" + k] = v
    return sh


LAYER_GROUPS = [[0], [1], [2], [3]]


def kernel(**inputs):
    x = np.ascontiguousarray(np.asarray(inputs["x"], dtype=np.float32))
    B = x.shape[0]
    S = B // NCORES
    sh = prep_shared(inputs, NL_FULL)
    cur = x
    for grp in LAYER_GROUPS:
        nc, _ = build(NL_FULL, S, x.shape[1], layers=grp)
        in_maps = [dict(sh, x=np.ascontiguousarray(cur[S * c:S * (c + 1)])) for c in range(NCORES)]
        res = run_bass_kernel_spmd(nc, in_maps, core_ids=list(range(NCORES)))
        cur = np.concatenate([np.asarray(r["out"], dtype=np.float32) for r in res.results], axis=0)
    return cur
```

```python
import math
from contextlib import ExitStack

import numpy as np
import concourse.bass as bass
import concourse.mybir as mybir
from concourse.bass_utils import run_bass_kernel_spmd

F32 = mybir.dt.float32
BF16 = mybir.dt.bfloat16
AF = mybir.ActivationFunctionType
ALU = mybir.AluOpType
AX = mybir.AxisListType

D_MODEL = 1024
IN_COLS = 5648
EPS = 1e-6
NL_FULL = 4
SEQ_FULL = 2048
NCORES = 8
T = 128

C_Z, C_XBC, C_DT, C_Q, C_F, C_I, C_G, C_U, C_SG = 0, 1024, 2560, 2576, 3088, 3600, 4112, 4624, 5136
N_TMA = 1024 + 16 + 512 + 512
N_FMA = 1536 + 512 + 512
N_B = 1024
GELU_C1 = 0.7978845608028654
GELU_C2 = 0.044715 * GELU_C1


class Sched:
    def __init__(self, nc):
        self.nc = nc
        self.eng = {"pe": nc.tensor, "act": nc.scalar, "dve": nc.vector, "pool": nc.gpsimd, "sp": nc.sync}
        self.sem = {}
        self.cnt = {}
        self.waited = {}
        self.lastw = {}
        self.readers = {}
        self.pending = {}
        self.ninst = 0
        self.gen = {}
        for k in ("pe", "act", "dve", "pool"):
            self._mk(k)

    def _mk(self, k):
        self.sem[k] = self.nc.alloc_semaphore("s_" + k)
        self.cnt[k] = 0
        self.pending[k] = False

    def _phys(self, names, writing):
        out = []
        for b in names:
            if "#" in b:
                p, g = b.split("#")
                if writing:
                    if self.gen.get(p) != g and int(g) > int(self.gen.get(p, "-1")):
                        self.gen[p] = g
                assert self.gen.get(p) == g, f"stale PSUM bank use {b} (current gen {self.gen.get(p)})"
                b = p
            out.append(b)
        return out

    def _deps(self, reads, writes):
        reads[:] = self._phys(reads, False)
        writes[:] = self._phys(writes, True)
        deps = {}
        raw = {}
        for b in reads:
            lw = self.lastw.get(b)
            if lw:
                deps[lw[0]] = max(deps.get(lw[0], 0), lw[1])
                raw[lw[0]] = max(raw.get(lw[0], 0), lw[1])
        for b in writes:
            lw = self.lastw.get(b)
            if lw:
                deps[lw[0]] = max(deps.get(lw[0], 0), lw[1])
            for e, i in self.readers.get(b, {}).items():
                deps[e] = max(deps.get(e, 0), i)
        return deps, raw

    def _emit_waits(self, issuer, me, deps, raw):
        w = self.waited.setdefault(issuer, {})
        for src, idx in deps.items():
            if src == me:
                if me == "pe":
                    continue
            if idx > w.get(src, 0):
                self.eng[issuer].wait_ge(self.sem[src], idx)
                w[src] = idx
                self.ninst += 1

    def op(self, e, fn, reads=(), writes=(), inc=1):
        reads, writes = list(reads), list(writes)
        deps, raw = self._deps(reads, writes)
        self._emit_waits(e, e, deps, raw)
        ins = fn(self.eng[e])
        self.ninst += 1
        if inc:
            ins.then_inc(self.sem[e], 1)
            self.cnt[e] += 1
            idx = self.cnt[e]
            self.pending[e] = False
        else:
            idx = self.cnt[e] + 1
            self.pending[e] = True
        for b in reads:
            self.readers.setdefault(b, {})[e] = max(self.readers.get(b, {}).get(e, 0), idx)
        for b in writes:
            self.lastw[b] = (e, idx)
            self.readers[b] = {}
        return ins

    def dma(self, q, slot, out, in_, reads=(), writes=(), **kw):
        if slot not in self.sem:
            self._mk(slot)
        reads, writes = list(reads), list(writes)
        deps, raw = self._deps(reads, writes)
        self._emit_waits(q, None, deps, raw)
        ins = self.eng[q].dma_start(out=out, in_=in_, **kw)
        ins.then_inc(self.sem[slot], 16)
        self.ninst += 1
        self.cnt[slot] += 16
        idx = self.cnt[slot]
        for b in reads:
            self.readers.setdefault(b, {})[slot] = idx
        for b in writes:
            self.lastw[b] = (slot, idx)
            self.readers[b] = {}

    def barrier(self):
        for e in ("pe", "act", "dve", "pool", "sp"):
            w = self.waited.setdefault(e, {})
            for src, c in self.cnt.items():
                if c == 0 or (src == e and e == "pe"):
                    continue
                if c > w.get(src, 0):
                    self.eng[e].wait_ge(self.sem[src], c)
                    w[src] = c
                    self.ninst += 1

    def dma_sync(self, q, slot):
        w = self.waited.setdefault(q, {})
        c = self.cnt.get(slot, 0)
        if c > w.get(slot, 0):
            self.eng[q].wait_ge(self.sem[slot], c)
            w[slot] = c
            self.ninst += 1

    def finish(self, q, bufs):
        deps, raw = self._deps(list(bufs), [])
        self._emit_waits(q, None, deps, raw)
        for k, v in self.pending.items():
            assert not v, k


def host_consts():
    c = {}
    c["ident"] = np.eye(128, dtype=np.float32)
    tl = np.arange(128)
    c["utri"] = (tl[:, None] <= tl[None, :]).astype(np.float32)
    c["negm"] = np.where(tl[None, :] >= tl[:, None], 0.0, -30000.0).astype(np.float32)
    blk = (tl[:, None] // 64) == (tl[None, :] // 64)
    c["m64"] = ((tl[None, :] >= tl[:, None]) & blk).astype(np.float32)
    sc = np.ones((128, 512), np.float32)
    sc[:, 0::64] = 0.0
    c["scan0"] = sc
    band = np.zeros((8, 128, 240), np.float32)
    for a in range(8):
        for k in range(16 * a, 16 * a + 16):
            band[a, k, (k % 16) + 112] = 1.0
    c["band"] = band.transpose(1, 0, 2).reshape(128, 8 * 240).copy()
    c["m8"] = ((tl[None, :] // 16) >= (tl[:, None] // 16)).astype(np.float32)
    c["ones"] = np.ones((128, 128), np.float32)
    pm = np.zeros((128, 128), np.float32)
    for m in range(128):
        pm[(m % 8) * 16 + m // 8, m] = 1.0
    c["pm"] = pm
    c["pmT"] = np.ascontiguousarray(pm.T)
    return c


def build(NL, S, L, dbg=None, layers=None):
    assert S == 2 and L % 512 == 0
    NCH = L // T
    nc = bass.Bass("TRN2", target_bir_lowering=False)
    es = ExitStack()

    def din(name, shape, dt=F32):
        return nc.dram_tensor(name, list(shape), dt, kind="ExternalInput").ap()

    x_in = din("x", [S, L, D_MODEL])
    w_tma = din("w_tma", [NL, D_MODEL, N_TMA])
    w_fma = din("w_fma", [NL, D_MODEL, N_FMA])
    w_b = din("w_b", [NL, D_MODEL, N_B])
    w_out = din("w_out", [NL, 2048, D_MODEL])
    prew = din("prew", [NL, 128, 8])
    postw = din("postw", [NL, 1, D_MODEL])
    mixnw = din("mixnw", [NL, 128, 16])
    convw = din("convw", [NL, 128, 48])
    convb_pp = din("convb_pp", [NL, 128, 12])
    convb_row = din("convb_row", [NL, 1, 1536])
    dtb = din("dtb", [NL, 1, 16])
    alog = din("alog", [NL, 1, 16])
    ssdd = din("ssdd", [NL, 1, 16])
    hlb = din("hlb", [128, 4 * NL])
    lamre = din("lamre", [NL, 64, 32])
    lamim = din("lamim", [NL, 64, 32])
    lstep = din("lstep", [NL, 1, 32])
    bre = din("bre", [NL, 64, 512])
    bim = din("bim", [NL, 64, 512])
    cre = din("cre", [NL, 64, 512])
    cim = din("cim", [NL, 64, 512])
    s5d = din("s5d", [NL, 128, 32])
    gluw = din("gluw", [NL, 512, 1024])
    glub = din("glub", [NL, 1, 1024])
    consts = {k: din("c_" + k, v.shape) for k, v in host_consts().items()}

    out = nc.dram_tensor("out", [S, L, D_MODEL], F32, kind="ExternalOutput").ap()
    hT_d = nc.dram_tensor("hT_scr", [S * NCH, 128, 8 * 128], BF16, kind="Internal").ap()
    mixA_d = nc.dram_tensor("mixA_scr", [S, L, 1536], BF16, kind="Internal").ap()
    dbg_out = None
    if False:
        dbg_out = {}

    def sb(name, shape, dt=F32):
        return es.enter_context(nc.sbuf_tensor(name, list(shape), dt))

    ident_b = sb("ident_b", [128, 128], BF16)
    ident_f = sb("ident_f", [128, 128])
    utri = sb("utri", [128, 128])
    ones_f = sb("ones_f", [128, 128])
    ones_b = sb("ones_b", [128, 128], BF16)
    negm_b = sb("negm_b", [128, 128], BF16)
    m64 = sb("m64", [128, 128], BF16)
    m8 = sb("m8", [128, 128])
    pm_b = sb("pm_b", [128, 128], BF16)
    pmT_b = sb("pmT_b", [128, 128], BF16)
    hlb_t = sb("hlb_t", [128, 4, NL])
    lb_all = sb("lb_all", [128, 4, NL])
    neghalf = sb("neghalf", [128, 16])
    st8 = sb("st8", [128, 8])
    prew_t = sb("prew_t", [128, 8])
    mixnw_t = sb("mixnw_t", [128, 16])
    convw_t = sb("convw_t", [128, 48])
    convb_t = sb("convb_t", [128, 12])
    dtb_bc = sb("dtb_bc", [128, 16])
    a_bc = sb("a_bc", [128, 16])
    d_bc = sb("d_bc", [128, 16])
    lb_t = sb("lb_t", [128, 4])
    dt_t = sb("dt_t", [128, 16])
    dtmp = sb("dtmp", [128, 16])
    dtA = sb("dtA", [128, 16])
    acum = sb("acum", [128, 16])
    nacum = sb("nacum", [128, 16])
    eacum = sb("eacum", [128, 16])
    dte = sb("dte", [128, 16])
    cdec = sb("cdec", [128, 16])
    ecend = sb("ecend", [128, 8])
    s5d_t = sb("s5d_t", [128, 32])
    LA = sb("LA", [64, 2, 32])
    LB = sb("LB", [64, 2, 32])

    WREG = 38912
    wreg = sb("wreg", [128, WREG], BF16)
    wtma = wreg[:, 0:8 * N_TMA].rearrange("p (k n) -> p k n", k=8)
    wfma = wreg[:, 8 * N_TMA:8 * (N_TMA + N_FMA)].rearrange("p (k n) -> p k n", k=8)
    o = 0
    wb_v = wreg[:, o:o + 8 * N_B].rearrange("p (k n) -> p k n", k=8); o += 8 * N_B
    wout_v = wreg[:, o:o + 16 * 1024].rearrange("p (k n) -> p k n", k=16); o += 16 * 1024
    glu_v = wreg[:, o:o + 4 * 1024].rearrange("p (k n) -> p k n", k=4); o += 4 * 1024
    wsim_v = wreg[:, o:o + 32 * 64].rearrange("p (g n) -> p g n", g=32); o += 32 * 64
    wore_v = wreg[0:64, o:o + 32 * 128].rearrange("p (g n) -> p g n", g=32); o += 32 * 128
    woim_v = wreg[0:64, o:o + 32 * 128].rearrange("p (g n) -> p g n", g=32); o += 32 * 128
    assert o <= WREG, (o, WREG)
    reg2 = sb("reg2", [128, 48 * 128], BF16)
    dconv = reg2[:].rearrange("p (k j n) -> p k j n", k=4, j=12)
    tz_v = reg2[:, 0:4096].rearrange("p (g n) -> p g n", g=32)
    wsre_v = reg2[:, 4096:6144].rearrange("p (g n) -> p g n", g=32)

    ARENA = 58100
    arena = sb("arena", [128, ARENA], BF16)
    aoff = [0]

    def cv(n, dt=BF16, parts=128):
        ne = n * (2 if dt == F32 else 1)
        assert aoff[0] + ne <= ARENA, (aoff[0], ne, ARENA)
        ap = arena[0:parts, aoff[0]:aoff[0] + ne]
        aoff[0] += ne
        return ap.bitcast(F32) if dt == F32 else ap

    xt = [cv(1024, F32), cv(1024, F32)]
    xn = cv(1024)
    hTq = cv(4096).rearrange("p (k t) -> p k t", k=8)
    zs = cv(1024)
    vtok = cv(512)
    gs = cv(512)
    XT = [cv(12 * 260).rearrange("p (j t) -> p j t", j=12) for _ in range(2)]
    qsq = cv(2048).rearrange("p (h t) -> p h t", h=4)
    ef = cv(512, F32)
    xs = cv(1024, F32)
    btok = cv(256)
    bctq = cv(2048).rearrange("p (j t) -> p j t", j=4)
    xdt = cv(1024)
    xw = cv(1024)
    xsd = cv(1024)
    scT = cv(256).rearrange("p (g t) -> p g t", g=2)
    dec = cv(1024).rearrange("p (r t) -> p r t", r=8)
    MT = cv(1024).rearrange("p (r t) -> p r t", r=8)
    hst = [cv(1024, F32), cv(1024, F32)]
    hbf = [cv(1024), cv(1024)]
    L1 = cv(512, F32)
    L2 = cv(512, F32)
    cum = cv(512, F32)
    ecum = cv(512, F32)
    qTa = cv(512).rearrange("p (h t) -> p h t", h=4)
    qTb = cv(512).rearrange("p (h t) -> p h t", h=4)
    kT = cv(512).rearrange("p (h t) -> p h t", h=4)
    kend = cv(512).rearrange("p (h t) -> p h t", h=4)
    kendT = cv(512).rearrange("p (h t) -> p h t", h=4)
    attn = cv(512).rearrange("p (h t) -> p h t", h=4)
    Sst = [cv(512, F32).rearrange("p (h t) -> p h t", h=4) for _ in range(2)]
    Sbf = [cv(512).rearrange("p (h t) -> p h t", h=4) for _ in range(2)]
    ytmp = cv(512, F32)
    yg = cv(512, F32)
    otmp = cv(512, F32)
    mix = cv(2048)
    mixT = cv(1536).rearrange("p (k t) -> p k t", k=12)
    junk = dec.rearrange("p r t -> p (r t)")
    scan0 = cv(512, F32)
    convb_r = cv(1536, parts=1)
    endA = aoff[0]
    aoff[0] = 0
    xtB = [cv(1024, F32), cv(1024, F32)]
    hT8 = cv(8192).rearrange("p (k t) -> p k t", k=8)
    uT8 = cv(4096).rearrange("p (j t) -> p j t", j=4)
    ublk8 = cv(4096).rearrange("p (g n) -> p g n", g=32)
    gst = cv(2 * 32 * 2 * 65, parts=64).rearrange("p (r g s m) -> p r g s m", r=2, g=32, s=2)
    gyb8 = cv(4096).rearrange("p (g n) -> p g n", g=32)
    gyT8d = [cv(4096).rearrange("p (j t) -> p j t", j=4) for _ in range(2)]
    ytB = cv(512, F32)
    ygB = cv(512, F32)
    gatesB = cv(4096).rearrange("p (l n) -> p l n", l=8)
    L1B, L2B = ytB, ygB
    mixB = cv(2048)
    mixTB = cv(2048).rearrange("p (k t) -> p k t", k=16)
    junkB = cv(1024)
    postw_bc = cv(1024, F32)
    S2 = cv(128, F32, parts=64).rearrange("p (r g s) -> p r g s", r=2, g=32)
    rt = cv(128, F32, parts=64).rearrange("p (r g s) -> p r g s", r=2, g=32)
    ru = cv(128, F32, parts=64).rearrange("p (r g s) -> p r g s", r=2, g=32)
    glub_r = cv(1024, parts=1)
    endB = aoff[0]
    aoff[0] = 4096
    g32 = cv(40 * 32, F32, parts=64).rearrange("p (i g) -> p i g", i=40)
    v4 = lambda: cv(512, F32, parts=64).rearrange("p (g n) -> p g n", g=4)
    Zre, Zim, Yre, Yim, T1, T2 = v4(), v4(), v4(), v4(), v4(), v4()
    Bre_t, Bim_t, Cre_t, Cim_t = (cv(512, F32, parts=64) for _ in range(4))
    assert aoff[0] <= endB

    ps = [es.enter_context(nc.psum_tensor(f"ps{i}", [128, 512], F32)) for i in range(8)]
    psn = [f"ps{i}" for i in range(8)]
    ps_rr = [0]

    def nps():
        i = ps_rr[0] % 8
        ps_rr[0] += 1
        return ps[i], f"{psn[i]}#{ps_rr[0]}"

    sch = Sched(nc)
    blk = es.enter_context(nc.Block())
    op = sch.op

    def act(out_, in_, func, reads, writes, **kw):
        return op("act", lambda e: e.activation(out=out_, in_=in_, func=func, **kw), reads, writes)

    def tt(eng, out_, a, b, o_, reads, writes):
        return op(eng, lambda e: e.tensor_tensor(out=out_, in0=a, in1=b, op=o_), reads, writes)

    def ts(eng, out_, a, s1, s2, o0, o1, reads, writes):
        return op(eng, lambda e: e.tensor_scalar(out=out_, in0=a, scalar1=s1, scalar2=s2, op0=o0, op1=o1), reads, writes)

    def cp(eng, out_, in_, reads, writes):
        if eng == "act":
            return act(out_, in_, AF.Copy, reads, writes)
        return op(eng, lambda e: e.tensor_copy(out=out_, in_=in_), reads, writes)

    def mm(out_, lhsT, rhs, start, stop, reads, writes, inc=None):
        return op("pe", lambda e: e.matmul(out_, lhsT, rhs, start=start, stop=stop), reads, writes,
                  inc=(1 if stop else 0) if inc is None else inc)

    def rsqrt(out_, in_, scale, reads, writes, n):
        ts("dve", out_, in_, scale, EPS, ALU.mult, ALU.add, reads, writes)
        op("pool", lambda e: e.tensor_tensor(out=out_, in0=out_, in1=neghalf[:, 0:n], op=ALU.pow), list(writes) + ["neghalf"], writes)

    def ld(q, out_, in_, w):
        sch.dma(q, "d_" + w[0], out_, in_, (), w)


    HALF = ARENA // 4
    stg = [arena[:, 0:2 * HALF].bitcast(F32), arena[:, 2 * HALF:4 * HALF].bitcast(F32)]
    eng_rr = [0]

    def stage_load(items):
        rounds, cur, off = [], [], 0
        for it in items:
            n = it[0].shape[-1]
            if off + n > HALF:
                rounds.append(cur)
                cur, off = [], 0
            cur.append((it, off, n))
            off += n
        rounds.append(cur)
        for ri, rnd in enumerate(rounds):
            b = ri % 2
            for ii, (it, off, n) in enumerate(rnd):
                q = ("sp", "act")[ii % 2]
                sch.dma(q, f"stg{b}_{q}", stg[b][:, off:off + n], it[1], (), [f"stg{b}_{q}"])
            for (it, off, n) in rnd:
                e = ("dve", "pool", "act")[eng_rr[0] % 3]
                eng_rr[0] += 1
                rd = [f"stg{b}_sp", f"stg{b}_act"] + ([it[4]] if len(it) > 4 else [])
                src = stg[b][:, off:off + n]
                if e == "act":
                    act(it[0], src, AF.Copy, rd, [it[3]], scale=it[2])
                else:
                    ts(e, it[0], src, it[2], None, ALU.mult, ALU.bypass, rd, [it[3]])

    ld("sp", ident_f[:], consts["ident"], ["ident_f"])
    ld("sp", utri[:], consts["utri"], ["utri"])
    ld("sp", ones_f[:], consts["ones"], ["ones_f"])
    ld("sp", m8[:], consts["m8"], ["m8"])
    ld("sp", hlb_t[:].rearrange("p h l -> p (h l)"), hlb, ["hlb_t"])
    ld("pool", ident_b[:], consts["ident"], ["ident_b"])
    ld("pool", ones_b[:], consts["ones"], ["ones_b"])
    ld("pool", negm_b[:], consts["negm"], ["negm_b"])
    ld("pool", m64[:], consts["m64"], ["m64"])
    ld("pool", pm_b[:], consts["pm"], ["pm_b"])
    ld("pool", pmT_b[:], consts["pmT"], ["pmT_b"])
    op("dve", lambda e: e.memset(neghalf[:], -0.5), (), ["neghalf"])
    act(hlb_t[:], hlb_t[:], AF.Exp, ["hlb_t"], ["hlb_t"])
    op("dve", lambda e: e.tensor_reduce(out=st8[:, 0:4], in_=hlb_t[:], axis=AX.X, op=ALU.add), ["hlb_t"], ["st8"])
    op("dve", lambda e: e.reciprocal(out=st8[:, 0:4], in_=st8[:, 0:4]), ["st8"], ["st8"])
    tt("dve", hlb_t[:], hlb_t[:], st8[:, 0:4].unsqueeze(2).broadcast_to([128, 4, NL]), ALU.mult, ["hlb_t", "st8"], ["hlb_t"])
    op("dve", lambda e: e.memset(lb_all[:], 0.0), (), ["lb_all"])
    for l in range(1, NL):
        tt("dve", lb_all[:, :, l], lb_all[:, :, l - 1], hlb_t[:, :, l], ALU.add, ["lb_all", "hlb_t"], ["lb_all"])

    NQ = L // 256
    NSC = L // 512
    layers = list(range(NL)) if layers is None else list(layers)

    def gci(s, c):
        return s * NCH + c

    for layer in layers:
        xsrc = x_in if layer == layers[0] else out
        xrd = ["out_d"] if layer != layers[0] else []
        sch.barrier()
        ld("sp", prew_t[:], prew[layer], ["prew_t"])
        ld("sp", convw_t[:], convw[layer], ["convw_t"])
        ld("sp", convb_t[:], convb_pp[layer], ["convb_t"])
        ld("sp", dtb_bc[:], dtb[layer].partition_broadcast(128), ["dtb_bc"])
        ld("sp", a_bc[:], alog[layer].partition_broadcast(128), ["a_bc"])
        ld("sp", d_bc[:], ssdd[layer].partition_broadcast(128), ["d_bc"])
        act(a_bc[:], a_bc[:], AF.Exp, ["a_bc"], ["a_bc"])
        ts("dve", a_bc[:], a_bc[:], -1.0, None, ALU.mult, ALU.bypass, ["a_bc"], ["a_bc"])
        cp("dve", lb_t[:], lb_all[:, :, layer], ["lb_all"], ["lb_t"])
        itemsA = []
        for k in range(8):
            itemsA.append((wtma[:, k, :], w_tma[layer, k * 128:(k + 1) * 128, :], prew_t[:, k:k + 1], "wtma", "prew_t"))
            itemsA.append((wfma[:, k, :], w_fma[layer, k * 128:(k + 1) * 128, :], prew_t[:, k:k + 1], "wfma", "prew_t"))
        stage_load(itemsA)
        sch.barrier()
        ld("sp", scan0, consts["scan0"], ["scan0"])
        ld("pool", convb_r, convb_row[layer], ["convb_r"])
        for k in range(4):
            for j in range(12):
                ts("dve", dconv[:, k, j, :], ident_b[:], convw_t[:, k * 12 + j:k * 12 + j + 1], None, ALU.mult, ALU.bypass,
                   ["ident_b", "convw_t"], ["dconv"])
        op("dve", lambda e: e.memset(qTa, 0.0), (), ["qTa"])
        op("dve", lambda e: e.memset(qTb, 0.0), (), ["qTb"])
        for s in range(2):
            op("dve", lambda e: e.memset(XT[s][:, :, 0:3], 0.0), (), [f"XT{s}"])
            op("dve", lambda e: e.memset(hst[s], 0.0), (), [f"hst{s}"])
            op("pool", lambda e: e.memset(hbf[s], 0.0), (), [f"hbf{s}"])
            op("dve", lambda e: e.memset(Sst[s], 0.0), (), [f"Sst{s}"])
            op("pool", lambda e: e.memset(Sbf[s], 0.0), (), [f"Sbf{s}"])

        xcnt = [0]
        if dbg and dbg.get("stop") == "A0":
            break
        for qd in range(NQ):
            quad = [(s, 2 * qd + pp) for s in range(2) for pp in range(2)]
            for qi, (s, c) in enumerate(quad):
                xb, xbn = xt[xcnt[0] % 2], f"xt{xcnt[0] % 2}"
                xcnt[0] += 1
                sch.dma("sp", "x" + xbn, xb, xsrc[s, c * T:(c + 1) * T, :], xrd, [xbn])
                act(xn, xb, AF.Square, [xbn], ["xn", "st8"], accum_out=st8[:, 0:1])
                rsqrt(st8[:, 1:2], st8[:, 0:1], 1.0 / D_MODEL, ["st8"], ["st8"], 1)
                ts("dve", xn, xb, st8[:, 1:2], None, ALU.mult, ALU.bypass, [xbn, "st8"], ["xn"])
                p0, p0n = nps()
                p0b = p0[:].bitcast(BF16)
                for k in range(8):
                    op("pe", lambda e: e.transpose(p0b[:, k * 128:(k + 1) * 128], xn[:, k * 128:(k + 1) * 128], ident_b[:]),
                       ["xn", "ident_b"], [p0n], inc=1 if k == 7 else 0)
                cp("act", hTq[:, :, qi * 128:(qi + 1) * 128], p0b.rearrange("p (k t) -> p k t", k=8), [p0n], ["hTq"])
                sch.dma("sp", "hts", hT_d[gci(s, c)].rearrange("p (k t) -> p k t", k=8), hTq[:, :, qi * 128:(qi + 1) * 128], ["hTq"], ["hT_d"])
            if dbg and dbg.get("stop") == "A1":
                break
            njc = int(dbg.get("njc", 16)) if dbg else 16
            for jc in range(njc):
                pf, pfn = nps()
                for k in range(8):
                    mm(pf[:], wfma[:, k, jc * 128:(jc + 1) * 128], hTq[:, k, :], k == 0, k == 7, ["hTq", "wfma"], [pfn])
                if dbg and dbg.get("evac") == "junk":
                    cp("act", xn[:, 0:512], pf[:], [pfn], ["xn"])
                elif jc < 12:
                    ev = dbg.get("evac", "same") if dbg else "same"
                    for s in range(2):
                        if ev == "act_only" and s == 1:
                            continue
                        if ev == "dve_only" and s == 0:
                            continue
                        c0_ = 4 if ev == "even" else 3
                        eng_ = "act" if s == 0 else "dve"
                        if ev == "swap":
                            eng_ = "dve" if s == 0 else "act"
                        if ev == "same":
                            eng_ = "act" if jc % 2 == 0 else "dve"
                        cp(eng_, XT[s][:, jc, c0_:c0_ + 256], pf[:, s * 256:(s + 1) * 256], [pfn], [f"XT{s}"])
                else:
                    act(qsq[:, jc - 12, :], pf[:], AF.Silu, [pfn], ["qsq"])
            if dbg and dbg.get("stop") == "A15":
                break
            for s in range(2):
                for half in range(2):
                    pb, pbn = nps()
                    for j2 in range(2):
                        jj = half * 2 + j2
                        j = 8 + jj
                        for k in range(4):
                            mm(pb[:, j2 * 256:(j2 + 1) * 256], dconv[:, k, j, :], XT[s][:, j, k:k + 256], k == 0, k == 3, [f"XT{s}", "dconv"], [pbn],
                               inc=1 if (k == 3 and j2 == 1) else 0)
                    for j2 in range(2):
                        jj = half * 2 + j2
                        act(bctq[:, jj, s * 256:(s + 1) * 256], pb[:, j2 * 256:(j2 + 1) * 256], AF.Silu, [pbn, "convb_t"], ["bctq"],
                            bias=convb_t[:, 8 + jj:9 + jj])
            if dbg and dbg.get("stop") == "A2":
                break
            for qi, (s, c) in enumerate(quad):
                pp_ = c % 2
                tsl = slice(qi * 128, (qi + 1) * 128)
                XTs, XTn = XT[s], f"XT{s}"
                hs, hsn, hb, hbn = hst[s], f"hst{s}", hbf[s], f"hbf{s}"
                Ss, Ssn, Sb, Sbn = Sst[s], f"Sst{s}", Sbf[s], f"Sbf{s}"

                def tm_slab(c0, n):
                    pz, pzn = nps()
                    for k in range(8):
                        mm(pz[:, 0:n], hTq[:, k, tsl], wtma[:, k, c0:c0 + n], k == 0, k == 7, ["hTq", "wtma"], [pzn])
                    return pz, pzn
                for h2 in range(2):
                    pz, pzn = tm_slab(h2 * 512, 512)
                    act(zs[:, h2 * 512:(h2 + 1) * 512], pz[:], AF.Silu, [pzn], ["zs"])
                pz, pzn = tm_slab(1040 + 512, 512)
                act(gs, pz[:], AF.Silu, [pzn], ["gs"])
                pz, pzn = tm_slab(1040, 512)
                cp("act", vtok, pz[:], [pzn], ["vtok"])
                pd, pdn = tm_slab(1024, 16)
                tt("dve", dtmp[:], pd[:, 0:16], dtb_bc[:], ALU.add, [pdn, "dtb_bc"], ["dtmp"])
                pfq, pfqn = nps()
                for jj in range(4):
                    for k in range(8):
                        mm(pfq[:, jj * 128:(jj + 1) * 128], wfma[:, k, (16 + jj) * 128:(17 + jj) * 128], hTq[:, k, tsl], k == 0, k == 7,
                           ["hTq", "wfma"], [pfqn], inc=1 if (k == 7 and jj == 3) else 0)
                pc = [nps() for _ in range(3)]
                w0 = pp_ * 128
                for j in range(10):
                    pcj, pcjn = pc[j // 4]
                    o_ = pcj[:, (j % 4) * 128:(j % 4 + 1) * 128]
                    mm(o_, ones_b[0:1, :], convb_r[0:1, j * 128:(j + 1) * 128], True, False, ["ones_b", "convb_r"], [pcjn])
                    for k in range(4):
                        last = (k == 3)
                        mm(o_, XTs[:, j, w0 + k:w0 + k + 128], dconv[:, k, j, :], False, last, [XTn, "dconv"], [pcjn],
                           inc=1 if (last and (j % 4 == 3 or j == 9)) else 0)
                act(xs[:, 0:512], pc[0][0][:], AF.Silu, [pc[0][1]], ["xs"])
                act(xs[:, 512:1024], pc[1][0][:], AF.Silu, [pc[1][1]], ["xs"])
                act(btok, pc[2][0][:, 0:256], AF.Silu, [pc[2][1]], ["btok"])
                act(dtmp[:], dtmp[:], AF.Exp, ["dtmp"], ["dtmp"])
                act(dt_t[:], dtmp[:], AF.Ln, ["dtmp"], ["dt_t"], bias=1.0)
                act(ef, pfq[:], AF.Exp, [pfqn], ["ef"], scale=-1.0)
                tt("dve", dtA[:], dt_t[:], a_bc[:], ALU.mult, ["dt_t", "a_bc"], ["dtA"])
                pa, pan = nps()
                mm(pa[:, 0:16], utri[:], dtA[:], True, True, ["utri", "dtA"], [pan], inc=0)
                mm(pa[:, 16:32], ones_f[:], dtA[:], True, True, ["ones_f", "dtA"], [pan])
                cp("dve", acum[:], pa[:, 0:16], [pan], ["acum"])
                ts("dve", nacum[:], pa[:, 0:16], -1.0, None, ALU.mult, ALU.bypass, [pan], ["nacum"])
                act(eacum[:], pa[:, 0:16], AF.Exp, [pan], ["eacum"])
                act(cdec[:], pa[:, 16:32], AF.Exp, [pan], ["cdec"])
                tt("dve", dte[:], pa[:, 16:32], acum[:], ALU.subtract, [pan, "acum"], ["dte"])
                act(dte[:], dte[:], AF.Exp, ["dte"], ["dte"])
                tt("dve", dte[:], dte[:], dt_t[:], ALU.mult, ["dte", "dt_t"], ["dte"])
                xs3 = xs.rearrange("p (r q) -> p r q", r=16)
                tt("dve", xdt.rearrange("p (r q) -> p r q", r=16), xs3, dt_t[:].unsqueeze(2).broadcast_to([128, 16, 64]), ALU.mult,
                   ["xs", "dt_t"], ["xdt"])
                tt("dve", xw.rearrange("p (r q) -> p r q", r=16), xs3, dte[:].unsqueeze(2).broadcast_to([128, 16, 64]), ALU.mult,
                   ["xs", "dte"], ["xw"])
                tt("pool", xsd.rearrange("p (r q) -> p r q", r=16), xs3, d_bc[:].unsqueeze(2).broadcast_to([128, 16, 64]), ALU.mult,
                   ["xs", "d_bc"], ["xsd"])
                for g in range(2):
                    psc, pscn = nps()
                    mm(psc[:, 0:128], bctq[:, g, tsl], bctq[:, 2 + g, tsl], True, True, ["bctq"], [pscn])
                    cp("dve", scT[:, g, :], psc[:, 0:128], [pscn], ["scT"])
                    pab = [nps(), nps()]
                    for r in range(8):
                        pq, pqn = pab[r // 4]
                        o_ = pq[:, (r % 4) * 128:(r % 4 + 1) * 128]
                        hh = g * 8 + r
                        mm(o_, dtA[:, hh:hh + 1].broadcast_to([128, 128]), utri[:], True, False, ["dtA", "utri"], [pqn])
                        mm(o_, ident_b[:], negm_b[:], False, True, ["ident_b", "negm_b"], [pqn], inc=1 if r % 4 == 3 else 0)
                    for r in range(8):
                        pq, pqn = pab[r // 4]
                        hh = g * 8 + r
                        act(dec[:, r, :], pq[:, (r % 4) * 128:(r % 4 + 1) * 128], AF.Exp, [pqn, "nacum"], ["dec"], bias=nacum[:, hh:hh + 1])
                    tt("dve", MT, dec, scT[:, g:g + 1, :].broadcast_to([128, 8, 128]), ALU.mult, ["dec", "scT"], ["MT"])
                    py, pyn = nps()
                    mm(py[:], ident_b[:], xsd[:, g * 512:(g + 1) * 512], True, False, ["ident_b", "xsd"], [pyn])
                    for r in range(8):
                        hh = g * 8 + r
                        mm(py[:, r * 64:(r + 1) * 64], MT[:, r, :], xdt[:, hh * 64:(hh + 1) * 64], False, r == 7, ["MT", "xdt"], [pyn])
                    po, pon = nps()
                    mm(po[:], bctq[:, 2 + g, tsl], hb[:, g * 512:(g + 1) * 512], True, True, ["bctq", hbn], [pon])
                    tt("dve", ytmp.rearrange("p (r q) -> p r q", r=8), po[:].rearrange("p (r q) -> p r q", r=8),
                       eacum[:, g * 8:(g + 1) * 8].unsqueeze(2).broadcast_to([128, 8, 64]), ALU.mult, [pon, "eacum"], ["ytmp"])
                    tt("dve", ytmp, ytmp, py[:], ALU.add, ["ytmp", pyn], ["ytmp"])
                    tt("dve", yg, ytmp, zs[:, g * 512:(g + 1) * 512], ALU.mult, ["ytmp", "zs"], ["yg"])
                    act(junk[:, 0:512], yg, AF.Square, ["yg"], ["dec", "st8"], accum_out=st8[:, 2:3])
                    rsqrt(st8[:, 3:4], st8[:, 2:3], 1.0 / 512, ["st8"], ["st8"], 1)
                    ts("dve", mix[:, g * 512:(g + 1) * 512], yg, st8[:, 3:4], None, ALU.mult, ALU.bypass, ["yg", "st8"], ["mix"])
                    ph, phn = nps()
                    mm(ph[:], btok[:, g * 128:(g + 1) * 128], xw[:, g * 512:(g + 1) * 512], True, True, ["btok", "xw"], [phn])
                    hv = hs[:, g * 512:(g + 1) * 512]
                    tt("dve", hv.rearrange("p (r q) -> p r q", r=8), hv.rearrange("p (r q) -> p r q", r=8),
                       cdec[:, g * 8:(g + 1) * 8].unsqueeze(2).broadcast_to([128, 8, 64]), ALU.mult, [hsn, "cdec"], [hsn])
                    tt("dve", hv, hv, ph[:], ALU.add, [hsn, phn], [hsn])
                    cp("pool", hb[:, g * 512:(g + 1) * 512], hv, [hsn], [hbn])
                act(L2, ef, AF.Ln, ["ef"], ["L2"], bias=1.0)
                for h in range(4):
                    act(L1[:, h * 128:(h + 1) * 128], ef[:, h * 128:(h + 1) * 128], AF.Ln, ["ef", "lb_t"], ["L1"], bias=1.0, scale=lb_t[:, h:h + 1])
                tt("dve", L1, L1, L2, ALU.subtract, ["L1", "L2"], ["L1"])
                act(L2, L1, AF.Exp, ["L1"], ["L2"])
                ts("dve", L2, L2, -1.0, 1.0, ALU.mult, ALU.add, ["L2"], ["L2"])
                op("dve", lambda e: e.tensor_tensor_scan(out=cum, data0=scan0, data1=L1, initial=0.0, op0=ALU.mult, op1=ALU.add),
                   ["scan0", "L1"], ["cum"])
                cum4 = cum.rearrange("p (j t) -> p j t", j=8)
                act(ecend[:], cum4[:, :, 63], AF.Exp, ["cum"], ["ecend"])
                act(ecum, cum, AF.Exp, ["cum"], ["ecum"])
                q4 = qsq[:, :, tsl]
                e4 = ecum.rearrange("p (h t) -> p h t", h=4)
                tt("dve", qTa[:, :, 0:64], q4[:, :, 0:64], e4[:, :, 0:64], ALU.mult, ["qsq", "ecum"], ["qTa"])
                tt("dve", qTb[:, :, 64:128], q4[:, :, 64:128], e4[:, :, 64:128], ALU.mult, ["qsq", "ecum"], ["qTb"])
                act(ecum, cum, AF.Exp, ["cum"], ["ecum"], scale=-1.0)
                tt("dve", ecum, L2, ecum, ALU.mult, ["L2", "ecum"], ["ecum"])
                cp("pool", kT.rearrange("p h t -> p (h t)"), ecum, ["ecum"], ["kT"])
                tt("dve", kend.rearrange("p h (b t) -> p (h b) t", b=2), ecum.rearrange("p (j t) -> p j t", j=8),
                   ecend[:].unsqueeze(2).broadcast_to([128, 8, 64]), ALU.mult, ["ecum", "ecend"], ["kend"])
                pk, pkn = nps()
                pkb = pk[:].bitcast(BF16)
                for h in range(4):
                    op("pe", lambda e: e.transpose(pkb[:, h * 128:(h + 1) * 128], kend[:, h, :], ident_b[:]), ["kend", "ident_b"], [pkn],
                       inc=1 if h == 3 else 0)
                cp("act", kendT.rearrange("p h t -> p (h t)"), pkb[:, 0:512], [pkn], ["kendT"])
                pat, patn = nps()
                for h in range(4):
                    mm(pat[:, h * 128:(h + 1) * 128], kT[:, h, :], qTa[:, h, :], True, False, ["kT", "qTa"], [patn])
                    mm(pat[:, h * 128:(h + 1) * 128], kT[:, h, :], qTb[:, h, :], False, True, ["kT", "qTb"], [patn], inc=1 if h == 3 else 0)
                tt("dve", attn, pat[:].rearrange("p (h t) -> p h t", h=4), m64[:].unsqueeze(1).broadcast_to([128, 4, 128]), ALU.mult,
                   [patn, "m64"], ["attn"])
                pho, phon = nps()
                for h in range(4):
                    o_ = pho[:, h * 128:(h + 1) * 128]
                    mm(o_, attn[:, h, :], vtok[:, h * 128:(h + 1) * 128], h == 0, False, ["attn", "vtok"], [phon])
                    mm(o_, qTa[:, h, :], Sb[:, h, :], False, False, ["qTa", Sbn], [phon])
                for b2 in range(2):
                    pst, pstn = nps()
                    for h in range(4):
                        mm(pst[:, h * 128:(h + 1) * 128], kendT[b2 * 64:(b2 + 1) * 64, h, :], vtok[b2 * 64:(b2 + 1) * 64, h * 128:(h + 1) * 128],
                           True, True, ["kendT", "vtok"], [pstn], inc=1 if h == 3 else 0)
                    ec = ecend[:].rearrange("p (h b) -> p h b", b=2)[:, :, b2:b2 + 1].broadcast_to([128, 4, 128])
                    tt("dve", Ss, Ss, ec, ALU.mult, [Ssn, "ecend"], [Ssn])
                    tt("dve", Ss.rearrange("p h v -> p (h v)"), Ss.rearrange("p h v -> p (h v)"), pst[:], ALU.add, [Ssn, pstn], [Ssn])
                    cp("pool", Sb, Ss, [Ssn], [Sbn])
                    if b2 == 0:
                        for h in range(4):
                            mm(pho[:, h * 128:(h + 1) * 128], qTb[:, h, :], Sb[:, h, :], False, h == 3, ["qTb", Sbn], [phon], inc=1 if h == 3 else 0)
                for h in range(4):
                    act(junk[:, 0:128], pho[:, h * 128:(h + 1) * 128], AF.Square, [phon], ["dec", "st8"], accum_out=st8[:, 4 + h:5 + h])
                rsqrt(st8[:, 4:8], st8[:, 4:8], 1.0 / 128, ["st8"], ["st8"], 4)
                tt("dve", otmp.rearrange("p (h v) -> p h v", h=4), pho[:].rearrange("p (h v) -> p h v", h=4),
                   st8[:, 4:8].unsqueeze(2).broadcast_to([128, 4, 128]), ALU.mult, [phon, "st8"], ["otmp"])
                tt("dve", mix[:, 1024:1536], otmp, gs, ALU.mult, ["otmp", "gs"], ["mix"])
                sch.dma("sp", "mts", mixA_d[s, c * T:(c + 1) * T, :], mix[:, 0:1536], ["mix"], ["mixA_d"])
            for s in range(2):
                cp("pool", XT[s][:, :, 0:3], XT[s][:, :, 256:259], [f"XT{s}"], [f"XT{s}"])

        if dbg and dbg.get("stop") in ("A", "A1", "A2", "A15"):
            break
        sch.barrier()
        ld("sp", mixnw_t[:], mixnw[layer], ["mixnw_t"])
        itemsB = []
        for k in range(8):
            itemsB.append((wb_v[:, k, :], w_b[layer, k * 128:(k + 1) * 128, :], prew_t[:, k:k + 1], "wb", "prew_t"))
        for k in range(16):
            itemsB.append((wout_v[:, k, :], w_out[layer, k * 128:(k + 1) * 128, :], mixnw_t[:, k:k + 1], "wout", "mixnw_t"))
        for k in range(4):
            itemsB.append((glu_v[:, k, :], gluw[layer, k * 128:(k + 1) * 128, :], 0.5, "glu"))
        stage_load(itemsB)
        sch.barrier()
        ld("sp", postw_bc, postw[layer].partition_broadcast(128), ["postw_bc"])
        ld("sp", s5d_t[:], s5d[layer], ["s5d_t"])
        ld("pool", glub_r, glub[layer], ["glub_r"])
        ld("sp", g32[:, 0, :], lamre[layer], ["g_lr"])
        ld("sp", g32[:, 1, :], lamim[layer], ["g_li"])
        ld("sp", g32[:, 2, :], lstep[layer].partition_broadcast(64), ["g_st"])
        ld("sp", Bre_t, bre[layer], ["Bre"])
        ld("sp", Bim_t, bim[layer], ["Bim"])
        ld("sp", Cre_t, cre[layer], ["Cre"])
        ld("sp", Cim_t, cim[layer], ["Cim"])
        GN = ["g32"]

        def gq(i):
            return g32[:, i, :]

        def gmul(o_, a, b):
            tt("dve", gq(o_), gq(a), gq(b), ALU.mult, GN + ["g_lr", "g_li", "g_st"], GN)

        def gadd(o_, a, b, o2=ALU.add):
            tt("dve", gq(o_), gq(a), gq(b), o2, GN, GN)

        def gts(o_, a, m_, a_):
            ts("dve", gq(o_), gq(a), m_, a_, ALU.mult, ALU.add, GN + ["g_lr", "g_li", "g_st"], GN)

        I_LR, I_LI, I_ST, I_X, I_ANG, I_MAG, I_MAGI, I_C, I_S, I_T1, I_T2, I_LRE, I_LIM, I_IRE, I_IIM, I_CRE, I_CIM, I_8RE, I_8IM, I_Y, I_P, I_NX, I_A16, I_DEN = range(24)
        act(gq(I_ST), gq(I_ST), AF.Exp, ["g_st"], GN + ["g_st"])
        ts("dve", gq(I_LR), gq(I_LR), -1e-4, None, ALU.min, ALU.bypass, ["g_lr"], GN + ["g_lr"])
        gmul(I_X, I_LR, I_ST)
        gmul(I_ANG, I_LI, I_ST)
        gts(I_NX, I_X, -1.0, 0.0)

        def expser(o_, xi):
            gts(o_, xi, 1.0 / 6, 1.0)
            for kf in (5, 4, 3, 2, 1):
                gmul(o_, o_, xi)
                gts(o_, o_, 1.0 / kf, 1.0)
        expser(I_MAG, I_X)
        expser(I_MAGI, I_NX)
        gts(I_A16, I_ANG, 1.0 / 16, 0.0)
        gmul(I_Y, I_A16, I_A16)
        sc_ = [1.0, -1.0 / 6, 1.0 / 120, -1.0 / 5040, 1.0 / 362880, -1.0 / 39916800, 1.0 / 6227020800]
        cc_ = [1.0, -0.5, 1.0 / 24, -1.0 / 720, 1.0 / 40320, -1.0 / 3628800, 1.0 / 479001600, -1.0 / 87178291200]
        gts(I_P, I_Y, sc_[6], sc_[5])
        for kf in (4, 3, 2, 1, 0):
            gmul(I_P, I_P, I_Y)
            gts(I_P, I_P, 1.0, sc_[kf])
        gmul(I_S, I_P, I_A16)
        gts(I_P, I_Y, cc_[7], cc_[6])
        for kf in (5, 4, 3, 2, 1, 0):
            gmul(I_P, I_P, I_Y)
            gts(I_P, I_P, 1.0, cc_[kf])
        gts(I_C, I_P, 1.0, 0.0)

        def csq(re, im):
            gmul(I_T1, re, re)
            gmul(I_T2, im, im)
            gmul(im, re, im)
            gts(im, im, 2.0, 0.0)
            gadd(re, I_T1, I_T2, ALU.subtract)
        for _ in range(4):
            csq(I_C, I_S)
        gmul(I_LRE, I_MAG, I_C)
        gmul(I_LIM, I_MAG, I_S)
        gmul(I_IRE, I_MAGI, I_C)
        gmul(I_IIM, I_MAGI, I_S)
        gts(I_IIM, I_IIM, -1.0, 0.0)
        gts(I_P, I_LRE, 1.0, -1.0)
        gmul(I_T1, I_LR, I_LR)
        gmul(I_T2, I_LI, I_LI)
        gadd(I_DEN, I_T1, I_T2)
        op("dve", lambda e: e.reciprocal(out=gq(I_DEN), in_=gq(I_DEN)), GN, GN)
        gmul(I_T1, I_P, I_LR)
        gmul(I_T2, I_LIM, I_LI)
        gadd(I_CRE, I_T1, I_T2)
        gmul(I_CRE, I_CRE, I_DEN)
        gmul(I_T1, I_LIM, I_LR)
        gmul(I_T2, I_P, I_LI)
        gadd(I_CIM, I_T1, I_T2, ALU.subtract)
        gmul(I_CIM, I_CIM, I_DEN)
        gts(I_8RE, I_LRE, 1.0, 0.0)
        gts(I_8IM, I_LIM, 1.0, 0.0)
        for _ in range(3):
            csq(I_8RE, I_8IM)
        cp("dve", LA[:, 0, :], gq(I_8RE), GN, ["LA"])
        cp("dve", LA[:, 1, :], gq(I_8RE), GN, ["LA"])
        ts("dve", LB[:, 0, :], gq(I_8IM), -1.0, None, ALU.mult, ALU.bypass, GN, ["LB"])
        cp("dve", LB[:, 1, :], gq(I_8IM), GN, ["LB"])

        def cmul(eng, ore, oim, are, aim, xre, xim, n, rd, wr):
            ab = lambda a_: a_.unsqueeze(2).broadcast_to([64, 4, n])
            tv1, tv2 = T1[:, :, 0:n], T2[:, :, 0:n]
            tt(eng, tv1, xre, ab(are), ALU.mult, rd + GN, ["T1"])
            tt(eng, tv2, xim, ab(aim), ALU.mult, rd + GN, ["T2"])
            tt(eng, ore, tv1, tv2, ALU.subtract, ["T1", "T2"], wr)
            tt(eng, tv1, xim, ab(are), ALU.mult, rd + GN, ["T1"])
            tt(eng, tv2, xre, ab(aim), ALU.mult, rd + GN, ["T2"])
            tt(eng, oim, tv1, tv2, ALU.add, ["T1", "T2"], wr)

        for gb in range(8):
            g0 = gb * 4
            sl = slice(g0, g0 + 4)
            Z4r = Zre.rearrange("p g (s h) -> p g s h", s=8)
            Z4i = Zim.rearrange("p g (s h) -> p g s h", s=8)
            Y4r = Yre.rearrange("p g (s h) -> p g s h", s=8)
            Y4i = Yim.rearrange("p g (s h) -> p g s h", s=8)
            Bv = lambda tile_: tile_[0:64, g0 * 16:(g0 + 4) * 16].rearrange("p (g h) -> p g h", g=4)
            cmul("dve", Z4r[:, :, 7, :], Z4i[:, :, 7, :], gq(I_CRE)[:, sl], gq(I_CIM)[:, sl], Bv(Bre_t), Bv(Bim_t), 16, ["Bre", "Bim"], ["Zre", "Zim"])
            for s8 in range(6, -1, -1):
                cmul("dve", Z4r[:, :, s8, :], Z4i[:, :, s8, :], gq(I_LRE)[:, sl], gq(I_LIM)[:, sl], Z4r[:, :, s8 + 1, :], Z4i[:, :, s8 + 1, :], 16,
                     ["Zre", "Zim"], ["Zre", "Zim"])
            cp("dve", Y4r[:, :, 7, :], Bv(Cre_t), ["Cre"], ["Yre"])
            cp("dve", Y4i[:, :, 7, :], Bv(Cim_t), ["Cim"], ["Yim"])
            for l8 in range(6, -1, -1):
                cmul("dve", Y4r[:, :, l8, :], Y4i[:, :, l8, :], gq(I_IRE)[:, sl], gq(I_IIM)[:, sl], Y4r[:, :, l8 + 1, :], Y4i[:, :, l8 + 1, :], 16,
                     ["Yre", "Yim"], ["Yre", "Yim"])
            ts("dve", T1, Yim, -1.0, None, ALU.mult, ALU.bypass, ["Yim"], ["T1"])
            for hb_ in range(1):
                pt_, ptn = nps()
                for gi in range(4):
                    gg = gi
                    mm(pt_[:, gi * 128:(gi + 1) * 128], Zre[:, gg, :], Yre[:, gg, :], True, False, ["Zre", "Yre"], [ptn])
                    mm(pt_[:, gi * 128:(gi + 1) * 128], Zim[:, gg, :], T1[:, gg, :], False, True, ["Zim", "T1"], [ptn], inc=1 if gi == 3 else 0)
                tt("dve", tz_v[:, g0 + hb_ * 4:g0 + hb_ * 4 + 4, :], pt_[:].rearrange("p (g n) -> p g n", g=4),
                   m8[:].unsqueeze(1).broadcast_to([128, 4, 128]), ALU.mult, [ptn, "m8"], ["tz"])
            for (Zt, Zn, dst, dn) in ((Zre, "Zre", wsre_v, "wsre"), (Zim, "Zim", wsim_v, "wsim")):
                pw_, pwn = nps()
                for gi in range(4):
                    mm(pw_[:, gi * 64:(gi + 1) * 64], Zt[:, gi, :], ident_f[0:64, 0:64], True, True, [Zn, "ident_f"], [pwn], inc=1 if gi == 3 else 0)
                cp("act", dst[:, sl, :], pw_[:, 0:256].rearrange("p (g n) -> p g n", g=4), [pwn], [dn])
            a8 = lambda i: gq(i)[:, sl].unsqueeze(2).broadcast_to([64, 4, 128])
            tt("dve", T1, Yre, a8(I_8RE), ALU.mult, ["Yre"] + GN, ["T1"])
            tt("dve", T2, Yim, a8(I_8IM), ALU.mult, ["Yim"] + GN, ["T2"])
            tt("dve", wore_v[:, sl, :], T1, T2, ALU.subtract, ["T1", "T2"], ["wore"])
            tt("dve", T1, Yim, a8(I_8RE), ALU.mult, ["Yim"] + GN, ["T1"])
            tt("dve", T2, Yre, a8(I_8IM), ALU.mult, ["Yre"] + GN, ["T2"])
            tt("dve", T1, T1, T2, ALU.add, ["T1", "T2"], ["T1"])
            ts("dve", woim_v[:, sl, :], T1, -1.0, None, ALU.mult, ALU.bypass, ["T1"], ["woim"])
        op("dve", lambda e: e.memset(S2, 0.0), (), ["S2"])
        sch.barrier()
        op("pool", lambda e: e.memset(gst[:, :, :, :, 0], 0.0), (), ["gsts", "gstw"])

        if dbg and dbg.get("stop") == "B0":
            break
        def f_loads(sc):
            tiles = [(s, 4 * sc + cc) for s in range(2) for cc in range(4)]
            for ti, (s, c) in enumerate(tiles):
                sch.dma("sp", "htl", hT8[:, :, ti * 128:(ti + 1) * 128], hT_d[gci(s, c)].rearrange("p (k t) -> p k t", k=8), ["hT_d"], ["hT8"])

        def f_uproj(sc):
            for j in range(4):
                for hf in range(2):
                    pu, pun = nps()
                    for s4 in range(4):
                        s8 = hf * 4 + s4
                        for k in range(8):
                            rhs = hT8[:, k, :].rearrange("p (n s) -> p s n", s=8)[:, s8, :]
                            mm(pu[:, s4 * 128:(s4 + 1) * 128], wb_v[:, k, j * 128:(j + 1) * 128], rhs, k == 0, k == 7, ["hT8", "wb"], [pun],
                               inc=1 if (k == 7 and s4 == 3) else 0)
                    cp("act" if hf == 0 else "dve", uT8[:, j, hf * 512:(hf + 1) * 512], pu[:], [pun], ["uT8"])
            for g8 in range(8):
                for s8 in range(8):
                    qd_ = "sp"
                    sch.dma(qd_, "blk_" + qd_, ublk8[16 * s8:16 * s8 + 16, :, :].rearrange("p (j g) n -> p j g n", j=4)[:, :, g8, :],
                            uT8[16 * g8:16 * g8 + 16, :, s8 * 128:(s8 + 1) * 128], ["uT8"], ["ublk8_" + qd_])

        def f_statein(sc):
            for q4_ in range(4):
                for ri, (wsv, wsn) in enumerate(((wsre_v, "wsre"), (wsim_v, "wsim"))):
                    pw2 = [nps(), nps()]
                    for g8 in range(8):
                        g = q4_ * 8 + g8
                        pw_, pwn = pw2[g8 // 4]
                        mm(pw_[0:64, (g8 % 4) * 128:(g8 % 4 + 1) * 128], wsv[:, g, :], ublk8[:, g, :], g8 % 4 == 0, g8 % 4 == 3, [wsn, "ublk8_sp"], [pwn])
                    for hb_ in range(2):
                        pw_, pwn = pw2[hb_]
                        g0_ = q4_ * 8 + hb_ * 4
                        cp("act" if hb_ == 0 else "dve", gst[:, ri, g0_:g0_ + 4, :, 1:65],
                           pw_[0:64, :].rearrange("p (g s m) -> p g s m", g=4, s=2), [pwn], ["gstw", "gsts"])

        def f_rec(m0, m1):
            LAb = LA[:].unsqueeze(3).broadcast_to([64, 2, 32, 2])
            for m in range(m0, m1):
                tt("dve", rt, S2, LAb, ALU.mult, ["S2", "LA"], ["rt"])
                tt("dve", ru[:, 0, :, :], S2[:, 1, :, :], LB[:, 0, :].unsqueeze(2).broadcast_to([64, 32, 2]), ALU.mult, ["S2", "LB"], ["ru"])
                tt("dve", ru[:, 1, :, :], S2[:, 0, :, :], LB[:, 1, :].unsqueeze(2).broadcast_to([64, 32, 2]), ALU.mult, ["S2", "LB"], ["ru"])
                tt("dve", rt, rt, ru, ALU.add, ["rt", "ru"], ["rt"])
                tt("dve", S2, rt, gst[:, :, :, :, 1 + m], ALU.add, ["rt", "gstw"], ["S2"])
                cp("act", gst[:, :, :, :, 1 + m], S2, ["S2"], ["gsts"])

        def f_y(sc):
            for q4_ in range(4):
                py2 = [nps(), nps()]
                for g8 in range(8):
                    g = q4_ * 8 + g8
                    py_, pyn_ = py2[g8 // 4]
                    o_ = py_[:, (g8 % 4) * 128:(g8 % 4 + 1) * 128]
                    mm(o_, tz_v[:, g, :], ublk8[:, g, :], g8 % 4 == 0, False, ["tz", "ublk8_sp"], [pyn_])
                    for s_ in range(2):
                        mm(o_[:, s_ * 64:(s_ + 1) * 64], wore_v[:, g, :], gst[:, 0, g, s_, 0:64], False, False, ["wore", "gsts"], [pyn_])
                        mm(o_[:, s_ * 64:(s_ + 1) * 64], woim_v[:, g, :], gst[:, 1, g, s_, 0:64], False, g8 % 4 == 3 and s_ == 1, ["woim", "gsts"], [pyn_])
                for hb_ in range(2):
                    py_, pyn_ = py2[hb_]
                    g0_ = q4_ * 8 + hb_ * 4
                    ub = ublk8[:, g0_:g0_ + 4, :]
                    tt("dve", ytB.rearrange("p (g n) -> p g n", g=4), ub, s5d_t[:, g0_:g0_ + 4].unsqueeze(2).broadcast_to([128, 4, 128]), ALU.mult,
                       ["ublk8_sp", "s5d_t"], ["ytB"])
                    tt("dve", ytB, ytB, py_[:], ALU.add, ["ytB", pyn_], ["ytB"])
                    act(ygB, ytB, AF.Square, ["ytB"], ["ygB"])
                    ts("dve", ygB, ygB, GELU_C2, GELU_C1, ALU.mult, ALU.add, ["ygB"], ["ygB"])
                    tt("dve", ygB, ygB, ytB, ALU.mult, ["ygB", "ytB"], ["ygB"])
                    act(ygB, ygB, AF.Tanh, ["ygB"], ["ygB"])
                    op("dve", lambda e: e.scalar_tensor_tensor(out=gyb8[:, g0_:g0_ + 4, :].rearrange("p g n -> p (g n)"), in0=ygB, scalar=1.0, in1=ytB,
                                                                op0=ALU.add, op1=ALU.mult), ["ygB", "ytB"], ["gyb8"])
            cp("pool", gst[:, :, :, :, 0], gst[:, :, :, :, 64], ["gsts"], ["gsts"])
            for g8 in range(8):
                for l8 in range(8):
                    qd_ = "sp"
                    sch.dma(qd_, "ubl_" + qd_, gyT8d[sc % 2][16 * g8:16 * g8 + 16, :, l8 * 128:(l8 + 1) * 128],
                            gyb8[16 * l8:16 * l8 + 16, :, :].rearrange("p (j g) n -> p j g n", j=4)[:, :, g8, :], ["gyb8"], [f"gyT8_{sc % 2}"])

        def f_gates(sc):
            for l8 in range(8):
                pg_, pgn = nps()
                for k in range(8):
                    lhs = hT8[:, k, :].rearrange("p (n l) -> p l n", l=8)[:, l8, :]
                    mm(pg_[:], lhs, wb_v[:, k, 512:1024], k == 0, k == 7, ["hT8", "wb"], [pgn])
                act(gatesB[:, l8, :], pg_[:], AF.Silu, [pgn], ["gatesB"])

        def f_tile(sc, l8):
            base = 512 * sc
            if True:
                xb, xbn = xtB[l8 % 2], f"xtB{l8 % 2}"
                for s_ in range(2):
                    sch.dma("pool", f"x{xbn}_{s_}", xb[64 * s_:64 * s_ + 64, :],
                            xsrc[s_, base:base + 512, :].rearrange("(m l) d -> l m d", l=8)[l8], xrd, [f"{xbn}_{s_}"])
                    sch.dma("pool", f"mxl_{s_}", mixB[64 * s_:64 * s_ + 64, 0:1536],
                            mixA_d[s_, base:base + 512, :].rearrange("(m l) d -> l m d", l=8)[l8], ["mixA_d"], [f"mixB_{s_}"])
                pv = [nps(), nps()]
                for hf in range(2):
                    pp, ppn = pv[hf]
                    mm(pp[:], ones_b[0:1, :], glub_r[0:1, hf * 512:(hf + 1) * 512], True, False, ["ones_b", "glub_r"], [ppn])
                    for j in range(4):
                        mm(pp[:], gyT8d[sc % 2][:, j, l8 * 128:(l8 + 1) * 128], glu_v[:, j, hf * 512:(hf + 1) * 512], False, j == 3,
                           [f"gyT8_{sc % 2}", "glu"], [ppn])
                act(L1B, pv[1][0][:], AF.Tanh, [pv[1][1]], ["ytB"], scale=0.5)
                ts("dve", L1B, L1B, 0.5, 0.5, ALU.mult, ALU.add, ["ytB"], ["ytB"])
                tt("dve", L2B, pv[0][0][:], L1B, ALU.mult, [pv[0][1], "ytB"], ["ygB"])
                tt("dve", L2B, L2B, gatesB[:, l8, :], ALU.mult, ["ygB", "gatesB"], ["ygB"])
                act(junkB[:, 0:512], L2B, AF.Square, ["ygB"], ["junkB", "st8"], accum_out=st8[:, 2:3])
                rsqrt(st8[:, 3:4], st8[:, 2:3], 1.0 / 512, ["st8"], ["st8"], 1)
                ts("dve", mixB[:, 1536:2048], L2B, st8[:, 3:4], None, ALU.mult, ALU.bypass, ["ygB", "st8"], ["mixB_S"])
                pm1, pm1n = nps()
                pm2, pm2n = nps()
                pm1b, pm2b = pm1[:].bitcast(BF16), pm2[:].bitcast(BF16)
                for j in range(16):
                    dst, dn = (pm1b, pm1n) if j < 8 else (pm2b, pm2n)
                    jj = j % 8
                    op("pe", lambda e: e.transpose(dst[:, jj * 128:(jj + 1) * 128], mixB[:, j * 128:(j + 1) * 128], ident_b[:]),
                       ["mixB_0", "mixB_1", "mixB_S", "ident_b"], [dn], inc=1 if j in (7, 15) else 0)
                mT2 = mixTB.rearrange("p k t -> p (k t)")
                cp("act", mT2[:, 0:1024], pm1b, [pm1n], ["mixTB"])
                cp("dve", mT2[:, 1024:2048], pm2b, [pm2n], ["mixTB"])
                po_ = [nps(), nps()]
                for n2 in range(2):
                    pp, ppn = po_[n2]
                    for kk in range(16):
                        mm(pp[:], mixTB[:, kk, :], wout_v[:, kk, n2 * 512:(n2 + 1) * 512], kk == 0, kk == 15, ["mixTB", "wout"], [ppn])
                for n2 in range(2):
                    act(junkB[:, n2 * 512:(n2 + 1) * 512], po_[n2][0][:], AF.Square, [po_[n2][1]], ["junkB", "st8"], accum_out=st8[:, 4 + n2:5 + n2])
                tt("dve", st8[:, 6:7], st8[:, 4:5], st8[:, 5:6], ALU.add, ["st8"], ["st8"])
                rsqrt(st8[:, 7:8], st8[:, 6:7], 1.0 / D_MODEL, ["st8"], ["st8"], 1)
                for n2 in range(2):
                    op("dve", lambda e: e.scalar_tensor_tensor(out=ygB, in0=po_[n2][0][:], scalar=st8[:, 7:8], in1=postw_bc[:, n2 * 512:(n2 + 1) * 512],
                                                                op0=ALU.mult, op1=ALU.mult), [po_[n2][1], "st8", "postw_bc"], ["ygB"])
                    tt("dve", xb[:, n2 * 512:(n2 + 1) * 512], xb[:, n2 * 512:(n2 + 1) * 512], ygB, ALU.add, [f"{xbn}_0", f"{xbn}_1", "ygB"],
                       [f"{xbn}_0", f"{xbn}_1"])
                for s_ in range(2):
                    sch.dma("pool", "ost", out[s_, base:base + 512, :].rearrange("(m l) d -> l m d", l=8)[l8], xb[64 * s_:64 * s_ + 64, :],
                            [f"{xbn}_0", f"{xbn}_1"], ["out_d"])

        f_loads(0); f_uproj(0); f_statein(0); f_rec(0, 64); f_y(0); f_gates(0)
        for sc in range(NSC):
            nxt = sc + 1 < NSC
            if nxt:
                f_loads(sc + 1)
                f_uproj(sc + 1)
            f_tile(sc, 0)
            f_tile(sc, 1)
            if nxt:
                f_statein(sc + 1)
            for l8 in range(2, 7):
                if nxt:
                    f_rec((l8 - 2) * 13, min(64, (l8 - 1) * 13))
                f_tile(sc, l8)
            if nxt:
                f_y(sc + 1)
            f_tile(sc, 7)
            if nxt:
                f_gates(sc + 1)
    sch.barrier()
    sch.finish("sp", ["mixA_d", "hT_d", "out_d"])
    es.close()
    return nc, sch


def prep_shared(inp, NL):
    f = lambda a: np.ascontiguousarray(np.asarray(a, dtype=np.float32))
    w_in = np.asarray(inp["w_in"], dtype=np.float32)[:NL]
    sh = {}
    sh["w_tma"] = f(np.concatenate([w_in[:, :, C_Z:C_Z + 1024], w_in[:, :, C_DT:C_DT + 16], w_in[:, :, C_I:C_I + 512],
                                    w_in[:, :, C_G:C_G + 512]], axis=2))
    sh["w_fma"] = f(np.concatenate([w_in[:, :, C_XBC:C_XBC + 1536], w_in[:, :, C_Q:C_Q + 512], w_in[:, :, C_F:C_F + 512]], axis=2))
    sh["w_b"] = f(w_in[:, :, C_U:C_U + 1024])
    sh["w_out"] = f(np.asarray(inp["w_out"])[:NL])
    sh["prew"] = f(np.asarray(inp["pre_norm_w"])[:NL].reshape(NL, 8, 128).transpose(0, 2, 1))
    sh["postw"] = f(np.asarray(inp["post_norm_w"])[:NL].reshape(NL, 1, D_MODEL))
    mixnw = np.concatenate([np.asarray(inp["ssd_norm_w"])[:NL], np.asarray(inp["hgrn_norm_w"])[:NL], np.asarray(inp["s5_norm_w"])[:NL]], axis=1)
    sh["mixnw"] = f(mixnw.reshape(NL, 16, 128).transpose(0, 2, 1))
    cw = np.asarray(inp["ssd_conv_w"])[:NL]
    sh["convw"] = f(cw.reshape(NL, 4, 12, 128).transpose(0, 3, 1, 2).reshape(NL, 128, 48))
    cb = np.asarray(inp["ssd_conv_b"])[:NL]
    sh["convb_pp"] = f(cb.reshape(NL, 12, 128).transpose(0, 2, 1))
    sh["convb_row"] = f(cb.reshape(NL, 1, 1536))
    sh["dtb"] = f(np.asarray(inp["ssd_dt_bias"])[:NL].reshape(NL, 1, 16))
    sh["alog"] = f(np.asarray(inp["ssd_a_log"])[:NL].reshape(NL, 1, 16))
    sh["ssdd"] = f(np.asarray(inp["ssd_d"])[:NL].reshape(NL, 1, 16))
    hl = np.asarray(inp["hgrn_lower_bounds"])[:NL]
    sh["hlb"] = f(hl.reshape(NL, 4, 128).transpose(2, 1, 0).reshape(128, 4 * NL))
    sh["lamre"] = f(np.asarray(inp["s5_lambda_re"])[:NL].transpose(0, 2, 1))
    sh["lamim"] = f(np.asarray(inp["s5_lambda_im"])[:NL].transpose(0, 2, 1))
    sh["lstep"] = f(np.asarray(inp["s5_log_step"])[:NL].reshape(NL, 1, 32))
    sh["bre"] = f(np.asarray(inp["s5_b_re"])[:NL].transpose(0, 2, 1, 3).reshape(NL, 64, 512))
    sh["bim"] = f(np.asarray(inp["s5_b_im"])[:NL].transpose(0, 2, 1, 3).reshape(NL, 64, 512))
    sh["cre"] = f(np.asarray(inp["s5_c_re"])[:NL].transpose(0, 3, 1, 2).reshape(NL, 64, 512))
    sh["cim"] = f(np.asarray(inp["s5_c_im"])[:NL].transpose(0, 3, 1, 2).reshape(NL, 64, 512))
    d5 = np.asarray(inp["s5_d"])[:NL].reshape(NL, 32, 16)
    sh["s5d"] = f(np.broadcast_to(d5.transpose(0, 2, 1)[:, None, :, :], (NL, 8, 16, 32)).reshape(NL, 128, 32))
    sh["gluw"] = f(np.asarray(inp["s5_glu_w"])[:NL])
    sh["glub"] = f(np.asarray(inp["s5_glu_b"])[:NL].reshape(NL, 1, 1024))
    for k, v in host_consts().items():
        sh["c_" + k] = v
    return sh


LAYER_GROUPS = [[0, 1, 2, 3]]


def kernel(**inputs):
    x = np.ascontiguousarray(np.asarray(inputs["x"], dtype=np.float32))
    B = x.shape[0]
    S = B // NCORES
    sh = prep_shared(inputs, NL_FULL)
    cur = x
    for grp in LAYER_GROUPS:
        nc, _ = build(NL_FULL, S, x.shape[1], layers=grp)
        in_maps = [dict(sh, x=np.ascontiguousarray(cur[S * c:S * (c + 1)])) for c in range(NCORES)]
        res = run_bass_kernel_spmd(nc, in_maps, core_ids=list(range(NCORES)))
        cur = np.concatenate([np.asarray(r["out"], dtype=np.float32) for r in res.results], axis=0)
    return cur
```

```python
import math
from contextlib import ExitStack

import numpy as np
import concourse.bass as bass
import concourse.mybir as mybir
from concourse.bass_utils import run_bass_kernel_spmd

F32 = mybir.dt.float32
BF16 = mybir.dt.bfloat16
AF = mybir.ActivationFunctionType
ALU = mybir.AluOpType
AX = mybir.AxisListType

D_MODEL = 1024
IN_COLS = 5648
EPS = 1e-6
NL_FULL = 4
SEQ_FULL = 2048
NCORES = 8
T = 128

C_Z, C_XBC, C_DT, C_Q, C_F, C_I, C_G, C_U, C_SG = 0, 1024, 2560, 2576, 3088, 3600, 4112, 4624, 5136
N_TMA = 1024 + 16 + 512 + 512
N_FMA = 1536 + 512 + 512
N_B = 1024
GELU_C1 = 0.7978845608028654
GELU_C2 = 0.044715 * GELU_C1


class Sched:
    def __init__(self, nc):
        self.nc = nc
        self.eng = {"pe": nc.tensor, "act": nc.scalar, "dve": nc.vector, "pool": nc.gpsimd, "sp": nc.sync}
        self.sem = {}
        self.cnt = {}
        self.waited = {}
        self.lastw = {}
        self.readers = {}
        self.pending = {}
        self.ninst = 0
        self.gen = {}
        for k in ("pe", "act", "dve", "pool"):
            self._mk(k)

    def _mk(self, k):
        self.sem[k] = self.nc.alloc_semaphore("s_" + k)
        self.cnt[k] = 0
        self.pending[k] = False

    def _phys(self, names, writing):
        out = []
        for b in names:
            if "#" in b:
                p, g = b.split("#")
                if writing:
                    if self.gen.get(p) != g and int(g) > int(self.gen.get(p, "-1")):
                        self.gen[p] = g
                assert self.gen.get(p) == g, f"stale PSUM bank use {b} (current gen {self.gen.get(p)})"
                b = p
            out.append(b)
        return out

    def _deps(self, reads, writes):
        reads[:] = self._phys(reads, False)
        writes[:] = self._phys(writes, True)
        deps = {}
        raw = {}
        for b in reads:
            lw = self.lastw.get(b)
            if lw:
                deps[lw[0]] = max(deps.get(lw[0], 0), lw[1])
                raw[lw[0]] = max(raw.get(lw[0], 0), lw[1])
        for b in writes:
            lw = self.lastw.get(b)
            if lw:
                deps[lw[0]] = max(deps.get(lw[0], 0), lw[1])
            for e, i in self.readers.get(b, {}).items():
                deps[e] = max(deps.get(e, 0), i)
        return deps, raw

    def _emit_waits(self, issuer, me, deps, raw):
        w = self.waited.setdefault(issuer, {})
        for src, idx in deps.items():
            if src == me:
                if me == "pe":
                    continue
            if idx > w.get(src, 0):
                self.eng[issuer].wait_ge(self.sem[src], idx)
                w[src] = idx
                self.ninst += 1

    def op(self, e, fn, reads=(), writes=(), inc=1):
        reads, writes = list(reads), list(writes)
        deps, raw = self._deps(reads, writes)
        self._emit_waits(e, e, deps, raw)
        ins = fn(self.eng[e])
        self.ninst += 1
        if inc:
            ins.then_inc(self.sem[e], 1)
            self.cnt[e] += 1
            idx = self.cnt[e]
            self.pending[e] = False
        else:
            idx = self.cnt[e] + 1
            self.pending[e] = True
        for b in reads:
            self.readers.setdefault(b, {})[e] = max(self.readers.get(b, {}).get(e, 0), idx)
        for b in writes:
            self.lastw[b] = (e, idx)
            self.readers[b] = {}
        return ins

    def dma(self, q, slot, out, in_, reads=(), writes=(), **kw):
        if slot not in self.sem:
            self._mk(slot)
        reads, writes = list(reads), list(writes)
        deps, raw = self._deps(reads, writes)
        self._emit_waits(q, None, deps, raw)
        ins = self.eng[q].dma_start(out=out, in_=in_, **kw)
        ins.then_inc(self.sem[slot], 16)
        self.ninst += 1
        self.cnt[slot] += 16
        idx = self.cnt[slot]
        for b in reads:
            self.readers.setdefault(b, {})[slot] = idx
        for b in writes:
            self.lastw[b] = (slot, idx)
            self.readers[b] = {}

    def barrier(self):
        for e in ("pe", "act", "dve", "pool", "sp"):
            w = self.waited.setdefault(e, {})
            for src, c in self.cnt.items():
                if c == 0 or (src == e and e == "pe"):
                    continue
                if c > w.get(src, 0):
                    self.eng[e].wait_ge(self.sem[src], c)
                    w[src] = c
                    self.ninst += 1

    def dma_sync(self, q, slot):
        w = self.waited.setdefault(q, {})
        c = self.cnt.get(slot, 0)
        if c > w.get(slot, 0):
            self.eng[q].wait_ge(self.sem[slot], c)
            w[slot] = c
            self.ninst += 1

    def finish(self, q, bufs):
        deps, raw = self._deps(list(bufs), [])
        self._emit_waits(q, None, deps, raw)
        for k, v in self.pending.items():
            assert not v, k


def host_consts():
    c = {}
    c["ident"] = np.eye(128, dtype=np.float32)
    tl = np.arange(128)
    c["utri"] = (tl[:, None] <= tl[None, :]).astype(np.float32)
    c["negm"] = np.where(tl[None, :] >= tl[:, None], 0.0, -30000.0).astype(np.float32)
    blk = (tl[:, None] // 64) == (tl[None, :] // 64)
    c["m64"] = ((tl[None, :] >= tl[:, None]) & blk).astype(np.float32)
    sc = np.ones((128, 512), np.float32)
    sc[:, 0::64] = 0.0
    c["scan0"] = sc
    band = np.zeros((8, 128, 240), np.float32)
    for a in range(8):
        for k in range(16 * a, 16 * a + 16):
            band[a, k, (k % 16) + 112] = 1.0
    c["band"] = band.transpose(1, 0, 2).reshape(128, 8 * 240).copy()
    c["m8"] = ((tl[None, :] // 16) >= (tl[:, None] // 16)).astype(np.float32)
    c["ones"] = np.ones((128, 128), np.float32)
    pm = np.zeros((128, 128), np.float32)
    for m in range(128):
        pm[(m % 8) * 16 + m // 8, m] = 1.0
    c["pm"] = pm
    c["pmT"] = np.ascontiguousarray(pm.T)
    return c


def build(NL, S, L, dbg=None, layers=None):
    assert S == 2 and L % 512 == 0
    NCH = L // T
    nc = bass.Bass("TRN2", target_bir_lowering=False)
    es = ExitStack()

    def din(name, shape, dt=F32):
        return nc.dram_tensor(name, list(shape), dt, kind="ExternalInput").ap()

    x_in = din("x", [S, L, D_MODEL])
    w_tma = din("w_tma", [NL, D_MODEL, N_TMA])
    w_fma = din("w_fma", [NL, D_MODEL, N_FMA])
    w_b = din("w_b", [NL, D_MODEL, N_B])
    w_out = din("w_out", [NL, 2048, D_MODEL])
    prew = din("prew", [NL, 128, 8])
    postw = din("postw", [NL, 1, D_MODEL])
    mixnw = din("mixnw", [NL, 128, 16])
    convw = din("convw", [NL, 128, 48])
    convb_pp = din("convb_pp", [NL, 128, 12])
    convb_row = din("convb_row", [NL, 1, 1536])
    dtb = din("dtb", [NL, 1, 16])
    alog = din("alog", [NL, 1, 16])
    ssdd = din("ssdd", [NL, 1, 16])
    hlb = din("hlb", [128, 4 * NL])
    lamre = din("lamre", [NL, 64, 32])
    lamim = din("lamim", [NL, 64, 32])
    lstep = din("lstep", [NL, 1, 32])
    bre = din("bre", [NL, 64, 512])
    bim = din("bim", [NL, 64, 512])
    cre = din("cre", [NL, 64, 512])
    cim = din("cim", [NL, 64, 512])
    s5d = din("s5d", [NL, 128, 32])
    gluw = din("gluw", [NL, 512, 1024])
    glub = din("glub", [NL, 1, 1024])
    consts = {k: din("c_" + k, v.shape) for k, v in host_consts().items()}

    out = nc.dram_tensor("out", [S, L, D_MODEL], F32, kind="ExternalOutput").ap()
    hT_d = nc.dram_tensor("hT_scr", [S * NCH, 128, 8 * 128], BF16, kind="Internal").ap()
    mixA_d = nc.dram_tensor("mixA_scr", [S, L, 1536], BF16, kind="Internal").ap()
    dbg_out = None
    if False:
        dbg_out = {}

    def sb(name, shape, dt=F32):
        return es.enter_context(nc.sbuf_tensor(name, list(shape), dt))

    ident_b = sb("ident_b", [128, 128], BF16)
    ident_f = sb("ident_f", [128, 128])
    utri = sb("utri", [128, 128])
    ones_f = sb("ones_f", [128, 128])
    ones_b = sb("ones_b", [128, 128], BF16)
    negm_b = sb("negm_b", [128, 128], BF16)
    m64 = sb("m64", [128, 128], BF16)
    m8 = sb("m8", [128, 128])
    pm_b = sb("pm_b", [128, 128], BF16)
    pmT_b = sb("pmT_b", [128, 128], BF16)
    hlb_t = sb("hlb_t", [128, 4, NL])
    lb_all = sb("lb_all", [128, 4, NL])
    neghalf = sb("neghalf", [128, 16])
    st8 = sb("st8", [128, 8])
    prew_t = sb("prew_t", [128, 8])
    mixnw_t = sb("mixnw_t", [128, 16])
    convw_t = sb("convw_t", [128, 48])
    convb_t = sb("convb_t", [128, 12])
    dtb_bc = sb("dtb_bc", [128, 16])
    a_bc = sb("a_bc", [128, 16])
    d_bc = sb("d_bc", [128, 16])
    lb_t = sb("lb_t", [128, 4])
    dt_t = sb("dt_t", [128, 16])
    dtmp = sb("dtmp", [128, 16])
    dtA = sb("dtA", [128, 16])
    acum = sb("acum", [128, 16])
    nacum = sb("nacum", [128, 16])
    eacum = sb("eacum", [128, 16])
    dte = sb("dte", [128, 16])
    cdec = sb("cdec", [128, 16])
    ecend = sb("ecend", [128, 8])
    s5d_t = sb("s5d_t", [128, 32])
    LA = sb("LA", [64, 2, 32])
    LB = sb("LB", [64, 2, 32])

    WREG = 38912
    wreg = sb("wreg", [128, WREG], BF16)
    wtma = wreg[:, 0:8 * N_TMA].rearrange("p (k n) -> p k n", k=8)
    wfma = wreg[:, 8 * N_TMA:8 * (N_TMA + N_FMA)].rearrange("p (k n) -> p k n", k=8)
    o = 0
    wb_v = wreg[:, o:o + 8 * N_B].rearrange("p (k n) -> p k n", k=8); o += 8 * N_B
    wout_v = wreg[:, o:o + 16 * 1024].rearrange("p (k n) -> p k n", k=16); o += 16 * 1024
    glu_v = wreg[:, o:o + 4 * 1024].rearrange("p (k n) -> p k n", k=4); o += 4 * 1024
    wsim_v = wreg[:, o:o + 32 * 64].rearrange("p (g n) -> p g n", g=32); o += 32 * 64
    wore_v = wreg[0:64, o:o + 32 * 128].rearrange("p (g n) -> p g n", g=32); o += 32 * 128
    woim_v = wreg[0:64, o:o + 32 * 128].rearrange("p (g n) -> p g n", g=32); o += 32 * 128
    assert o <= WREG, (o, WREG)
    reg2 = sb("reg2", [128, 48 * 128], BF16)
    dconv = reg2[:].rearrange("p (k j n) -> p k j n", k=4, j=12)
    tz_v = reg2[:, 0:4096].rearrange("p (g n) -> p g n", g=32)
    wsre_v = reg2[:, 4096:6144].rearrange("p (g n) -> p g n", g=32)

    ARENA = 58100
    arena = sb("arena", [128, ARENA], BF16)
    aoff = [0]

    def cv(n, dt=BF16, parts=128):
        ne = n * (2 if dt == F32 else 1)
        assert aoff[0] + ne <= ARENA, (aoff[0], ne, ARENA)
        ap = arena[0:parts, aoff[0]:aoff[0] + ne]
        aoff[0] += ne
        return ap.bitcast(F32) if dt == F32 else ap

    xt = [cv(1024, F32), cv(1024, F32)]
    xn = cv(1024)
    hTq = cv(4096).rearrange("p (k t) -> p k t", k=8)
    zs = cv(1024)
    vtok = cv(512)
    gs = cv(512)
    XT = [cv(12 * 260).rearrange("p (j t) -> p j t", j=12) for _ in range(2)]
    qsq = cv(2048).rearrange("p (h t) -> p h t", h=4)
    ef = cv(512, F32)
    xs = cv(1024, F32)
    btok = cv(256)
    bctq = cv(2048).rearrange("p (j t) -> p j t", j=4)
    xdt = cv(1024)
    xw = cv(1024)
    xsd = cv(1024)
    scT = cv(256).rearrange("p (g t) -> p g t", g=2)
    dec = cv(1024).rearrange("p (r t) -> p r t", r=8)
    MT = cv(1024).rearrange("p (r t) -> p r t", r=8)
    hst = [cv(1024, F32), cv(1024, F32)]
    hbf = [cv(1024), cv(1024)]
    L1 = cv(512, F32)
    L2 = cv(512, F32)
    cum = cv(512, F32)
    ecum = cv(512, F32)
    qTa = cv(512).rearrange("p (h t) -> p h t", h=4)
    qTb = cv(512).rearrange("p (h t) -> p h t", h=4)
    kT = cv(512).rearrange("p (h t) -> p h t", h=4)
    kend = cv(512).rearrange("p (h t) -> p h t", h=4)
    kendT = cv(512).rearrange("p (h t) -> p h t", h=4)
    attn = cv(512).rearrange("p (h t) -> p h t", h=4)
    Sst = [cv(512, F32).rearrange("p (h t) -> p h t", h=4) for _ in range(2)]
    Sbf = [cv(512).rearrange("p (h t) -> p h t", h=4) for _ in range(2)]
    ytmp = cv(512, F32)
    yg = cv(512, F32)
    otmp = cv(512, F32)
    mix = cv(2048)
    mixT = cv(1536).rearrange("p (k t) -> p k t", k=12)
    junk = dec.rearrange("p r t -> p (r t)")
    scan0 = cv(512, F32)
    convb_r = cv(1536, parts=1)
    endA = aoff[0]
    aoff[0] = 0
    xtB = [cv(1024, F32), cv(1024, F32)]
    hT8 = cv(8192).rearrange("p (k t) -> p k t", k=8)
    uT8 = cv(4096).rearrange("p (j t) -> p j t", j=4)
    ublk8 = cv(4096).rearrange("p (g n) -> p g n", g=32)
    gst = cv(2 * 32 * 2 * 65, parts=64).rearrange("p (r g s m) -> p r g s m", r=2, g=32, s=2)
    gyb8 = cv(4096).rearrange("p (g n) -> p g n", g=32)
    gyT8 = cv(4096).rearrange("p (j t) -> p j t", j=4)
    ytB = cv(512, F32)
    ygB = cv(512, F32)
    gatesB = cv(4096).rearrange("p (l n) -> p l n", l=8)
    L1B = cv(512, F32)
    L2B = cv(512, F32)
    mixB = cv(2048)
    mixTB = cv(2048).rearrange("p (k t) -> p k t", k=16)
    junkB = cv(1024)
    postw_bc = cv(1024, F32)
    S2p = [cv(128, F32, parts=64).rearrange("p (r g s) -> p r g s", r=2, g=32) for _ in range(2)]
    rt = cv(128, F32, parts=64).rearrange("p (r g s) -> p r g s", r=2, g=32)
    ru = cv(128, F32, parts=64).rearrange("p (r g s) -> p r g s", r=2, g=32)
    glub_r = cv(1024, parts=1)
    endB = aoff[0]
    aoff[0] = 4096
    g32 = cv(40 * 32, F32, parts=64).rearrange("p (i g) -> p i g", i=40)
    v4 = lambda: cv(512, F32, parts=64).rearrange("p (g n) -> p g n", g=4)
    Zre, Zim, Yre, Yim, T1, T2 = v4(), v4(), v4(), v4(), v4(), v4()
    Bre_t, Bim_t, Cre_t, Cim_t = (cv(512, F32, parts=64) for _ in range(4))
    assert aoff[0] <= endB

    ps = [es.enter_context(nc.psum_tensor(f"ps{i}", [128, 512], F32)) for i in range(8)]
    psn = [f"ps{i}" for i in range(8)]
    ps_rr = [0]

    def nps():
        i = ps_rr[0] % 8
        ps_rr[0] += 1
        return ps[i], f"{psn[i]}#{ps_rr[0]}"

    sch = Sched(nc)
    blk = es.enter_context(nc.Block())
    op = sch.op

    def act(out_, in_, func, reads, writes, **kw):
        return op("act", lambda e: e.activation(out=out_, in_=in_, func=func, **kw), reads, writes)

    def tt(eng, out_, a, b, o_, reads, writes):
        return op(eng, lambda e: e.tensor_tensor(out=out_, in0=a, in1=b, op=o_), reads, writes)

    def ts(eng, out_, a, s1, s2, o0, o1, reads, writes):
        return op(eng, lambda e: e.tensor_scalar(out=out_, in0=a, scalar1=s1, scalar2=s2, op0=o0, op1=o1), reads, writes)

    def cp(eng, out_, in_, reads, writes):
        if eng == "act":
            return act(out_, in_, AF.Copy, reads, writes)
        return op(eng, lambda e: e.tensor_copy(out=out_, in_=in_), reads, writes)

    def mm(out_, lhsT, rhs, start, stop, reads, writes, inc=None):
        return op("pe", lambda e: e.matmul(out_, lhsT, rhs, start=start, stop=stop), reads, writes,
                  inc=(1 if stop else 0) if inc is None else inc)

    def rsqrt(out_, in_, scale, reads, writes, n):
        ts("dve", out_, in_, scale, EPS, ALU.mult, ALU.add, reads, writes)
        op("pool", lambda e: e.tensor_tensor(out=out_, in0=out_, in1=neghalf[:, 0:n], op=ALU.pow), list(writes) + ["neghalf"], writes)

    def ld(q, out_, in_, w):
        sch.dma(q, "d_" + w[0], out_, in_, (), w)


    HALF = ARENA // 4
    stg = [arena[:, 0:2 * HALF].bitcast(F32), arena[:, 2 * HALF:4 * HALF].bitcast(F32)]
    eng_rr = [0]

    def stage_load(items):
        rounds, cur, off = [], [], 0
        for it in items:
            n = it[0].shape[-1]
            if off + n > HALF:
                rounds.append(cur)
                cur, off = [], 0
            cur.append((it, off, n))
            off += n
        rounds.append(cur)
        for ri, rnd in enumerate(rounds):
            b = ri % 2
            for ii, (it, off, n) in enumerate(rnd):
                q = ("sp", "act")[ii % 2]
                sch.dma(q, f"stg{b}_{q}", stg[b][:, off:off + n], it[1], (), [f"stg{b}_{q}"])
            for (it, off, n) in rnd:
                e = ("dve", "act")[eng_rr[0] % 2]
                eng_rr[0] += 1
                rd = [f"stg{b}_sp", f"stg{b}_act"] + ([it[4]] if len(it) > 4 else [])
                src = stg[b][:, off:off + n]
                if e == "act":
                    act(it[0], src, AF.Copy, rd, [it[3]], scale=it[2])
                else:
                    ts(e, it[0], src, it[2], None, ALU.mult, ALU.bypass, rd, [it[3]])

    ld("sp", ident_f[:], consts["ident"], ["ident_f"])
    ld("sp", utri[:], consts["utri"], ["utri"])
    ld("sp", ones_f[:], consts["ones"], ["ones_f"])
    ld("sp", m8[:], consts["m8"], ["m8"])
    ld("sp", hlb_t[:].rearrange("p h l -> p (h l)"), hlb, ["hlb_t"])
    ld("pool", ident_b[:], consts["ident"], ["ident_b"])
    ld("pool", ones_b[:], consts["ones"], ["ones_b"])
    ld("pool", negm_b[:], consts["negm"], ["negm_b"])
    ld("pool", m64[:], consts["m64"], ["m64"])
    ld("pool", pm_b[:], consts["pm"], ["pm_b"])
    ld("pool", pmT_b[:], consts["pmT"], ["pmT_b"])
    op("dve", lambda e: e.memset(neghalf[:], -0.5), (), ["neghalf"])
    act(hlb_t[:], hlb_t[:], AF.Exp, ["hlb_t"], ["hlb_t"])
    op("dve", lambda e: e.tensor_reduce(out=st8[:, 0:4], in_=hlb_t[:], axis=AX.X, op=ALU.add), ["hlb_t"], ["st8"])
    op("dve", lambda e: e.reciprocal(out=st8[:, 0:4], in_=st8[:, 0:4]), ["st8"], ["st8"])
    tt("dve", hlb_t[:], hlb_t[:], st8[:, 0:4].unsqueeze(2).broadcast_to([128, 4, NL]), ALU.mult, ["hlb_t", "st8"], ["hlb_t"])
    op("dve", lambda e: e.memset(lb_all[:], 0.0), (), ["lb_all"])
    for l in range(1, NL):
        tt("dve", lb_all[:, :, l], lb_all[:, :, l - 1], hlb_t[:, :, l], ALU.add, ["lb_all", "hlb_t"], ["lb_all"])

    NQ = L // 256
    NSC = L // 512
    layers = list(range(NL)) if layers is None else list(layers)

    def gci(s, c):
        return s * NCH + c

    for layer in layers:
        xsrc = x_in if layer == layers[0] else out
        xrd = ["out_d"] if layer != layers[0] else []
        sch.barrier()
        ld("sp", prew_t[:], prew[layer], ["prew_t"])
        ld("sp", convw_t[:], convw[layer], ["convw_t"])
        ld("sp", convb_t[:], convb_pp[layer], ["convb_t"])
        ld("sp", dtb_bc[:], dtb[layer].partition_broadcast(128), ["dtb_bc"])
        ld("sp", a_bc[:], alog[layer].partition_broadcast(128), ["a_bc"])
        ld("sp", d_bc[:], ssdd[layer].partition_broadcast(128), ["d_bc"])
        act(a_bc[:], a_bc[:], AF.Exp, ["a_bc"], ["a_bc"])
        ts("dve", a_bc[:], a_bc[:], -1.0, None, ALU.mult, ALU.bypass, ["a_bc"], ["a_bc"])
        cp("dve", lb_t[:], lb_all[:, :, layer], ["lb_all"], ["lb_t"])
        itemsA = []
        for k in range(8):
            itemsA.append((wtma[:, k, :], w_tma[layer, k * 128:(k + 1) * 128, :], prew_t[:, k:k + 1], "wtma", "prew_t"))
            itemsA.append((wfma[:, k, :], w_fma[layer, k * 128:(k + 1) * 128, :], prew_t[:, k:k + 1], "wfma", "prew_t"))
        stage_load(itemsA)
        sch.barrier()
        ld("sp", scan0, consts["scan0"], ["scan0"])
        ld("pool", convb_r, convb_row[layer], ["convb_r"])
        for k in range(4):
            for j in range(12):
                ts("dve", dconv[:, k, j, :], ident_b[:], convw_t[:, k * 12 + j:k * 12 + j + 1], None, ALU.mult, ALU.bypass,
                   ["ident_b", "convw_t"], ["dconv"])
        op("dve", lambda e: e.memset(qTa, 0.0), (), ["qTa"])
        op("dve", lambda e: e.memset(qTb, 0.0), (), ["qTb"])
        for s in range(2):
            op("dve", lambda e: e.memset(XT[s][:, :, 0:3], 0.0), (), [f"XT{s}"])
            op("dve", lambda e: e.memset(hst[s], 0.0), (), [f"hst{s}"])
            op("pool", lambda e: e.memset(hbf[s], 0.0), (), [f"hbf{s}"])
            op("dve", lambda e: e.memset(Sst[s], 0.0), (), [f"Sst{s}"])
            op("pool", lambda e: e.memset(Sbf[s], 0.0), (), [f"Sbf{s}"])

        xcnt = [0]
        if dbg and dbg.get("stop") == "A0":
            break
        for qd in range(NQ):
            quad = [(s, 2 * qd + pp) for s in range(2) for pp in range(2)]
            for qi, (s, c) in enumerate(quad):
                xb, xbn = xt[xcnt[0] % 2], f"xt{xcnt[0] % 2}"
                xcnt[0] += 1
                sch.dma("sp", "x" + xbn, xb, xsrc[s, c * T:(c + 1) * T, :], xrd, [xbn])
                act(xn, xb, AF.Square, [xbn], ["xn", "st8"], accum_out=st8[:, 0:1])
                rsqrt(st8[:, 1:2], st8[:, 0:1], 1.0 / D_MODEL, ["st8"], ["st8"], 1)
                ts("dve", xn, xb, st8[:, 1:2], None, ALU.mult, ALU.bypass, [xbn, "st8"], ["xn"])
                p0, p0n = nps()
                p0b = p0[:].bitcast(BF16)
                for k in range(8):
                    op("pe", lambda e: e.transpose(p0b[:, k * 128:(k + 1) * 128], xn[:, k * 128:(k + 1) * 128], ident_b[:]),
                       ["xn", "ident_b"], [p0n], inc=1 if k == 7 else 0)
                cp("act", hTq[:, :, qi * 128:(qi + 1) * 128], p0b.rearrange("p (k t) -> p k t", k=8), [p0n], ["hTq"])
                sch.dma("sp", "hts", hT_d[gci(s, c)].rearrange("p (k t) -> p k t", k=8), hTq[:, :, qi * 128:(qi + 1) * 128], ["hTq"], ["hT_d"])
            if dbg and dbg.get("stop") == "A1":
                break
            njc = int(dbg.get("njc", 16)) if dbg else 16
            for jc in range(njc):
                pf, pfn = nps()
                for k in range(8):
                    mm(pf[:], wfma[:, k, jc * 128:(jc + 1) * 128], hTq[:, k, :], k == 0, k == 7, ["hTq", "wfma"], [pfn])
                if dbg and dbg.get("evac") == "junk":
                    cp("act", xn[:, 0:512], pf[:], [pfn], ["xn"])
                elif jc < 12:
                    ev = dbg.get("evac", "same") if dbg else "same"
                    for s in range(2):
                        if ev == "act_only" and s == 1:
                            continue
                        if ev == "dve_only" and s == 0:
                            continue
                        c0_ = 4 if ev == "even" else 3
                        eng_ = "act" if s == 0 else "dve"
                        if ev == "swap":
                            eng_ = "dve" if s == 0 else "act"
                        if ev == "same":
                            eng_ = "act" if jc % 2 == 0 else "dve"
                        cp(eng_, XT[s][:, jc, c0_:c0_ + 256], pf[:, s * 256:(s + 1) * 256], [pfn], [f"XT{s}"])
                else:
                    act(qsq[:, jc - 12, :], pf[:], AF.Silu, [pfn], ["qsq"])
            if dbg and dbg.get("stop") == "A15":
                break
            for s in range(2):
                for half in range(2):
                    pb, pbn = nps()
                    for j2 in range(2):
                        jj = half * 2 + j2
                        j = 8 + jj
                        for k in range(4):
                            mm(pb[:, j2 * 256:(j2 + 1) * 256], dconv[:, k, j, :], XT[s][:, j, k:k + 256], k == 0, k == 3, [f"XT{s}", "dconv"], [pbn],
                               inc=1 if (k == 3 and j2 == 1) else 0)
                    for j2 in range(2):
                        jj = half * 2 + j2
                        act(bctq[:, jj, s * 256:(s + 1) * 256], pb[:, j2 * 256:(j2 + 1) * 256], AF.Silu, [pbn, "convb_t"], ["bctq"],
                            bias=convb_t[:, 8 + jj:9 + jj])
            if dbg and dbg.get("stop") == "A2":
                break
            for qi, (s, c) in enumerate(quad):
                pp_ = c % 2
                tsl = slice(qi * 128, (qi + 1) * 128)
                XTs, XTn = XT[s], f"XT{s}"
                hs, hsn, hb, hbn = hst[s], f"hst{s}", hbf[s], f"hbf{s}"
                Ss, Ssn, Sb, Sbn = Sst[s], f"Sst{s}", Sbf[s], f"Sbf{s}"

                def tm_slab(c0, n):
                    pz, pzn = nps()
                    for k in range(8):
                        mm(pz[:, 0:n], hTq[:, k, tsl], wtma[:, k, c0:c0 + n], k == 0, k == 7, ["hTq", "wtma"], [pzn])
                    return pz, pzn
                for h2 in range(2):
                    pz, pzn = tm_slab(h2 * 512, 512)
                    act(zs[:, h2 * 512:(h2 + 1) * 512], pz[:], AF.Silu, [pzn], ["zs"])
                pz, pzn = tm_slab(1040 + 512, 512)
                act(gs, pz[:], AF.Silu, [pzn], ["gs"])
                pz, pzn = tm_slab(1040, 512)
                cp("act", vtok, pz[:], [pzn], ["vtok"])
                pd, pdn = tm_slab(1024, 16)
                tt("dve", dtmp[:], pd[:, 0:16], dtb_bc[:], ALU.add, [pdn, "dtb_bc"], ["dtmp"])
                pfq, pfqn = nps()
                for jj in range(4):
                    for k in range(8):
                        mm(pfq[:, jj * 128:(jj + 1) * 128], wfma[:, k, (16 + jj) * 128:(17 + jj) * 128], hTq[:, k, tsl], k == 0, k == 7,
                           ["hTq", "wfma"], [pfqn], inc=1 if (k == 7 and jj == 3) else 0)
                pc = [nps() for _ in range(3)]
                w0 = pp_ * 128
                for j in range(10):
                    pcj, pcjn = pc[j // 4]
                    o_ = pcj[:, (j % 4) * 128:(j % 4 + 1) * 128]
                    mm(o_, ones_b[0:1, :], convb_r[0:1, j * 128:(j + 1) * 128], True, False, ["ones_b", "convb_r"], [pcjn])
                    for k in range(4):
                        last = (k == 3)
                        mm(o_, XTs[:, j, w0 + k:w0 + k + 128], dconv[:, k, j, :], False, last, [XTn, "dconv"], [pcjn],
                           inc=1 if (last and (j % 4 == 3 or j == 9)) else 0)
                act(xs[:, 0:512], pc[0][0][:], AF.Silu, [pc[0][1]], ["xs"])
                act(xs[:, 512:1024], pc[1][0][:], AF.Silu, [pc[1][1]], ["xs"])
                act(btok, pc[2][0][:, 0:256], AF.Silu, [pc[2][1]], ["btok"])
                act(dtmp[:], dtmp[:], AF.Exp, ["dtmp"], ["dtmp"])
                act(dt_t[:], dtmp[:], AF.Ln, ["dtmp"], ["dt_t"], bias=1.0)
                act(ef, pfq[:], AF.Exp, [pfqn], ["ef"], scale=-1.0)
                tt("dve", dtA[:], dt_t[:], a_bc[:], ALU.mult, ["dt_t", "a_bc"], ["dtA"])
                pa, pan = nps()
                mm(pa[:, 0:16], utri[:], dtA[:], True, True, ["utri", "dtA"], [pan], inc=0)
                mm(pa[:, 16:32], ones_f[:], dtA[:], True, True, ["ones_f", "dtA"], [pan])
                cp("dve", acum[:], pa[:, 0:16], [pan], ["acum"])
                ts("dve", nacum[:], pa[:, 0:16], -1.0, None, ALU.mult, ALU.bypass, [pan], ["nacum"])
                act(eacum[:], pa[:, 0:16], AF.Exp, [pan], ["eacum"])
                act(cdec[:], pa[:, 16:32], AF.Exp, [pan], ["cdec"])
                tt("dve", dte[:], pa[:, 16:32], acum[:], ALU.subtract, [pan, "acum"], ["dte"])
                act(dte[:], dte[:], AF.Exp, ["dte"], ["dte"])
                tt("dve", dte[:], dte[:], dt_t[:], ALU.mult, ["dte", "dt_t"], ["dte"])
                xs3 = xs.rearrange("p (r q) -> p r q", r=16)
                tt("dve", xdt.rearrange("p (r q) -> p r q", r=16), xs3, dt_t[:].unsqueeze(2).broadcast_to([128, 16, 64]), ALU.mult,
                   ["xs", "dt_t"], ["xdt"])
                tt("dve", xw.rearrange("p (r q) -> p r q", r=16), xs3, dte[:].unsqueeze(2).broadcast_to([128, 16, 64]), ALU.mult,
                   ["xs", "dte"], ["xw"])
                tt("pool", xsd.rearrange("p (r q) -> p r q", r=16), xs3, d_bc[:].unsqueeze(2).broadcast_to([128, 16, 64]), ALU.mult,
                   ["xs", "d_bc"], ["xsd"])
                for g in range(2):
                    psc, pscn = nps()
                    mm(psc[:, 0:128], bctq[:, g, tsl], bctq[:, 2 + g, tsl], True, True, ["bctq"], [pscn])
                    cp("dve", scT[:, g, :], psc[:, 0:128], [pscn], ["scT"])
                    pab = [nps(), nps()]
                    for r in range(8):
                        pq, pqn = pab[r // 4]
                        o_ = pq[:, (r % 4) * 128:(r % 4 + 1) * 128]
                        hh = g * 8 + r
                        mm(o_, dtA[:, hh:hh + 1].broadcast_to([128, 128]), utri[:], True, False, ["dtA", "utri"], [pqn])
                        mm(o_, ident_b[:], negm_b[:], False, True, ["ident_b", "negm_b"], [pqn], inc=1 if r % 4 == 3 else 0)
                    for r in range(8):
                        pq, pqn = pab[r // 4]
                        hh = g * 8 + r
                        act(dec[:, r, :], pq[:, (r % 4) * 128:(r % 4 + 1) * 128], AF.Exp, [pqn, "nacum"], ["dec"], bias=nacum[:, hh:hh + 1])
                    tt("dve", MT, dec, scT[:, g:g + 1, :].broadcast_to([128, 8, 128]), ALU.mult, ["dec", "scT"], ["MT"])
                    py, pyn = nps()
                    mm(py[:], ident_b[:], xsd[:, g * 512:(g + 1) * 512], True, False, ["ident_b", "xsd"], [pyn])
                    for r in range(8):
                        hh = g * 8 + r
                        mm(py[:, r * 64:(r + 1) * 64], MT[:, r, :], xdt[:, hh * 64:(hh + 1) * 64], False, r == 7, ["MT", "xdt"], [pyn])
                    po, pon = nps()
                    mm(po[:], bctq[:, 2 + g, tsl], hb[:, g * 512:(g + 1) * 512], True, True, ["bctq", hbn], [pon])
                    tt("dve", ytmp.rearrange("p (r q) -> p r q", r=8), po[:].rearrange("p (r q) -> p r q", r=8),
                       eacum[:, g * 8:(g + 1) * 8].unsqueeze(2).broadcast_to([128, 8, 64]), ALU.mult, [pon, "eacum"], ["ytmp"])
                    tt("dve", ytmp, ytmp, py[:], ALU.add, ["ytmp", pyn], ["ytmp"])
                    tt("dve", yg, ytmp, zs[:, g * 512:(g + 1) * 512], ALU.mult, ["ytmp", "zs"], ["yg"])
                    act(junk[:, 0:512], yg, AF.Square, ["yg"], ["dec", "st8"], accum_out=st8[:, 2:3])
                    rsqrt(st8[:, 3:4], st8[:, 2:3], 1.0 / 512, ["st8"], ["st8"], 1)
                    ts("dve", mix[:, g * 512:(g + 1) * 512], yg, st8[:, 3:4], None, ALU.mult, ALU.bypass, ["yg", "st8"], ["mix"])
                    ph, phn = nps()
                    mm(ph[:], btok[:, g * 128:(g + 1) * 128], xw[:, g * 512:(g + 1) * 512], True, True, ["btok", "xw"], [phn])
                    hv = hs[:, g * 512:(g + 1) * 512]
                    tt("dve", hv.rearrange("p (r q) -> p r q", r=8), hv.rearrange("p (r q) -> p r q", r=8),
                       cdec[:, g * 8:(g + 1) * 8].unsqueeze(2).broadcast_to([128, 8, 64]), ALU.mult, [hsn, "cdec"], [hsn])
                    tt("dve", hv, hv, ph[:], ALU.add, [hsn, phn], [hsn])
                    cp("pool", hb[:, g * 512:(g + 1) * 512], hv, [hsn], [hbn])
                act(L2, ef, AF.Ln, ["ef"], ["L2"], bias=1.0)
                for h in range(4):
                    act(L1[:, h * 128:(h + 1) * 128], ef[:, h * 128:(h + 1) * 128], AF.Ln, ["ef", "lb_t"], ["L1"], bias=1.0, scale=lb_t[:, h:h + 1])
                tt("dve", L1, L1, L2, ALU.subtract, ["L1", "L2"], ["L1"])
                act(L2, L1, AF.Exp, ["L1"], ["L2"])
                ts("dve", L2, L2, -1.0, 1.0, ALU.mult, ALU.add, ["L2"], ["L2"])
                op("dve", lambda e: e.tensor_tensor_scan(out=cum, data0=scan0, data1=L1, initial=0.0, op0=ALU.mult, op1=ALU.add),
                   ["scan0", "L1"], ["cum"])
                cum4 = cum.rearrange("p (j t) -> p j t", j=8)
                act(ecend[:], cum4[:, :, 63], AF.Exp, ["cum"], ["ecend"])
                act(ecum, cum, AF.Exp, ["cum"], ["ecum"])
                q4 = qsq[:, :, tsl]
                e4 = ecum.rearrange("p (h t) -> p h t", h=4)
                tt("dve", qTa[:, :, 0:64], q4[:, :, 0:64], e4[:, :, 0:64], ALU.mult, ["qsq", "ecum"], ["qTa"])
                tt("dve", qTb[:, :, 64:128], q4[:, :, 64:128], e4[:, :, 64:128], ALU.mult, ["qsq", "ecum"], ["qTb"])
                act(ecum, cum, AF.Exp, ["cum"], ["ecum"], scale=-1.0)
                tt("dve", ecum, L2, ecum, ALU.mult, ["L2", "ecum"], ["ecum"])
                cp("pool", kT.rearrange("p h t -> p (h t)"), ecum, ["ecum"], ["kT"])
                tt("dve", kend.rearrange("p h (b t) -> p (h b) t", b=2), ecum.rearrange("p (j t) -> p j t", j=8),
                   ecend[:].unsqueeze(2).broadcast_to([128, 8, 64]), ALU.mult, ["ecum", "ecend"], ["kend"])
                pk, pkn = nps()
                pkb = pk[:].bitcast(BF16)
                for h in range(4):
                    op("pe", lambda e: e.transpose(pkb[:, h * 128:(h + 1) * 128], kend[:, h, :], ident_b[:]), ["kend", "ident_b"], [pkn],
                       inc=1 if h == 3 else 0)
                cp("act", kendT.rearrange("p h t -> p (h t)"), pkb[:, 0:512], [pkn], ["kendT"])
                pat, patn = nps()
                for h in range(4):
                    mm(pat[:, h * 128:(h + 1) * 128], kT[:, h, :], qTa[:, h, :], True, False, ["kT", "qTa"], [patn])
                    mm(pat[:, h * 128:(h + 1) * 128], kT[:, h, :], qTb[:, h, :], False, True, ["kT", "qTb"], [patn], inc=1 if h == 3 else 0)
                tt("dve", attn, pat[:].rearrange("p (h t) -> p h t", h=4), m64[:].unsqueeze(1).broadcast_to([128, 4, 128]), ALU.mult,
                   [patn, "m64"], ["attn"])
                pho, phon = nps()
                for h in range(4):
                    o_ = pho[:, h * 128:(h + 1) * 128]
                    mm(o_, attn[:, h, :], vtok[:, h * 128:(h + 1) * 128], h == 0, False, ["attn", "vtok"], [phon])
                    mm(o_, qTa[:, h, :], Sb[:, h, :], False, False, ["qTa", Sbn], [phon])
                for b2 in range(2):
                    pst, pstn = nps()
                    for h in range(4):
                        mm(pst[:, h * 128:(h + 1) * 128], kendT[b2 * 64:(b2 + 1) * 64, h, :], vtok[b2 * 64:(b2 + 1) * 64, h * 128:(h + 1) * 128],
                           True, True, ["kendT", "vtok"], [pstn], inc=1 if h == 3 else 0)
                    ec = ecend[:].rearrange("p (h b) -> p h b", b=2)[:, :, b2:b2 + 1].broadcast_to([128, 4, 128])
                    tt("dve", Ss, Ss, ec, ALU.mult, [Ssn, "ecend"], [Ssn])
                    tt("dve", Ss.rearrange("p h v -> p (h v)"), Ss.rearrange("p h v -> p (h v)"), pst[:], ALU.add, [Ssn, pstn], [Ssn])
                    cp("pool", Sb, Ss, [Ssn], [Sbn])
                    if b2 == 0:
                        for h in range(4):
                            mm(pho[:, h * 128:(h + 1) * 128], qTb[:, h, :], Sb[:, h, :], False, h == 3, ["qTb", Sbn], [phon], inc=1 if h == 3 else 0)
                for h in range(4):
                    act(junk[:, 0:128], pho[:, h * 128:(h + 1) * 128], AF.Square, [phon], ["dec", "st8"], accum_out=st8[:, 4 + h:5 + h])
                rsqrt(st8[:, 4:8], st8[:, 4:8], 1.0 / 128, ["st8"], ["st8"], 4)
                tt("dve", otmp.rearrange("p (h v) -> p h v", h=4), pho[:].rearrange("p (h v) -> p h v", h=4),
                   st8[:, 4:8].unsqueeze(2).broadcast_to([128, 4, 128]), ALU.mult, [phon, "st8"], ["otmp"])
                tt("dve", mix[:, 1024:1536], otmp, gs, ALU.mult, ["otmp", "gs"], ["mix"])
                sch.dma("sp", "mts", mixA_d[s, c * T:(c + 1) * T, :], mix[:, 0:1536], ["mix"], ["mixA_d"])
            for s in range(2):
                cp("pool", XT[s][:, :, 0:3], XT[s][:, :, 256:259], [f"XT{s}"], [f"XT{s}"])

        if dbg and dbg.get("stop") in ("A", "A1", "A2", "A15"):
            break
        sch.barrier()
        ld("sp", mixnw_t[:], mixnw[layer], ["mixnw_t"])
        itemsB = []
        for k in range(8):
            itemsB.append((wb_v[:, k, :], w_b[layer, k * 128:(k + 1) * 128, :], prew_t[:, k:k + 1], "wb", "prew_t"))
        for k in range(16):
            itemsB.append((wout_v[:, k, :], w_out[layer, k * 128:(k + 1) * 128, :], mixnw_t[:, k:k + 1], "wout", "mixnw_t"))
        for k in range(4):
            itemsB.append((glu_v[:, k, :], gluw[layer, k * 128:(k + 1) * 128, :], 0.5, "glu"))
        stage_load(itemsB)
        sch.barrier()
        ld("sp", postw_bc, postw[layer].partition_broadcast(128), ["postw_bc"])
        ld("sp", s5d_t[:], s5d[layer], ["s5d_t"])
        ld("pool", glub_r, glub[layer], ["glub_r"])
        ld("sp", g32[:, 0, :], lamre[layer], ["g_lr"])
        ld("sp", g32[:, 1, :], lamim[layer], ["g_li"])
        ld("sp", g32[:, 2, :], lstep[layer].partition_broadcast(64), ["g_st"])
        ld("sp", Bre_t, bre[layer], ["Bre"])
        ld("sp", Bim_t, bim[layer], ["Bim"])
        ld("sp", Cre_t, cre[layer], ["Cre"])
        ld("sp", Cim_t, cim[layer], ["Cim"])
        GN = ["g32"]

        def gq(i):
            return g32[:, i, :]

        def gmul(o_, a, b):
            tt("dve", gq(o_), gq(a), gq(b), ALU.mult, GN + ["g_lr", "g_li", "g_st"], GN)

        def gadd(o_, a, b, o2=ALU.add):
            tt("dve", gq(o_), gq(a), gq(b), o2, GN, GN)

        def gts(o_, a, m_, a_):
            ts("dve", gq(o_), gq(a), m_, a_, ALU.mult, ALU.add, GN + ["g_lr", "g_li", "g_st"], GN)

        I_LR, I_LI, I_ST, I_X, I_ANG, I_MAG, I_MAGI, I_C, I_S, I_T1, I_T2, I_LRE, I_LIM, I_IRE, I_IIM, I_CRE, I_CIM, I_8RE, I_8IM, I_Y, I_P, I_NX, I_A16, I_DEN = range(24)
        act(gq(I_ST), gq(I_ST), AF.Exp, ["g_st"], GN + ["g_st"])
        ts("dve", gq(I_LR), gq(I_LR), -1e-4, None, ALU.min, ALU.bypass, ["g_lr"], GN + ["g_lr"])
        gmul(I_X, I_LR, I_ST)
        gmul(I_ANG, I_LI, I_ST)
        gts(I_NX, I_X, -1.0, 0.0)

        def expser(o_, xi):
            gts(o_, xi, 1.0 / 6, 1.0)
            for kf in (5, 4, 3, 2, 1):
                gmul(o_, o_, xi)
                gts(o_, o_, 1.0 / kf, 1.0)
        expser(I_MAG, I_X)
        expser(I_MAGI, I_NX)
        gts(I_A16, I_ANG, 1.0 / 16, 0.0)
        gmul(I_Y, I_A16, I_A16)
        sc_ = [1.0, -1.0 / 6, 1.0 / 120, -1.0 / 5040, 1.0 / 362880, -1.0 / 39916800, 1.0 / 6227020800]
        cc_ = [1.0, -0.5, 1.0 / 24, -1.0 / 720, 1.0 / 40320, -1.0 / 3628800, 1.0 / 479001600, -1.0 / 87178291200]
        gts(I_P, I_Y, sc_[6], sc_[5])
        for kf in (4, 3, 2, 1, 0):
            gmul(I_P, I_P, I_Y)
            gts(I_P, I_P, 1.0, sc_[kf])
        gmul(I_S, I_P, I_A16)
        gts(I_P, I_Y, cc_[7], cc_[6])
        for kf in (5, 4, 3, 2, 1, 0):
            gmul(I_P, I_P, I_Y)
            gts(I_P, I_P, 1.0, cc_[kf])
        gts(I_C, I_P, 1.0, 0.0)

        def csq(re, im):
            gmul(I_T1, re, re)
            gmul(I_T2, im, im)
            gmul(im, re, im)
            gts(im, im, 2.0, 0.0)
            gadd(re, I_T1, I_T2, ALU.subtract)
        for _ in range(4):
            csq(I_C, I_S)
        gmul(I_LRE, I_MAG, I_C)
        gmul(I_LIM, I_MAG, I_S)
        gmul(I_IRE, I_MAGI, I_C)
        gmul(I_IIM, I_MAGI, I_S)
        gts(I_IIM, I_IIM, -1.0, 0.0)
        gts(I_P, I_LRE, 1.0, -1.0)
        gmul(I_T1, I_LR, I_LR)
        gmul(I_T2, I_LI, I_LI)
        gadd(I_DEN, I_T1, I_T2)
        op("dve", lambda e: e.reciprocal(out=gq(I_DEN), in_=gq(I_DEN)), GN, GN)
        gmul(I_T1, I_P, I_LR)
        gmul(I_T2, I_LIM, I_LI)
        gadd(I_CRE, I_T1, I_T2)
        gmul(I_CRE, I_CRE, I_DEN)
        gmul(I_T1, I_LIM, I_LR)
        gmul(I_T2, I_P, I_LI)
        gadd(I_CIM, I_T1, I_T2, ALU.subtract)
        gmul(I_CIM, I_CIM, I_DEN)
        gts(I_8RE, I_LRE, 1.0, 0.0)
        gts(I_8IM, I_LIM, 1.0, 0.0)
        for _ in range(3):
            csq(I_8RE, I_8IM)
        cp("dve", LA[:, 0, :], gq(I_8RE), GN, ["LA"])
        cp("dve", LA[:, 1, :], gq(I_8RE), GN, ["LA"])
        ts("dve", LB[:, 0, :], gq(I_8IM), -1.0, None, ALU.mult, ALU.bypass, GN, ["LB"])
        cp("dve", LB[:, 1, :], gq(I_8IM), GN, ["LB"])

        def cmul(eng, ore, oim, are, aim, xre, xim, n, rd, wr):
            ab = lambda a_: a_.unsqueeze(2).broadcast_to([64, 4, n])
            tv1, tv2 = T1[:, :, 0:n], T2[:, :, 0:n]
            tt(eng, tv1, xre, ab(are), ALU.mult, rd + GN, ["T1"])
            tt(eng, tv2, xim, ab(aim), ALU.mult, rd + GN, ["T2"])
            tt(eng, ore, tv1, tv2, ALU.subtract, ["T1", "T2"], wr)
            tt(eng, tv1, xim, ab(are), ALU.mult, rd + GN, ["T1"])
            tt(eng, tv2, xre, ab(aim), ALU.mult, rd + GN, ["T2"])
            tt(eng, oim, tv1, tv2, ALU.add, ["T1", "T2"], wr)

        for gb in range(8):
            g0 = gb * 4
            sl = slice(g0, g0 + 4)
            Z4r = Zre.rearrange("p g (s h) -> p g s h", s=8)
            Z4i = Zim.rearrange("p g (s h) -> p g s h", s=8)
            Y4r = Yre.rearrange("p g (s h) -> p g s h", s=8)
            Y4i = Yim.rearrange("p g (s h) -> p g s h", s=8)
            Bv = lambda tile_: tile_[0:64, g0 * 16:(g0 + 4) * 16].rearrange("p (g h) -> p g h", g=4)
            cmul("dve", Z4r[:, :, 7, :], Z4i[:, :, 7, :], gq(I_CRE)[:, sl], gq(I_CIM)[:, sl], Bv(Bre_t), Bv(Bim_t), 16, ["Bre", "Bim"], ["Zre", "Zim"])
            for s8 in range(6, -1, -1):
                cmul("dve", Z4r[:, :, s8, :], Z4i[:, :, s8, :], gq(I_LRE)[:, sl], gq(I_LIM)[:, sl], Z4r[:, :, s8 + 1, :], Z4i[:, :, s8 + 1, :], 16,
                     ["Zre", "Zim"], ["Zre", "Zim"])
            cp("dve", Y4r[:, :, 7, :], Bv(Cre_t), ["Cre"], ["Yre"])
            cp("dve", Y4i[:, :, 7, :], Bv(Cim_t), ["Cim"], ["Yim"])
            for l8 in range(6, -1, -1):
                cmul("dve", Y4r[:, :, l8, :], Y4i[:, :, l8, :], gq(I_IRE)[:, sl], gq(I_IIM)[:, sl], Y4r[:, :, l8 + 1, :], Y4i[:, :, l8 + 1, :], 16,
                     ["Yre", "Yim"], ["Yre", "Yim"])
            ts("dve", T1, Yim, -1.0, None, ALU.mult, ALU.bypass, ["Yim"], ["T1"])
            for hb_ in range(1):
                pt_, ptn = nps()
                for gi in range(4):
                    gg = gi
                    mm(pt_[:, gi * 128:(gi + 1) * 128], Zre[:, gg, :], Yre[:, gg, :], True, False, ["Zre", "Yre"], [ptn])
                    mm(pt_[:, gi * 128:(gi + 1) * 128], Zim[:, gg, :], T1[:, gg, :], False, True, ["Zim", "T1"], [ptn], inc=1 if gi == 3 else 0)
                tt("dve", tz_v[:, g0 + hb_ * 4:g0 + hb_ * 4 + 4, :], pt_[:].rearrange("p (g n) -> p g n", g=4),
                   m8[:].unsqueeze(1).broadcast_to([128, 4, 128]), ALU.mult, [ptn, "m8"], ["tz"])
            for (Zt, Zn, dst, dn) in ((Zre, "Zre", wsre_v, "wsre"), (Zim, "Zim", wsim_v, "wsim")):
                pw_, pwn = nps()
                for gi in range(4):
                    mm(pw_[:, gi * 64:(gi + 1) * 64], Zt[:, gi, :], ident_f[0:64, 0:64], True, True, [Zn, "ident_f"], [pwn], inc=1 if gi == 3 else 0)
                cp("act", dst[:, sl, :], pw_[:, 0:256].rearrange("p (g n) -> p g n", g=4), [pwn], [dn])
            a8 = lambda i: gq(i)[:, sl].unsqueeze(2).broadcast_to([64, 4, 128])
            tt("dve", T1, Yre, a8(I_8RE), ALU.mult, ["Yre"] + GN, ["T1"])
            tt("dve", T2, Yim, a8(I_8IM), ALU.mult, ["Yim"] + GN, ["T2"])
            tt("dve", wore_v[:, sl, :], T1, T2, ALU.subtract, ["T1", "T2"], ["wore"])
            tt("dve", T1, Yim, a8(I_8RE), ALU.mult, ["Yim"] + GN, ["T1"])
            tt("dve", T2, Yre, a8(I_8IM), ALU.mult, ["Yre"] + GN, ["T2"])
            tt("dve", T1, T1, T2, ALU.add, ["T1", "T2"], ["T1"])
            ts("dve", woim_v[:, sl, :], T1, -1.0, None, ALU.mult, ALU.bypass, ["T1"], ["woim"])
        op("dve", lambda e: e.memset(S2p[0], 0.0), (), ["S2_0"])
        sch.barrier()
        op("pool", lambda e: e.memset(gst[:, :, :, :, 0], 0.0), (), ["gsts", "gstw"])

        if dbg and dbg.get("stop") == "B0":
            break
        def f_loads(sc):
            tiles = [(s, 4 * sc + cc) for s in range(2) for cc in range(4)]
            for ti, (s, c) in enumerate(tiles):
                sch.dma("sp", "htl", hT8[:, :, ti * 128:(ti + 1) * 128], hT_d[gci(s, c)].rearrange("p (k t) -> p k t", k=8), ["hT_d"], ["hT8"])

        def f_uproj(sc):
            for j in range(4):
                for hf in range(2):
                    pu, pun = nps()
                    for s4 in range(4):
                        s8 = hf * 4 + s4
                        for k in range(8):
                            rhs = hT8[:, k, :].rearrange("p (n s) -> p s n", s=8)[:, s8, :]
                            mm(pu[:, s4 * 128:(s4 + 1) * 128], wb_v[:, k, j * 128:(j + 1) * 128], rhs, k == 0, k == 7, ["hT8", "wb"], [pun],
                               inc=1 if (k == 7 and s4 == 3) else 0)
                    cp("act" if hf == 0 else "dve", uT8[:, j, hf * 512:(hf + 1) * 512], pu[:], [pun], ["uT8"])
            for g8 in range(8):
                for s8 in range(8):
                    qd_ = ("sp", "act")[(g8 * 8 + s8) % 2]
                    sch.dma(qd_, "blk_" + qd_, ublk8[16 * s8:16 * s8 + 16, :, :].rearrange("p (j g) n -> p j g n", j=4)[:, :, g8, :],
                            uT8[16 * g8:16 * g8 + 16, :, s8 * 128:(s8 + 1) * 128], ["uT8"], ["ublk8_" + qd_])

        def f_statein(sc):
            for q4_ in range(4):
                for ri, (wsv, wsn) in enumerate(((wsre_v, "wsre"), (wsim_v, "wsim"))):
                    pw2 = [nps(), nps()]
                    for g8 in range(8):
                        g = q4_ * 8 + g8
                        pw_, pwn = pw2[g8 // 4]
                        mm(pw_[0:64, (g8 % 4) * 128:(g8 % 4 + 1) * 128], wsv[:, g, :], ublk8[:, g, :], g8 % 4 == 0, g8 % 4 == 3, [wsn, "ublk8_sp", "ublk8_act"], [pwn])
                    for hb_ in range(2):
                        pw_, pwn = pw2[hb_]
                        g0_ = q4_ * 8 + hb_ * 4
                        cp("act" if hb_ == 0 else "dve", gst[:, ri, g0_:g0_ + 4, :, 1:65],
                           pw_[0:64, :].rearrange("p (g s m) -> p g s m", g=4, s=2), [pwn], ["gstw", "gsts"])

        def f_rec(m0, m1):
            LAb = LA[:].unsqueeze(3).broadcast_to([64, 2, 32, 2])
            for m in range(m0, m1):
                Si, Sin_, So, Son_ = S2p[m % 2], f"S2_{m % 2}", S2p[(m + 1) % 2], f"S2_{(m + 1) % 2}"
                tt("dve", rt, Si, LAb, ALU.mult, [Sin_, "LA"], ["rt"])
                tt("dve", ru[:, 0, :, :], Si[:, 1, :, :], LB[:, 0, :].unsqueeze(2).broadcast_to([64, 32, 2]), ALU.mult, [Sin_, "LB"], ["ru"])
                tt("dve", ru[:, 1, :, :], Si[:, 0, :, :], LB[:, 1, :].unsqueeze(2).broadcast_to([64, 32, 2]), ALU.mult, [Sin_, "LB"], ["ru"])
                tt("dve", rt, rt, ru, ALU.add, ["rt", "ru"], ["rt"])
                tt("dve", So, rt, gst[:, :, :, :, 1 + m], ALU.add, ["rt", "gstw"], [Son_])
                cp("pool", gst[:, :, :, :, 1 + m], So, [Son_], ["gsts"])

        def f_y(sc):
            for q4_ in range(4):
                py2 = [nps(), nps()]
                for g8 in range(8):
                    g = q4_ * 8 + g8
                    py_, pyn_ = py2[g8 // 4]
                    o_ = py_[:, (g8 % 4) * 128:(g8 % 4 + 1) * 128]
                    mm(o_, tz_v[:, g, :], ublk8[:, g, :], g8 % 4 == 0, False, ["tz", "ublk8_sp", "ublk8_act"], [pyn_])
                    for s_ in range(2):
                        mm(o_[:, s_ * 64:(s_ + 1) * 64], wore_v[:, g, :], gst[:, 0, g, s_, 0:64], False, False, ["wore", "gsts"], [pyn_])
                        mm(o_[:, s_ * 64:(s_ + 1) * 64], woim_v[:, g, :], gst[:, 1, g, s_, 0:64], False, g8 % 4 == 3 and s_ == 1, ["woim", "gsts"], [pyn_])
                for hb_ in range(2):
                    py_, pyn_ = py2[hb_]
                    g0_ = q4_ * 8 + hb_ * 4
                    ub = ublk8[:, g0_:g0_ + 4, :]
                    tt("dve", ytB.rearrange("p (g n) -> p g n", g=4), ub, s5d_t[:, g0_:g0_ + 4].unsqueeze(2).broadcast_to([128, 4, 128]), ALU.mult,
                       ["ublk8_sp", "ublk8_act", "s5d_t"], ["ytB"])
                    tt("dve", ytB, ytB, py_[:], ALU.add, ["ytB", pyn_], ["ytB"])
                    act(ygB, ytB, AF.Square, ["ytB"], ["ygB"])
                    ts("dve", ygB, ygB, GELU_C2, GELU_C1, ALU.mult, ALU.add, ["ygB"], ["ygB"])
                    tt("dve", ygB, ygB, ytB, ALU.mult, ["ygB", "ytB"], ["ygB"])
                    act(ygB, ygB, AF.Tanh, ["ygB"], ["ygB"])
                    op("dve", lambda e: e.scalar_tensor_tensor(out=gyb8[:, g0_:g0_ + 4, :].rearrange("p g n -> p (g n)"), in0=ygB, scalar=1.0, in1=ytB,
                                                                op0=ALU.add, op1=ALU.mult), ["ygB", "ytB"], ["gyb8"])
            cp("pool", gst[:, :, :, :, 0], gst[:, :, :, :, 64], ["gsts"], ["gsts"])
            for g8 in range(8):
                for l8 in range(8):
                    qd_ = ("sp", "act")[(g8 * 8 + l8) % 2]
                    sch.dma(qd_, "ubl_" + qd_, gyT8[16 * g8:16 * g8 + 16, :, l8 * 128:(l8 + 1) * 128],
                            gyb8[16 * l8:16 * l8 + 16, :, :].rearrange("p (j g) n -> p j g n", j=4)[:, :, g8, :], ["gyb8"], ["gyT8_" + qd_])

        def f_gates(sc):
            for l8 in range(8):
                pg_, pgn = nps()
                for k in range(8):
                    lhs = hT8[:, k, :].rearrange("p (n l) -> p l n", l=8)[:, l8, :]
                    mm(pg_[:], lhs, wb_v[:, k, 512:1024], k == 0, k == 7, ["hT8", "wb"], [pgn])
                act(gatesB[:, l8, :], pg_[:], AF.Silu, [pgn], ["gatesB"])

        def f_tile(sc, l8):
            base = 512 * sc
            if True:
                xb, xbn = xtB[l8 % 2], f"xtB{l8 % 2}"
                for s_ in range(2):
                    sch.dma("sp", f"x{xbn}_{s_}", xb[64 * s_:64 * s_ + 64, :],
                            xsrc[s_, base:base + 512, :].rearrange("(m l) d -> l m d", l=8)[l8], xrd, [f"{xbn}_{s_}"])
                    sch.dma("act", f"mxl_{s_}", mixB[64 * s_:64 * s_ + 64, 0:1536],
                            mixA_d[s_, base:base + 512, :].rearrange("(m l) d -> l m d", l=8)[l8], ["mixA_d"], [f"mixB_{s_}"])
                pv = [nps(), nps()]
                for hf in range(2):
                    pp, ppn = pv[hf]
                    mm(pp[:], ones_b[0:1, :], glub_r[0:1, hf * 512:(hf + 1) * 512], True, False, ["ones_b", "glub_r"], [ppn])
                    for j in range(4):
                        mm(pp[:], gyT8[:, j, l8 * 128:(l8 + 1) * 128], glu_v[:, j, hf * 512:(hf + 1) * 512], False, j == 3,
                           ["gyT8_sp", "gyT8_act", "glu"], [ppn])
                act(L1B, pv[1][0][:], AF.Tanh, [pv[1][1]], ["L1B"], scale=0.5)
                ts("dve", L1B, L1B, 0.5, 0.5, ALU.mult, ALU.add, ["L1B"], ["L1B"])
                tt("dve", L2B, pv[0][0][:], L1B, ALU.mult, [pv[0][1], "L1B"], ["L2B"])
                tt("dve", L2B, L2B, gatesB[:, l8, :], ALU.mult, ["L2B", "gatesB"], ["L2B"])
                act(junkB[:, 0:512], L2B, AF.Square, ["L2B"], ["junkB", "st8"], accum_out=st8[:, 2:3])
                rsqrt(st8[:, 3:4], st8[:, 2:3], 1.0 / 512, ["st8"], ["st8"], 1)
                ts("dve", mixB[:, 1536:2048], L2B, st8[:, 3:4], None, ALU.mult, ALU.bypass, ["L2B", "st8"], ["mixB_S"])
                pm1, pm1n = nps()
                pm2, pm2n = nps()
                pm1b, pm2b = pm1[:].bitcast(BF16), pm2[:].bitcast(BF16)
                for j in range(16):
                    dst, dn = (pm1b, pm1n) if j < 8 else (pm2b, pm2n)
                    jj = j % 8
                    op("pe", lambda e: e.transpose(dst[:, jj * 128:(jj + 1) * 128], mixB[:, j * 128:(j + 1) * 128], ident_b[:]),
                       ["mixB_0", "mixB_1", "mixB_S", "ident_b"], [dn], inc=1 if j in (7, 15) else 0)
                mT2 = mixTB.rearrange("p k t -> p (k t)")
                cp("act", mT2[:, 0:1024], pm1b, [pm1n], ["mixTB"])
                cp("dve", mT2[:, 1024:2048], pm2b, [pm2n], ["mixTB"])
                po_ = [nps(), nps()]
                for n2 in range(2):
                    pp, ppn = po_[n2]
                    for kk in range(16):
                        mm(pp[:], mixTB[:, kk, :], wout_v[:, kk, n2 * 512:(n2 + 1) * 512], kk == 0, kk == 15, ["mixTB", "wout"], [ppn])
                for n2 in range(2):
                    act(junkB[:, n2 * 512:(n2 + 1) * 512], po_[n2][0][:], AF.Square, [po_[n2][1]], ["junkB", "st8"], accum_out=st8[:, 4 + n2:5 + n2])
                tt("dve", st8[:, 6:7], st8[:, 4:5], st8[:, 5:6], ALU.add, ["st8"], ["st8"])
                rsqrt(st8[:, 7:8], st8[:, 6:7], 1.0 / D_MODEL, ["st8"], ["st8"], 1)
                for n2 in range(2):
                    op("dve", lambda e: e.scalar_tensor_tensor(out=ygB, in0=po_[n2][0][:], scalar=st8[:, 7:8], in1=postw_bc[:, n2 * 512:(n2 + 1) * 512],
                                                                op0=ALU.mult, op1=ALU.mult), [po_[n2][1], "st8", "postw_bc"], ["ygB"])
                    tt("dve", xb[:, n2 * 512:(n2 + 1) * 512], xb[:, n2 * 512:(n2 + 1) * 512], ygB, ALU.add, [f"{xbn}_0", f"{xbn}_1", "ygB"],
                       [f"{xbn}_0", f"{xbn}_1"])
                for s_ in range(2):
                    sch.dma("sp", "ost", out[s_, base:base + 512, :].rearrange("(m l) d -> l m d", l=8)[l8], xb[64 * s_:64 * s_ + 64, :],
                            [f"{xbn}_0", f"{xbn}_1"], ["out_d"])

        f_loads(0); f_uproj(0); f_statein(0); f_rec(0, 64); f_y(0); f_gates(0)
        for sc in range(NSC):
            nxt = sc + 1 < NSC
            if nxt:
                f_loads(sc + 1)
                f_uproj(sc + 1)
            f_tile(sc, 0)
            f_tile(sc, 1)
            if nxt:
                f_statein(sc + 1)
            for l8 in range(2, 8):
                if nxt:
                    f_rec((l8 - 2) * 11, min(64, (l8 - 1) * 11))
                f_tile(sc, l8)
            if nxt:
                f_y(sc + 1)
                f_gates(sc + 1)
    sch.barrier()
    sch.finish("sp", ["mixA_d", "hT_d", "out_d"])
    es.close()
    return nc, sch


def prep_shared(inp, NL):
    f = lambda a: np.ascontiguousarray(np.asarray(a, dtype=np.float32))
    w_in = np.asarray(inp["w_in"], dtype=np.float32)[:NL]
    sh = {}
    sh["w_tma"] = f(np.concatenate([w_in[:, :, C_Z:C_Z + 1024], w_in[:, :, C_DT:C_DT + 16], w_in[:, :, C_I:C_I + 512],
                                    w_in[:, :, C_G:C_G + 512]], axis=2))
    sh["w_fma"] = f(np.concatenate([w_in[:, :, C_XBC:C_XBC + 1536], w_in[:, :, C_Q:C_Q + 512], w_in[:, :, C_F:C_F + 512]], axis=2))
    sh["w_b"] = f(w_in[:, :, C_U:C_U + 1024])
    sh["w_out"] = f(np.asarray(inp["w_out"])[:NL])
    sh["prew"] = f(np.asarray(inp["pre_norm_w"])[:NL].reshape(NL, 8, 128).transpose(0, 2, 1))
    sh["postw"] = f(np.asarray(inp["post_norm_w"])[:NL].reshape(NL, 1, D_MODEL))
    mixnw = np.concatenate([np.asarray(inp["ssd_norm_w"])[:NL], np.asarray(inp["hgrn_norm_w"])[:NL], np.asarray(inp["s5_norm_w"])[:NL]], axis=1)
    sh["mixnw"] = f(mixnw.reshape(NL, 16, 128).transpose(0, 2, 1))
    cw = np.asarray(inp["ssd_conv_w"])[:NL]
    sh["convw"] = f(cw.reshape(NL, 4, 12, 128).transpose(0, 3, 1, 2).reshape(NL, 128, 48))
    cb = np.asarray(inp["ssd_conv_b"])[:NL]
    sh["convb_pp"] = f(cb.reshape(NL, 12, 128).transpose(0, 2, 1))
    sh["convb_row"] = f(cb.reshape(NL, 1, 1536))
    sh["dtb"] = f(np.asarray(inp["ssd_dt_bias"])[:NL].reshape(NL, 1, 16))
    sh["alog"] = f(np.asarray(inp["ssd_a_log"])[:NL].reshape(NL, 1, 16))
    sh["ssdd"] = f(np.asarray(inp["ssd_d"])[:NL].reshape(NL, 1, 16))
    hl = np.asarray(inp["hgrn_lower_bounds"])[:NL]
    sh["hlb"] = f(hl.reshape(NL, 4, 128).transpose(2, 1, 0).reshape(128, 4 * NL))
    sh["lamre"] = f(np.asarray(inp["s5_lambda_re"])[:NL].transpose(0, 2, 1))
    sh["lamim"] = f(np.asarray(inp["s5_lambda_im"])[:NL].transpose(0, 2, 1))
    sh["lstep"] = f(np.asarray(inp["s5_log_step"])[:NL].reshape(NL, 1, 32))
    sh["bre"] = f(np.asarray(inp["s5_b_re"])[:NL].transpose(0, 2, 1, 3).reshape(NL, 64, 512))
    sh["bim"] = f(np.asarray(inp["s5_b_im"])[:NL].transpose(0, 2, 1, 3).reshape(NL, 64, 512))
    sh["cre"] = f(np.asarray(inp["s5_c_re"])[:NL].transpose(0, 3, 1, 2).reshape(NL, 64, 512))
    sh["cim"] = f(np.asarray(inp["s5_c_im"])[:NL].transpose(0, 3, 1, 2).reshape(NL, 64, 512))
    d5 = np.asarray(inp["s5_d"])[:NL].reshape(NL, 32, 16)
    sh["s5d"] = f(np.broadcast_to(d5.transpose(0, 2, 1)[:, None, :, :], (NL, 8, 16, 32)).reshape(NL, 128, 32))
    sh["gluw"] = f(np.asarray(inp["s5_glu_w"])[:NL])
    sh["glub"] = f(np.asarray(inp["s5_glu_b"])[:NL].reshape(NL, 1, 1024))
    for k, v in host_consts().items():
        sh["c_" + k] = v
    return sh


LAYER_GROUPS = [[0, 1, 2, 3]]


def kernel(**inputs):
    x = np.ascontiguousarray(np.asarray(inputs["x"], dtype=np.float32))
    B = x.shape[0]
    S = B // NCORES
    sh = prep_shared(inputs, NL_FULL)
    cur = x
    for grp in LAYER_GROUPS:
        nc, _ = build(NL_FULL, S, x.shape[1], layers=grp)
        in_maps = [dict(sh, x=np.ascontiguousarray(cur[S * c:S * (c + 1)])) for c in range(NCORES)]
        res = run_bass_kernel_spmd(nc, in_maps, core_ids=list(range(NCORES)))
        cur = np.concatenate([np.asarray(r["out"], dtype=np.float32) for r in res.results], axis=0)
    return cur
```

```python
import math
from contextlib import ExitStack

import numpy as np
import concourse.bass as bass
import concourse.mybir as mybir
from concourse.bass_utils import run_bass_kernel_spmd

F32 = mybir.dt.float32
BF16 = mybir.dt.bfloat16
AF = mybir.ActivationFunctionType
ALU = mybir.AluOpType
AX = mybir.AxisListType

D_MODEL = 1024
IN_COLS = 5648
EPS = 1e-6
NL_FULL = 4
SEQ_FULL = 2048
NCORES = 8
T = 128

C_Z, C_XBC, C_DT, C_Q, C_F, C_I, C_G, C_U, C_SG = 0, 1024, 2560, 2576, 3088, 3600, 4112, 4624, 5136
N_TMA = 1024 + 16 + 512 + 512
N_FMA = 1536 + 512 + 512
N_B = 1024
GELU_C1 = 0.7978845608028654
GELU_C2 = 0.044715 * GELU_C1


class Sched:
    def __init__(self, nc):
        self.nc = nc
        self.eng = {"pe": nc.tensor, "act": nc.scalar, "dve": nc.vector, "pool": nc.gpsimd, "sp": nc.sync}
        self.sem = {}
        self.cnt = {}
        self.waited = {}
        self.lastw = {}
        self.readers = {}
        self.pending = {}
        self.ninst = 0
        self.gen = {}
        for k in ("pe", "act", "dve", "pool"):
            self._mk(k)

    def _mk(self, k):
        self.sem[k] = self.nc.alloc_semaphore("s_" + k)
        self.cnt[k] = 0
        self.pending[k] = False

    def _phys(self, names, writing):
        out = []
        for b in names:
            if "#" in b:
                p, g = b.split("#")
                if writing:
                    if self.gen.get(p) != g and int(g) > int(self.gen.get(p, "-1")):
                        self.gen[p] = g
                assert self.gen.get(p) == g, f"stale PSUM bank use {b} (current gen {self.gen.get(p)})"
                b = p
            out.append(b)
        return out

    def _deps(self, reads, writes):
        reads[:] = self._phys(reads, False)
        writes[:] = self._phys(writes, True)
        deps = {}
        raw = {}
        for b in reads:
            lw = self.lastw.get(b)
            if lw:
                deps[lw[0]] = max(deps.get(lw[0], 0), lw[1])
                raw[lw[0]] = max(raw.get(lw[0], 0), lw[1])
        for b in writes:
            lw = self.lastw.get(b)
            if lw:
                deps[lw[0]] = max(deps.get(lw[0], 0), lw[1])
            for e, i in self.readers.get(b, {}).items():
                deps[e] = max(deps.get(e, 0), i)
        return deps, raw

    def _emit_waits(self, issuer, me, deps, raw):
        w = self.waited.setdefault(issuer, {})
        for src, idx in deps.items():
            if src == me:
                if me == "pe":
                    continue
            if idx > w.get(src, 0):
                self.eng[issuer].wait_ge(self.sem[src], idx)
                w[src] = idx
                self.ninst += 1

    def op(self, e, fn, reads=(), writes=(), inc=1):
        reads, writes = list(reads), list(writes)
        deps, raw = self._deps(reads, writes)
        self._emit_waits(e, e, deps, raw)
        ins = fn(self.eng[e])
        self.ninst += 1
        if inc:
            ins.then_inc(self.sem[e], 1)
            self.cnt[e] += 1
            idx = self.cnt[e]
            self.pending[e] = False
        else:
            idx = self.cnt[e] + 1
            self.pending[e] = True
        for b in reads:
            self.readers.setdefault(b, {})[e] = max(self.readers.get(b, {}).get(e, 0), idx)
        for b in writes:
            self.lastw[b] = (e, idx)
            self.readers[b] = {}
        return ins

    def dma(self, q, slot, out, in_, reads=(), writes=(), **kw):
        if slot not in self.sem:
            self._mk(slot)
        reads, writes = list(reads), list(writes)
        deps, raw = self._deps(reads, writes)
        self._emit_waits(q, None, deps, raw)
        ins = self.eng[q].dma_start(out=out, in_=in_, **kw)
        ins.then_inc(self.sem[slot], 16)
        self.ninst += 1
        self.cnt[slot] += 16
        idx = self.cnt[slot]
        for b in reads:
            self.readers.setdefault(b, {})[slot] = idx
        for b in writes:
            self.lastw[b] = (slot, idx)
            self.readers[b] = {}

    def barrier(self):
        for e in ("pe", "act", "dve", "pool", "sp"):
            w = self.waited.setdefault(e, {})
            for src, c in self.cnt.items():
                if c == 0 or (src == e and e == "pe"):
                    continue
                if c > w.get(src, 0):
                    self.eng[e].wait_ge(self.sem[src], c)
                    w[src] = c
                    self.ninst += 1

    def dma_sync(self, q, slot):
        w = self.waited.setdefault(q, {})
        c = self.cnt.get(slot, 0)
        if c > w.get(slot, 0):
            self.eng[q].wait_ge(self.sem[slot], c)
            w[slot] = c
            self.ninst += 1

    def finish(self, q, bufs):
        deps, raw = self._deps(list(bufs), [])
        self._emit_waits(q, None, deps, raw)
        for k, v in self.pending.items():
            assert not v, k


def host_consts():
    c = {}
    c["ident"] = np.eye(128, dtype=np.float32)
    tl = np.arange(128)
    c["utri"] = (tl[:, None] <= tl[None, :]).astype(np.float32)
    c["negm"] = np.where(tl[None, :] >= tl[:, None], 0.0, -30000.0).astype(np.float32)
    blk = (tl[:, None] // 64) == (tl[None, :] // 64)
    c["m64"] = ((tl[None, :] >= tl[:, None]) & blk).astype(np.float32)
    sc = np.ones((128, 512), np.float32)
    sc[:, 0::64] = 0.0
    c["scan0"] = sc
    band = np.zeros((8, 128, 240), np.float32)
    for a in range(8):
        for k in range(16 * a, 16 * a + 16):
            band[a, k, (k % 16) + 112] = 1.0
    c["band"] = band.transpose(1, 0, 2).reshape(128, 8 * 240).copy()
    c["m8"] = ((tl[None, :] // 16) >= (tl[:, None] // 16)).astype(np.float32)
    c["ones"] = np.ones((128, 128), np.float32)
    pm = np.zeros((128, 128), np.float32)
    for m in range(128):
        pm[(m % 8) * 16 + m // 8, m] = 1.0
    c["pm"] = pm
    c["pmT"] = np.ascontiguousarray(pm.T)
    return c


def build(NL, S, L, dbg=None, layers=None):
    assert S == 2 and L % 512 == 0
    NCH = L // T
    nc = bass.Bass("TRN2", target_bir_lowering=False)
    es = ExitStack()

    def din(name, shape, dt=F32):
        return nc.dram_tensor(name, list(shape), dt, kind="ExternalInput").ap()

    x_in = din("x", [S, L, D_MODEL])
    w_tma = din("w_tma", [NL, D_MODEL, N_TMA])
    w_fma = din("w_fma", [NL, D_MODEL, N_FMA])
    w_b = din("w_b", [NL, D_MODEL, N_B])
    w_out = din("w_out", [NL, 2048, D_MODEL])
    prew = din("prew", [NL, 128, 8])
    postw = din("postw", [NL, 1, D_MODEL])
    mixnw = din("mixnw", [NL, 128, 16])
    convw = din("convw", [NL, 128, 48])
    convb_pp = din("convb_pp", [NL, 128, 12])
    convb_row = din("convb_row", [NL, 1, 1536])
    dtb = din("dtb", [NL, 1, 16])
    alog = din("alog", [NL, 1, 16])
    ssdd = din("ssdd", [NL, 1, 16])
    hlb = din("hlb", [128, 4 * NL])
    lamre = din("lamre", [NL, 64, 32])
    lamim = din("lamim", [NL, 64, 32])
    lstep = din("lstep", [NL, 1, 32])
    bre = din("bre", [NL, 64, 512])
    bim = din("bim", [NL, 64, 512])
    cre = din("cre", [NL, 64, 512])
    cim = din("cim", [NL, 64, 512])
    s5d = din("s5d", [NL, 128, 32])
    gluw = din("gluw", [NL, 512, 1024])
    glub = din("glub", [NL, 1, 1024])
    consts = {k: din("c_" + k, v.shape) for k, v in host_consts().items()}

    out = nc.dram_tensor("out", [S, L, D_MODEL], F32, kind="ExternalOutput").ap()
    hT_d = nc.dram_tensor("hT_scr", [S * NCH, 128, 8 * 128], BF16, kind="Internal").ap()
    mixA_d = nc.dram_tensor("mixA_scr", [S, L, 1536], BF16, kind="Internal").ap()
    dbg_out = None
    if False:
        dbg_out = {}

    def sb(name, shape, dt=F32):
        return es.enter_context(nc.sbuf_tensor(name, list(shape), dt))

    ident_b = sb("ident_b", [128, 128], BF16)
    ident_f = sb("ident_f", [128, 128])
    utri = sb("utri", [128, 128])
    ones_f = sb("ones_f", [128, 128])
    ones_b = sb("ones_b", [128, 128], BF16)
    negm_b = sb("negm_b", [128, 128], BF16)
    m64 = sb("m64", [128, 128], BF16)
    m8 = sb("m8", [128, 128])
    pm_b = sb("pm_b", [128, 128], BF16)
    pmT_b = sb("pmT_b", [128, 128], BF16)
    hlb_t = sb("hlb_t", [128, 4, NL])
    lb_all = sb("lb_all", [128, 4, NL])
    neghalf = sb("neghalf", [128, 16])
    st8 = sb("st8", [128, 8])
    prew_t = sb("prew_t", [128, 8])
    mixnw_t = sb("mixnw_t", [128, 16])
    convw_t = sb("convw_t", [128, 48])
    convb_t = sb("convb_t", [128, 12])
    dtb_bc = sb("dtb_bc", [128, 16])
    a_bc = sb("a_bc", [128, 16])
    d_bc = sb("d_bc", [128, 16])
    lb_t = sb("lb_t", [128, 4])
    dt_t = sb("dt_t", [128, 16])
    dtmp = sb("dtmp", [128, 16])
    dtA = sb("dtA", [128, 16])
    acum = sb("acum", [128, 16])
    nacum = sb("nacum", [128, 16])
    eacum = sb("eacum", [128, 16])
    dte = sb("dte", [128, 16])
    cdec = sb("cdec", [128, 16])
    ecend = sb("ecend", [128, 8])
    s5d_t = sb("s5d_t", [128, 32])
    LA = sb("LA", [64, 2, 32])
    LB = sb("LB", [64, 2, 32])

    WREG = 38912
    wreg = sb("wreg", [128, WREG], BF16)
    wtma = wreg[:, 0:8 * N_TMA].rearrange("p (k n) -> p k n", k=8)
    wfma = wreg[:, 8 * N_TMA:8 * (N_TMA + N_FMA)].rearrange("p (k n) -> p k n", k=8)
    o = 0
    wb_v = wreg[:, o:o + 8 * N_B].rearrange("p (k n) -> p k n", k=8); o += 8 * N_B
    wout_v = wreg[:, o:o + 16 * 1024].rearrange("p (k n) -> p k n", k=16); o += 16 * 1024
    glu_v = wreg[:, o:o + 4 * 1024].rearrange("p (k n) -> p k n", k=4); o += 4 * 1024
    wsim_v = wreg[:, o:o + 32 * 64].rearrange("p (g n) -> p g n", g=32); o += 32 * 64
    wore_v = wreg[0:64, o:o + 32 * 128].rearrange("p (g n) -> p g n", g=32); o += 32 * 128
    woim_v = wreg[0:64, o:o + 32 * 128].rearrange("p (g n) -> p g n", g=32); o += 32 * 128
    assert o <= WREG, (o, WREG)
    reg2 = sb("reg2", [128, 48 * 128], BF16)
    dconv = reg2[:].rearrange("p (k j n) -> p k j n", k=4, j=12)
    tz_v = reg2[:, 0:4096].rearrange("p (g n) -> p g n", g=32)
    wsre_v = reg2[:, 4096:6144].rearrange("p (g n) -> p g n", g=32)

    ARENA = 58100
    arena = sb("arena", [128, ARENA], BF16)
    aoff = [0]

    def cv(n, dt=BF16, parts=128):
        ne = n * (2 if dt == F32 else 1)
        assert aoff[0] + ne <= ARENA, (aoff[0], ne, ARENA)
        ap = arena[0:parts, aoff[0]:aoff[0] + ne]
        aoff[0] += ne
        return ap.bitcast(F32) if dt == F32 else ap

    xt = [cv(1024, F32), cv(1024, F32)]
    xn = cv(1024)
    hTq = cv(4096).rearrange("p (k t) -> p k t", k=8)
    zs = cv(1024)
    vtok = cv(512)
    gs = cv(512)
    XT = [cv(12 * 260).rearrange("p (j t) -> p j t", j=12) for _ in range(2)]
    qsq = cv(2048).rearrange("p (h t) -> p h t", h=4)
    ef = cv(512, F32)
    xs = cv(1024, F32)
    btok = cv(256)
    bctq = cv(2048).rearrange("p (j t) -> p j t", j=4)
    xdt = cv(1024)
    xw = cv(1024)
    xsd = cv(1024)
    scT = cv(256).rearrange("p (g t) -> p g t", g=2)
    dec = cv(1024).rearrange("p (r t) -> p r t", r=8)
    MT = cv(1024).rearrange("p (r t) -> p r t", r=8)
    hst = [cv(1024, F32), cv(1024, F32)]
    hbf = [cv(1024), cv(1024)]
    L1 = cv(512, F32)
    L2 = cv(512, F32)
    cum = cv(512, F32)
    ecum = cv(512, F32)
    qTa = cv(512).rearrange("p (h t) -> p h t", h=4)
    qTb = cv(512).rearrange("p (h t) -> p h t", h=4)
    kT = cv(512).rearrange("p (h t) -> p h t", h=4)
    kend = cv(512).rearrange("p (h t) -> p h t", h=4)
    kendT = cv(512).rearrange("p (h t) -> p h t", h=4)
    attn = cv(512).rearrange("p (h t) -> p h t", h=4)
    Sst = [cv(512, F32).rearrange("p (h t) -> p h t", h=4) for _ in range(2)]
    Sbf = [cv(512).rearrange("p (h t) -> p h t", h=4) for _ in range(2)]
    ytmp = cv(512, F32)
    yg = cv(512, F32)
    otmp = cv(512, F32)
    mix = cv(2048)
    mixT = cv(1536).rearrange("p (k t) -> p k t", k=12)
    junk = dec.rearrange("p r t -> p (r t)")
    scan0 = cv(512, F32)
    convb_r = cv(1536, parts=1)
    endA = aoff[0]
    aoff[0] = 0
    xtB = [cv(1024, F32), cv(1024, F32)]
    hT8 = cv(8192).rearrange("p (k t) -> p k t", k=8)
    uT8 = cv(4096).rearrange("p (j t) -> p j t", j=4)
    ublk8 = cv(4096).rearrange("p (g n) -> p g n", g=32)
    gst = cv(2 * 32 * 2 * 65, parts=64).rearrange("p (r g s m) -> p r g s m", r=2, g=32, s=2)
    gyb8 = cv(4096).rearrange("p (g n) -> p g n", g=32)
    gyT8 = cv(4096).rearrange("p (j t) -> p j t", j=4)
    ytB = cv(512, F32)
    ygB = cv(512, F32)
    gatesB = cv(4096).rearrange("p (l n) -> p l n", l=8)
    L1B = cv(512, F32)
    L2B = cv(512, F32)
    mixB = cv(2048)
    mixTB = cv(2048).rearrange("p (k t) -> p k t", k=16)
    junkB = cv(1024)
    postw_bc = cv(1024, F32)
    S2p = [cv(128, F32, parts=64).rearrange("p (r g s) -> p r g s", r=2, g=32) for _ in range(2)]
    rt = cv(128, F32, parts=64).rearrange("p (r g s) -> p r g s", r=2, g=32)
    ru = cv(128, F32, parts=64).rearrange("p (r g s) -> p r g s", r=2, g=32)
    glub_r = cv(1024, parts=1)
    endB = aoff[0]
    aoff[0] = 4096
    g32 = cv(40 * 32, F32, parts=64).rearrange("p (i g) -> p i g", i=40)
    v4 = lambda: cv(1024, F32, parts=64).rearrange("p (g n) -> p g n", g=8)
    Zre, Zim, Yre, Yim, T1, T2 = v4(), v4(), v4(), v4(), v4(), v4()
    Bre_t, Bim_t, Cre_t, Cim_t = (cv(512, F32, parts=64) for _ in range(4))
    assert aoff[0] <= endB

    ps = [es.enter_context(nc.psum_tensor(f"ps{i}", [128, 512], F32)) for i in range(8)]
    psn = [f"ps{i}" for i in range(8)]
    ps_rr = [0]

    def nps():
        i = ps_rr[0] % 8
        ps_rr[0] += 1
        return ps[i], f"{psn[i]}#{ps_rr[0]}"

    sch = Sched(nc)
    blk = es.enter_context(nc.Block())
    op = sch.op

    def act(out_, in_, func, reads, writes, **kw):
        return op("act", lambda e: e.activation(out=out_, in_=in_, func=func, **kw), reads, writes)

    def tt(eng, out_, a, b, o_, reads, writes):
        return op(eng, lambda e: e.tensor_tensor(out=out_, in0=a, in1=b, op=o_), reads, writes)

    def ts(eng, out_, a, s1, s2, o0, o1, reads, writes):
        return op(eng, lambda e: e.tensor_scalar(out=out_, in0=a, scalar1=s1, scalar2=s2, op0=o0, op1=o1), reads, writes)

    def cp(eng, out_, in_, reads, writes):
        if eng == "act":
            return act(out_, in_, AF.Copy, reads, writes)
        return op(eng, lambda e: e.tensor_copy(out=out_, in_=in_), reads, writes)

    def mm(out_, lhsT, rhs, start, stop, reads, writes, inc=None):
        return op("pe", lambda e: e.matmul(out_, lhsT, rhs, start=start, stop=stop), reads, writes,
                  inc=(1 if stop else 0) if inc is None else inc)

    def rsqrt(out_, in_, scale, reads, writes, n):
        ts("dve", out_, in_, scale, EPS, ALU.mult, ALU.add, reads, writes)
        op("pool", lambda e: e.tensor_tensor(out=out_, in0=out_, in1=neghalf[:, 0:n], op=ALU.pow), list(writes) + ["neghalf"], writes)

    def ld(q, out_, in_, w):
        sch.dma(q, "d_" + w[0], out_, in_, (), w)


    HALF = ARENA // 4
    stg = [arena[:, 0:2 * HALF].bitcast(F32), arena[:, 2 * HALF:4 * HALF].bitcast(F32)]
    eng_rr = [0]

    def stage_load(items):
        rounds, cur, off = [], [], 0
        for it in items:
            n = it[0].shape[-1]
            if off + n > HALF:
                rounds.append(cur)
                cur, off = [], 0
            cur.append((it, off, n))
            off += n
        rounds.append(cur)
        for ri, rnd in enumerate(rounds):
            b = ri % 2
            for ii, (it, off, n) in enumerate(rnd):
                q = ("sp", "act")[ii % 2]
                sch.dma(q, f"stg{b}_{q}", stg[b][:, off:off + n], it[1], (), [f"stg{b}_{q}"])
            for (it, off, n) in rnd:
                e = ("dve", "act")[eng_rr[0] % 2]
                eng_rr[0] += 1
                rd = [f"stg{b}_sp", f"stg{b}_act"] + ([it[4]] if len(it) > 4 else [])
                src = stg[b][:, off:off + n]
                if e == "act":
                    act(it[0], src, AF.Copy, rd, [it[3]], scale=it[2])
                else:
                    ts(e, it[0], src, it[2], None, ALU.mult, ALU.bypass, rd, [it[3]])

    ld("sp", ident_f[:], consts["ident"], ["ident_f"])
    ld("sp", utri[:], consts["utri"], ["utri"])
    ld("sp", ones_f[:], consts["ones"], ["ones_f"])
    ld("sp", m8[:], consts["m8"], ["m8"])
    ld("sp", hlb_t[:].rearrange("p h l -> p (h l)"), hlb, ["hlb_t"])
    ld("pool", ident_b[:], consts["ident"], ["ident_b"])
    ld("pool", ones_b[:], consts["ones"], ["ones_b"])
    ld("pool", negm_b[:], consts["negm"], ["negm_b"])
    ld("pool", m64[:], consts["m64"], ["m64"])
    ld("pool", pm_b[:], consts["pm"], ["pm_b"])
    ld("pool", pmT_b[:], consts["pmT"], ["pmT_b"])
    op("dve", lambda e: e.memset(neghalf[:], -0.5), (), ["neghalf"])
    act(hlb_t[:], hlb_t[:], AF.Exp, ["hlb_t"], ["hlb_t"])
    op("dve", lambda e: e.tensor_reduce(out=st8[:, 0:4], in_=hlb_t[:], axis=AX.X, op=ALU.add), ["hlb_t"], ["st8"])
    op("dve", lambda e: e.reciprocal(out=st8[:, 0:4], in_=st8[:, 0:4]), ["st8"], ["st8"])
    tt("dve", hlb_t[:], hlb_t[:], st8[:, 0:4].unsqueeze(2).broadcast_to([128, 4, NL]), ALU.mult, ["hlb_t", "st8"], ["hlb_t"])
    op("dve", lambda e: e.memset(lb_all[:], 0.0), (), ["lb_all"])
    for l in range(1, NL):
        tt("dve", lb_all[:, :, l], lb_all[:, :, l - 1], hlb_t[:, :, l], ALU.add, ["lb_all", "hlb_t"], ["lb_all"])

    NQ = L // 256
    NSC = L // 512
    layers = list(range(NL)) if layers is None else list(layers)

    def gci(s, c):
        return s * NCH + c

    for layer in layers:
        xsrc = x_in if layer == layers[0] else out
        xrd = ["out_d"] if layer != layers[0] else []
        sch.barrier()
        ld("sp", prew_t[:], prew[layer], ["prew_t"])
        ld("sp", convw_t[:], convw[layer], ["convw_t"])
        ld("sp", convb_t[:], convb_pp[layer], ["convb_t"])
        ld("sp", dtb_bc[:], dtb[layer].partition_broadcast(128), ["dtb_bc"])
        ld("sp", a_bc[:], alog[layer].partition_broadcast(128), ["a_bc"])
        ld("sp", d_bc[:], ssdd[layer].partition_broadcast(128), ["d_bc"])
        act(a_bc[:], a_bc[:], AF.Exp, ["a_bc"], ["a_bc"])
        ts("dve", a_bc[:], a_bc[:], -1.0, None, ALU.mult, ALU.bypass, ["a_bc"], ["a_bc"])
        cp("dve", lb_t[:], lb_all[:, :, layer], ["lb_all"], ["lb_t"])
        itemsA = []
        for k in range(8):
            itemsA.append((wtma[:, k, :], w_tma[layer, k * 128:(k + 1) * 128, :], prew_t[:, k:k + 1], "wtma", "prew_t"))
            itemsA.append((wfma[:, k, :], w_fma[layer, k * 128:(k + 1) * 128, :], prew_t[:, k:k + 1], "wfma", "prew_t"))
        stage_load(itemsA)
        sch.barrier()
        ld("sp", scan0, consts["scan0"], ["scan0"])
        ld("pool", convb_r, convb_row[layer], ["convb_r"])
        for k in range(4):
            for j in range(12):
                ts("dve", dconv[:, k, j, :], ident_b[:], convw_t[:, k * 12 + j:k * 12 + j + 1], None, ALU.mult, ALU.bypass,
                   ["ident_b", "convw_t"], ["dconv"])
        op("dve", lambda e: e.memset(qTa, 0.0), (), ["qTa"])
        op("dve", lambda e: e.memset(qTb, 0.0), (), ["qTb"])
        for s in range(2):
            op("dve", lambda e: e.memset(XT[s][:, :, 0:3], 0.0), (), [f"XT{s}"])
            op("dve", lambda e: e.memset(hst[s], 0.0), (), [f"hst{s}"])
            op("pool", lambda e: e.memset(hbf[s], 0.0), (), [f"hbf{s}"])
            op("dve", lambda e: e.memset(Sst[s], 0.0), (), [f"Sst{s}"])
            op("pool", lambda e: e.memset(Sbf[s], 0.0), (), [f"Sbf{s}"])

        xcnt = [0]
        if dbg and dbg.get("stop") == "A0":
            break
        for qd in range(NQ):
            quad = [(s, 2 * qd + pp) for s in range(2) for pp in range(2)]
            for qi, (s, c) in enumerate(quad):
                xb, xbn = xt[xcnt[0] % 2], f"xt{xcnt[0] % 2}"
                xcnt[0] += 1
                sch.dma("sp", "x" + xbn, xb, xsrc[s, c * T:(c + 1) * T, :], xrd, [xbn])
                act(xn, xb, AF.Square, [xbn], ["xn", "st8"], accum_out=st8[:, 0:1])
                rsqrt(st8[:, 1:2], st8[:, 0:1], 1.0 / D_MODEL, ["st8"], ["st8"], 1)
                ts("dve", xn, xb, st8[:, 1:2], None, ALU.mult, ALU.bypass, [xbn, "st8"], ["xn"])
                p0, p0n = nps()
                p0b = p0[:].bitcast(BF16)
                for k in range(8):
                    op("pe", lambda e: e.transpose(p0b[:, k * 128:(k + 1) * 128], xn[:, k * 128:(k + 1) * 128], ident_b[:]),
                       ["xn", "ident_b"], [p0n], inc=1 if k == 7 else 0)
                cp("act", hTq[:, :, qi * 128:(qi + 1) * 128], p0b.rearrange("p (k t) -> p k t", k=8), [p0n], ["hTq"])
                sch.dma("sp", "hts", hT_d[gci(s, c)].rearrange("p (k t) -> p k t", k=8), hTq[:, :, qi * 128:(qi + 1) * 128], ["hTq"], ["hT_d"])
            if dbg and dbg.get("stop") == "A1":
                break
            njc = int(dbg.get("njc", 16)) if dbg else 16
            for jc in range(njc):
                pf, pfn = nps()
                for k in range(8):
                    mm(pf[:], wfma[:, k, jc * 128:(jc + 1) * 128], hTq[:, k, :], k == 0, k == 7, ["hTq", "wfma"], [pfn])
                if dbg and dbg.get("evac") == "junk":
                    cp("act", xn[:, 0:512], pf[:], [pfn], ["xn"])
                elif jc < 12:
                    ev = dbg.get("evac", "same") if dbg else "same"
                    for s in range(2):
                        if ev == "act_only" and s == 1:
                            continue
                        if ev == "dve_only" and s == 0:
                            continue
                        c0_ = 4 if ev == "even" else 3
                        eng_ = "act" if s == 0 else "dve"
                        if ev == "swap":
                            eng_ = "dve" if s == 0 else "act"
                        if ev == "same":
                            eng_ = "act" if jc % 2 == 0 else "dve"
                        cp(eng_, XT[s][:, jc, c0_:c0_ + 256], pf[:, s * 256:(s + 1) * 256], [pfn], [f"XT{s}"])
                else:
                    act(qsq[:, jc - 12, :], pf[:], AF.Silu, [pfn], ["qsq"])
            if dbg and dbg.get("stop") == "A15":
                break
            for s in range(2):
                for half in range(2):
                    pb, pbn = nps()
                    for j2 in range(2):
                        jj = half * 2 + j2
                        j = 8 + jj
                        for k in range(4):
                            mm(pb[:, j2 * 256:(j2 + 1) * 256], dconv[:, k, j, :], XT[s][:, j, k:k + 256], k == 0, k == 3, [f"XT{s}", "dconv"], [pbn],
                               inc=1 if (k == 3 and j2 == 1) else 0)
                    for j2 in range(2):
                        jj = half * 2 + j2
                        act(bctq[:, jj, s * 256:(s + 1) * 256], pb[:, j2 * 256:(j2 + 1) * 256], AF.Silu, [pbn, "convb_t"], ["bctq"],
                            bias=convb_t[:, 8 + jj:9 + jj])
            if dbg and dbg.get("stop") == "A2":
                break
            for qi, (s, c) in enumerate(quad):
                pp_ = c % 2
                tsl = slice(qi * 128, (qi + 1) * 128)
                XTs, XTn = XT[s], f"XT{s}"
                hs, hsn, hb, hbn = hst[s], f"hst{s}", hbf[s], f"hbf{s}"
                Ss, Ssn, Sb, Sbn = Sst[s], f"Sst{s}", Sbf[s], f"Sbf{s}"

                def tm_slab(c0, n):
                    pz, pzn = nps()
                    for k in range(8):
                        mm(pz[:, 0:n], hTq[:, k, tsl], wtma[:, k, c0:c0 + n], k == 0, k == 7, ["hTq", "wtma"], [pzn])
                    return pz, pzn
                for h2 in range(2):
                    pz, pzn = tm_slab(h2 * 512, 512)
                    act(zs[:, h2 * 512:(h2 + 1) * 512], pz[:], AF.Silu, [pzn], ["zs"])
                pz, pzn = tm_slab(1040 + 512, 512)
                act(gs, pz[:], AF.Silu, [pzn], ["gs"])
                pz, pzn = tm_slab(1040, 512)
                cp("act", vtok, pz[:], [pzn], ["vtok"])
                pd, pdn = tm_slab(1024, 16)
                tt("dve", dtmp[:], pd[:, 0:16], dtb_bc[:], ALU.add, [pdn, "dtb_bc"], ["dtmp"])
                pfq, pfqn = nps()
                for jj in range(4):
                    for k in range(8):
                        mm(pfq[:, jj * 128:(jj + 1) * 128], wfma[:, k, (16 + jj) * 128:(17 + jj) * 128], hTq[:, k, tsl], k == 0, k == 7,
                           ["hTq", "wfma"], [pfqn], inc=1 if (k == 7 and jj == 3) else 0)
                pc = [nps() for _ in range(3)]
                w0 = pp_ * 128
                for j in range(10):
                    pcj, pcjn = pc[j // 4]
                    o_ = pcj[:, (j % 4) * 128:(j % 4 + 1) * 128]
                    mm(o_, ones_b[0:1, :], convb_r[0:1, j * 128:(j + 1) * 128], True, False, ["ones_b", "convb_r"], [pcjn])
                    for k in range(4):
                        last = (k == 3)
                        mm(o_, XTs[:, j, w0 + k:w0 + k + 128], dconv[:, k, j, :], False, last, [XTn, "dconv"], [pcjn],
                           inc=1 if (last and (j % 4 == 3 or j == 9)) else 0)
                act(xs[:, 0:512], pc[0][0][:], AF.Silu, [pc[0][1]], ["xs"])
                act(xs[:, 512:1024], pc[1][0][:], AF.Silu, [pc[1][1]], ["xs"])
                act(btok, pc[2][0][:, 0:256], AF.Silu, [pc[2][1]], ["btok"])
                act(dtmp[:], dtmp[:], AF.Exp, ["dtmp"], ["dtmp"])
                act(dt_t[:], dtmp[:], AF.Ln, ["dtmp"], ["dt_t"], bias=1.0)
                act(ef, pfq[:], AF.Exp, [pfqn], ["ef"], scale=-1.0)
                tt("dve", dtA[:], dt_t[:], a_bc[:], ALU.mult, ["dt_t", "a_bc"], ["dtA"])
                pa, pan = nps()
                mm(pa[:, 0:16], utri[:], dtA[:], True, True, ["utri", "dtA"], [pan], inc=0)
                mm(pa[:, 16:32], ones_f[:], dtA[:], True, True, ["ones_f", "dtA"], [pan])
                cp("dve", acum[:], pa[:, 0:16], [pan], ["acum"])
                ts("dve", nacum[:], pa[:, 0:16], -1.0, None, ALU.mult, ALU.bypass, [pan], ["nacum"])
                act(eacum[:], pa[:, 0:16], AF.Exp, [pan], ["eacum"])
                act(cdec[:], pa[:, 16:32], AF.Exp, [pan], ["cdec"])
                tt("dve", dte[:], pa[:, 16:32], acum[:], ALU.subtract, [pan, "acum"], ["dte"])
                act(dte[:], dte[:], AF.Exp, ["dte"], ["dte"])
                tt("dve", dte[:], dte[:], dt_t[:], ALU.mult, ["dte", "dt_t"], ["dte"])
                xs3 = xs.rearrange("p (r q) -> p r q", r=16)
                tt("dve", xdt.rearrange("p (r q) -> p r q", r=16), xs3, dt_t[:].unsqueeze(2).broadcast_to([128, 16, 64]), ALU.mult,
                   ["xs", "dt_t"], ["xdt"])
                tt("dve", xw.rearrange("p (r q) -> p r q", r=16), xs3, dte[:].unsqueeze(2).broadcast_to([128, 16, 64]), ALU.mult,
                   ["xs", "dte"], ["xw"])
                tt("pool", xsd.rearrange("p (r q) -> p r q", r=16), xs3, d_bc[:].unsqueeze(2).broadcast_to([128, 16, 64]), ALU.mult,
                   ["xs", "d_bc"], ["xsd"])
                for g in range(2):
                    psc, pscn = nps()
                    mm(psc[:, 0:128], bctq[:, g, tsl], bctq[:, 2 + g, tsl], True, True, ["bctq"], [pscn])
                    cp("dve", scT[:, g, :], psc[:, 0:128], [pscn], ["scT"])
                    pab = [nps(), nps()]
                    for r in range(8):
                        pq, pqn = pab[r // 4]
                        o_ = pq[:, (r % 4) * 128:(r % 4 + 1) * 128]
                        hh = g * 8 + r
                        mm(o_, dtA[:, hh:hh + 1].broadcast_to([128, 128]), utri[:], True, False, ["dtA", "utri"], [pqn])
                        mm(o_, ident_b[:], negm_b[:], False, True, ["ident_b", "negm_b"], [pqn], inc=1 if r % 4 == 3 else 0)
                    for r in range(8):
                        pq, pqn = pab[r // 4]
                        hh = g * 8 + r
                        act(dec[:, r, :], pq[:, (r % 4) * 128:(r % 4 + 1) * 128], AF.Exp, [pqn, "nacum"], ["dec"], bias=nacum[:, hh:hh + 1])
                    tt("dve", MT, dec, scT[:, g:g + 1, :].broadcast_to([128, 8, 128]), ALU.mult, ["dec", "scT"], ["MT"])
                    py, pyn = nps()
                    mm(py[:], ident_b[:], xsd[:, g * 512:(g + 1) * 512], True, False, ["ident_b", "xsd"], [pyn])
                    for r in range(8):
                        hh = g * 8 + r
                        mm(py[:, r * 64:(r + 1) * 64], MT[:, r, :], xdt[:, hh * 64:(hh + 1) * 64], False, r == 7, ["MT", "xdt"], [pyn])
                    po, pon = nps()
                    mm(po[:], bctq[:, 2 + g, tsl], hb[:, g * 512:(g + 1) * 512], True, True, ["bctq", hbn], [pon])
                    tt("dve", ytmp.rearrange("p (r q) -> p r q", r=8), po[:].rearrange("p (r q) -> p r q", r=8),
                       eacum[:, g * 8:(g + 1) * 8].unsqueeze(2).broadcast_to([128, 8, 64]), ALU.mult, [pon, "eacum"], ["ytmp"])
                    tt("dve", ytmp, ytmp, py[:], ALU.add, ["ytmp", pyn], ["ytmp"])
                    tt("dve", yg, ytmp, zs[:, g * 512:(g + 1) * 512], ALU.mult, ["ytmp", "zs"], ["yg"])
                    act(junk[:, 0:512], yg, AF.Square, ["yg"], ["dec", "st8"], accum_out=st8[:, 2:3])
                    rsqrt(st8[:, 3:4], st8[:, 2:3], 1.0 / 512, ["st8"], ["st8"], 1)
                    ts("dve", mix[:, g * 512:(g + 1) * 512], yg, st8[:, 3:4], None, ALU.mult, ALU.bypass, ["yg", "st8"], ["mix"])
                    ph, phn = nps()
                    mm(ph[:], btok[:, g * 128:(g + 1) * 128], xw[:, g * 512:(g + 1) * 512], True, True, ["btok", "xw"], [phn])
                    hv = hs[:, g * 512:(g + 1) * 512]
                    tt("dve", hv.rearrange("p (r q) -> p r q", r=8), hv.rearrange("p (r q) -> p r q", r=8),
                       cdec[:, g * 8:(g + 1) * 8].unsqueeze(2).broadcast_to([128, 8, 64]), ALU.mult, [hsn, "cdec"], [hsn])
                    tt("dve", hv, hv, ph[:], ALU.add, [hsn, phn], [hsn])
                    cp("pool", hb[:, g * 512:(g + 1) * 512], hv, [hsn], [hbn])
                act(L2, ef, AF.Ln, ["ef"], ["L2"], bias=1.0)
                for h in range(4):
                    act(L1[:, h * 128:(h + 1) * 128], ef[:, h * 128:(h + 1) * 128], AF.Ln, ["ef", "lb_t"], ["L1"], bias=1.0, scale=lb_t[:, h:h + 1])
                tt("dve", L1, L1, L2, ALU.subtract, ["L1", "L2"], ["L1"])
                act(L2, L1, AF.Exp, ["L1"], ["L2"])
                ts("dve", L2, L2, -1.0, 1.0, ALU.mult, ALU.add, ["L2"], ["L2"])
                op("dve", lambda e: e.tensor_tensor_scan(out=cum, data0=scan0, data1=L1, initial=0.0, op0=ALU.mult, op1=ALU.add),
                   ["scan0", "L1"], ["cum"])
                cum4 = cum.rearrange("p (j t) -> p j t", j=8)
                act(ecend[:], cum4[:, :, 63], AF.Exp, ["cum"], ["ecend"])
                act(ecum, cum, AF.Exp, ["cum"], ["ecum"])
                q4 = qsq[:, :, tsl]
                e4 = ecum.rearrange("p (h t) -> p h t", h=4)
                tt("dve", qTa[:, :, 0:64], q4[:, :, 0:64], e4[:, :, 0:64], ALU.mult, ["qsq", "ecum"], ["qTa"])
                tt("dve", qTb[:, :, 64:128], q4[:, :, 64:128], e4[:, :, 64:128], ALU.mult, ["qsq", "ecum"], ["qTb"])
                act(ecum, cum, AF.Exp, ["cum"], ["ecum"], scale=-1.0)
                tt("dve", ecum, L2, ecum, ALU.mult, ["L2", "ecum"], ["ecum"])
                cp("pool", kT.rearrange("p h t -> p (h t)"), ecum, ["ecum"], ["kT"])
                tt("dve", kend.rearrange("p h (b t) -> p (h b) t", b=2), ecum.rearrange("p (j t) -> p j t", j=8),
                   ecend[:].unsqueeze(2).broadcast_to([128, 8, 64]), ALU.mult, ["ecum", "ecend"], ["kend"])
                pk, pkn = nps()
                pkb = pk[:].bitcast(BF16)
                for h in range(4):
                    op("pe", lambda e: e.transpose(pkb[:, h * 128:(h + 1) * 128], kend[:, h, :], ident_b[:]), ["kend", "ident_b"], [pkn],
                       inc=1 if h == 3 else 0)
                cp("act", kendT.rearrange("p h t -> p (h t)"), pkb[:, 0:512], [pkn], ["kendT"])
                pat, patn = nps()
                for h in range(4):
                    mm(pat[:, h * 128:(h + 1) * 128], kT[:, h, :], qTa[:, h, :], True, False, ["kT", "qTa"], [patn])
                    mm(pat[:, h * 128:(h + 1) * 128], kT[:, h, :], qTb[:, h, :], False, True, ["kT", "qTb"], [patn], inc=1 if h == 3 else 0)
                tt("dve", attn, pat[:].rearrange("p (h t) -> p h t", h=4), m64[:].unsqueeze(1).broadcast_to([128, 4, 128]), ALU.mult,
                   [patn, "m64"], ["attn"])
                pho, phon = nps()
                for h in range(4):
                    o_ = pho[:, h * 128:(h + 1) * 128]
                    mm(o_, attn[:, h, :], vtok[:, h * 128:(h + 1) * 128], h == 0, False, ["attn", "vtok"], [phon])
                    mm(o_, qTa[:, h, :], Sb[:, h, :], False, False, ["qTa", Sbn], [phon])
                for b2 in range(2):
                    pst, pstn = nps()
                    for h in range(4):
                        mm(pst[:, h * 128:(h + 1) * 128], kendT[b2 * 64:(b2 + 1) * 64, h, :], vtok[b2 * 64:(b2 + 1) * 64, h * 128:(h + 1) * 128],
                           True, True, ["kendT", "vtok"], [pstn], inc=1 if h == 3 else 0)
                    ec = ecend[:].rearrange("p (h b) -> p h b", b=2)[:, :, b2:b2 + 1].broadcast_to([128, 4, 128])
                    tt("dve", Ss, Ss, ec, ALU.mult, [Ssn, "ecend"], [Ssn])
                    tt("dve", Ss.rearrange("p h v -> p (h v)"), Ss.rearrange("p h v -> p (h v)"), pst[:], ALU.add, [Ssn, pstn], [Ssn])
                    cp("pool", Sb, Ss, [Ssn], [Sbn])
                    if b2 == 0:
                        for h in range(4):
                            mm(pho[:, h * 128:(h + 1) * 128], qTb[:, h, :], Sb[:, h, :], False, h == 3, ["qTb", Sbn], [phon], inc=1 if h == 3 else 0)
                for h in range(4):
                    act(junk[:, 0:128], pho[:, h * 128:(h + 1) * 128], AF.Square, [phon], ["dec", "st8"], accum_out=st8[:, 4 + h:5 + h])
                rsqrt(st8[:, 4:8], st8[:, 4:8], 1.0 / 128, ["st8"], ["st8"], 4)
                tt("dve", otmp.rearrange("p (h v) -> p h v", h=4), pho[:].rearrange("p (h v) -> p h v", h=4),
                   st8[:, 4:8].unsqueeze(2).broadcast_to([128, 4, 128]), ALU.mult, [phon, "st8"], ["otmp"])
                tt("dve", mix[:, 1024:1536], otmp, gs, ALU.mult, ["otmp", "gs"], ["mix"])
                sch.dma("sp", "mts", mixA_d[s, c * T:(c + 1) * T, :], mix[:, 0:1536], ["mix"], ["mixA_d"])
            for s in range(2):
                cp("pool", XT[s][:, :, 0:3], XT[s][:, :, 256:259], [f"XT{s}"], [f"XT{s}"])

        if dbg and dbg.get("stop") in ("A", "A1", "A2", "A15"):
            break
        sch.barrier()
        ld("sp", mixnw_t[:], mixnw[layer], ["mixnw_t"])
        itemsB = []
        for k in range(8):
            itemsB.append((wb_v[:, k, :], w_b[layer, k * 128:(k + 1) * 128, :], prew_t[:, k:k + 1], "wb", "prew_t"))
        for k in range(16):
            itemsB.append((wout_v[:, k, :], w_out[layer, k * 128:(k + 1) * 128, :], mixnw_t[:, k:k + 1], "wout", "mixnw_t"))
        for k in range(4):
            itemsB.append((glu_v[:, k, :], gluw[layer, k * 128:(k + 1) * 128, :], 0.5, "glu"))
        stage_load(itemsB)
        sch.barrier()
        ld("sp", postw_bc, postw[layer].partition_broadcast(128), ["postw_bc"])
        ld("sp", s5d_t[:], s5d[layer], ["s5d_t"])
        ld("pool", glub_r, glub[layer], ["glub_r"])
        ld("sp", g32[:, 0, :], lamre[layer], ["g_lr"])
        ld("sp", g32[:, 1, :], lamim[layer], ["g_li"])
        ld("sp", g32[:, 2, :], lstep[layer].partition_broadcast(64), ["g_st"])
        ld("sp", Bre_t, bre[layer], ["Bre"])
        ld("sp", Bim_t, bim[layer], ["Bim"])
        ld("sp", Cre_t, cre[layer], ["Cre"])
        ld("sp", Cim_t, cim[layer], ["Cim"])
        GN = ["g32"]

        def gq(i):
            return g32[:, i, :]

        def gmul(o_, a, b):
            tt("dve", gq(o_), gq(a), gq(b), ALU.mult, GN + ["g_lr", "g_li", "g_st"], GN)

        def gadd(o_, a, b, o2=ALU.add):
            tt("dve", gq(o_), gq(a), gq(b), o2, GN, GN)

        def gts(o_, a, m_, a_):
            ts("dve", gq(o_), gq(a), m_, a_, ALU.mult, ALU.add, GN + ["g_lr", "g_li", "g_st"], GN)

        I_LR, I_LI, I_ST, I_X, I_ANG, I_MAG, I_MAGI, I_C, I_S, I_T1, I_T2, I_LRE, I_LIM, I_IRE, I_IIM, I_CRE, I_CIM, I_8RE, I_8IM, I_Y, I_P, I_NX, I_A16, I_DEN = range(24)
        act(gq(I_ST), gq(I_ST), AF.Exp, ["g_st"], GN + ["g_st"])
        ts("dve", gq(I_LR), gq(I_LR), -1e-4, None, ALU.min, ALU.bypass, ["g_lr"], GN + ["g_lr"])
        gmul(I_X, I_LR, I_ST)
        gmul(I_ANG, I_LI, I_ST)
        gts(I_NX, I_X, -1.0, 0.0)

        def expser(o_, xi):
            gts(o_, xi, 1.0 / 6, 1.0)
            for kf in (5, 4, 3, 2, 1):
                gmul(o_, o_, xi)
                gts(o_, o_, 1.0 / kf, 1.0)
        expser(I_MAG, I_X)
        expser(I_MAGI, I_NX)
        gts(I_A16, I_ANG, 1.0 / 16, 0.0)
        gmul(I_Y, I_A16, I_A16)
        sc_ = [1.0, -1.0 / 6, 1.0 / 120, -1.0 / 5040, 1.0 / 362880, -1.0 / 39916800, 1.0 / 6227020800]
        cc_ = [1.0, -0.5, 1.0 / 24, -1.0 / 720, 1.0 / 40320, -1.0 / 3628800, 1.0 / 479001600, -1.0 / 87178291200]
        gts(I_P, I_Y, sc_[6], sc_[5])
        for kf in (4, 3, 2, 1, 0):
            gmul(I_P, I_P, I_Y)
            gts(I_P, I_P, 1.0, sc_[kf])
        gmul(I_S, I_P, I_A16)
        gts(I_P, I_Y, cc_[7], cc_[6])
        for kf in (5, 4, 3, 2, 1, 0):
            gmul(I_P, I_P, I_Y)
            gts(I_P, I_P, 1.0, cc_[kf])
        gts(I_C, I_P, 1.0, 0.0)

        def csq(re, im):
            gmul(I_T1, re, re)
            gmul(I_T2, im, im)
            gmul(im, re, im)
            gts(im, im, 2.0, 0.0)
            gadd(re, I_T1, I_T2, ALU.subtract)
        for _ in range(4):
            csq(I_C, I_S)
        gmul(I_LRE, I_MAG, I_C)
        gmul(I_LIM, I_MAG, I_S)
        gmul(I_IRE, I_MAGI, I_C)
        gmul(I_IIM, I_MAGI, I_S)
        gts(I_IIM, I_IIM, -1.0, 0.0)
        gts(I_P, I_LRE, 1.0, -1.0)
        gmul(I_T1, I_LR, I_LR)
        gmul(I_T2, I_LI, I_LI)
        gadd(I_DEN, I_T1, I_T2)
        op("dve", lambda e: e.reciprocal(out=gq(I_DEN), in_=gq(I_DEN)), GN, GN)
        gmul(I_T1, I_P, I_LR)
        gmul(I_T2, I_LIM, I_LI)
        gadd(I_CRE, I_T1, I_T2)
        gmul(I_CRE, I_CRE, I_DEN)
        gmul(I_T1, I_LIM, I_LR)
        gmul(I_T2, I_P, I_LI)
        gadd(I_CIM, I_T1, I_T2, ALU.subtract)
        gmul(I_CIM, I_CIM, I_DEN)
        gts(I_8RE, I_LRE, 1.0, 0.0)
        gts(I_8IM, I_LIM, 1.0, 0.0)
        for _ in range(3):
            csq(I_8RE, I_8IM)
        cp("dve", LA[:, 0, :], gq(I_8RE), GN, ["LA"])
        cp("dve", LA[:, 1, :], gq(I_8RE), GN, ["LA"])
        ts("dve", LB[:, 0, :], gq(I_8IM), -1.0, None, ALU.mult, ALU.bypass, GN, ["LB"])
        cp("dve", LB[:, 1, :], gq(I_8IM), GN, ["LB"])

        def cmul(eng, ore, oim, are, aim, xre, xim, n, rd, wr):
            ab = lambda a_: a_.unsqueeze(2).broadcast_to([64, 8, n])
            tv1, tv2 = T1[:, :, 0:n], T2[:, :, 0:n]
            tt(eng, tv1, xre, ab(are), ALU.mult, rd + GN, ["T1"])
            tt(eng, tv2, xim, ab(aim), ALU.mult, rd + GN, ["T2"])
            tt(eng, ore, tv1, tv2, ALU.subtract, ["T1", "T2"], wr)
            tt(eng, tv1, xim, ab(are), ALU.mult, rd + GN, ["T1"])
            tt(eng, tv2, xre, ab(aim), ALU.mult, rd + GN, ["T2"])
            tt(eng, oim, tv1, tv2, ALU.add, ["T1", "T2"], wr)

        for gb in range(4):
            g0 = gb * 8
            sl = slice(g0, g0 + 8)
            Z4r = Zre.rearrange("p g (s h) -> p g s h", s=8)
            Z4i = Zim.rearrange("p g (s h) -> p g s h", s=8)
            Y4r = Yre.rearrange("p g (s h) -> p g s h", s=8)
            Y4i = Yim.rearrange("p g (s h) -> p g s h", s=8)
            Bv = lambda tile_: tile_[0:64, g0 * 16:(g0 + 8) * 16].rearrange("p (g h) -> p g h", g=8)
            cmul("dve", Z4r[:, :, 7, :], Z4i[:, :, 7, :], gq(I_CRE)[:, sl], gq(I_CIM)[:, sl], Bv(Bre_t), Bv(Bim_t), 16, ["Bre", "Bim"], ["Zre", "Zim"])
            for s8 in range(6, -1, -1):
                cmul("dve", Z4r[:, :, s8, :], Z4i[:, :, s8, :], gq(I_LRE)[:, sl], gq(I_LIM)[:, sl], Z4r[:, :, s8 + 1, :], Z4i[:, :, s8 + 1, :], 16,
                     ["Zre", "Zim"], ["Zre", "Zim"])
            cp("dve", Y4r[:, :, 7, :], Bv(Cre_t), ["Cre"], ["Yre"])
            cp("dve", Y4i[:, :, 7, :], Bv(Cim_t), ["Cim"], ["Yim"])
            for l8 in range(6, -1, -1):
                cmul("dve", Y4r[:, :, l8, :], Y4i[:, :, l8, :], gq(I_IRE)[:, sl], gq(I_IIM)[:, sl], Y4r[:, :, l8 + 1, :], Y4i[:, :, l8 + 1, :], 16,
                     ["Yre", "Yim"], ["Yre", "Yim"])
            ts("dve", T1, Yim, -1.0, None, ALU.mult, ALU.bypass, ["Yim"], ["T1"])
            for hb_ in range(2):
                pt_, ptn = nps()
                for gi in range(4):
                    gg = hb_ * 4 + gi
                    mm(pt_[:, gi * 128:(gi + 1) * 128], Zre[:, gg, :], Yre[:, gg, :], True, False, ["Zre", "Yre"], [ptn])
                    mm(pt_[:, gi * 128:(gi + 1) * 128], Zim[:, gg, :], T1[:, gg, :], False, True, ["Zim", "T1"], [ptn], inc=1 if gi == 3 else 0)
                tt("dve", tz_v[:, g0 + hb_ * 4:g0 + hb_ * 4 + 4, :], pt_[:].rearrange("p (g n) -> p g n", g=4),
                   m8[:].unsqueeze(1).broadcast_to([128, 4, 128]), ALU.mult, [ptn, "m8"], ["tz"])
            for (Zt, Zn, dst, dn) in ((Zre, "Zre", wsre_v, "wsre"), (Zim, "Zim", wsim_v, "wsim")):
                pw_, pwn = nps()
                for gi in range(8):
                    mm(pw_[:, gi * 64:(gi + 1) * 64], Zt[:, gi, :], ident_f[0:64, 0:64], True, True, [Zn, "ident_f"], [pwn], inc=1 if gi == 7 else 0)
                cp("act", dst[:, sl, :], pw_[:].rearrange("p (g n) -> p g n", g=8), [pwn], [dn])
            a8 = lambda i: gq(i)[:, sl].unsqueeze(2).broadcast_to([64, 8, 128])
            tt("dve", T1, Yre, a8(I_8RE), ALU.mult, ["Yre"] + GN, ["T1"])
            tt("dve", T2, Yim, a8(I_8IM), ALU.mult, ["Yim"] + GN, ["T2"])
            tt("dve", wore_v[:, sl, :], T1, T2, ALU.subtract, ["T1", "T2"], ["wore"])
            tt("dve", T1, Yim, a8(I_8RE), ALU.mult, ["Yim"] + GN, ["T1"])
            tt("dve", T2, Yre, a8(I_8IM), ALU.mult, ["Yre"] + GN, ["T2"])
            tt("dve", T1, T1, T2, ALU.add, ["T1", "T2"], ["T1"])
            ts("dve", woim_v[:, sl, :], T1, -1.0, None, ALU.mult, ALU.bypass, ["T1"], ["woim"])
        op("dve", lambda e: e.memset(S2p[0], 0.0), (), ["S2_0"])
        sch.barrier()
        op("pool", lambda e: e.memset(gst[:, :, :, :, 0], 0.0), (), ["gsts", "gstw"])

        if dbg and dbg.get("stop") == "B0":
            break
        def f_loads(sc):
            tiles = [(s, 4 * sc + cc) for s in range(2) for cc in range(4)]
            for ti, (s, c) in enumerate(tiles):
                sch.dma("sp", "htl", hT8[:, :, ti * 128:(ti + 1) * 128], hT_d[gci(s, c)].rearrange("p (k t) -> p k t", k=8), ["hT_d"], ["hT8"])

        def f_uproj(sc):
            for j in range(4):
                for hf in range(2):
                    pu, pun = nps()
                    for s4 in range(4):
                        s8 = hf * 4 + s4
                        for k in range(8):
                            rhs = hT8[:, k, :].rearrange("p (n s) -> p s n", s=8)[:, s8, :]
                            mm(pu[:, s4 * 128:(s4 + 1) * 128], wb_v[:, k, j * 128:(j + 1) * 128], rhs, k == 0, k == 7, ["hT8", "wb"], [pun],
                               inc=1 if (k == 7 and s4 == 3) else 0)
                    cp("act" if hf == 0 else "dve", uT8[:, j, hf * 512:(hf + 1) * 512], pu[:], [pun], ["uT8"])
            for g8 in range(8):
                for s8 in range(8):
                    qd_ = ("sp", "act")[(g8 * 8 + s8) % 2]
                    sch.dma(qd_, "blk_" + qd_, ublk8[16 * s8:16 * s8 + 16, :, :].rearrange("p (j g) n -> p j g n", j=4)[:, :, g8, :],
                            uT8[16 * g8:16 * g8 + 16, :, s8 * 128:(s8 + 1) * 128], ["uT8"], ["ublk8_" + qd_])

        def f_statein(sc):
            for q4_ in range(4):
                for ri, (wsv, wsn) in enumerate(((wsre_v, "wsre"), (wsim_v, "wsim"))):
                    pw2 = [nps(), nps()]
                    for g8 in range(8):
                        g = q4_ * 8 + g8
                        pw_, pwn = pw2[g8 // 4]
                        mm(pw_[0:64, (g8 % 4) * 128:(g8 % 4 + 1) * 128], wsv[:, g, :], ublk8[:, g, :], g8 % 4 == 0, g8 % 4 == 3, [wsn, "ublk8_sp", "ublk8_act"], [pwn])
                    for hb_ in range(2):
                        pw_, pwn = pw2[hb_]
                        g0_ = q4_ * 8 + hb_ * 4
                        cp("act" if hb_ == 0 else "dve", gst[:, ri, g0_:g0_ + 4, :, 1:65],
                           pw_[0:64, :].rearrange("p (g s m) -> p g s m", g=4, s=2), [pwn], ["gstw", "gsts"])

        def f_rec(m0, m1):
            LAb = LA[:].unsqueeze(3).broadcast_to([64, 2, 32, 2])
            for m in range(m0, m1):
                Si, Sin_, So, Son_ = S2p[m % 2], f"S2_{m % 2}", S2p[(m + 1) % 2], f"S2_{(m + 1) % 2}"
                tt("dve", rt, Si, LAb, ALU.mult, [Sin_, "LA"], ["rt"])
                tt("dve", ru[:, 0, :, :], Si[:, 1, :, :], LB[:, 0, :].unsqueeze(2).broadcast_to([64, 32, 2]), ALU.mult, [Sin_, "LB"], ["ru"])
                tt("dve", ru[:, 1, :, :], Si[:, 0, :, :], LB[:, 1, :].unsqueeze(2).broadcast_to([64, 32, 2]), ALU.mult, [Sin_, "LB"], ["ru"])
                tt("dve", rt, rt, ru, ALU.add, ["rt", "ru"], ["rt"])
                tt("dve", So, rt, gst[:, :, :, :, 1 + m], ALU.add, ["rt", "gstw"], [Son_])
                cp("pool", gst[:, :, :, :, 1 + m], So, [Son_], ["gsts"])

        def f_y(sc):
            for q4_ in range(4):
                py2 = [nps(), nps()]
                for g8 in range(8):
                    g = q4_ * 8 + g8
                    py_, pyn_ = py2[g8 // 4]
                    o_ = py_[:, (g8 % 4) * 128:(g8 % 4 + 1) * 128]
                    mm(o_, tz_v[:, g, :], ublk8[:, g, :], g8 % 4 == 0, False, ["tz", "ublk8_sp", "ublk8_act"], [pyn_])
                    for s_ in range(2):
                        mm(o_[:, s_ * 64:(s_ + 1) * 64], wore_v[:, g, :], gst[:, 0, g, s_, 0:64], False, False, ["wore", "gsts"], [pyn_])
                        mm(o_[:, s_ * 64:(s_ + 1) * 64], woim_v[:, g, :], gst[:, 1, g, s_, 0:64], False, g8 % 4 == 3 and s_ == 1, ["woim", "gsts"], [pyn_])
                for hb_ in range(2):
                    py_, pyn_ = py2[hb_]
                    g0_ = q4_ * 8 + hb_ * 4
                    ub = ublk8[:, g0_:g0_ + 4, :]
                    tt("dve", ytB.rearrange("p (g n) -> p g n", g=4), ub, s5d_t[:, g0_:g0_ + 4].unsqueeze(2).broadcast_to([128, 4, 128]), ALU.mult,
                       ["ublk8_sp", "ublk8_act", "s5d_t"], ["ytB"])
                    tt("dve", ytB, ytB, py_[:], ALU.add, ["ytB", pyn_], ["ytB"])
                    act(ygB, ytB, AF.Square, ["ytB"], ["ygB"])
                    ts("dve", ygB, ygB, GELU_C2, GELU_C1, ALU.mult, ALU.add, ["ygB"], ["ygB"])
                    tt("dve", ygB, ygB, ytB, ALU.mult, ["ygB", "ytB"], ["ygB"])
                    act(ygB, ygB, AF.Tanh, ["ygB"], ["ygB"])
                    op("dve", lambda e: e.scalar_tensor_tensor(out=gyb8[:, g0_:g0_ + 4, :].rearrange("p g n -> p (g n)"), in0=ygB, scalar=1.0, in1=ytB,
                                                                op0=ALU.add, op1=ALU.mult), ["ygB", "ytB"], ["gyb8"])
            cp("pool", gst[:, :, :, :, 0], gst[:, :, :, :, 64], ["gsts"], ["gsts"])
            for g8 in range(8):
                for l8 in range(8):
                    qd_ = ("sp", "act")[(g8 * 8 + l8) % 2]
                    sch.dma(qd_, "ubl_" + qd_, gyT8[16 * g8:16 * g8 + 16, :, l8 * 128:(l8 + 1) * 128],
                            gyb8[16 * l8:16 * l8 + 16, :, :].rearrange("p (j g) n -> p j g n", j=4)[:, :, g8, :], ["gyb8"], ["gyT8_" + qd_])

        def f_gates(sc):
            for l8 in range(8):
                pg_, pgn = nps()
                for k in range(8):
                    lhs = hT8[:, k, :].rearrange("p (n l) -> p l n", l=8)[:, l8, :]
                    mm(pg_[:], lhs, wb_v[:, k, 512:1024], k == 0, k == 7, ["hT8", "wb"], [pgn])
                act(gatesB[:, l8, :], pg_[:], AF.Silu, [pgn], ["gatesB"])

        def f_tile(sc, l8):
            base = 512 * sc
            if True:
                xb, xbn = xtB[l8 % 2], f"xtB{l8 % 2}"
                for s_ in range(2):
                    sch.dma("sp", f"x{xbn}_{s_}", xb[64 * s_:64 * s_ + 64, :],
                            xsrc[s_, base:base + 512, :].rearrange("(m l) d -> l m d", l=8)[l8], xrd, [f"{xbn}_{s_}"])
                    sch.dma("act", f"mxl_{s_}", mixB[64 * s_:64 * s_ + 64, 0:1536],
                            mixA_d[s_, base:base + 512, :].rearrange("(m l) d -> l m d", l=8)[l8], ["mixA_d"], [f"mixB_{s_}"])
                pv = [nps(), nps()]
                for hf in range(2):
                    pp, ppn = pv[hf]
                    mm(pp[:], ones_b[0:1, :], glub_r[0:1, hf * 512:(hf + 1) * 512], True, False, ["ones_b", "glub_r"], [ppn])
                    for j in range(4):
                        mm(pp[:], gyT8[:, j, l8 * 128:(l8 + 1) * 128], glu_v[:, j, hf * 512:(hf + 1) * 512], False, j == 3,
                           ["gyT8_sp", "gyT8_act", "glu"], [ppn])
                act(L1B, pv[1][0][:], AF.Tanh, [pv[1][1]], ["L1B"], scale=0.5)
                ts("dve", L1B, L1B, 0.5, 0.5, ALU.mult, ALU.add, ["L1B"], ["L1B"])
                tt("dve", L2B, pv[0][0][:], L1B, ALU.mult, [pv[0][1], "L1B"], ["L2B"])
                tt("dve", L2B, L2B, gatesB[:, l8, :], ALU.mult, ["L2B", "gatesB"], ["L2B"])
                act(junkB[:, 0:512], L2B, AF.Square, ["L2B"], ["junkB", "st8"], accum_out=st8[:, 2:3])
                rsqrt(st8[:, 3:4], st8[:, 2:3], 1.0 / 512, ["st8"], ["st8"], 1)
                ts("dve", mixB[:, 1536:2048], L2B, st8[:, 3:4], None, ALU.mult, ALU.bypass, ["L2B", "st8"], ["mixB_S"])
                pm1, pm1n = nps()
                pm2, pm2n = nps()
                pm1b, pm2b = pm1[:].bitcast(BF16), pm2[:].bitcast(BF16)
                for j in range(16):
                    dst, dn = (pm1b, pm1n) if j < 8 else (pm2b, pm2n)
                    jj = j % 8
                    op("pe", lambda e: e.transpose(dst[:, jj * 128:(jj + 1) * 128], mixB[:, j * 128:(j + 1) * 128], ident_b[:]),
                       ["mixB_0", "mixB_1", "mixB_S", "ident_b"], [dn], inc=1 if j in (7, 15) else 0)
                mT2 = mixTB.rearrange("p k t -> p (k t)")
                cp("act", mT2[:, 0:1024], pm1b, [pm1n], ["mixTB"])
                cp("dve", mT2[:, 1024:2048], pm2b, [pm2n], ["mixTB"])
                po_ = [nps(), nps()]
                for n2 in range(2):
                    pp, ppn = po_[n2]
                    for kk in range(16):
                        mm(pp[:], mixTB[:, kk, :], wout_v[:, kk, n2 * 512:(n2 + 1) * 512], kk == 0, kk == 15, ["mixTB", "wout"], [ppn])
                for n2 in range(2):
                    act(junkB[:, n2 * 512:(n2 + 1) * 512], po_[n2][0][:], AF.Square, [po_[n2][1]], ["junkB", "st8"], accum_out=st8[:, 4 + n2:5 + n2])
                tt("dve", st8[:, 6:7], st8[:, 4:5], st8[:, 5:6], ALU.add, ["st8"], ["st8"])
                rsqrt(st8[:, 7:8], st8[:, 6:7], 1.0 / D_MODEL, ["st8"], ["st8"], 1)
                for n2 in range(2):
                    op("dve", lambda e: e.scalar_tensor_tensor(out=ygB, in0=po_[n2][0][:], scalar=st8[:, 7:8], in1=postw_bc[:, n2 * 512:(n2 + 1) * 512],
                                                                op0=ALU.mult, op1=ALU.mult), [po_[n2][1], "st8", "postw_bc"], ["ygB"])
                    tt("dve", xb[:, n2 * 512:(n2 + 1) * 512], xb[:, n2 * 512:(n2 + 1) * 512], ygB, ALU.add, [f"{xbn}_0", f"{xbn}_1", "ygB"],
                       [f"{xbn}_0", f"{xbn}_1"])
                for s_ in range(2):
                    sch.dma("sp", "ost", out[s_, base:base + 512, :].rearrange("(m l) d -> l m d", l=8)[l8], xb[64 * s_:64 * s_ + 64, :],
                            [f"{xbn}_0", f"{xbn}_1"], ["out_d"])

        f_loads(0); f_uproj(0); f_statein(0); f_rec(0, 64); f_y(0); f_gates(0)
        for sc in range(NSC):
            nxt = sc + 1 < NSC
            if nxt:
                f_loads(sc + 1)
                f_uproj(sc + 1)
            f_tile(sc, 0)
            f_tile(sc, 1)
            if nxt:
                f_statein(sc + 1)
            for l8 in range(2, 8):
                if nxt:
                    f_rec((l8 - 2) * 11, min(64, (l8 - 1) * 11))
                f_tile(sc, l8)
            if nxt:
                f_y(sc + 1)
                f_gates(sc + 1)
    sch.barrier()
    sch.finish("sp", ["mixA_d", "hT_d", "out_d"])
    es.close()
    return nc, sch


def prep_shared(inp, NL):
    f = lambda a: np.ascontiguousarray(np.asarray(a, dtype=np.float32))
    w_in = np.asarray(inp["w_in"], dtype=np.float32)[:NL]
    sh = {}
    sh["w_tma"] = f(np.concatenate([w_in[:, :, C_Z:C_Z + 1024], w_in[:, :, C_DT:C_DT + 16], w_in[:, :, C_I:C_I + 512],
                                    w_in[:, :, C_G:C_G + 512]], axis=2))
    sh["w_fma"] = f(np.concatenate([w_in[:, :, C_XBC:C_XBC + 1536], w_in[:, :, C_Q:C_Q + 512], w_in[:, :, C_F:C_F + 512]], axis=2))
    sh["w_b"] = f(w_in[:, :, C_U:C_U + 1024])
    sh["w_out"] = f(np.asarray(inp["w_out"])[:NL])
    sh["prew"] = f(np.asarray(inp["pre_norm_w"])[:NL].reshape(NL, 8, 128).transpose(0, 2, 1))
    sh["postw"] = f(np.asarray(inp["post_norm_w"])[:NL].reshape(NL, 1, D_MODEL))
    mixnw = np.concatenate([np.asarray(inp["ssd_norm_w"])[:NL], np.asarray(inp["hgrn_norm_w"])[:NL], np.asarray(inp["s5_norm_w"])[:NL]], axis=1)
    sh["mixnw"] = f(mixnw.reshape(NL, 16, 128).transpose(0, 2, 1))
    cw = np.asarray(inp["ssd_conv_w"])[:NL]
    sh["convw"] = f(cw.reshape(NL, 4, 12, 128).transpose(0, 3, 1, 2).reshape(NL, 128, 48))
    cb = np.asarray(inp["ssd_conv_b"])[:NL]
    sh["convb_pp"] = f(cb.reshape(NL, 12, 128).transpose(0, 2, 1))
    sh["convb_row"] = f(cb.reshape(NL, 1, 1536))
    sh["dtb"] = f(np.asarray(inp["ssd_dt_bias"])[:NL].reshape(NL, 1, 16))
    sh["alog"] = f(np.asarray(inp["ssd_a_log"])[:NL].reshape(NL, 1, 16))
    sh["ssdd"] = f(np.asarray(inp["ssd_d"])[:NL].reshape(NL, 1, 16))
    hl = np.asarray(inp["hgrn_lower_bounds"])[:NL]
    sh["hlb"] = f(hl.reshape(NL, 4, 128).transpose(2, 1, 0).reshape(128, 4 * NL))
    sh["lamre"] = f(np.asarray(inp["s5_lambda_re"])[:NL].transpose(0, 2, 1))
    sh["lamim"] = f(np.asarray(inp["s5_lambda_im"])[:NL].transpose(0, 2, 1))
    sh["lstep"] = f(np.asarray(inp["s5_log_step"])[:NL].reshape(NL, 1, 32))
    sh["bre"] = f(np.asarray(inp["s5_b_re"])[:NL].transpose(0, 2, 1, 3).reshape(NL, 64, 512))
    sh["bim"] = f(np.asarray(inp["s5_b_im"])[:NL].transpose(0, 2, 1, 3).reshape(NL, 64, 512))
    sh["cre"] = f(np.asarray(inp["s5_c_re"])[:NL].transpose(0, 3, 1, 2).reshape(NL, 64, 512))
    sh["cim"] = f(np.asarray(inp["s5_c_im"])[:NL].transpose(0, 3, 1, 2).reshape(NL, 64, 512))
    d5 = np.asarray(inp["s5_d"])[:NL].reshape(NL, 32, 16)
    sh["s5d"] = f(np.broadcast_to(d5.transpose(0, 2, 1)[:, None, :, :], (NL, 8, 16, 32)).reshape(NL, 128, 32))
    sh["gluw"] = f(np.asarray(inp["s5_glu_w"])[:NL])
    sh["glub"] = f(np.asarray(inp["s5_glu_b"])[:NL].reshape(NL, 1, 1024))
    for k, v in host_consts().items():
        sh["c_" + k] = v
    return sh


LAYER_GROUPS = [[0, 1, 2, 3]]


def kernel(**inputs):
    x = np.ascontiguousarray(np.asarray(inputs["x"], dtype=np.float32))
    B = x.shape[0]
    S = B // NCORES
    sh = prep_shared(inputs, NL_FULL)
    cur = x
    for grp in LAYER_GROUPS:
        nc, _ = build(NL_FULL, S, x.shape[1], layers=grp)
        in_maps = [dict(sh, x=np.ascontiguousarray(cur[S * c:S * (c + 1)])) for c in range(NCORES)]
        res = run_bass_kernel_spmd(nc, in_maps, core_ids=list(range(NCORES)))
        cur = np.concatenate([np.asarray(r["out"], dtype=np.float32) for r in res.results], axis=0)
    return cur
```

```python
import math
from contextlib import ExitStack

import numpy as np
import concourse.bass as bass
import concourse.mybir as mybir
from concourse.bass_utils import run_bass_kernel_spmd

F32 = mybir.dt.float32
BF16 = mybir.dt.bfloat16
AF = mybir.ActivationFunctionType
ALU = mybir.AluOpType
AX = mybir.AxisListType

D_MODEL = 1024
IN_COLS = 5648
EPS = 1e-6
NL_FULL = 4
SEQ_FULL = 2048
NCORES = 8
T = 128

C_Z, C_XBC, C_DT, C_Q, C_F, C_I, C_G, C_U, C_SG = 0, 1024, 2560, 2576, 3088, 3600, 4112, 4624, 5136
N_TMA = 1024 + 16 + 512 + 512
N_FMA = 1536 + 512 + 512
N_B = 1024
GELU_C1 = 0.7978845608028654
GELU_C2 = 0.044715 * GELU_C1


class Sched:
    def __init__(self, nc):
        self.nc = nc
        self.eng = {"pe": nc.tensor, "act": nc.scalar, "dve": nc.vector, "pool": nc.gpsimd, "sp": nc.sync}
        self.sem = {}
        self.cnt = {}
        self.waited = {}
        self.lastw = {}
        self.readers = {}
        self.pending = {}
        self.ninst = 0
        self.gen = {}
        for k in ("pe", "act", "dve", "pool"):
            self._mk(k)

    def _mk(self, k):
        self.sem[k] = self.nc.alloc_semaphore("s_" + k)
        self.cnt[k] = 0
        self.pending[k] = False

    def _phys(self, names, writing):
        out = []
        for b in names:
            if "#" in b:
                p, g = b.split("#")
                if writing:
                    if self.gen.get(p) != g and int(g) > int(self.gen.get(p, "-1")):
                        self.gen[p] = g
                assert self.gen.get(p) == g, f"stale PSUM bank use {b} (current gen {self.gen.get(p)})"
                b = p
            out.append(b)
        return out

    def _deps(self, reads, writes):
        reads[:] = self._phys(reads, False)
        writes[:] = self._phys(writes, True)
        deps = {}
        raw = {}
        for b in reads:
            lw = self.lastw.get(b)
            if lw:
                deps[lw[0]] = max(deps.get(lw[0], 0), lw[1])
                raw[lw[0]] = max(raw.get(lw[0], 0), lw[1])
        for b in writes:
            lw = self.lastw.get(b)
            if lw:
                deps[lw[0]] = max(deps.get(lw[0], 0), lw[1])
            for e, i in self.readers.get(b, {}).items():
                deps[e] = max(deps.get(e, 0), i)
        return deps, raw

    def _emit_waits(self, issuer, me, deps, raw):
        w = self.waited.setdefault(issuer, {})
        for src, idx in deps.items():
            if src == me:
                if me == "pe":
                    continue
            if idx > w.get(src, 0):
                self.eng[issuer].wait_ge(self.sem[src], idx)
                w[src] = idx
                self.ninst += 1

    def op(self, e, fn, reads=(), writes=(), inc=1):
        reads, writes = list(reads), list(writes)
        deps, raw = self._deps(reads, writes)
        self._emit_waits(e, e, deps, raw)
        ins = fn(self.eng[e])
        self.ninst += 1
        if inc:
            ins.then_inc(self.sem[e], 1)
            self.cnt[e] += 1
            idx = self.cnt[e]
            self.pending[e] = False
        else:
            idx = self.cnt[e] + 1
            self.pending[e] = True
        for b in reads:
            self.readers.setdefault(b, {})[e] = max(self.readers.get(b, {}).get(e, 0), idx)
        for b in writes:
            self.lastw[b] = (e, idx)
            self.readers[b] = {}
        return ins

    def dma(self, q, slot, out, in_, reads=(), writes=(), **kw):
        if slot not in self.sem:
            self._mk(slot)
        reads, writes = list(reads), list(writes)
        deps, raw = self._deps(reads, writes)
        self._emit_waits(q, None, deps, raw)
        ins = self.eng[q].dma_start(out=out, in_=in_, **kw)
        ins.then_inc(self.sem[slot], 16)
        self.ninst += 1
        self.cnt[slot] += 16
        idx = self.cnt[slot]
        for b in reads:
            self.readers.setdefault(b, {})[slot] = idx
        for b in writes:
            self.lastw[b] = (slot, idx)
            self.readers[b] = {}

    def barrier(self):
        for e in ("pe", "act", "dve", "pool", "sp"):
            w = self.waited.setdefault(e, {})
            for src, c in self.cnt.items():
                if c == 0 or (src == e and e == "pe"):
                    continue
                if c > w.get(src, 0):
                    self.eng[e].wait_ge(self.sem[src], c)
                    w[src] = c
                    self.ninst += 1

    def dma_sync(self, q, slot):
        w = self.waited.setdefault(q, {})
        c = self.cnt.get(slot, 0)
        if c > w.get(slot, 0):
            self.eng[q].wait_ge(self.sem[slot], c)
            w[slot] = c
            self.ninst += 1

    def finish(self, q, bufs):
        deps, raw = self._deps(list(bufs), [])
        self._emit_waits(q, None, deps, raw)
        for k, v in self.pending.items():
            assert not v, k


def host_consts():
    c = {}
    c["ident"] = np.eye(128, dtype=np.float32)
    tl = np.arange(128)
    c["utri"] = (tl[:, None] <= tl[None, :]).astype(np.float32)
    c["negm"] = np.where(tl[None, :] >= tl[:, None], 0.0, -30000.0).astype(np.float32)
    blk = (tl[:, None] // 64) == (tl[None, :] // 64)
    c["m64"] = ((tl[None, :] >= tl[:, None]) & blk).astype(np.float32)
    sc = np.ones((128, 512), np.float32)
    sc[:, 0::64] = 0.0
    c["scan0"] = sc
    band = np.zeros((8, 128, 240), np.float32)
    for a in range(8):
        for k in range(16 * a, 16 * a + 16):
            band[a, k, (k % 16) + 112] = 1.0
    c["band"] = band.transpose(1, 0, 2).reshape(128, 8 * 240).copy()
    c["m8"] = ((tl[None, :] // 16) >= (tl[:, None] // 16)).astype(np.float32)
    c["ones"] = np.ones((128, 128), np.float32)
    pm = np.zeros((128, 128), np.float32)
    for m in range(128):
        pm[(m % 8) * 16 + m // 8, m] = 1.0
    c["pm"] = pm
    c["pmT"] = np.ascontiguousarray(pm.T)
    return c


def build(NL, S, L, dbg=None, layers=None):
    assert S == 2 and L % 512 == 0
    NCH = L // T
    nc = bass.Bass("TRN2", target_bir_lowering=False)
    es = ExitStack()

    def din(name, shape, dt=F32):
        return nc.dram_tensor(name, list(shape), dt, kind="ExternalInput").ap()

    x_in = din("x", [S, L, D_MODEL])
    w_tma = din("w_tma", [NL, D_MODEL, N_TMA])
    w_fma = din("w_fma", [NL, D_MODEL, N_FMA])
    w_b = din("w_b", [NL, D_MODEL, N_B])
    w_out = din("w_out", [NL, 2048, D_MODEL])
    prew = din("prew", [NL, 128, 8])
    postw = din("postw", [NL, 1, D_MODEL])
    mixnw = din("mixnw", [NL, 128, 16])
    convw = din("convw", [NL, 128, 48])
    convb_pp = din("convb_pp", [NL, 128, 12])
    convb_row = din("convb_row", [NL, 1, 1536])
    dtb = din("dtb", [NL, 1, 16])
    alog = din("alog", [NL, 1, 16])
    ssdd = din("ssdd", [NL, 1, 16])
    hlb = din("hlb", [128, 4 * NL])
    lamre = din("lamre", [NL, 64, 32])
    lamim = din("lamim", [NL, 64, 32])
    lstep = din("lstep", [NL, 1, 32])
    bre = din("bre", [NL, 64, 512])
    bim = din("bim", [NL, 64, 512])
    cre = din("cre", [NL, 64, 512])
    cim = din("cim", [NL, 64, 512])
    s5d = din("s5d", [NL, 128, 32])
    gluw = din("gluw", [NL, 512, 1024])
    glub = din("glub", [NL, 1, 1024])
    consts = {k: din("c_" + k, v.shape) for k, v in host_consts().items()}

    out = nc.dram_tensor("out", [S, L, D_MODEL], F32, kind="ExternalOutput").ap()
    hT_d = nc.dram_tensor("hT_scr", [S * NCH, 128, 8 * 128], BF16, kind="Internal").ap()
    mixA_d = nc.dram_tensor("mixA_scr", [S, L, 1536], BF16, kind="Internal").ap()
    dbg_out = None
    if False:
        dbg_out = {}

    def sb(name, shape, dt=F32):
        return es.enter_context(nc.sbuf_tensor(name, list(shape), dt))

    ident_b = sb("ident_b", [128, 128], BF16)
    ident_f = sb("ident_f", [128, 128])
    utri = sb("utri", [128, 128])
    ones_f = sb("ones_f", [128, 128])
    ones_b = sb("ones_b", [128, 128], BF16)
    negm_b = sb("negm_b", [128, 128], BF16)
    m64 = sb("m64", [128, 128], BF16)
    m8 = sb("m8", [128, 128])
    pm_b = sb("pm_b", [128, 128], BF16)
    pmT_b = sb("pmT_b", [128, 128], BF16)
    hlb_t = sb("hlb_t", [128, 4, NL])
    lb_all = sb("lb_all", [128, 4, NL])
    neghalf = sb("neghalf", [128, 16])
    st8 = sb("st8", [128, 8])
    prew_t = sb("prew_t", [128, 8])
    mixnw_t = sb("mixnw_t", [128, 16])
    convw_t = sb("convw_t", [128, 48])
    convb_t = sb("convb_t", [128, 12])
    dtb_bc = sb("dtb_bc", [128, 16])
    a_bc = sb("a_bc", [128, 16])
    d_bc = sb("d_bc", [128, 16])
    lb_t = sb("lb_t", [128, 4])
    dt_t = sb("dt_t", [128, 16])
    dtmp = sb("dtmp", [128, 16])
    dtA = sb("dtA", [128, 16])
    acum = sb("acum", [128, 16])
    nacum = sb("nacum", [128, 16])
    eacum = sb("eacum", [128, 16])
    dte = sb("dte", [128, 16])
    cdec = sb("cdec", [128, 16])
    ecend = sb("ecend", [128, 8])
    s5d_t = sb("s5d_t", [128, 32])
    LA = sb("LA", [64, 2, 32])
    LB = sb("LB", [64, 2, 32])

    WREG = 38912
    wreg = sb("wreg", [128, WREG], BF16)
    wtma = wreg[:, 0:8 * N_TMA].rearrange("p (k n) -> p k n", k=8)
    wfma = wreg[:, 8 * N_TMA:8 * (N_TMA + N_FMA)].rearrange("p (k n) -> p k n", k=8)
    o = 0
    wb_v = wreg[:, o:o + 8 * N_B].rearrange("p (k n) -> p k n", k=8); o += 8 * N_B
    wout_v = wreg[:, o:o + 16 * 1024].rearrange("p (k n) -> p k n", k=16); o += 16 * 1024
    glu_v = wreg[:, o:o + 4 * 1024].rearrange("p (k n) -> p k n", k=4); o += 4 * 1024
    wsim_v = wreg[:, o:o + 32 * 64].rearrange("p (g n) -> p g n", g=32); o += 32 * 64
    wore_v = wreg[0:64, o:o + 32 * 128].rearrange("p (g n) -> p g n", g=32); o += 32 * 128
    woim_v = wreg[0:64, o:o + 32 * 128].rearrange("p (g n) -> p g n", g=32); o += 32 * 128
    assert o <= WREG, (o, WREG)
    reg2 = sb("reg2", [128, 48 * 128], BF16)
    dconv = reg2[:].rearrange("p (k j n) -> p k j n", k=4, j=12)
    tz_v = reg2[:, 0:4096].rearrange("p (g n) -> p g n", g=32)
    wsre_v = reg2[:, 4096:6144].rearrange("p (g n) -> p g n", g=32)

    ARENA = 58100
    arena = sb("arena", [128, ARENA], BF16)
    aoff = [0]

    def cv(n, dt=BF16, parts=128):
        ne = n * (2 if dt == F32 else 1)
        assert aoff[0] + ne <= ARENA, (aoff[0], ne, ARENA)
        ap = arena[0:parts, aoff[0]:aoff[0] + ne]
        aoff[0] += ne
        return ap.bitcast(F32) if dt == F32 else ap

    xt = [cv(1024, F32), cv(1024, F32)]
    xn = cv(1024)
    hTq = cv(4096).rearrange("p (k t) -> p k t", k=8)
    zs = cv(1024)
    vtok = cv(512)
    gs = cv(512)
    XT = [cv(12 * 260).rearrange("p (j t) -> p j t", j=12) for _ in range(2)]
    qsq = cv(2048).rearrange("p (h t) -> p h t", h=4)
    ef = cv(512, F32)
    xs = cv(1024, F32)
    btok = cv(256)
    bctq = cv(2048).rearrange("p (j t) -> p j t", j=4)
    xdt = cv(1024)
    xw = cv(1024)
    xsd = cv(1024)
    scT = cv(256).rearrange("p (g t) -> p g t", g=2)
    dec = cv(1024).rearrange("p (r t) -> p r t", r=8)
    MT = cv(1024).rearrange("p (r t) -> p r t", r=8)
    hst = [cv(1024, F32), cv(1024, F32)]
    hbf = [cv(1024), cv(1024)]
    L1 = cv(512, F32)
    L2 = cv(512, F32)
    cum = cv(512, F32)
    ecum = cv(512, F32)
    qTa = cv(512).rearrange("p (h t) -> p h t", h=4)
    qTb = cv(512).rearrange("p (h t) -> p h t", h=4)
    kT = cv(512).rearrange("p (h t) -> p h t", h=4)
    kend = cv(512).rearrange("p (h t) -> p h t", h=4)
    kendT = cv(512).rearrange("p (h t) -> p h t", h=4)
    attn = cv(512).rearrange("p (h t) -> p h t", h=4)
    Sst = [cv(512, F32).rearrange("p (h t) -> p h t", h=4) for _ in range(2)]
    Sbf = [cv(512).rearrange("p (h t) -> p h t", h=4) for _ in range(2)]
    ytmp = cv(512, F32)
    yg = cv(512, F32)
    otmp = cv(512, F32)
    mix = cv(2048)
    mixT = cv(1536).rearrange("p (k t) -> p k t", k=12)
    junk = dec.rearrange("p r t -> p (r t)")
    scan0 = cv(512, F32)
    convb_r = cv(1536, parts=1)
    endA = aoff[0]
    aoff[0] = 0
    xtB = [cv(1024, F32), cv(1024, F32)]
    hT8 = cv(8192).rearrange("p (k t) -> p k t", k=8)
    uT8 = cv(4096).rearrange("p (j t) -> p j t", j=4)
    ublk8 = cv(4096).rearrange("p (g n) -> p g n", g=32)
    gst = cv(2 * 32 * 2 * 65, parts=64).rearrange("p (r g s m) -> p r g s m", r=2, g=32, s=2)
    gyb8 = cv(4096).rearrange("p (g n) -> p g n", g=32)
    gyT8 = cv(4096).rearrange("p (j t) -> p j t", j=4)
    ytB = cv(512, F32)
    ygB = cv(512, F32)
    gatesB = cv(4096).rearrange("p (l n) -> p l n", l=8)
    L1B = cv(512, F32)
    L2B = cv(512, F32)
    mixB = cv(2048)
    mixTB = cv(2048).rearrange("p (k t) -> p k t", k=16)
    junkB = cv(1024)
    postw_bc = cv(1024, F32)
    S2p = [cv(128, F32, parts=64).rearrange("p (r g s) -> p r g s", r=2, g=32) for _ in range(2)]
    rt = cv(128, F32, parts=64).rearrange("p (r g s) -> p r g s", r=2, g=32)
    ru = cv(128, F32, parts=64).rearrange("p (r g s) -> p r g s", r=2, g=32)
    glub_r = cv(1024, parts=1)
    endB = aoff[0]
    aoff[0] = 4096
    g32 = cv(40 * 32, F32, parts=64).rearrange("p (i g) -> p i g", i=40)
    v4 = lambda: cv(2048, F32, parts=64).rearrange("p (g n) -> p g n", g=16)
    Zre, Zim, Yre, Yim, T1, T2 = v4(), v4(), v4(), v4(), v4(), v4()
    Bre_t, Bim_t, Cre_t, Cim_t = (cv(512, F32, parts=64) for _ in range(4))
    assert aoff[0] <= endB

    ps = [es.enter_context(nc.psum_tensor(f"ps{i}", [128, 512], F32)) for i in range(8)]
    psn = [f"ps{i}" for i in range(8)]
    ps_rr = [0]

    def nps():
        i = ps_rr[0] % 8
        ps_rr[0] += 1
        return ps[i], f"{psn[i]}#{ps_rr[0]}"

    sch = Sched(nc)
    blk = es.enter_context(nc.Block())
    op = sch.op

    def act(out_, in_, func, reads, writes, **kw):
        return op("act", lambda e: e.activation(out=out_, in_=in_, func=func, **kw), reads, writes)

    def tt(eng, out_, a, b, o_, reads, writes):
        return op(eng, lambda e: e.tensor_tensor(out=out_, in0=a, in1=b, op=o_), reads, writes)

    def ts(eng, out_, a, s1, s2, o0, o1, reads, writes):
        return op(eng, lambda e: e.tensor_scalar(out=out_, in0=a, scalar1=s1, scalar2=s2, op0=o0, op1=o1), reads, writes)

    def cp(eng, out_, in_, reads, writes):
        if eng == "act":
            return act(out_, in_, AF.Copy, reads, writes)
        return op(eng, lambda e: e.tensor_copy(out=out_, in_=in_), reads, writes)

    def mm(out_, lhsT, rhs, start, stop, reads, writes, inc=None):
        return op("pe", lambda e: e.matmul(out_, lhsT, rhs, start=start, stop=stop), reads, writes,
                  inc=(1 if stop else 0) if inc is None else inc)

    def rsqrt(out_, in_, scale, reads, writes, n):
        ts("dve", out_, in_, scale, EPS, ALU.mult, ALU.add, reads, writes)
        op("pool", lambda e: e.tensor_tensor(out=out_, in0=out_, in1=neghalf[:, 0:n], op=ALU.pow), list(writes) + ["neghalf"], writes)

    def ld(q, out_, in_, w):
        sch.dma(q, "d_" + w[0], out_, in_, (), w)


    HALF = ARENA // 4
    stg = [arena[:, 0:2 * HALF].bitcast(F32), arena[:, 2 * HALF:4 * HALF].bitcast(F32)]
    eng_rr = [0]

    def stage_load(items):
        rounds, cur, off = [], [], 0
        for it in items:
            n = it[0].shape[-1]
            if off + n > HALF:
                rounds.append(cur)
                cur, off = [], 0
            cur.append((it, off, n))
            off += n
        rounds.append(cur)
        for ri, rnd in enumerate(rounds):
            b = ri % 2
            for ii, (it, off, n) in enumerate(rnd):
                q = ("sp", "act")[ii % 2]
                sch.dma(q, f"stg{b}_{q}", stg[b][:, off:off + n], it[1], (), [f"stg{b}_{q}"])
            for (it, off, n) in rnd:
                e = ("dve", "act")[eng_rr[0] % 2]
                eng_rr[0] += 1
                rd = [f"stg{b}_sp", f"stg{b}_act"] + ([it[4]] if len(it) > 4 else [])
                src = stg[b][:, off:off + n]
                if e == "act":
                    act(it[0], src, AF.Copy, rd, [it[3]], scale=it[2])
                else:
                    ts(e, it[0], src, it[2], None, ALU.mult, ALU.bypass, rd, [it[3]])

    ld("sp", ident_f[:], consts["ident"], ["ident_f"])
    ld("sp", utri[:], consts["utri"], ["utri"])
    ld("sp", ones_f[:], consts["ones"], ["ones_f"])
    ld("sp", m8[:], consts["m8"], ["m8"])
    ld("sp", hlb_t[:].rearrange("p h l -> p (h l)"), hlb, ["hlb_t"])
    ld("pool", ident_b[:], consts["ident"], ["ident_b"])
    ld("pool", ones_b[:], consts["ones"], ["ones_b"])
    ld("pool", negm_b[:], consts["negm"], ["negm_b"])
    ld("pool", m64[:], consts["m64"], ["m64"])
    ld("pool", pm_b[:], consts["pm"], ["pm_b"])
    ld("pool", pmT_b[:], consts["pmT"], ["pmT_b"])
    op("dve", lambda e: e.memset(neghalf[:], -0.5), (), ["neghalf"])
    act(hlb_t[:], hlb_t[:], AF.Exp, ["hlb_t"], ["hlb_t"])
    op("dve", lambda e: e.tensor_reduce(out=st8[:, 0:4], in_=hlb_t[:], axis=AX.X, op=ALU.add), ["hlb_t"], ["st8"])
    op("dve", lambda e: e.reciprocal(out=st8[:, 0:4], in_=st8[:, 0:4]), ["st8"], ["st8"])
    tt("dve", hlb_t[:], hlb_t[:], st8[:, 0:4].unsqueeze(2).broadcast_to([128, 4, NL]), ALU.mult, ["hlb_t", "st8"], ["hlb_t"])
    op("dve", lambda e: e.memset(lb_all[:], 0.0), (), ["lb_all"])
    for l in range(1, NL):
        tt("dve", lb_all[:, :, l], lb_all[:, :, l - 1], hlb_t[:, :, l], ALU.add, ["lb_all", "hlb_t"], ["lb_all"])

    NQ = L // 256
    NSC = L // 512
    layers = list(range(NL)) if layers is None else list(layers)

    def gci(s, c):
        return s * NCH + c

    for layer in layers:
        xsrc = x_in if layer == layers[0] else out
        xrd = ["out_d"] if layer != layers[0] else []
        sch.barrier()
        ld("sp", prew_t[:], prew[layer], ["prew_t"])
        ld("sp", convw_t[:], convw[layer], ["convw_t"])
        ld("sp", convb_t[:], convb_pp[layer], ["convb_t"])
        ld("sp", dtb_bc[:], dtb[layer].partition_broadcast(128), ["dtb_bc"])
        ld("sp", a_bc[:], alog[layer].partition_broadcast(128), ["a_bc"])
        ld("sp", d_bc[:], ssdd[layer].partition_broadcast(128), ["d_bc"])
        act(a_bc[:], a_bc[:], AF.Exp, ["a_bc"], ["a_bc"])
        ts("dve", a_bc[:], a_bc[:], -1.0, None, ALU.mult, ALU.bypass, ["a_bc"], ["a_bc"])
        cp("dve", lb_t[:], lb_all[:, :, layer], ["lb_all"], ["lb_t"])
        itemsA = []
        for k in range(8):
            itemsA.append((wtma[:, k, :], w_tma[layer, k * 128:(k + 1) * 128, :], prew_t[:, k:k + 1], "wtma", "prew_t"))
            itemsA.append((wfma[:, k, :], w_fma[layer, k * 128:(k + 1) * 128, :], prew_t[:, k:k + 1], "wfma", "prew_t"))
        stage_load(itemsA)
        sch.barrier()
        ld("sp", scan0, consts["scan0"], ["scan0"])
        ld("pool", convb_r, convb_row[layer], ["convb_r"])
        for k in range(4):
            for j in range(12):
                ts("dve", dconv[:, k, j, :], ident_b[:], convw_t[:, k * 12 + j:k * 12 + j + 1], None, ALU.mult, ALU.bypass,
                   ["ident_b", "convw_t"], ["dconv"])
        op("dve", lambda e: e.memset(qTa, 0.0), (), ["qTa"])
        op("dve", lambda e: e.memset(qTb, 0.0), (), ["qTb"])
        for s in range(2):
            op("dve", lambda e: e.memset(XT[s][:, :, 0:3], 0.0), (), [f"XT{s}"])
            op("dve", lambda e: e.memset(hst[s], 0.0), (), [f"hst{s}"])
            op("pool", lambda e: e.memset(hbf[s], 0.0), (), [f"hbf{s}"])
            op("dve", lambda e: e.memset(Sst[s], 0.0), (), [f"Sst{s}"])
            op("pool", lambda e: e.memset(Sbf[s], 0.0), (), [f"Sbf{s}"])

        xcnt = [0]
        if dbg and dbg.get("stop") == "A0":
            break
        for qd in range(NQ):
            quad = [(s, 2 * qd + pp) for s in range(2) for pp in range(2)]
            for qi, (s, c) in enumerate(quad):
                xb, xbn = xt[xcnt[0] % 2], f"xt{xcnt[0] % 2}"
                xcnt[0] += 1
                sch.dma("sp", "x" + xbn, xb, xsrc[s, c * T:(c + 1) * T, :], xrd, [xbn])
                act(xn, xb, AF.Square, [xbn], ["xn", "st8"], accum_out=st8[:, 0:1])
                rsqrt(st8[:, 1:2], st8[:, 0:1], 1.0 / D_MODEL, ["st8"], ["st8"], 1)
                ts("dve", xn, xb, st8[:, 1:2], None, ALU.mult, ALU.bypass, [xbn, "st8"], ["xn"])
                p0, p0n = nps()
                p0b = p0[:].bitcast(BF16)
                for k in range(8):
                    op("pe", lambda e: e.transpose(p0b[:, k * 128:(k + 1) * 128], xn[:, k * 128:(k + 1) * 128], ident_b[:]),
                       ["xn", "ident_b"], [p0n], inc=1 if k == 7 else 0)
                cp("act", hTq[:, :, qi * 128:(qi + 1) * 128], p0b.rearrange("p (k t) -> p k t", k=8), [p0n], ["hTq"])
                sch.dma("sp", "hts", hT_d[gci(s, c)].rearrange("p (k t) -> p k t", k=8), hTq[:, :, qi * 128:(qi + 1) * 128], ["hTq"], ["hT_d"])
            if dbg and dbg.get("stop") == "A1":
                break
            njc = int(dbg.get("njc", 16)) if dbg else 16
            for jc in range(njc):
                pf, pfn = nps()
                for k in range(8):
                    mm(pf[:], wfma[:, k, jc * 128:(jc + 1) * 128], hTq[:, k, :], k == 0, k == 7, ["hTq", "wfma"], [pfn])
                if dbg and dbg.get("evac") == "junk":
                    cp("act", xn[:, 0:512], pf[:], [pfn], ["xn"])
                elif jc < 12:
                    ev = dbg.get("evac", "same") if dbg else "same"
                    for s in range(2):
                        if ev == "act_only" and s == 1:
                            continue
                        if ev == "dve_only" and s == 0:
                            continue
                        c0_ = 4 if ev == "even" else 3
                        eng_ = "act" if s == 0 else "dve"
                        if ev == "swap":
                            eng_ = "dve" if s == 0 else "act"
                        if ev == "same":
                            eng_ = "act" if jc % 2 == 0 else "dve"
                        cp(eng_, XT[s][:, jc, c0_:c0_ + 256], pf[:, s * 256:(s + 1) * 256], [pfn], [f"XT{s}"])
                else:
                    act(qsq[:, jc - 12, :], pf[:], AF.Silu, [pfn], ["qsq"])
            if dbg and dbg.get("stop") == "A15":
                break
            for s in range(2):
                for half in range(2):
                    pb, pbn = nps()
                    for j2 in range(2):
                        jj = half * 2 + j2
                        j = 8 + jj
                        for k in range(4):
                            mm(pb[:, j2 * 256:(j2 + 1) * 256], dconv[:, k, j, :], XT[s][:, j, k:k + 256], k == 0, k == 3, [f"XT{s}", "dconv"], [pbn],
                               inc=1 if (k == 3 and j2 == 1) else 0)
                    for j2 in range(2):
                        jj = half * 2 + j2
                        act(bctq[:, jj, s * 256:(s + 1) * 256], pb[:, j2 * 256:(j2 + 1) * 256], AF.Silu, [pbn, "convb_t"], ["bctq"],
                            bias=convb_t[:, 8 + jj:9 + jj])
            if dbg and dbg.get("stop") == "A2":
                break
            for qi, (s, c) in enumerate(quad):
                pp_ = c % 2
                tsl = slice(qi * 128, (qi + 1) * 128)
                XTs, XTn = XT[s], f"XT{s}"
                hs, hsn, hb, hbn = hst[s], f"hst{s}", hbf[s], f"hbf{s}"
                Ss, Ssn, Sb, Sbn = Sst[s], f"Sst{s}", Sbf[s], f"Sbf{s}"

                def tm_slab(c0, n):
                    pz, pzn = nps()
                    for k in range(8):
                        mm(pz[:, 0:n], hTq[:, k, tsl], wtma[:, k, c0:c0 + n], k == 0, k == 7, ["hTq", "wtma"], [pzn])
                    return pz, pzn
                for h2 in range(2):
                    pz, pzn = tm_slab(h2 * 512, 512)
                    act(zs[:, h2 * 512:(h2 + 1) * 512], pz[:], AF.Silu, [pzn], ["zs"])
                pz, pzn = tm_slab(1040 + 512, 512)
                act(gs, pz[:], AF.Silu, [pzn], ["gs"])
                pz, pzn = tm_slab(1040, 512)
                cp("act", vtok, pz[:], [pzn], ["vtok"])
                pd, pdn = tm_slab(1024, 16)
                tt("dve", dtmp[:], pd[:, 0:16], dtb_bc[:], ALU.add, [pdn, "dtb_bc"], ["dtmp"])
                pfq, pfqn = nps()
                for jj in range(4):
                    for k in range(8):
                        mm(pfq[:, jj * 128:(jj + 1) * 128], wfma[:, k, (16 + jj) * 128:(17 + jj) * 128], hTq[:, k, tsl], k == 0, k == 7,
                           ["hTq", "wfma"], [pfqn], inc=1 if (k == 7 and jj == 3) else 0)
                pc = [nps() for _ in range(3)]
                w0 = pp_ * 128
                for j in range(10):
                    pcj, pcjn = pc[j // 4]
                    o_ = pcj[:, (j % 4) * 128:(j % 4 + 1) * 128]
                    mm(o_, ones_b[0:1, :], convb_r[0:1, j * 128:(j + 1) * 128], True, False, ["ones_b", "convb_r"], [pcjn])
                    for k in range(4):
                        last = (k == 3)
                        mm(o_, XTs[:, j, w0 + k:w0 + k + 128], dconv[:, k, j, :], False, last, [XTn, "dconv"], [pcjn],
                           inc=1 if (last and (j % 4 == 3 or j == 9)) else 0)
                act(xs[:, 0:512], pc[0][0][:], AF.Silu, [pc[0][1]], ["xs"])
                act(xs[:, 512:1024], pc[1][0][:], AF.Silu, [pc[1][1]], ["xs"])
                act(btok, pc[2][0][:, 0:256], AF.Silu, [pc[2][1]], ["btok"])
                act(dtmp[:], dtmp[:], AF.Exp, ["dtmp"], ["dtmp"])
                act(dt_t[:], dtmp[:], AF.Ln, ["dtmp"], ["dt_t"], bias=1.0)
                act(ef, pfq[:], AF.Exp, [pfqn], ["ef"], scale=-1.0)
                tt("dve", dtA[:], dt_t[:], a_bc[:], ALU.mult, ["dt_t", "a_bc"], ["dtA"])
                pa, pan = nps()
                mm(pa[:, 0:16], utri[:], dtA[:], True, True, ["utri", "dtA"], [pan], inc=0)
                mm(pa[:, 16:32], ones_f[:], dtA[:], True, True, ["ones_f", "dtA"], [pan])
                cp("dve", acum[:], pa[:, 0:16], [pan], ["acum"])
                ts("dve", nacum[:], pa[:, 0:16], -1.0, None, ALU.mult, ALU.bypass, [pan], ["nacum"])
                act(eacum[:], pa[:, 0:16], AF.Exp, [pan], ["eacum"])
                act(cdec[:], pa[:, 16:32], AF.Exp, [pan], ["cdec"])
                tt("dve", dte[:], pa[:, 16:32], acum[:], ALU.subtract, [pan, "acum"], ["dte"])
                act(dte[:], dte[:], AF.Exp, ["dte"], ["dte"])
                tt("dve", dte[:], dte[:], dt_t[:], ALU.mult, ["dte", "dt_t"], ["dte"])
                xs3 = xs.rearrange("p (r q) -> p r q", r=16)
                tt("dve", xdt.rearrange("p (r q) -> p r q", r=16), xs3, dt_t[:].unsqueeze(2).broadcast_to([128, 16, 64]), ALU.mult,
                   ["xs", "dt_t"], ["xdt"])
                tt("dve", xw.rearrange("p (r q) -> p r q", r=16), xs3, dte[:].unsqueeze(2).broadcast_to([128, 16, 64]), ALU.mult,
                   ["xs", "dte"], ["xw"])
                tt("pool", xsd.rearrange("p (r q) -> p r q", r=16), xs3, d_bc[:].unsqueeze(2).broadcast_to([128, 16, 64]), ALU.mult,
                   ["xs", "d_bc"], ["xsd"])
                for g in range(2):
                    psc, pscn = nps()
                    mm(psc[:, 0:128], bctq[:, g, tsl], bctq[:, 2 + g, tsl], True, True, ["bctq"], [pscn])
                    cp("dve", scT[:, g, :], psc[:, 0:128], [pscn], ["scT"])
                    pab = [nps(), nps()]
                    for r in range(8):
                        pq, pqn = pab[r // 4]
                        o_ = pq[:, (r % 4) * 128:(r % 4 + 1) * 128]
                        hh = g * 8 + r
                        mm(o_, dtA[:, hh:hh + 1].broadcast_to([128, 128]), utri[:], True, False, ["dtA", "utri"], [pqn])
                        mm(o_, ident_b[:], negm_b[:], False, True, ["ident_b", "negm_b"], [pqn], inc=1 if r % 4 == 3 else 0)
                    for r in range(8):
                        pq, pqn = pab[r // 4]
                        hh = g * 8 + r
                        act(dec[:, r, :], pq[:, (r % 4) * 128:(r % 4 + 1) * 128], AF.Exp, [pqn, "nacum"], ["dec"], bias=nacum[:, hh:hh + 1])
                    tt("dve", MT, dec, scT[:, g:g + 1, :].broadcast_to([128, 8, 128]), ALU.mult, ["dec", "scT"], ["MT"])
                    py, pyn = nps()
                    mm(py[:], ident_b[:], xsd[:, g * 512:(g + 1) * 512], True, False, ["ident_b", "xsd"], [pyn])
                    for r in range(8):
                        hh = g * 8 + r
                        mm(py[:, r * 64:(r + 1) * 64], MT[:, r, :], xdt[:, hh * 64:(hh + 1) * 64], False, r == 7, ["MT", "xdt"], [pyn])
                    po, pon = nps()
                    mm(po[:], bctq[:, 2 + g, tsl], hb[:, g * 512:(g + 1) * 512], True, True, ["bctq", hbn], [pon])
                    tt("dve", ytmp.rearrange("p (r q) -> p r q", r=8), po[:].rearrange("p (r q) -> p r q", r=8),
                       eacum[:, g * 8:(g + 1) * 8].unsqueeze(2).broadcast_to([128, 8, 64]), ALU.mult, [pon, "eacum"], ["ytmp"])
                    tt("dve", ytmp, ytmp, py[:], ALU.add, ["ytmp", pyn], ["ytmp"])
                    tt("dve", yg, ytmp, zs[:, g * 512:(g + 1) * 512], ALU.mult, ["ytmp", "zs"], ["yg"])
                    act(junk[:, 0:512], yg, AF.Square, ["yg"], ["dec", "st8"], accum_out=st8[:, 2:3])
                    rsqrt(st8[:, 3:4], st8[:, 2:3], 1.0 / 512, ["st8"], ["st8"], 1)
                    ts("dve", mix[:, g * 512:(g + 1) * 512], yg, st8[:, 3:4], None, ALU.mult, ALU.bypass, ["yg", "st8"], ["mix"])
                    ph, phn = nps()
                    mm(ph[:], btok[:, g * 128:(g + 1) * 128], xw[:, g * 512:(g + 1) * 512], True, True, ["btok", "xw"], [phn])
                    hv = hs[:, g * 512:(g + 1) * 512]
                    tt("dve", hv.rearrange("p (r q) -> p r q", r=8), hv.rearrange("p (r q) -> p r q", r=8),
                       cdec[:, g * 8:(g + 1) * 8].unsqueeze(2).broadcast_to([128, 8, 64]), ALU.mult, [hsn, "cdec"], [hsn])
                    tt("dve", hv, hv, ph[:], ALU.add, [hsn, phn], [hsn])
                    cp("pool", hb[:, g * 512:(g + 1) * 512], hv, [hsn], [hbn])
                act(L2, ef, AF.Ln, ["ef"], ["L2"], bias=1.0)
                for h in range(4):
                    act(L1[:, h * 128:(h + 1) * 128], ef[:, h * 128:(h + 1) * 128], AF.Ln, ["ef", "lb_t"], ["L1"], bias=1.0, scale=lb_t[:, h:h + 1])
                tt("dve", L1, L1, L2, ALU.subtract, ["L1", "L2"], ["L1"])
                act(L2, L1, AF.Exp, ["L1"], ["L2"])
                ts("dve", L2, L2, -1.0, 1.0, ALU.mult, ALU.add, ["L2"], ["L2"])
                op("dve", lambda e: e.tensor_tensor_scan(out=cum, data0=scan0, data1=L1, initial=0.0, op0=ALU.mult, op1=ALU.add),
                   ["scan0", "L1"], ["cum"])
                cum4 = cum.rearrange("p (j t) -> p j t", j=8)
                act(ecend[:], cum4[:, :, 63], AF.Exp, ["cum"], ["ecend"])
                act(ecum, cum, AF.Exp, ["cum"], ["ecum"])
                q4 = qsq[:, :, tsl]
                e4 = ecum.rearrange("p (h t) -> p h t", h=4)
                tt("dve", qTa[:, :, 0:64], q4[:, :, 0:64], e4[:, :, 0:64], ALU.mult, ["qsq", "ecum"], ["qTa"])
                tt("dve", qTb[:, :, 64:128], q4[:, :, 64:128], e4[:, :, 64:128], ALU.mult, ["qsq", "ecum"], ["qTb"])
                act(ecum, cum, AF.Exp, ["cum"], ["ecum"], scale=-1.0)
                tt("dve", ecum, L2, ecum, ALU.mult, ["L2", "ecum"], ["ecum"])
                cp("pool", kT.rearrange("p h t -> p (h t)"), ecum, ["ecum"], ["kT"])
                tt("dve", kend.rearrange("p h (b t) -> p (h b) t", b=2), ecum.rearrange("p (j t) -> p j t", j=8),
                   ecend[:].unsqueeze(2).broadcast_to([128, 8, 64]), ALU.mult, ["ecum", "ecend"], ["kend"])
                pk, pkn = nps()
                pkb = pk[:].bitcast(BF16)
                for h in range(4):
                    op("pe", lambda e: e.transpose(pkb[:, h * 128:(h + 1) * 128], kend[:, h, :], ident_b[:]), ["kend", "ident_b"], [pkn],
                       inc=1 if h == 3 else 0)
                cp("act", kendT.rearrange("p h t -> p (h t)"), pkb[:, 0:512], [pkn], ["kendT"])
                pat, patn = nps()
                for h in range(4):
                    mm(pat[:, h * 128:(h + 1) * 128], kT[:, h, :], qTa[:, h, :], True, False, ["kT", "qTa"], [patn])
                    mm(pat[:, h * 128:(h + 1) * 128], kT[:, h, :], qTb[:, h, :], False, True, ["kT", "qTb"], [patn], inc=1 if h == 3 else 0)
                tt("dve", attn, pat[:].rearrange("p (h t) -> p h t", h=4), m64[:].unsqueeze(1).broadcast_to([128, 4, 128]), ALU.mult,
                   [patn, "m64"], ["attn"])
                pho, phon = nps()
                for h in range(4):
                    o_ = pho[:, h * 128:(h + 1) * 128]
                    mm(o_, attn[:, h, :], vtok[:, h * 128:(h + 1) * 128], h == 0, False, ["attn", "vtok"], [phon])
                    mm(o_, qTa[:, h, :], Sb[:, h, :], False, False, ["qTa", Sbn], [phon])
                for b2 in range(2):
                    pst, pstn = nps()
                    for h in range(4):
                        mm(pst[:, h * 128:(h + 1) * 128], kendT[b2 * 64:(b2 + 1) * 64, h, :], vtok[b2 * 64:(b2 + 1) * 64, h * 128:(h + 1) * 128],
                           True, True, ["kendT", "vtok"], [pstn], inc=1 if h == 3 else 0)
                    ec = ecend[:].rearrange("p (h b) -> p h b", b=2)[:, :, b2:b2 + 1].broadcast_to([128, 4, 128])
                    tt("dve", Ss, Ss, ec, ALU.mult, [Ssn, "ecend"], [Ssn])
                    tt("dve", Ss.rearrange("p h v -> p (h v)"), Ss.rearrange("p h v -> p (h v)"), pst[:], ALU.add, [Ssn, pstn], [Ssn])
                    cp("pool", Sb, Ss, [Ssn], [Sbn])
                    if b2 == 0:
                        for h in range(4):
                            mm(pho[:, h * 128:(h + 1) * 128], qTb[:, h, :], Sb[:, h, :], False, h == 3, ["qTb", Sbn], [phon], inc=1 if h == 3 else 0)
                for h in range(4):
                    act(junk[:, 0:128], pho[:, h * 128:(h + 1) * 128], AF.Square, [phon], ["dec", "st8"], accum_out=st8[:, 4 + h:5 + h])
                rsqrt(st8[:, 4:8], st8[:, 4:8], 1.0 / 128, ["st8"], ["st8"], 4)
                tt("dve", otmp.rearrange("p (h v) -> p h v", h=4), pho[:].rearrange("p (h v) -> p h v", h=4),
                   st8[:, 4:8].unsqueeze(2).broadcast_to([128, 4, 128]), ALU.mult, [phon, "st8"], ["otmp"])
                tt("dve", mix[:, 1024:1536], otmp, gs, ALU.mult, ["otmp", "gs"], ["mix"])
                sch.dma("sp", "mts", mixA_d[s, c * T:(c + 1) * T, :], mix[:, 0:1536], ["mix"], ["mixA_d"])
            for s in range(2):
                cp("pool", XT[s][:, :, 0:3], XT[s][:, :, 256:259], [f"XT{s}"], [f"XT{s}"])

        if dbg and dbg.get("stop") in ("A", "A1", "A2", "A15"):
            break
        sch.barrier()
        ld("sp", mixnw_t[:], mixnw[layer], ["mixnw_t"])
        itemsB = []
        for k in range(8):
            itemsB.append((wb_v[:, k, :], w_b[layer, k * 128:(k + 1) * 128, :], prew_t[:, k:k + 1], "wb", "prew_t"))
        for k in range(16):
            itemsB.append((wout_v[:, k, :], w_out[layer, k * 128:(k + 1) * 128, :], mixnw_t[:, k:k + 1], "wout", "mixnw_t"))
        for k in range(4):
            itemsB.append((glu_v[:, k, :], gluw[layer, k * 128:(k + 1) * 128, :], 0.5, "glu"))
        stage_load(itemsB)
        sch.barrier()
        ld("sp", postw_bc, postw[layer].partition_broadcast(128), ["postw_bc"])
        ld("sp", s5d_t[:], s5d[layer], ["s5d_t"])
        ld("pool", glub_r, glub[layer], ["glub_r"])
        ld("sp", g32[:, 0, :], lamre[layer], ["g_lr"])
        ld("sp", g32[:, 1, :], lamim[layer], ["g_li"])
        ld("sp", g32[:, 2, :], lstep[layer].partition_broadcast(64), ["g_st"])
        ld("sp", Bre_t, bre[layer], ["Bre"])
        ld("sp", Bim_t, bim[layer], ["Bim"])
        ld("sp", Cre_t, cre[layer], ["Cre"])
        ld("sp", Cim_t, cim[layer], ["Cim"])
        GN = ["g32"]

        def gq(i):
            return g32[:, i, :]

        def gmul(o_, a, b):
            tt("dve", gq(o_), gq(a), gq(b), ALU.mult, GN + ["g_lr", "g_li", "g_st"], GN)

        def gadd(o_, a, b, o2=ALU.add):
            tt("dve", gq(o_), gq(a), gq(b), o2, GN, GN)

        def gts(o_, a, m_, a_):
            ts("dve", gq(o_), gq(a), m_, a_, ALU.mult, ALU.add, GN + ["g_lr", "g_li", "g_st"], GN)

        I_LR, I_LI, I_ST, I_X, I_ANG, I_MAG, I_MAGI, I_C, I_S, I_T1, I_T2, I_LRE, I_LIM, I_IRE, I_IIM, I_CRE, I_CIM, I_8RE, I_8IM, I_Y, I_P, I_NX, I_A16, I_DEN = range(24)
        act(gq(I_ST), gq(I_ST), AF.Exp, ["g_st"], GN + ["g_st"])
        ts("dve", gq(I_LR), gq(I_LR), -1e-4, None, ALU.min, ALU.bypass, ["g_lr"], GN + ["g_lr"])
        gmul(I_X, I_LR, I_ST)
        gmul(I_ANG, I_LI, I_ST)
        gts(I_NX, I_X, -1.0, 0.0)

        def expser(o_, xi):
            gts(o_, xi, 1.0 / 6, 1.0)
            for kf in (5, 4, 3, 2, 1):
                gmul(o_, o_, xi)
                gts(o_, o_, 1.0 / kf, 1.0)
        expser(I_MAG, I_X)
        expser(I_MAGI, I_NX)
        gts(I_A16, I_ANG, 1.0 / 16, 0.0)
        gmul(I_Y, I_A16, I_A16)
        sc_ = [1.0, -1.0 / 6, 1.0 / 120, -1.0 / 5040, 1.0 / 362880, -1.0 / 39916800, 1.0 / 6227020800]
        cc_ = [1.0, -0.5, 1.0 / 24, -1.0 / 720, 1.0 / 40320, -1.0 / 3628800, 1.0 / 479001600, -1.0 / 87178291200]
        gts(I_P, I_Y, sc_[6], sc_[5])
        for kf in (4, 3, 2, 1, 0):
            gmul(I_P, I_P, I_Y)
            gts(I_P, I_P, 1.0, sc_[kf])
        gmul(I_S, I_P, I_A16)
        gts(I_P, I_Y, cc_[7], cc_[6])
        for kf in (5, 4, 3, 2, 1, 0):
            gmul(I_P, I_P, I_Y)
            gts(I_P, I_P, 1.0, cc_[kf])
        gts(I_C, I_P, 1.0, 0.0)

        def csq(re, im):
            gmul(I_T1, re, re)
            gmul(I_T2, im, im)
            gmul(im, re, im)
            gts(im, im, 2.0, 0.0)
            gadd(re, I_T1, I_T2, ALU.subtract)
        for _ in range(4):
            csq(I_C, I_S)
        gmul(I_LRE, I_MAG, I_C)
        gmul(I_LIM, I_MAG, I_S)
        gmul(I_IRE, I_MAGI, I_C)
        gmul(I_IIM, I_MAGI, I_S)
        gts(I_IIM, I_IIM, -1.0, 0.0)
        gts(I_P, I_LRE, 1.0, -1.0)
        gmul(I_T1, I_LR, I_LR)
        gmul(I_T2, I_LI, I_LI)
        gadd(I_DEN, I_T1, I_T2)
        op("dve", lambda e: e.reciprocal(out=gq(I_DEN), in_=gq(I_DEN)), GN, GN)
        gmul(I_T1, I_P, I_LR)
        gmul(I_T2, I_LIM, I_LI)
        gadd(I_CRE, I_T1, I_T2)
        gmul(I_CRE, I_CRE, I_DEN)
        gmul(I_T1, I_LIM, I_LR)
        gmul(I_T2, I_P, I_LI)
        gadd(I_CIM, I_T1, I_T2, ALU.subtract)
        gmul(I_CIM, I_CIM, I_DEN)
        gts(I_8RE, I_LRE, 1.0, 0.0)
        gts(I_8IM, I_LIM, 1.0, 0.0)
        for _ in range(3):
            csq(I_8RE, I_8IM)
        cp("dve", LA[:, 0, :], gq(I_8RE), GN, ["LA"])
        cp("dve", LA[:, 1, :], gq(I_8RE), GN, ["LA"])
        ts("dve", LB[:, 0, :], gq(I_8IM), -1.0, None, ALU.mult, ALU.bypass, GN, ["LB"])
        cp("dve", LB[:, 1, :], gq(I_8IM), GN, ["LB"])

        def cmul(eng, ore, oim, are, aim, xre, xim, n, rd, wr):
            ab = lambda a_: a_.unsqueeze(2).broadcast_to([64, 16, n])
            tv1, tv2 = T1[:, :, 0:n], T2[:, :, 0:n]
            tt(eng, tv1, xre, ab(are), ALU.mult, rd + GN, ["T1"])
            tt(eng, tv2, xim, ab(aim), ALU.mult, rd + GN, ["T2"])
            tt(eng, ore, tv1, tv2, ALU.subtract, ["T1", "T2"], wr)
            tt(eng, tv1, xim, ab(are), ALU.mult, rd + GN, ["T1"])
            tt(eng, tv2, xre, ab(aim), ALU.mult, rd + GN, ["T2"])
            tt(eng, oim, tv1, tv2, ALU.add, ["T1", "T2"], wr)

        for gb in range(2):
            g0 = gb * 16
            sl = slice(g0, g0 + 16)
            Z4r = Zre.rearrange("p g (s h) -> p g s h", s=8)
            Z4i = Zim.rearrange("p g (s h) -> p g s h", s=8)
            Y4r = Yre.rearrange("p g (s h) -> p g s h", s=8)
            Y4i = Yim.rearrange("p g (s h) -> p g s h", s=8)
            Bv = lambda tile_: tile_[0:64, g0 * 16:(g0 + 16) * 16].rearrange("p (g h) -> p g h", g=16)
            cmul("dve", Z4r[:, :, 7, :], Z4i[:, :, 7, :], gq(I_CRE)[:, sl], gq(I_CIM)[:, sl], Bv(Bre_t), Bv(Bim_t), 16, ["Bre", "Bim"], ["Zre", "Zim"])
            for s8 in range(6, -1, -1):
                cmul("dve", Z4r[:, :, s8, :], Z4i[:, :, s8, :], gq(I_LRE)[:, sl], gq(I_LIM)[:, sl], Z4r[:, :, s8 + 1, :], Z4i[:, :, s8 + 1, :], 16,
                     ["Zre", "Zim"], ["Zre", "Zim"])
            cp("dve", Y4r[:, :, 7, :], Bv(Cre_t), ["Cre"], ["Yre"])
            cp("dve", Y4i[:, :, 7, :], Bv(Cim_t), ["Cim"], ["Yim"])
            for l8 in range(6, -1, -1):
                cmul("dve", Y4r[:, :, l8, :], Y4i[:, :, l8, :], gq(I_IRE)[:, sl], gq(I_IIM)[:, sl], Y4r[:, :, l8 + 1, :], Y4i[:, :, l8 + 1, :], 16,
                     ["Yre", "Yim"], ["Yre", "Yim"])
            ts("dve", T1, Yim, -1.0, None, ALU.mult, ALU.bypass, ["Yim"], ["T1"])
            for hb_ in range(4):
                pt_, ptn = nps()
                for gi in range(4):
                    gg = hb_ * 4 + gi
                    mm(pt_[:, gi * 128:(gi + 1) * 128], Zre[:, gg, :], Yre[:, gg, :], True, False, ["Zre", "Yre"], [ptn])
                    mm(pt_[:, gi * 128:(gi + 1) * 128], Zim[:, gg, :], T1[:, gg, :], False, True, ["Zim", "T1"], [ptn], inc=1 if gi == 3 else 0)
                tt("dve", tz_v[:, g0 + hb_ * 4:g0 + hb_ * 4 + 4, :], pt_[:].rearrange("p (g n) -> p g n", g=4),
                   m8[:].unsqueeze(1).broadcast_to([128, 4, 128]), ALU.mult, [ptn, "m8"], ["tz"])
            for (Zt, Zn, dst, dn) in ((Zre, "Zre", wsre_v, "wsre"), (Zim, "Zim", wsim_v, "wsim")):
                for half_ in range(2):
                    pw_, pwn = nps()
                    for gi in range(8):
                        mm(pw_[:, gi * 64:(gi + 1) * 64], Zt[:, half_ * 8 + gi, :], ident_f[0:64, 0:64], True, True, [Zn, "ident_f"], [pwn],
                           inc=1 if gi == 7 else 0)
                    cp("act", dst[:, g0 + half_ * 8:g0 + half_ * 8 + 8, :], pw_[:].rearrange("p (g n) -> p g n", g=8), [pwn], [dn])
            a8 = lambda i: gq(i)[:, sl].unsqueeze(2).broadcast_to([64, 16, 128])
            tt("dve", T1, Yre, a8(I_8RE), ALU.mult, ["Yre"] + GN, ["T1"])
            tt("dve", T2, Yim, a8(I_8IM), ALU.mult, ["Yim"] + GN, ["T2"])
            tt("dve", wore_v[:, sl, :], T1, T2, ALU.subtract, ["T1", "T2"], ["wore"])
            tt("dve", T1, Yim, a8(I_8RE), ALU.mult, ["Yim"] + GN, ["T1"])
            tt("dve", T2, Yre, a8(I_8IM), ALU.mult, ["Yre"] + GN, ["T2"])
            tt("dve", T1, T1, T2, ALU.add, ["T1", "T2"], ["T1"])
            ts("dve", woim_v[:, sl, :], T1, -1.0, None, ALU.mult, ALU.bypass, ["T1"], ["woim"])
        op("dve", lambda e: e.memset(S2p[0], 0.0), (), ["S2_0"])
        sch.barrier()
        op("pool", lambda e: e.memset(gst[:, :, :, :, 0], 0.0), (), ["gsts", "gstw"])

        if dbg and dbg.get("stop") == "B0":
            break
        def f_loads(sc):
            tiles = [(s, 4 * sc + cc) for s in range(2) for cc in range(4)]
            for ti, (s, c) in enumerate(tiles):
                sch.dma("sp", "htl", hT8[:, :, ti * 128:(ti + 1) * 128], hT_d[gci(s, c)].rearrange("p (k t) -> p k t", k=8), ["hT_d"], ["hT8"])

        def f_uproj(sc):
            for j in range(4):
                for hf in range(2):
                    pu, pun = nps()
                    for s4 in range(4):
                        s8 = hf * 4 + s4
                        for k in range(8):
                            rhs = hT8[:, k, :].rearrange("p (n s) -> p s n", s=8)[:, s8, :]
                            mm(pu[:, s4 * 128:(s4 + 1) * 128], wb_v[:, k, j * 128:(j + 1) * 128], rhs, k == 0, k == 7, ["hT8", "wb"], [pun],
                               inc=1 if (k == 7 and s4 == 3) else 0)
                    cp("act" if hf == 0 else "dve", uT8[:, j, hf * 512:(hf + 1) * 512], pu[:], [pun], ["uT8"])
            for g8 in range(8):
                for s8 in range(8):
                    qd_ = ("sp", "act")[(g8 * 8 + s8) % 2]
                    sch.dma(qd_, "blk_" + qd_, ublk8[16 * s8:16 * s8 + 16, :, :].rearrange("p (j g) n -> p j g n", j=4)[:, :, g8, :],
                            uT8[16 * g8:16 * g8 + 16, :, s8 * 128:(s8 + 1) * 128], ["uT8"], ["ublk8_" + qd_])

        def f_statein(sc):
            for q4_ in range(4):
                for ri, (wsv, wsn) in enumerate(((wsre_v, "wsre"), (wsim_v, "wsim"))):
                    pw2 = [nps(), nps()]
                    for g8 in range(8):
                        g = q4_ * 8 + g8
                        pw_, pwn = pw2[g8 // 4]
                        mm(pw_[0:64, (g8 % 4) * 128:(g8 % 4 + 1) * 128], wsv[:, g, :], ublk8[:, g, :], g8 % 4 == 0, g8 % 4 == 3, [wsn, "ublk8_sp", "ublk8_act"], [pwn])
                    for hb_ in range(2):
                        pw_, pwn = pw2[hb_]
                        g0_ = q4_ * 8 + hb_ * 4
                        cp("act" if hb_ == 0 else "dve", gst[:, ri, g0_:g0_ + 4, :, 1:65],
                           pw_[0:64, :].rearrange("p (g s m) -> p g s m", g=4, s=2), [pwn], ["gstw", "gsts"])

        def f_rec(m0, m1):
            LAb = LA[:].unsqueeze(3).broadcast_to([64, 2, 32, 2])
            for m in range(m0, m1):
                Si, Sin_, So, Son_ = S2p[m % 2], f"S2_{m % 2}", S2p[(m + 1) % 2], f"S2_{(m + 1) % 2}"
                tt("dve", rt, Si, LAb, ALU.mult, [Sin_, "LA"], ["rt"])
                tt("dve", ru[:, 0, :, :], Si[:, 1, :, :], LB[:, 0, :].unsqueeze(2).broadcast_to([64, 32, 2]), ALU.mult, [Sin_, "LB"], ["ru"])
                tt("dve", ru[:, 1, :, :], Si[:, 0, :, :], LB[:, 1, :].unsqueeze(2).broadcast_to([64, 32, 2]), ALU.mult, [Sin_, "LB"], ["ru"])
                tt("dve", rt, rt, ru, ALU.add, ["rt", "ru"], ["rt"])
                tt("dve", So, rt, gst[:, :, :, :, 1 + m], ALU.add, ["rt", "gstw"], [Son_])
                cp("pool", gst[:, :, :, :, 1 + m], So, [Son_], ["gsts"])

        def f_y(sc):
            for q4_ in range(4):
                py2 = [nps(), nps()]
                for g8 in range(8):
                    g = q4_ * 8 + g8
                    py_, pyn_ = py2[g8 // 4]
                    o_ = py_[:, (g8 % 4) * 128:(g8 % 4 + 1) * 128]
                    mm(o_, tz_v[:, g, :], ublk8[:, g, :], g8 % 4 == 0, False, ["tz", "ublk8_sp", "ublk8_act"], [pyn_])
                    for s_ in range(2):
                        mm(o_[:, s_ * 64:(s_ + 1) * 64], wore_v[:, g, :], gst[:, 0, g, s_, 0:64], False, False, ["wore", "gsts"], [pyn_])
                        mm(o_[:, s_ * 64:(s_ + 1) * 64], woim_v[:, g, :], gst[:, 1, g, s_, 0:64], False, g8 % 4 == 3 and s_ == 1, ["woim", "gsts"], [pyn_])
                for hb_ in range(2):
                    py_, pyn_ = py2[hb_]
                    g0_ = q4_ * 8 + hb_ * 4
                    ub = ublk8[:, g0_:g0_ + 4, :]
                    tt("dve", ytB.rearrange("p (g n) -> p g n", g=4), ub, s5d_t[:, g0_:g0_ + 4].unsqueeze(2).broadcast_to([128, 4, 128]), ALU.mult,
                       ["ublk8_sp", "ublk8_act", "s5d_t"], ["ytB"])
                    tt("dve", ytB, ytB, py_[:], ALU.add, ["ytB", pyn_], ["ytB"])
                    act(ygB, ytB, AF.Square, ["ytB"], ["ygB"])
                    ts("dve", ygB, ygB, GELU_C2, GELU_C1, ALU.mult, ALU.add, ["ygB"], ["ygB"])
                    tt("dve", ygB, ygB, ytB, ALU.mult, ["ygB", "ytB"], ["ygB"])
                    act(ygB, ygB, AF.Tanh, ["ygB"], ["ygB"])
                    op("dve", lambda e: e.scalar_tensor_tensor(out=gyb8[:, g0_:g0_ + 4, :].rearrange("p g n -> p (g n)"), in0=ygB, scalar=1.0, in1=ytB,
                                                                op0=ALU.add, op1=ALU.mult), ["ygB", "ytB"], ["gyb8"])
            cp("pool", gst[:, :, :, :, 0], gst[:, :, :, :, 64], ["gsts"], ["gsts"])
            for g8 in range(8):
                for l8 in range(8):
                    qd_ = ("sp", "act")[(g8 * 8 + l8) % 2]
                    sch.dma(qd_, "ubl_" + qd_, gyT8[16 * g8:16 * g8 + 16, :, l8 * 128:(l8 + 1) * 128],
                            gyb8[16 * l8:16 * l8 + 16, :, :].rearrange("p (j g) n -> p j g n", j=4)[:, :, g8, :], ["gyb8"], ["gyT8_" + qd_])

        def f_gates(sc):
            for l8 in range(8):
                pg_, pgn = nps()
                for k in range(8):
                    lhs = hT8[:, k, :].rearrange("p (n l) -> p l n", l=8)[:, l8, :]
                    mm(pg_[:], lhs, wb_v[:, k, 512:1024], k == 0, k == 7, ["hT8", "wb"], [pgn])
                act(gatesB[:, l8, :], pg_[:], AF.Silu, [pgn], ["gatesB"])

        def f_tile(sc, l8):
            base = 512 * sc
            if True:
                xb, xbn = xtB[l8 % 2], f"xtB{l8 % 2}"
                for s_ in range(2):
                    sch.dma("sp", f"x{xbn}_{s_}", xb[64 * s_:64 * s_ + 64, :],
                            xsrc[s_, base:base + 512, :].rearrange("(m l) d -> l m d", l=8)[l8], xrd, [f"{xbn}_{s_}"])
                    sch.dma("act", f"mxl_{s_}", mixB[64 * s_:64 * s_ + 64, 0:1536],
                            mixA_d[s_, base:base + 512, :].rearrange("(m l) d -> l m d", l=8)[l8], ["mixA_d"], [f"mixB_{s_}"])
                pv = [nps(), nps()]
                for hf in range(2):
                    pp, ppn = pv[hf]
                    mm(pp[:], ones_b[0:1, :], glub_r[0:1, hf * 512:(hf + 1) * 512], True, False, ["ones_b", "glub_r"], [ppn])
                    for j in range(4):
                        mm(pp[:], gyT8[:, j, l8 * 128:(l8 + 1) * 128], glu_v[:, j, hf * 512:(hf + 1) * 512], False, j == 3,
                           ["gyT8_sp", "gyT8_act", "glu"], [ppn])
                act(L1B, pv[1][0][:], AF.Tanh, [pv[1][1]], ["L1B"], scale=0.5)
                ts("dve", L1B, L1B, 0.5, 0.5, ALU.mult, ALU.add, ["L1B"], ["L1B"])
                tt("dve", L2B, pv[0][0][:], L1B, ALU.mult, [pv[0][1], "L1B"], ["L2B"])
                tt("dve", L2B, L2B, gatesB[:, l8, :], ALU.mult, ["L2B", "gatesB"], ["L2B"])
                act(junkB[:, 0:512], L2B, AF.Square, ["L2B"], ["junkB", "st8"], accum_out=st8[:, 2:3])
                rsqrt(st8[:, 3:4], st8[:, 2:3], 1.0 / 512, ["st8"], ["st8"], 1)
                ts("dve", mixB[:, 1536:2048], L2B, st8[:, 3:4], None, ALU.mult, ALU.bypass, ["L2B", "st8"], ["mixB_S"])
                pm1, pm1n = nps()
                pm2, pm2n = nps()
                pm1b, pm2b = pm1[:].bitcast(BF16), pm2[:].bitcast(BF16)
                for j in range(16):
                    dst, dn = (pm1b, pm1n) if j < 8 else (pm2b, pm2n)
                    jj = j % 8
                    op("pe", lambda e: e.transpose(dst[:, jj * 128:(jj + 1) * 128], mixB[:, j * 128:(j + 1) * 128], ident_b[:]),
                       ["mixB_0", "mixB_1", "mixB_S", "ident_b"], [dn], inc=1 if j in (7, 15) else 0)
                mT2 = mixTB.rearrange("p k t -> p (k t)")
                cp("act", mT2[:, 0:1024], pm1b, [pm1n], ["mixTB"])
                cp("dve", mT2[:, 1024:2048], pm2b, [pm2n], ["mixTB"])
                po_ = [nps(), nps()]
                for n2 in range(2):
                    pp, ppn = po_[n2]
                    for kk in range(16):
                        mm(pp[:], mixTB[:, kk, :], wout_v[:, kk, n2 * 512:(n2 + 1) * 512], kk == 0, kk == 15, ["mixTB", "wout"], [ppn])
                for n2 in range(2):
                    act(junkB[:, n2 * 512:(n2 + 1) * 512], po_[n2][0][:], AF.Square, [po_[n2][1]], ["junkB", "st8"], accum_out=st8[:, 4 + n2:5 + n2])
                tt("dve", st8[:, 6:7], st8[:, 4:5], st8[:, 5:6], ALU.add, ["st8"], ["st8"])
                rsqrt(st8[:, 7:8], st8[:, 6:7], 1.0 / D_MODEL, ["st8"], ["st8"], 1)
                for n2 in range(2):
                    op("dve", lambda e: e.scalar_tensor_tensor(out=ygB, in0=po_[n2][0][:], scalar=st8[:, 7:8], in1=postw_bc[:, n2 * 512:(n2 + 1) * 512],
                                                                op0=ALU.mult, op1=ALU.mult), [po_[n2][1], "st8", "postw_bc"], ["ygB"])
                    tt("dve", xb[:, n2 * 512:(n2 + 1) * 512], xb[:, n2 * 512:(n2 + 1) * 512], ygB, ALU.add, [f"{xbn}_0", f"{xbn}_1", "ygB"],
                       [f"{xbn}_0", f"{xbn}_1"])
                for s_ in range(2):
                    sch.dma("sp", "ost", out[s_, base:base + 512, :].rearrange("(m l) d -> l m d", l=8)[l8], xb[64 * s_:64 * s_ + 64, :],
                            [f"{xbn}_0", f"{xbn}_1"], ["out_d"])

        f_loads(0); f_uproj(0); f_statein(0); f_rec(0, 64); f_y(0); f_gates(0)
        for sc in range(NSC):
            nxt = sc + 1 < NSC
            if nxt:
                f_loads(sc + 1)
                f_uproj(sc + 1)
            f_tile(sc, 0)
            f_tile(sc, 1)
            if nxt:
                f_statein(sc + 1)
            for l8 in range(2, 8):
                if nxt:
                    f_rec((l8 - 2) * 11, min(64, (l8 - 1) * 11))
                f_tile(sc, l8)
            if nxt:
                f_y(sc + 1)
                f_gates(sc + 1)
    sch.barrier()
    sch.finish("sp", ["mixA_d", "hT_d", "out_d"])
    es.close()
    return nc, sch


def prep_shared(inp, NL):
    f = lambda a: np.ascontiguousarray(np.asarray(a, dtype=np.float32))
    w_in = np.asarray(inp["w_in"], dtype=np.float32)[:NL]
    sh = {}
    sh["w_tma"] = f(np.concatenate([w_in[:, :, C_Z:C_Z + 1024], w_in[:, :, C_DT:C_DT + 16], w_in[:, :, C_I:C_I + 512],
                                    w_in[:, :, C_G:C_G + 512]], axis=2))
    sh["w_fma"] = f(np.concatenate([w_in[:, :, C_XBC:C_XBC + 1536], w_in[:, :, C_Q:C_Q + 512], w_in[:, :, C_F:C_F + 512]], axis=2))
    sh["w_b"] = f(w_in[:, :, C_U:C_U + 1024])
    sh["w_out"] = f(np.asarray(inp["w_out"])[:NL])
    sh["prew"] = f(np.asarray(inp["pre_norm_w"])[:NL].reshape(NL, 8, 128).transpose(0, 2, 1))
    sh["postw"] = f(np.asarray(inp["post_norm_w"])[:NL].reshape(NL, 1, D_MODEL))
    mixnw = np.concatenate([np.asarray(inp["ssd_norm_w"])[:NL], np.asarray(inp["hgrn_norm_w"])[:NL], np.asarray(inp["s5_norm_w"])[:NL]], axis=1)
    sh["mixnw"] = f(mixnw.reshape(NL, 16, 128).transpose(0, 2, 1))
    cw = np.asarray(inp["ssd_conv_w"])[:NL]
    sh["convw"] = f(cw.reshape(NL, 4, 12, 128).transpose(0, 3, 1, 2).reshape(NL, 128, 48))
    cb = np.asarray(inp["ssd_conv_b"])[:NL]
    sh["convb_pp"] = f(cb.reshape(NL, 12, 128).transpose(0, 2, 1))
    sh["convb_row"] = f(cb.reshape(NL, 1, 1536))
    sh["dtb"] = f(np.asarray(inp["ssd_dt_bias"])[:NL].reshape(NL, 1, 16))
    sh["alog"] = f(np.asarray(inp["ssd_a_log"])[:NL].reshape(NL, 1, 16))
    sh["ssdd"] = f(np.asarray(inp["ssd_d"])[:NL].reshape(NL, 1, 16))
    hl = np.asarray(inp["hgrn_lower_bounds"])[:NL]
    sh["hlb"] = f(hl.reshape(NL, 4, 128).transpose(2, 1, 0).reshape(128, 4 * NL))
    sh["lamre"] = f(np.asarray(inp["s5_lambda_re"])[:NL].transpose(0, 2, 1))
    sh["lamim"] = f(np.asarray(inp["s5_lambda_im"])[:NL].transpose(0, 2, 1))
    sh["lstep"] = f(np.asarray(inp["s5_log_step"])[:NL].reshape(NL, 1, 32))
    sh["bre"] = f(np.asarray(inp["s5_b_re"])[:NL].transpose(0, 2, 1, 3).reshape(NL, 64, 512))
    sh["bim"] = f(np.asarray(inp["s5_b_im"])[:NL].transpose(0, 2, 1, 3).reshape(NL, 64, 512))
    sh["cre"] = f(np.asarray(inp["s5_c_re"])[:NL].transpose(0, 3, 1, 2).reshape(NL, 64, 512))
    sh["cim"] = f(np.asarray(inp["s5_c_im"])[:NL].transpose(0, 3, 1, 2).reshape(NL, 64, 512))
    d5 = np.asarray(inp["s5_d"])[:NL].reshape(NL, 32, 16)
    sh["s5d"] = f(np.broadcast_to(d5.transpose(0, 2, 1)[:, None, :, :], (NL, 8, 16, 32)).reshape(NL, 128, 32))
    sh["gluw"] = f(np.asarray(inp["s5_glu_w"])[:NL])
    sh["glub"] = f(np.asarray(inp["s5_glu_b"])[:NL].reshape(NL, 1, 1024))
    for k, v in host_consts().items():
        sh["c_" + k] = v
    return sh


LAYER_GROUPS = [[0, 1, 2, 3]]


def kernel(**inputs):
    x = np.ascontiguousarray(np.asarray(inputs["x"], dtype=np.float32))
    B = x.shape[0]
    S = B // NCORES
    sh = prep_shared(inputs, NL_FULL)
    cur = x
    for grp in LAYER_GROUPS:
        nc, _ = build(NL_FULL, S, x.shape[1], layers=grp)
        in_maps = [dict(sh, x=np.ascontiguousarray(cur[S * c:S * (c + 1)])) for c in range(NCORES)]
        res = run_bass_kernel_spmd(nc, in_maps, core_ids=list(range(NCORES)))
        cur = np.concatenate([np.asarray(r["out"], dtype=np.float32) for r in res.results], axis=0)
    return cur
```

```python
import math
from contextlib import ExitStack

import numpy as np
import concourse.bass as bass
import concourse.mybir as mybir
from concourse.bass_utils import run_bass_kernel_spmd

F32 = mybir.dt.float32
BF16 = mybir.dt.bfloat16
AF = mybir.ActivationFunctionType
ALU = mybir.AluOpType
AX = mybir.AxisListType

D_MODEL = 1024
IN_COLS = 5648
EPS = 1e-6
NL_FULL = 4
SEQ_FULL = 2048
NCORES = 8
T = 128

C_Z, C_XBC, C_DT, C_Q, C_F, C_I, C_G, C_U, C_SG = 0, 1024, 2560, 2576, 3088, 3600, 4112, 4624, 5136
N_TMA = 1024 + 16 + 512 + 512
N_FMA = 1536 + 512 + 512
N_B = 1024
GELU_C1 = 0.7978845608028654
GELU_C2 = 0.044715 * GELU_C1


class Sched:
    def __init__(self, nc):
        self.nc = nc
        self.eng = {"pe": nc.tensor, "act": nc.scalar, "dve": nc.vector, "pool": nc.gpsimd, "sp": nc.sync}
        self.sem = {}
        self.cnt = {}
        self.waited = {}
        self.lastw = {}
        self.readers = {}
        self.pending = {}
        self.ninst = 0
        self.gen = {}
        for k in ("pe", "act", "dve", "pool"):
            self._mk(k)

    def _mk(self, k):
        self.sem[k] = self.nc.alloc_semaphore("s_" + k)
        self.cnt[k] = 0
        self.pending[k] = False

    def _phys(self, names, writing):
        out = []
        for b in names:
            if "#" in b:
                p, g = b.split("#")
                if writing:
                    if self.gen.get(p) != g and int(g) > int(self.gen.get(p, "-1")):
                        self.gen[p] = g
                assert self.gen.get(p) == g, f"stale PSUM bank use {b} (current gen {self.gen.get(p)})"
                b = p
            out.append(b)
        return out

    def _deps(self, reads, writes):
        reads[:] = self._phys(reads, False)
        writes[:] = self._phys(writes, True)
        deps = {}
        raw = {}
        for b in reads:
            lw = self.lastw.get(b)
            if lw:
                deps[lw[0]] = max(deps.get(lw[0], 0), lw[1])
                raw[lw[0]] = max(raw.get(lw[0], 0), lw[1])
        for b in writes:
            lw = self.lastw.get(b)
            if lw:
                deps[lw[0]] = max(deps.get(lw[0], 0), lw[1])
            for e, i in self.readers.get(b, {}).items():
                deps[e] = max(deps.get(e, 0), i)
        return deps, raw

    def _emit_waits(self, issuer, me, deps, raw):
        w = self.waited.setdefault(issuer, {})
        for src, idx in deps.items():
            if src == me:
                if me == "pe":
                    continue
            if idx > w.get(src, 0):
                self.eng[issuer].wait_ge(self.sem[src], idx)
                w[src] = idx
                self.ninst += 1

    def op(self, e, fn, reads=(), writes=(), inc=1):
        reads, writes = list(reads), list(writes)
        deps, raw = self._deps(reads, writes)
        self._emit_waits(e, e, deps, raw)
        ins = fn(self.eng[e])
        self.ninst += 1
        if inc:
            ins.then_inc(self.sem[e], 1)
            self.cnt[e] += 1
            idx = self.cnt[e]
            self.pending[e] = False
        else:
            idx = self.cnt[e] + 1
            self.pending[e] = True
        for b in reads:
            self.readers.setdefault(b, {})[e] = max(self.readers.get(b, {}).get(e, 0), idx)
        for b in writes:
            self.lastw[b] = (e, idx)
            self.readers[b] = {}
        return ins

    def dma(self, q, slot, out, in_, reads=(), writes=(), **kw):
        if slot not in self.sem:
            self._mk(slot)
        reads, writes = list(reads), list(writes)
        deps, raw = self._deps(reads, writes)
        self._emit_waits(q, None, deps, raw)
        ins = self.eng[q].dma_start(out=out, in_=in_, **kw)
        ins.then_inc(self.sem[slot], 16)
        self.ninst += 1
        self.cnt[slot] += 16
        idx = self.cnt[slot]
        for b in reads:
            self.readers.setdefault(b, {})[slot] = idx
        for b in writes:
            self.lastw[b] = (slot, idx)
            self.readers[b] = {}

    def barrier(self):
        for e in ("pe", "act", "dve", "pool", "sp"):
            w = self.waited.setdefault(e, {})
            for src, c in self.cnt.items():
                if c == 0 or (src == e and e == "pe"):
                    continue
                if c > w.get(src, 0):
                    self.eng[e].wait_ge(self.sem[src], c)
                    w[src] = c
                    self.ninst += 1

    def dma_sync(self, q, slot):
        w = self.waited.setdefault(q, {})
        c = self.cnt.get(slot, 0)
        if c > w.get(slot, 0):
            self.eng[q].wait_ge(self.sem[slot], c)
            w[slot] = c
            self.ninst += 1

    def finish(self, q, bufs):
        deps, raw = self._deps(list(bufs), [])
        self._emit_waits(q, None, deps, raw)
        for k, v in self.pending.items():
            assert not v, k


def host_consts():
    c = {}
    c["ident"] = np.eye(128, dtype=np.float32)
    tl = np.arange(128)
    c["utri"] = (tl[:, None] <= tl[None, :]).astype(np.float32)
    c["negm"] = np.where(tl[None, :] >= tl[:, None], 0.0, -30000.0).astype(np.float32)
    blk = (tl[:, None] // 64) == (tl[None, :] // 64)
    c["m64"] = ((tl[None, :] >= tl[:, None]) & blk).astype(np.float32)
    sc = np.ones((128, 512), np.float32)
    sc[:, 0::64] = 0.0
    c["scan0"] = sc
    band = np.zeros((8, 128, 240), np.float32)
    for a in range(8):
        for k in range(16 * a, 16 * a + 16):
            band[a, k, (k % 16) + 112] = 1.0
    c["band"] = band.transpose(1, 0, 2).reshape(128, 8 * 240).copy()
    c["m8"] = ((tl[None, :] // 16) >= (tl[:, None] // 16)).astype(np.float32)
    c["ones"] = np.ones((128, 128), np.float32)
    pm = np.zeros((128, 128), np.float32)
    for m in range(128):
        pm[(m % 8) * 16 + m // 8, m] = 1.0
    c["pm"] = pm
    c["pmT"] = np.ascontiguousarray(pm.T)
    return c


def build(NL, S, L, dbg=None, layers=None):
    assert S == 2 and L % 512 == 0
    NCH = L // T
    nc = bass.Bass("TRN2", target_bir_lowering=False)
    es = ExitStack()

    def din(name, shape, dt=F32):
        return nc.dram_tensor(name, list(shape), dt, kind="ExternalInput").ap()

    x_in = din("x", [S, L, D_MODEL])
    w_tma = din("w_tma", [NL, D_MODEL, N_TMA])
    w_fma = din("w_fma", [NL, D_MODEL, N_FMA])
    w_b = din("w_b", [NL, D_MODEL, N_B])
    w_out = din("w_out", [NL, 2048, D_MODEL])
    prew = din("prew", [NL, 128, 8])
    postw = din("postw", [NL, 1, D_MODEL])
    mixnw = din("mixnw", [NL, 128, 16])
    convw = din("convw", [NL, 128, 48])
    convb_pp = din("convb_pp", [NL, 128, 12])
    convb_row = din("convb_row", [NL, 1, 1536])
    dtb = din("dtb", [NL, 1, 16])
    alog = din("alog", [NL, 1, 16])
    ssdd = din("ssdd", [NL, 1, 16])
    hlb = din("hlb", [128, 4 * NL])
    lamre = din("lamre", [NL, 64, 32])
    lamim = din("lamim", [NL, 64, 32])
    lstep = din("lstep", [NL, 1, 32])
    bre = din("bre", [NL, 64, 512])
    bim = din("bim", [NL, 64, 512])
    cre = din("cre", [NL, 64, 512])
    cim = din("cim", [NL, 64, 512])
    s5d = din("s5d", [NL, 128, 32])
    gluw = din("gluw", [NL, 512, 1024])
    glub = din("glub", [NL, 1, 1024])
    consts = {k: din("c_" + k, v.shape) for k, v in host_consts().items()}

    out = nc.dram_tensor("out", [S, L, D_MODEL], F32, kind="ExternalOutput").ap()
    hT_d = nc.dram_tensor("hT_scr", [S * NCH, 128, 8 * 128], BF16, kind="Internal").ap()
    mixA_d = nc.dram_tensor("mixA_scr", [S, L, 1536], BF16, kind="Internal").ap()
    dbg_out = None
    if False:
        dbg_out = {}

    def sb(name, shape, dt=F32):
        return es.enter_context(nc.sbuf_tensor(name, list(shape), dt))

    ident_b = sb("ident_b", [128, 128], BF16)
    ident_f = sb("ident_f", [128, 128])
    utri = sb("utri", [128, 128])
    ones_f = sb("ones_f", [128, 128])
    ones_b = sb("ones_b", [128, 128], BF16)
    negm_b = sb("negm_b", [128, 128], BF16)
    m64 = sb("m64", [128, 128], BF16)
    m8 = sb("m8", [128, 128])
    pm_b = sb("pm_b", [128, 128], BF16)
    pmT_b = sb("pmT_b", [128, 128], BF16)
    hlb_t = sb("hlb_t", [128, 4, NL])
    lb_all = sb("lb_all", [128, 4, NL])
    neghalf = sb("neghalf", [128, 16])
    st8 = sb("st8", [128, 8])
    prew_t = sb("prew_t", [128, 8])
    mixnw_t = sb("mixnw_t", [128, 16])
    convw_t = sb("convw_t", [128, 48])
    convb_t = sb("convb_t", [128, 12])
    dtb_bc = sb("dtb_bc", [128, 16])
    a_bc = sb("a_bc", [128, 16])
    d_bc = sb("d_bc", [128, 16])
    lb_t = sb("lb_t", [128, 4])
    dt_t = sb("dt_t", [128, 16])
    dtmp = sb("dtmp", [128, 16])
    dtA = sb("dtA", [128, 16])
    acum = sb("acum", [128, 16])
    nacum = sb("nacum", [128, 16])
    eacum = sb("eacum", [128, 16])
    dte = sb("dte", [128, 16])
    cdec = sb("cdec", [128, 16])
    ecend = sb("ecend", [128, 8])
    s5d_t = sb("s5d_t", [128, 32])
    LA = sb("LA", [64, 2, 32])
    LB = sb("LB", [64, 2, 32])

    WREG = 38912
    wreg = sb("wreg", [128, WREG], BF16)
    wtma = wreg[:, 0:8 * N_TMA].rearrange("p (k n) -> p k n", k=8)
    wfma = wreg[:, 8 * N_TMA:8 * (N_TMA + N_FMA)].rearrange("p (k n) -> p k n", k=8)
    o = 0
    wb_v = wreg[:, o:o + 8 * N_B].rearrange("p (k n) -> p k n", k=8); o += 8 * N_B
    wout_v = wreg[:, o:o + 16 * 1024].rearrange("p (k n) -> p k n", k=16); o += 16 * 1024
    glu_v = wreg[:, o:o + 4 * 1024].rearrange("p (k n) -> p k n", k=4); o += 4 * 1024
    wsim_v = wreg[:, o:o + 32 * 64].rearrange("p (g n) -> p g n", g=32); o += 32 * 64
    wore_v = wreg[0:64, o:o + 32 * 128].rearrange("p (g n) -> p g n", g=32); o += 32 * 128
    woim_v = wreg[0:64, o:o + 32 * 128].rearrange("p (g n) -> p g n", g=32); o += 32 * 128
    assert o <= WREG, (o, WREG)
    reg2 = sb("reg2", [128, 48 * 128], BF16)
    dconv = reg2[:].rearrange("p (k j n) -> p k j n", k=4, j=12)
    tz_v = reg2[:, 0:4096].rearrange("p (g n) -> p g n", g=32)
    wsre_v = reg2[:, 4096:6144].rearrange("p (g n) -> p g n", g=32)

    ARENA = 58100
    arena = sb("arena", [128, ARENA], BF16)
    aoff = [0]

    def cv(n, dt=BF16, parts=128):
        ne = n * (2 if dt == F32 else 1)
        assert aoff[0] + ne <= ARENA, (aoff[0], ne, ARENA)
        ap = arena[0:parts, aoff[0]:aoff[0] + ne]
        aoff[0] += ne
        return ap.bitcast(F32) if dt == F32 else ap

    xt = [cv(1024, F32), cv(1024, F32)]
    xn = cv(1024)
    hTq = cv(4096).rearrange("p (k t) -> p k t", k=8)
    zs = cv(1024)
    vtok = cv(512)
    gs = cv(512)
    XT = [cv(12 * 260).rearrange("p (j t) -> p j t", j=12) for _ in range(2)]
    qsq = cv(2048).rearrange("p (h t) -> p h t", h=4)
    ef = cv(512, F32)
    xs = cv(1024, F32)
    btok = cv(256)
    bctq = cv(2048).rearrange("p (j t) -> p j t", j=4)
    xdt = cv(1024)
    xw = cv(1024)
    xsd = cv(1024)
    scT = cv(256).rearrange("p (g t) -> p g t", g=2)
    dec = cv(1024).rearrange("p (r t) -> p r t", r=8)
    MT = cv(1024).rearrange("p (r t) -> p r t", r=8)
    hst = [cv(1024, F32), cv(1024, F32)]
    hbf = [cv(1024), cv(1024)]
    L1 = cv(512, F32)
    L2 = cv(512, F32)
    cum = cv(512, F32)
    ecum = cv(512, F32)
    qTa = cv(512).rearrange("p (h t) -> p h t", h=4)
    qTb = cv(512).rearrange("p (h t) -> p h t", h=4)
    kT = cv(512).rearrange("p (h t) -> p h t", h=4)
    kend = cv(512).rearrange("p (h t) -> p h t", h=4)
    kendT = cv(512).rearrange("p (h t) -> p h t", h=4)
    attn = cv(512).rearrange("p (h t) -> p h t", h=4)
    Sst = [cv(512, F32).rearrange("p (h t) -> p h t", h=4) for _ in range(2)]
    Sbf = [cv(512).rearrange("p (h t) -> p h t", h=4) for _ in range(2)]
    ytmp = cv(512, F32)
    yg = cv(512, F32)
    otmp = cv(512, F32)
    mix = cv(2048)
    mixT = cv(1536).rearrange("p (k t) -> p k t", k=12)
    junk = dec.rearrange("p r t -> p (r t)")
    scan0 = cv(512, F32)
    convb_r = cv(1536, parts=1)
    endA = aoff[0]
    aoff[0] = 0
    xtB = [cv(1024, F32), cv(1024, F32)]
    hT8 = cv(8192).rearrange("p (k t) -> p k t", k=8)
    uT8 = cv(4096).rearrange("p (j t) -> p j t", j=4)
    ublk8 = cv(4096).rearrange("p (g n) -> p g n", g=32)
    gst = cv(2 * 32 * 2 * 65, parts=64).rearrange("p (r g s m) -> p r g s m", r=2, g=32, s=2)
    gyb8 = cv(4096).rearrange("p (g n) -> p g n", g=32)
    gyT8 = cv(4096).rearrange("p (j t) -> p j t", j=4)
    ytB = cv(512, F32)
    ygB = cv(512, F32)
    gatesB = cv(4096).rearrange("p (l n) -> p l n", l=8)
    L1B = cv(512, F32)
    L2B = cv(512, F32)
    mixB = cv(2048)
    mixTB = cv(2048).rearrange("p (k t) -> p k t", k=16)
    junkB = cv(1024)
    postw_bc = cv(1024, F32)
    S2p = [cv(128, F32, parts=64).rearrange("p (r g s) -> p r g s", r=2, g=32) for _ in range(2)]
    rt = cv(128, F32, parts=64).rearrange("p (r g s) -> p r g s", r=2, g=32)
    ru = cv(128, F32, parts=64).rearrange("p (r g s) -> p r g s", r=2, g=32)
    glub_r = cv(1024, parts=1)
    endB = aoff[0]
    aoff[0] = 4096
    g32 = cv(40 * 32, F32, parts=64).rearrange("p (i g) -> p i g", i=40)
    v4 = lambda: cv(2048, F32, parts=64).rearrange("p (g n) -> p g n", g=16)
    Zre, Zim, Yre, Yim, T1, T2 = v4(), v4(), v4(), v4(), v4(), v4()
    Bre_t, Bim_t, Cre_t, Cim_t = (cv(512, F32, parts=64) for _ in range(4))
    assert aoff[0] <= endB

    ps = [es.enter_context(nc.psum_tensor(f"ps{i}", [128, 512], F32)) for i in range(8)]
    psn = [f"ps{i}" for i in range(8)]
    ps_rr = [0]

    def nps():
        i = ps_rr[0] % 8
        ps_rr[0] += 1
        return ps[i], f"{psn[i]}#{ps_rr[0]}"

    sch = Sched(nc)
    blk = es.enter_context(nc.Block())
    op = sch.op

    def act(out_, in_, func, reads, writes, **kw):
        return op("act", lambda e: e.activation(out=out_, in_=in_, func=func, **kw), reads, writes)

    def tt(eng, out_, a, b, o_, reads, writes):
        return op(eng, lambda e: e.tensor_tensor(out=out_, in0=a, in1=b, op=o_), reads, writes)

    def ts(eng, out_, a, s1, s2, o0, o1, reads, writes):
        return op(eng, lambda e: e.tensor_scalar(out=out_, in0=a, scalar1=s1, scalar2=s2, op0=o0, op1=o1), reads, writes)

    def cp(eng, out_, in_, reads, writes):
        if eng == "act":
            return act(out_, in_, AF.Copy, reads, writes)
        return op(eng, lambda e: e.tensor_copy(out=out_, in_=in_), reads, writes)

    def mm(out_, lhsT, rhs, start, stop, reads, writes, inc=None):
        return op("pe", lambda e: e.matmul(out_, lhsT, rhs, start=start, stop=stop), reads, writes,
                  inc=(1 if stop else 0) if inc is None else inc)

    def rsqrt(out_, in_, scale, reads, writes, n):
        ts("dve", out_, in_, scale, EPS, ALU.mult, ALU.add, reads, writes)
        op("pool", lambda e: e.tensor_tensor(out=out_, in0=out_, in1=neghalf[:, 0:n], op=ALU.pow), list(writes) + ["neghalf"], writes)

    def ld(q, out_, in_, w):
        sch.dma(q, "d_" + w[0], out_, in_, (), w)


    HALF = ARENA // 4
    stg = [arena[:, 0:2 * HALF].bitcast(F32), arena[:, 2 * HALF:4 * HALF].bitcast(F32)]
    eng_rr = [0]

    def stage_load(items):
        rounds, cur, off = [], [], 0
        for it in items:
            n = it[0].shape[-1]
            if off + n > HALF:
                rounds.append(cur)
                cur, off = [], 0
            cur.append((it, off, n))
            off += n
        rounds.append(cur)
        for ri, rnd in enumerate(rounds):
            b = ri % 2
            for ii, (it, off, n) in enumerate(rnd):
                q = ("sp", "act")[ii % 2]
                sch.dma(q, f"stg{b}_{q}", stg[b][:, off:off + n], it[1], (), [f"stg{b}_{q}"])
            for (it, off, n) in rnd:
                e = ("dve", "act")[eng_rr[0] % 2]
                eng_rr[0] += 1
                rd = [f"stg{b}_sp", f"stg{b}_act"] + ([it[4]] if len(it) > 4 else [])
                src = stg[b][:, off:off + n]
                if e == "act":
                    act(it[0], src, AF.Copy, rd, [it[3]], scale=it[2])
                else:
                    ts(e, it[0], src, it[2], None, ALU.mult, ALU.bypass, rd, [it[3]])

    ld("sp", ident_f[:], consts["ident"], ["ident_f"])
    ld("sp", utri[:], consts["utri"], ["utri"])
    ld("sp", ones_f[:], consts["ones"], ["ones_f"])
    ld("sp", m8[:], consts["m8"], ["m8"])
    ld("sp", hlb_t[:].rearrange("p h l -> p (h l)"), hlb, ["hlb_t"])
    ld("pool", ident_b[:], consts["ident"], ["ident_b"])
    ld("pool", ones_b[:], consts["ones"], ["ones_b"])
    ld("pool", negm_b[:], consts["negm"], ["negm_b"])
    ld("pool", m64[:], consts["m64"], ["m64"])
    ld("pool", pm_b[:], consts["pm"], ["pm_b"])
    ld("pool", pmT_b[:], consts["pmT"], ["pmT_b"])
    op("dve", lambda e: e.memset(neghalf[:], -0.5), (), ["neghalf"])
    act(hlb_t[:], hlb_t[:], AF.Exp, ["hlb_t"], ["hlb_t"])
    op("dve", lambda e: e.tensor_reduce(out=st8[:, 0:4], in_=hlb_t[:], axis=AX.X, op=ALU.add), ["hlb_t"], ["st8"])
    op("dve", lambda e: e.reciprocal(out=st8[:, 0:4], in_=st8[:, 0:4]), ["st8"], ["st8"])
    tt("dve", hlb_t[:], hlb_t[:], st8[:, 0:4].unsqueeze(2).broadcast_to([128, 4, NL]), ALU.mult, ["hlb_t", "st8"], ["hlb_t"])
    op("dve", lambda e: e.memset(lb_all[:], 0.0), (), ["lb_all"])
    for l in range(1, NL):
        tt("dve", lb_all[:, :, l], lb_all[:, :, l - 1], hlb_t[:, :, l], ALU.add, ["lb_all", "hlb_t"], ["lb_all"])

    NQ = L // 256
    NSC = L // 512
    layers = list(range(NL)) if layers is None else list(layers)

    def gci(s, c):
        return s * NCH + c

    for layer in layers:
        xsrc = x_in if layer == layers[0] else out
        xrd = ["out_d"] if layer != layers[0] else []
        sch.barrier()
        ld("sp", prew_t[:], prew[layer], ["prew_t"])
        ld("sp", convw_t[:], convw[layer], ["convw_t"])
        ld("sp", convb_t[:], convb_pp[layer], ["convb_t"])
        ld("sp", dtb_bc[:], dtb[layer].partition_broadcast(128), ["dtb_bc"])
        ld("sp", a_bc[:], alog[layer].partition_broadcast(128), ["a_bc"])
        ld("sp", d_bc[:], ssdd[layer].partition_broadcast(128), ["d_bc"])
        act(a_bc[:], a_bc[:], AF.Exp, ["a_bc"], ["a_bc"])
        ts("dve", a_bc[:], a_bc[:], -1.0, None, ALU.mult, ALU.bypass, ["a_bc"], ["a_bc"])
        cp("dve", lb_t[:], lb_all[:, :, layer], ["lb_all"], ["lb_t"])
        itemsA = []
        for k in range(8):
            itemsA.append((wtma[:, k, :], w_tma[layer, k * 128:(k + 1) * 128, :], prew_t[:, k:k + 1], "wtma", "prew_t"))
            itemsA.append((wfma[:, k, :], w_fma[layer, k * 128:(k + 1) * 128, :], prew_t[:, k:k + 1], "wfma", "prew_t"))
        stage_load(itemsA)
        sch.barrier()
        ld("sp", scan0, consts["scan0"], ["scan0"])
        ld("pool", convb_r, convb_row[layer], ["convb_r"])
        for k in range(4):
            for j in range(12):
                ts("dve", dconv[:, k, j, :], ident_b[:], convw_t[:, k * 12 + j:k * 12 + j + 1], None, ALU.mult, ALU.bypass,
                   ["ident_b", "convw_t"], ["dconv"])
        op("dve", lambda e: e.memset(qTa, 0.0), (), ["qTa"])
        op("dve", lambda e: e.memset(qTb, 0.0), (), ["qTb"])
        for s in range(2):
            op("dve", lambda e: e.memset(XT[s][:, :, 0:3], 0.0), (), [f"XT{s}"])
            op("dve", lambda e: e.memset(hst[s], 0.0), (), [f"hst{s}"])
            op("pool", lambda e: e.memset(hbf[s], 0.0), (), [f"hbf{s}"])
            op("dve", lambda e: e.memset(Sst[s], 0.0), (), [f"Sst{s}"])
            op("pool", lambda e: e.memset(Sbf[s], 0.0), (), [f"Sbf{s}"])

        xcnt = [0]
        allch = [(s_, 2 * qd_ + pp_) for qd_ in range(NQ) for s_ in range(2) for pp_ in range(2)]
        if dbg and dbg.get("stop") == "A0":
            break
        for qd in range(NQ):
            quad = [(s, 2 * qd + pp) for s in range(2) for pp in range(2)]
            for qi, (s, c) in enumerate(quad):
                ia = xcnt[0]
                xcnt[0] += 1
                xb, xbn = xt[ia % 2], f"xt{ia % 2}"

                def _xload(i_):
                    s_, c_ = allch[i_]
                    sch.dma("sp", f"xxt{i_ % 2}", xt[i_ % 2], xsrc[s_, c_ * T:(c_ + 1) * T, :], xrd, [f"xt{i_ % 2}"])
                if ia == 0:
                    _xload(0)
                if ia + 1 < len(allch):
                    _xload(ia + 1)
                act(xn, xb, AF.Square, [xbn], ["xn", "st8"], accum_out=st8[:, 0:1])
                rsqrt(st8[:, 1:2], st8[:, 0:1], 1.0 / D_MODEL, ["st8"], ["st8"], 1)
                ts("dve", xn, xb, st8[:, 1:2], None, ALU.mult, ALU.bypass, [xbn, "st8"], ["xn"])
                p0, p0n = nps()
                p0b = p0[:].bitcast(BF16)
                for k in range(8):
                    op("pe", lambda e: e.transpose(p0b[:, k * 128:(k + 1) * 128], xn[:, k * 128:(k + 1) * 128], ident_b[:]),
                       ["xn", "ident_b"], [p0n], inc=1 if k == 7 else 0)
                cp("act", hTq[:, :, qi * 128:(qi + 1) * 128], p0b.rearrange("p (k t) -> p k t", k=8), [p0n], ["hTq"])
                sch.dma("sp", "hts", hT_d[gci(s, c)].rearrange("p (k t) -> p k t", k=8), hTq[:, :, qi * 128:(qi + 1) * 128], ["hTq"], ["hT_d"])
            if dbg and dbg.get("stop") == "A1":
                break
            njc = int(dbg.get("njc", 16)) if dbg else 16
            for jc in range(njc):
                pf, pfn = nps()
                for k in range(8):
                    mm(pf[:], wfma[:, k, jc * 128:(jc + 1) * 128], hTq[:, k, :], k == 0, k == 7, ["hTq", "wfma"], [pfn])
                if dbg and dbg.get("evac") == "junk":
                    cp("act", xn[:, 0:512], pf[:], [pfn], ["xn"])
                elif jc < 12:
                    ev = dbg.get("evac", "same") if dbg else "same"
                    for s in range(2):
                        if ev == "act_only" and s == 1:
                            continue
                        if ev == "dve_only" and s == 0:
                            continue
                        c0_ = 4 if ev == "even" else 3
                        eng_ = "act" if s == 0 else "dve"
                        if ev == "swap":
                            eng_ = "dve" if s == 0 else "act"
                        if ev == "same":
                            eng_ = "act" if jc % 2 == 0 else "dve"
                        cp(eng_, XT[s][:, jc, c0_:c0_ + 256], pf[:, s * 256:(s + 1) * 256], [pfn], [f"XT{s}"])
                else:
                    act(qsq[:, jc - 12, :], pf[:], AF.Silu, [pfn], ["qsq"])
            if dbg and dbg.get("stop") == "A15":
                break
            for s in range(2):
                for half in range(2):
                    pb, pbn = nps()
                    for j2 in range(2):
                        jj = half * 2 + j2
                        j = 8 + jj
                        for k in range(4):
                            mm(pb[:, j2 * 256:(j2 + 1) * 256], dconv[:, k, j, :], XT[s][:, j, k:k + 256], k == 0, k == 3, [f"XT{s}", "dconv"], [pbn],
                               inc=1 if (k == 3 and j2 == 1) else 0)
                    for j2 in range(2):
                        jj = half * 2 + j2
                        act(bctq[:, jj, s * 256:(s + 1) * 256], pb[:, j2 * 256:(j2 + 1) * 256], AF.Silu, [pbn, "convb_t"], ["bctq"],
                            bias=convb_t[:, 8 + jj:9 + jj])
            if dbg and dbg.get("stop") == "A2":
                break
            for qi, (s, c) in enumerate(quad):
                pp_ = c % 2
                tsl = slice(qi * 128, (qi + 1) * 128)
                XTs, XTn = XT[s], f"XT{s}"
                hs, hsn, hb, hbn = hst[s], f"hst{s}", hbf[s], f"hbf{s}"
                Ss, Ssn, Sb, Sbn = Sst[s], f"Sst{s}", Sbf[s], f"Sbf{s}"

                def tm_slab(c0, n):
                    pz, pzn = nps()
                    for k in range(8):
                        mm(pz[:, 0:n], hTq[:, k, tsl], wtma[:, k, c0:c0 + n], k == 0, k == 7, ["hTq", "wtma"], [pzn])
                    return pz, pzn
                for h2 in range(2):
                    pz, pzn = tm_slab(h2 * 512, 512)
                    act(zs[:, h2 * 512:(h2 + 1) * 512], pz[:], AF.Silu, [pzn], ["zs"])
                pz, pzn = tm_slab(1040 + 512, 512)
                act(gs, pz[:], AF.Silu, [pzn], ["gs"])
                pz, pzn = tm_slab(1040, 512)
                cp("act", vtok, pz[:], [pzn], ["vtok"])
                pd, pdn = tm_slab(1024, 16)
                tt("dve", dtmp[:], pd[:, 0:16], dtb_bc[:], ALU.add, [pdn, "dtb_bc"], ["dtmp"])
                pfq, pfqn = nps()
                for jj in range(4):
                    for k in range(8):
                        mm(pfq[:, jj * 128:(jj + 1) * 128], wfma[:, k, (16 + jj) * 128:(17 + jj) * 128], hTq[:, k, tsl], k == 0, k == 7,
                           ["hTq", "wfma"], [pfqn], inc=1 if (k == 7 and jj == 3) else 0)
                pc = [nps() for _ in range(3)]
                w0 = pp_ * 128
                for j in range(10):
                    pcj, pcjn = pc[j // 4]
                    o_ = pcj[:, (j % 4) * 128:(j % 4 + 1) * 128]
                    mm(o_, ones_b[0:1, :], convb_r[0:1, j * 128:(j + 1) * 128], True, False, ["ones_b", "convb_r"], [pcjn])
                    for k in range(4):
                        last = (k == 3)
                        mm(o_, XTs[:, j, w0 + k:w0 + k + 128], dconv[:, k, j, :], False, last, [XTn, "dconv"], [pcjn],
                           inc=1 if (last and (j % 4 == 3 or j == 9)) else 0)
                act(xs[:, 0:512], pc[0][0][:], AF.Silu, [pc[0][1]], ["xs"])
                act(xs[:, 512:1024], pc[1][0][:], AF.Silu, [pc[1][1]], ["xs"])
                act(btok, pc[2][0][:, 0:256], AF.Silu, [pc[2][1]], ["btok"])
                act(dtmp[:], dtmp[:], AF.Exp, ["dtmp"], ["dtmp"])
                act(dt_t[:], dtmp[:], AF.Ln, ["dtmp"], ["dt_t"], bias=1.0)
                act(ef, pfq[:], AF.Exp, [pfqn], ["ef"], scale=-1.0)
                tt("dve", dtA[:], dt_t[:], a_bc[:], ALU.mult, ["dt_t", "a_bc"], ["dtA"])
                pa, pan = nps()
                mm(pa[:, 0:16], utri[:], dtA[:], True, True, ["utri", "dtA"], [pan], inc=0)
                mm(pa[:, 16:32], ones_f[:], dtA[:], True, True, ["ones_f", "dtA"], [pan])
                cp("dve", acum[:], pa[:, 0:16], [pan], ["acum"])
                ts("dve", nacum[:], pa[:, 0:16], -1.0, None, ALU.mult, ALU.bypass, [pan], ["nacum"])
                act(eacum[:], pa[:, 0:16], AF.Exp, [pan], ["eacum"])
                act(cdec[:], pa[:, 16:32], AF.Exp, [pan], ["cdec"])
                tt("dve", dte[:], pa[:, 16:32], acum[:], ALU.subtract, [pan, "acum"], ["dte"])
                act(dte[:], dte[:], AF.Exp, ["dte"], ["dte"])
                tt("dve", dte[:], dte[:], dt_t[:], ALU.mult, ["dte", "dt_t"], ["dte"])
                xs3 = xs.rearrange("p (r q) -> p r q", r=16)
                tt("dve", xdt.rearrange("p (r q) -> p r q", r=16), xs3, dt_t[:].unsqueeze(2).broadcast_to([128, 16, 64]), ALU.mult,
                   ["xs", "dt_t"], ["xdt"])
                tt("dve", xw.rearrange("p (r q) -> p r q", r=16), xs3, dte[:].unsqueeze(2).broadcast_to([128, 16, 64]), ALU.mult,
                   ["xs", "dte"], ["xw"])
                tt("pool", xsd.rearrange("p (r q) -> p r q", r=16), xs3, d_bc[:].unsqueeze(2).broadcast_to([128, 16, 64]), ALU.mult,
                   ["xs", "d_bc"], ["xsd"])
                for g in range(2):
                    psc, pscn = nps()
                    mm(psc[:, 0:128], bctq[:, g, tsl], bctq[:, 2 + g, tsl], True, True, ["bctq"], [pscn])
                    cp("dve", scT[:, g, :], psc[:, 0:128], [pscn], ["scT"])
                    pab = [nps(), nps()]
                    for r in range(8):
                        pq, pqn = pab[r // 4]
                        o_ = pq[:, (r % 4) * 128:(r % 4 + 1) * 128]
                        hh = g * 8 + r
                        mm(o_, dtA[:, hh:hh + 1].broadcast_to([128, 128]), utri[:], True, False, ["dtA", "utri"], [pqn])
                        mm(o_, ident_b[:], negm_b[:], False, True, ["ident_b", "negm_b"], [pqn], inc=1 if r % 4 == 3 else 0)
                    for r in range(8):
                        pq, pqn = pab[r // 4]
                        hh = g * 8 + r
                        act(dec[:, r, :], pq[:, (r % 4) * 128:(r % 4 + 1) * 128], AF.Exp, [pqn, "nacum"], ["dec"], bias=nacum[:, hh:hh + 1])
                    tt("dve", MT, dec, scT[:, g:g + 1, :].broadcast_to([128, 8, 128]), ALU.mult, ["dec", "scT"], ["MT"])
                    py, pyn = nps()
                    mm(py[:], ident_b[:], xsd[:, g * 512:(g + 1) * 512], True, False, ["ident_b", "xsd"], [pyn])
                    for r in range(8):
                        hh = g * 8 + r
                        mm(py[:, r * 64:(r + 1) * 64], MT[:, r, :], xdt[:, hh * 64:(hh + 1) * 64], False, r == 7, ["MT", "xdt"], [pyn])
                    po, pon = nps()
                    mm(po[:], bctq[:, 2 + g, tsl], hb[:, g * 512:(g + 1) * 512], True, True, ["bctq", hbn], [pon])
                    tt("dve", ytmp.rearrange("p (r q) -> p r q", r=8), po[:].rearrange("p (r q) -> p r q", r=8),
                       eacum[:, g * 8:(g + 1) * 8].unsqueeze(2).broadcast_to([128, 8, 64]), ALU.mult, [pon, "eacum"], ["ytmp"])
                    tt("dve", ytmp, ytmp, py[:], ALU.add, ["ytmp", pyn], ["ytmp"])
                    tt("dve", yg, ytmp, zs[:, g * 512:(g + 1) * 512], ALU.mult, ["ytmp", "zs"], ["yg"])
                    act(junk[:, 0:512], yg, AF.Square, ["yg"], ["dec", "st8"], accum_out=st8[:, 2:3])
                    rsqrt(st8[:, 3:4], st8[:, 2:3], 1.0 / 512, ["st8"], ["st8"], 1)
                    ts("dve", mix[:, g * 512:(g + 1) * 512], yg, st8[:, 3:4], None, ALU.mult, ALU.bypass, ["yg", "st8"], ["mix"])
                    ph, phn = nps()
                    mm(ph[:], btok[:, g * 128:(g + 1) * 128], xw[:, g * 512:(g + 1) * 512], True, True, ["btok", "xw"], [phn])
                    hv = hs[:, g * 512:(g + 1) * 512]
                    tt("dve", hv.rearrange("p (r q) -> p r q", r=8), hv.rearrange("p (r q) -> p r q", r=8),
                       cdec[:, g * 8:(g + 1) * 8].unsqueeze(2).broadcast_to([128, 8, 64]), ALU.mult, [hsn, "cdec"], [hsn])
                    tt("dve", hv, hv, ph[:], ALU.add, [hsn, phn], [hsn])
                    cp("pool", hb[:, g * 512:(g + 1) * 512], hv, [hsn], [hbn])
                act(L2, ef, AF.Ln, ["ef"], ["L2"], bias=1.0)
                for h in range(4):
                    act(L1[:, h * 128:(h + 1) * 128], ef[:, h * 128:(h + 1) * 128], AF.Ln, ["ef", "lb_t"], ["L1"], bias=1.0, scale=lb_t[:, h:h + 1])
                tt("dve", L1, L1, L2, ALU.subtract, ["L1", "L2"], ["L1"])
                act(L2, L1, AF.Exp, ["L1"], ["L2"])
                ts("dve", L2, L2, -1.0, 1.0, ALU.mult, ALU.add, ["L2"], ["L2"])
                op("dve", lambda e: e.tensor_tensor_scan(out=cum, data0=scan0, data1=L1, initial=0.0, op0=ALU.mult, op1=ALU.add),
                   ["scan0", "L1"], ["cum"])
                cum4 = cum.rearrange("p (j t) -> p j t", j=8)
                act(ecend[:], cum4[:, :, 63], AF.Exp, ["cum"], ["ecend"])
                act(ecum, cum, AF.Exp, ["cum"], ["ecum"])
                q4 = qsq[:, :, tsl]
                e4 = ecum.rearrange("p (h t) -> p h t", h=4)
                tt("dve", qTa[:, :, 0:64], q4[:, :, 0:64], e4[:, :, 0:64], ALU.mult, ["qsq", "ecum"], ["qTa"])
                tt("dve", qTb[:, :, 64:128], q4[:, :, 64:128], e4[:, :, 64:128], ALU.mult, ["qsq", "ecum"], ["qTb"])
                act(ecum, cum, AF.Exp, ["cum"], ["ecum"], scale=-1.0)
                tt("dve", ecum, L2, ecum, ALU.mult, ["L2", "ecum"], ["ecum"])
                cp("pool", kT.rearrange("p h t -> p (h t)"), ecum, ["ecum"], ["kT"])
                tt("dve", kend.rearrange("p h (b t) -> p (h b) t", b=2), ecum.rearrange("p (j t) -> p j t", j=8),
                   ecend[:].unsqueeze(2).broadcast_to([128, 8, 64]), ALU.mult, ["ecum", "ecend"], ["kend"])
                pk, pkn = nps()
                pkb = pk[:].bitcast(BF16)
                for h in range(4):
                    op("pe", lambda e: e.transpose(pkb[:, h * 128:(h + 1) * 128], kend[:, h, :], ident_b[:]), ["kend", "ident_b"], [pkn],
                       inc=1 if h == 3 else 0)
                cp("act", kendT.rearrange("p h t -> p (h t)"), pkb[:, 0:512], [pkn], ["kendT"])
                pat, patn = nps()
                for h in range(4):
                    mm(pat[:, h * 128:(h + 1) * 128], kT[:, h, :], qTa[:, h, :], True, False, ["kT", "qTa"], [patn])
                    mm(pat[:, h * 128:(h + 1) * 128], kT[:, h, :], qTb[:, h, :], False, True, ["kT", "qTb"], [patn], inc=1 if h == 3 else 0)
                tt("dve", attn, pat[:].rearrange("p (h t) -> p h t", h=4), m64[:].unsqueeze(1).broadcast_to([128, 4, 128]), ALU.mult,
                   [patn, "m64"], ["attn"])
                pho, phon = nps()
                for h in range(4):
                    o_ = pho[:, h * 128:(h + 1) * 128]
                    mm(o_, attn[:, h, :], vtok[:, h * 128:(h + 1) * 128], h == 0, False, ["attn", "vtok"], [phon])
                    mm(o_, qTa[:, h, :], Sb[:, h, :], False, False, ["qTa", Sbn], [phon])
                for b2 in range(2):
                    pst, pstn = nps()
                    for h in range(4):
                        mm(pst[:, h * 128:(h + 1) * 128], kendT[b2 * 64:(b2 + 1) * 64, h, :], vtok[b2 * 64:(b2 + 1) * 64, h * 128:(h + 1) * 128],
                           True, True, ["kendT", "vtok"], [pstn], inc=1 if h == 3 else 0)
                    ec = ecend[:].rearrange("p (h b) -> p h b", b=2)[:, :, b2:b2 + 1].broadcast_to([128, 4, 128])
                    tt("dve", Ss, Ss, ec, ALU.mult, [Ssn, "ecend"], [Ssn])
                    tt("dve", Ss.rearrange("p h v -> p (h v)"), Ss.rearrange("p h v -> p (h v)"), pst[:], ALU.add, [Ssn, pstn], [Ssn])
                    cp("pool", Sb, Ss, [Ssn], [Sbn])
                    if b2 == 0:
                        for h in range(4):
                            mm(pho[:, h * 128:(h + 1) * 128], qTb[:, h, :], Sb[:, h, :], False, h == 3, ["qTb", Sbn], [phon], inc=1 if h == 3 else 0)
                for h in range(4):
                    act(junk[:, 0:128], pho[:, h * 128:(h + 1) * 128], AF.Square, [phon], ["dec", "st8"], accum_out=st8[:, 4 + h:5 + h])
                rsqrt(st8[:, 4:8], st8[:, 4:8], 1.0 / 128, ["st8"], ["st8"], 4)
                tt("dve", otmp.rearrange("p (h v) -> p h v", h=4), pho[:].rearrange("p (h v) -> p h v", h=4),
                   st8[:, 4:8].unsqueeze(2).broadcast_to([128, 4, 128]), ALU.mult, [phon, "st8"], ["otmp"])
                tt("dve", mix[:, 1024:1536], otmp, gs, ALU.mult, ["otmp", "gs"], ["mix"])
                sch.dma("sp", "mts", mixA_d[s, c * T:(c + 1) * T, :], mix[:, 0:1536], ["mix"], ["mixA_d"])
            for s in range(2):
                cp("pool", XT[s][:, :, 0:3], XT[s][:, :, 256:259], [f"XT{s}"], [f"XT{s}"])

        if dbg and dbg.get("stop") in ("A", "A1", "A2", "A15"):
            break
        sch.barrier()
        ld("sp", mixnw_t[:], mixnw[layer], ["mixnw_t"])
        itemsB = []
        for k in range(8):
            itemsB.append((wb_v[:, k, :], w_b[layer, k * 128:(k + 1) * 128, :], prew_t[:, k:k + 1], "wb", "prew_t"))
        for k in range(16):
            itemsB.append((wout_v[:, k, :], w_out[layer, k * 128:(k + 1) * 128, :], mixnw_t[:, k:k + 1], "wout", "mixnw_t"))
        for k in range(4):
            itemsB.append((glu_v[:, k, :], gluw[layer, k * 128:(k + 1) * 128, :], 0.5, "glu"))
        stage_load(itemsB)
        sch.barrier()
        ld("sp", postw_bc, postw[layer].partition_broadcast(128), ["postw_bc"])
        ld("sp", s5d_t[:], s5d[layer], ["s5d_t"])
        ld("pool", glub_r, glub[layer], ["glub_r"])
        ld("sp", g32[:, 0, :], lamre[layer], ["g_lr"])
        ld("sp", g32[:, 1, :], lamim[layer], ["g_li"])
        ld("sp", g32[:, 2, :], lstep[layer].partition_broadcast(64), ["g_st"])
        ld("sp", Bre_t, bre[layer], ["Bre"])
        ld("sp", Bim_t, bim[layer], ["Bim"])
        ld("sp", Cre_t, cre[layer], ["Cre"])
        ld("sp", Cim_t, cim[layer], ["Cim"])
        GN = ["g32"]

        def gq(i):
            return g32[:, i, :]

        def gmul(o_, a, b):
            tt("dve", gq(o_), gq(a), gq(b), ALU.mult, GN + ["g_lr", "g_li", "g_st"], GN)

        def gadd(o_, a, b, o2=ALU.add):
            tt("dve", gq(o_), gq(a), gq(b), o2, GN, GN)

        def gts(o_, a, m_, a_):
            ts("dve", gq(o_), gq(a), m_, a_, ALU.mult, ALU.add, GN + ["g_lr", "g_li", "g_st"], GN)

        I_LR, I_LI, I_ST, I_X, I_ANG, I_MAG, I_MAGI, I_C, I_S, I_T1, I_T2, I_LRE, I_LIM, I_IRE, I_IIM, I_CRE, I_CIM, I_8RE, I_8IM, I_Y, I_P, I_NX, I_A16, I_DEN = range(24)
        act(gq(I_ST), gq(I_ST), AF.Exp, ["g_st"], GN + ["g_st"])
        ts("dve", gq(I_LR), gq(I_LR), -1e-4, None, ALU.min, ALU.bypass, ["g_lr"], GN + ["g_lr"])
        gmul(I_X, I_LR, I_ST)
        gmul(I_ANG, I_LI, I_ST)
        gts(I_NX, I_X, -1.0, 0.0)

        def expser(o_, xi):
            gts(o_, xi, 1.0 / 6, 1.0)
            for kf in (5, 4, 3, 2, 1):
                gmul(o_, o_, xi)
                gts(o_, o_, 1.0 / kf, 1.0)
        expser(I_MAG, I_X)
        expser(I_MAGI, I_NX)
        gts(I_A16, I_ANG, 1.0 / 16, 0.0)
        gmul(I_Y, I_A16, I_A16)
        sc_ = [1.0, -1.0 / 6, 1.0 / 120, -1.0 / 5040, 1.0 / 362880, -1.0 / 39916800, 1.0 / 6227020800]
        cc_ = [1.0, -0.5, 1.0 / 24, -1.0 / 720, 1.0 / 40320, -1.0 / 3628800, 1.0 / 479001600, -1.0 / 87178291200]
        gts(I_P, I_Y, sc_[6], sc_[5])
        for kf in (4, 3, 2, 1, 0):
            gmul(I_P, I_P, I_Y)
            gts(I_P, I_P, 1.0, sc_[kf])
        gmul(I_S, I_P, I_A16)
        gts(I_P, I_Y, cc_[7], cc_[6])
        for kf in (5, 4, 3, 2, 1, 0):
            gmul(I_P, I_P, I_Y)
            gts(I_P, I_P, 1.0, cc_[kf])
        gts(I_C, I_P, 1.0, 0.0)

        def csq(re, im):
            gmul(I_T1, re, re)
            gmul(I_T2, im, im)
            gmul(im, re, im)
            gts(im, im, 2.0, 0.0)
            gadd(re, I_T1, I_T2, ALU.subtract)
        for _ in range(4):
            csq(I_C, I_S)
        gmul(I_LRE, I_MAG, I_C)
        gmul(I_LIM, I_MAG, I_S)
        gmul(I_IRE, I_MAGI, I_C)
        gmul(I_IIM, I_MAGI, I_S)
        gts(I_IIM, I_IIM, -1.0, 0.0)
        gts(I_P, I_LRE, 1.0, -1.0)
        gmul(I_T1, I_LR, I_LR)
        gmul(I_T2, I_LI, I_LI)
        gadd(I_DEN, I_T1, I_T2)
        op("dve", lambda e: e.reciprocal(out=gq(I_DEN), in_=gq(I_DEN)), GN, GN)
        gmul(I_T1, I_P, I_LR)
        gmul(I_T2, I_LIM, I_LI)
        gadd(I_CRE, I_T1, I_T2)
        gmul(I_CRE, I_CRE, I_DEN)
        gmul(I_T1, I_LIM, I_LR)
        gmul(I_T2, I_P, I_LI)
        gadd(I_CIM, I_T1, I_T2, ALU.subtract)
        gmul(I_CIM, I_CIM, I_DEN)
        gts(I_8RE, I_LRE, 1.0, 0.0)
        gts(I_8IM, I_LIM, 1.0, 0.0)
        for _ in range(3):
            csq(I_8RE, I_8IM)
        cp("dve", LA[:, 0, :], gq(I_8RE), GN, ["LA"])
        cp("dve", LA[:, 1, :], gq(I_8RE), GN, ["LA"])
        ts("dve", LB[:, 0, :], gq(I_8IM), -1.0, None, ALU.mult, ALU.bypass, GN, ["LB"])
        cp("dve", LB[:, 1, :], gq(I_8IM), GN, ["LB"])

        def cmul(eng, ore, oim, are, aim, xre, xim, n, rd, wr):
            ab = lambda a_: a_.unsqueeze(2).broadcast_to([64, 16, n])
            tv1, tv2 = T1[:, :, 0:n], T2[:, :, 0:n]
            tt(eng, tv1, xre, ab(are), ALU.mult, rd + GN, ["T1"])
            tt(eng, tv2, xim, ab(aim), ALU.mult, rd + GN, ["T2"])
            tt(eng, ore, tv1, tv2, ALU.subtract, ["T1", "T2"], wr)
            tt(eng, tv1, xim, ab(are), ALU.mult, rd + GN, ["T1"])
            tt(eng, tv2, xre, ab(aim), ALU.mult, rd + GN, ["T2"])
            tt(eng, oim, tv1, tv2, ALU.add, ["T1", "T2"], wr)

        for gb in range(2):
            g0 = gb * 16
            sl = slice(g0, g0 + 16)
            Z4r = Zre.rearrange("p g (s h) -> p g s h", s=8)
            Z4i = Zim.rearrange("p g (s h) -> p g s h", s=8)
            Y4r = Yre.rearrange("p g (s h) -> p g s h", s=8)
            Y4i = Yim.rearrange("p g (s h) -> p g s h", s=8)
            Bv = lambda tile_: tile_[0:64, g0 * 16:(g0 + 16) * 16].rearrange("p (g h) -> p g h", g=16)
            cmul("dve", Z4r[:, :, 7, :], Z4i[:, :, 7, :], gq(I_CRE)[:, sl], gq(I_CIM)[:, sl], Bv(Bre_t), Bv(Bim_t), 16, ["Bre", "Bim"], ["Zre", "Zim"])
            for s8 in range(6, -1, -1):
                cmul("dve", Z4r[:, :, s8, :], Z4i[:, :, s8, :], gq(I_LRE)[:, sl], gq(I_LIM)[:, sl], Z4r[:, :, s8 + 1, :], Z4i[:, :, s8 + 1, :], 16,
                     ["Zre", "Zim"], ["Zre", "Zim"])
            cp("dve", Y4r[:, :, 7, :], Bv(Cre_t), ["Cre"], ["Yre"])
            cp("dve", Y4i[:, :, 7, :], Bv(Cim_t), ["Cim"], ["Yim"])
            for l8 in range(6, -1, -1):
                cmul("dve", Y4r[:, :, l8, :], Y4i[:, :, l8, :], gq(I_IRE)[:, sl], gq(I_IIM)[:, sl], Y4r[:, :, l8 + 1, :], Y4i[:, :, l8 + 1, :], 16,
                     ["Yre", "Yim"], ["Yre", "Yim"])
            ts("dve", T1, Yim, -1.0, None, ALU.mult, ALU.bypass, ["Yim"], ["T1"])
            for hb_ in range(4):
                pt_, ptn = nps()
                for gi in range(4):
                    gg = hb_ * 4 + gi
                    mm(pt_[:, gi * 128:(gi + 1) * 128], Zre[:, gg, :], Yre[:, gg, :], True, False, ["Zre", "Yre"], [ptn])
                    mm(pt_[:, gi * 128:(gi + 1) * 128], Zim[:, gg, :], T1[:, gg, :], False, True, ["Zim", "T1"], [ptn], inc=1 if gi == 3 else 0)
                tt("dve", tz_v[:, g0 + hb_ * 4:g0 + hb_ * 4 + 4, :], pt_[:].rearrange("p (g n) -> p g n", g=4),
                   m8[:].unsqueeze(1).broadcast_to([128, 4, 128]), ALU.mult, [ptn, "m8"], ["tz"])
            for (Zt, Zn, dst, dn) in ((Zre, "Zre", wsre_v, "wsre"), (Zim, "Zim", wsim_v, "wsim")):
                for half_ in range(2):
                    pw_, pwn = nps()
                    for gi in range(8):
                        mm(pw_[:, gi * 64:(gi + 1) * 64], Zt[:, half_ * 8 + gi, :], ident_f[0:64, 0:64], True, True, [Zn, "ident_f"], [pwn],
                           inc=1 if gi == 7 else 0)
                    cp("act", dst[:, g0 + half_ * 8:g0 + half_ * 8 + 8, :], pw_[:].rearrange("p (g n) -> p g n", g=8), [pwn], [dn])
            a8 = lambda i: gq(i)[:, sl].unsqueeze(2).broadcast_to([64, 16, 128])
            tt("dve", T1, Yre, a8(I_8RE), ALU.mult, ["Yre"] + GN, ["T1"])
            tt("dve", T2, Yim, a8(I_8IM), ALU.mult, ["Yim"] + GN, ["T2"])
            tt("dve", wore_v[:, sl, :], T1, T2, ALU.subtract, ["T1", "T2"], ["wore"])
            tt("dve", T1, Yim, a8(I_8RE), ALU.mult, ["Yim"] + GN, ["T1"])
            tt("dve", T2, Yre, a8(I_8IM), ALU.mult, ["Yre"] + GN, ["T2"])
            tt("dve", T1, T1, T2, ALU.add, ["T1", "T2"], ["T1"])
            ts("dve", woim_v[:, sl, :], T1, -1.0, None, ALU.mult, ALU.bypass, ["T1"], ["woim"])
        op("dve", lambda e: e.memset(S2p[0], 0.0), (), ["S2_0"])
        sch.barrier()
        op("pool", lambda e: e.memset(gst[:, :, :, :, 0], 0.0), (), ["gsts", "gstw"])

        if dbg and dbg.get("stop") == "B0":
            break
        def f_loads(sc):
            tiles = [(s, 4 * sc + cc) for s in range(2) for cc in range(4)]
            for ti, (s, c) in enumerate(tiles):
                sch.dma("sp", "htl", hT8[:, :, ti * 128:(ti + 1) * 128], hT_d[gci(s, c)].rearrange("p (k t) -> p k t", k=8), ["hT_d"], ["hT8"])

        def f_uproj(sc):
            for j in range(4):
                for hf in range(2):
                    pu, pun = nps()
                    for s4 in range(4):
                        s8 = hf * 4 + s4
                        for k in range(8):
                            rhs = hT8[:, k, :].rearrange("p (n s) -> p s n", s=8)[:, s8, :]
                            mm(pu[:, s4 * 128:(s4 + 1) * 128], wb_v[:, k, j * 128:(j + 1) * 128], rhs, k == 0, k == 7, ["hT8", "wb"], [pun],
                               inc=1 if (k == 7 and s4 == 3) else 0)
                    cp("act" if hf == 0 else "dve", uT8[:, j, hf * 512:(hf + 1) * 512], pu[:], [pun], ["uT8"])
            for g8 in range(8):
                for s8 in range(8):
                    qd_ = ("sp", "act")[(g8 * 8 + s8) % 2]
                    sch.dma(qd_, "blk_" + qd_, ublk8[16 * s8:16 * s8 + 16, :, :].rearrange("p (j g) n -> p j g n", j=4)[:, :, g8, :],
                            uT8[16 * g8:16 * g8 + 16, :, s8 * 128:(s8 + 1) * 128], ["uT8"], ["ublk8_" + qd_])

        def f_statein(sc):
            for q4_ in range(4):
                for ri, (wsv, wsn) in enumerate(((wsre_v, "wsre"), (wsim_v, "wsim"))):
                    pw2 = [nps(), nps()]
                    for g8 in range(8):
                        g = q4_ * 8 + g8
                        pw_, pwn = pw2[g8 // 4]
                        mm(pw_[0:64, (g8 % 4) * 128:(g8 % 4 + 1) * 128], wsv[:, g, :], ublk8[:, g, :], g8 % 4 == 0, g8 % 4 == 3, [wsn, "ublk8_sp", "ublk8_act"], [pwn])
                    for hb_ in range(2):
                        pw_, pwn = pw2[hb_]
                        g0_ = q4_ * 8 + hb_ * 4
                        cp("act" if hb_ == 0 else "dve", gst[:, ri, g0_:g0_ + 4, :, 1:65],
                           pw_[0:64, :].rearrange("p (g s m) -> p g s m", g=4, s=2), [pwn], ["gstw", "gsts"])

        def f_rec(m0, m1):
            LAb = LA[:].unsqueeze(3).broadcast_to([64, 2, 32, 2])
            for m in range(m0, m1):
                Si, Sin_, So, Son_ = S2p[m % 2], f"S2_{m % 2}", S2p[(m + 1) % 2], f"S2_{(m + 1) % 2}"
                tt("dve", rt, Si, LAb, ALU.mult, [Sin_, "LA"], ["rt"])
                tt("dve", ru[:, 0, :, :], Si[:, 1, :, :], LB[:, 0, :].unsqueeze(2).broadcast_to([64, 32, 2]), ALU.mult, [Sin_, "LB"], ["ru"])
                tt("dve", ru[:, 1, :, :], Si[:, 0, :, :], LB[:, 1, :].unsqueeze(2).broadcast_to([64, 32, 2]), ALU.mult, [Sin_, "LB"], ["ru"])
                tt("dve", rt, rt, ru, ALU.add, ["rt", "ru"], ["rt"])
                tt("dve", So, rt, gst[:, :, :, :, 1 + m], ALU.add, ["rt", "gstw"], [Son_])
                cp("pool", gst[:, :, :, :, 1 + m], So, [Son_], ["gsts"])

        def f_y(sc):
            for q4_ in range(4):
                py2 = [nps(), nps()]
                for g8 in range(8):
                    g = q4_ * 8 + g8
                    py_, pyn_ = py2[g8 // 4]
                    o_ = py_[:, (g8 % 4) * 128:(g8 % 4 + 1) * 128]
                    mm(o_, tz_v[:, g, :], ublk8[:, g, :], g8 % 4 == 0, False, ["tz", "ublk8_sp", "ublk8_act"], [pyn_])
                    for s_ in range(2):
                        mm(o_[:, s_ * 64:(s_ + 1) * 64], wore_v[:, g, :], gst[:, 0, g, s_, 0:64], False, False, ["wore", "gsts"], [pyn_])
                        mm(o_[:, s_ * 64:(s_ + 1) * 64], woim_v[:, g, :], gst[:, 1, g, s_, 0:64], False, g8 % 4 == 3 and s_ == 1, ["woim", "gsts"], [pyn_])
                for hb_ in range(2):
                    py_, pyn_ = py2[hb_]
                    g0_ = q4_ * 8 + hb_ * 4
                    ub = ublk8[:, g0_:g0_ + 4, :]
                    tt("dve", ytB.rearrange("p (g n) -> p g n", g=4), ub, s5d_t[:, g0_:g0_ + 4].unsqueeze(2).broadcast_to([128, 4, 128]), ALU.mult,
                       ["ublk8_sp", "ublk8_act", "s5d_t"], ["ytB"])
                    tt("dve", ytB, ytB, py_[:], ALU.add, ["ytB", pyn_], ["ytB"])
                    act(ygB, ytB, AF.Square, ["ytB"], ["ygB"])
                    ts("dve", ygB, ygB, GELU_C2, GELU_C1, ALU.mult, ALU.add, ["ygB"], ["ygB"])
                    tt("dve", ygB, ygB, ytB, ALU.mult, ["ygB", "ytB"], ["ygB"])
                    act(ygB, ygB, AF.Tanh, ["ygB"], ["ygB"])
                    op("dve", lambda e: e.scalar_tensor_tensor(out=gyb8[:, g0_:g0_ + 4, :].rearrange("p g n -> p (g n)"), in0=ygB, scalar=1.0, in1=ytB,
                                                                op0=ALU.add, op1=ALU.mult), ["ygB", "ytB"], ["gyb8"])
            cp("pool", gst[:, :, :, :, 0], gst[:, :, :, :, 64], ["gsts"], ["gsts"])
            for g8 in range(8):
                for l8 in range(8):
                    qd_ = ("sp", "act")[(g8 * 8 + l8) % 2]
                    sch.dma(qd_, "ubl_" + qd_, gyT8[16 * g8:16 * g8 + 16, :, l8 * 128:(l8 + 1) * 128],
                            gyb8[16 * l8:16 * l8 + 16, :, :].rearrange("p (j g) n -> p j g n", j=4)[:, :, g8, :], ["gyb8"], ["gyT8_" + qd_])

        def f_gates(sc):
            for l8 in range(8):
                pg_, pgn = nps()
                for k in range(8):
                    lhs = hT8[:, k, :].rearrange("p (n l) -> p l n", l=8)[:, l8, :]
                    mm(pg_[:], lhs, wb_v[:, k, 512:1024], k == 0, k == 7, ["hT8", "wb"], [pgn])
                act(gatesB[:, l8, :], pg_[:], AF.Silu, [pgn], ["gatesB"])

        def f_tile(sc, l8):
            base = 512 * sc
            if True:
                xb, xbn = xtB[l8 % 2], f"xtB{l8 % 2}"
                for s_ in range(2):
                    sch.dma("sp", f"x{xbn}_{s_}", xb[64 * s_:64 * s_ + 64, :],
                            xsrc[s_, base:base + 512, :].rearrange("(m l) d -> l m d", l=8)[l8], xrd, [f"{xbn}_{s_}"])
                    sch.dma("act", f"mxl_{s_}", mixB[64 * s_:64 * s_ + 64, 0:1536],
                            mixA_d[s_, base:base + 512, :].rearrange("(m l) d -> l m d", l=8)[l8], ["mixA_d"], [f"mixB_{s_}"])
                pv = [nps(), nps()]
                for hf in range(2):
                    pp, ppn = pv[hf]
                    mm(pp[:], ones_b[0:1, :], glub_r[0:1, hf * 512:(hf + 1) * 512], True, False, ["ones_b", "glub_r"], [ppn])
                    for j in range(4):
                        mm(pp[:], gyT8[:, j, l8 * 128:(l8 + 1) * 128], glu_v[:, j, hf * 512:(hf + 1) * 512], False, j == 3,
                           ["gyT8_sp", "gyT8_act", "glu"], [ppn])
                act(L1B, pv[1][0][:], AF.Tanh, [pv[1][1]], ["L1B"], scale=0.5)
                ts("dve", L1B, L1B, 0.5, 0.5, ALU.mult, ALU.add, ["L1B"], ["L1B"])
                tt("dve", L2B, pv[0][0][:], L1B, ALU.mult, [pv[0][1], "L1B"], ["L2B"])
                tt("dve", L2B, L2B, gatesB[:, l8, :], ALU.mult, ["L2B", "gatesB"], ["L2B"])
                act(junkB[:, 0:512], L2B, AF.Square, ["L2B"], ["junkB", "st8"], accum_out=st8[:, 2:3])
                rsqrt(st8[:, 3:4], st8[:, 2:3], 1.0 / 512, ["st8"], ["st8"], 1)
                ts("dve", mixB[:, 1536:2048], L2B, st8[:, 3:4], None, ALU.mult, ALU.bypass, ["L2B", "st8"], ["mixB_S"])
                pm1, pm1n = nps()
                pm2, pm2n = nps()
                pm1b, pm2b = pm1[:].bitcast(BF16), pm2[:].bitcast(BF16)
                for j in range(16):
                    dst, dn = (pm1b, pm1n) if j < 8 else (pm2b, pm2n)
                    jj = j % 8
                    op("pe", lambda e: e.transpose(dst[:, jj * 128:(jj + 1) * 128], mixB[:, j * 128:(j + 1) * 128], ident_b[:]),
                       ["mixB_0", "mixB_1", "mixB_S", "ident_b"], [dn], inc=1 if j in (7, 15) else 0)
                mT2 = mixTB.rearrange("p k t -> p (k t)")
                cp("act", mT2[:, 0:1024], pm1b, [pm1n], ["mixTB"])
                cp("dve", mT2[:, 1024:2048], pm2b, [pm2n], ["mixTB"])
                po_ = [nps(), nps()]
                for n2 in range(2):
                    pp, ppn = po_[n2]
                    for kk in range(16):
                        mm(pp[:], mixTB[:, kk, :], wout_v[:, kk, n2 * 512:(n2 + 1) * 512], kk == 0, kk == 15, ["mixTB", "wout"], [ppn])
                for n2 in range(2):
                    act(junkB[:, n2 * 512:(n2 + 1) * 512], po_[n2][0][:], AF.Square, [po_[n2][1]], ["junkB", "st8"], accum_out=st8[:, 4 + n2:5 + n2])
                tt("dve", st8[:, 6:7], st8[:, 4:5], st8[:, 5:6], ALU.add, ["st8"], ["st8"])
                rsqrt(st8[:, 7:8], st8[:, 6:7], 1.0 / D_MODEL, ["st8"], ["st8"], 1)
                for n2 in range(2):
                    op("dve", lambda e: e.scalar_tensor_tensor(out=ygB, in0=po_[n2][0][:], scalar=st8[:, 7:8], in1=postw_bc[:, n2 * 512:(n2 + 1) * 512],
                                                                op0=ALU.mult, op1=ALU.mult), [po_[n2][1], "st8", "postw_bc"], ["ygB"])
                    tt("dve", xb[:, n2 * 512:(n2 + 1) * 512], xb[:, n2 * 512:(n2 + 1) * 512], ygB, ALU.add, [f"{xbn}_0", f"{xbn}_1", "ygB"],
                       [f"{xbn}_0", f"{xbn}_1"])
                for s_ in range(2):
                    sch.dma("sp", "ost", out[s_, base:base + 512, :].rearrange("(m l) d -> l m d", l=8)[l8], xb[64 * s_:64 * s_ + 64, :],
                            [f"{xbn}_0", f"{xbn}_1"], ["out_d"])

        f_loads(0); f_uproj(0); f_statein(0); f_rec(0, 64); f_y(0); f_gates(0)
        for sc in range(NSC):
            nxt = sc + 1 < NSC
            if nxt:
                f_loads(sc + 1)
                f_uproj(sc + 1)
            f_tile(sc, 0)
            f_tile(sc, 1)
            if nxt:
                f_statein(sc + 1)
            for l8 in range(2, 8):
                if nxt:
                    f_rec((l8 - 2) * 11, min(64, (l8 - 1) * 11))
                f_tile(sc, l8)
            if nxt:
                f_y(sc + 1)
                f_gates(sc + 1)
    sch.barrier()
    sch.finish("sp", ["mixA_d", "hT_d", "out_d"])
    es.close()
    return nc, sch


def prep_shared(inp, NL):
    f = lambda a: np.ascontiguousarray(np.asarray(a, dtype=np.float32))
    w_in = np.asarray(inp["w_in"], dtype=np.float32)[:NL]
    sh = {}
    sh["w_tma"] = f(np.concatenate([w_in[:, :, C_Z:C_Z + 1024], w_in[:, :, C_DT:C_DT + 16], w_in[:, :, C_I:C_I + 512],
                                    w_in[:, :, C_G:C_G + 512]], axis=2))
    sh["w_fma"] = f(np.concatenate([w_in[:, :, C_XBC:C_XBC + 1536], w_in[:, :, C_Q:C_Q + 512], w_in[:, :, C_F:C_F + 512]], axis=2))
    sh["w_b"] = f(w_in[:, :, C_U:C_U + 1024])
    sh["w_out"] = f(np.asarray(inp["w_out"])[:NL])
    sh["prew"] = f(np.asarray(inp["pre_norm_w"])[:NL].reshape(NL, 8, 128).transpose(0, 2, 1))
    sh["postw"] = f(np.asarray(inp["post_norm_w"])[:NL].reshape(NL, 1, D_MODEL))
    mixnw = np.concatenate([np.asarray(inp["ssd_norm_w"])[:NL], np.asarray(inp["hgrn_norm_w"])[:NL], np.asarray(inp["s5_norm_w"])[:NL]], axis=1)
    sh["mixnw"] = f(mixnw.reshape(NL, 16, 128).transpose(0, 2, 1))
    cw = np.asarray(inp["ssd_conv_w"])[:NL]
    sh["convw"] = f(cw.reshape(NL, 4, 12, 128).transpose(0, 3, 1, 2).reshape(NL, 128, 48))
    cb = np.asarray(inp["ssd_conv_b"])[:NL]
    sh["convb_pp"] = f(cb.reshape(NL, 12, 128).transpose(0, 2, 1))
    sh["convb_row"] = f(cb.reshape(NL, 1, 1536))
    sh["dtb"] = f(np.asarray(inp["ssd_dt_bias"])[:NL].reshape(NL, 1, 16))
    sh["alog"] = f(np.asarray(inp["ssd_a_log"])[:NL].reshape(NL, 1, 16))
    sh["ssdd"] = f(np.asarray(inp["ssd_d"])[:NL].reshape(NL, 1, 16))
    hl = np.asarray(inp["hgrn_lower_bounds"])[:NL]
    sh["hlb"] = f(hl.reshape(NL, 4, 128).transpose(2, 1, 0).reshape(128, 4 * NL))
    sh["lamre"] = f(np.asarray(inp["s5_lambda_re"])[:NL].transpose(0, 2, 1))
    sh["lamim"] = f(np.asarray(inp["s5_lambda_im"])[:NL].transpose(0, 2, 1))
    sh["lstep"] = f(np.asarray(inp["s5_log_step"])[:NL].reshape(NL, 1, 32))
    sh["bre"] = f(np.asarray(inp["s5_b_re"])[:NL].transpose(0, 2, 1, 3).reshape(NL, 64, 512))
    sh["bim"] = f(np.asarray(inp["s5_b_im"])[:NL].transpose(0, 2, 1, 3).reshape(NL, 64, 512))
    sh["cre"] = f(np.asarray(inp["s5_c_re"])[:NL].transpose(0, 3, 1, 2).reshape(NL, 64, 512))
    sh["cim"] = f(np.asarray(inp["s5_c_im"])[:NL].transpose(0, 3, 1, 2).reshape(NL, 64, 512))
    d5 = np.asarray(inp["s5_d"])[:NL].reshape(NL, 32, 16)
    sh["s5d"] = f(np.broadcast_to(d5.transpose(0, 2, 1)[:, None, :, :], (NL, 8, 16, 32)).reshape(NL, 128, 32))
    sh["gluw"] = f(np.asarray(inp["s5_glu_w"])[:NL])
    sh["glub"] = f(np.asarray(inp["s5_glu_b"])[:NL].reshape(NL, 1, 1024))
    for k, v in host_consts().items():
        sh["c_" + k] = v
    return sh


LAYER_GROUPS = [[0, 1, 2, 3]]


def kernel(**inputs):
    x = np.ascontiguousarray(np.asarray(inputs["x"], dtype=np.float32))
    B = x.shape[0]
    S = B // NCORES
    sh = prep_shared(inputs, NL_FULL)
    cur = x
    for grp in LAYER_GROUPS:
        nc, _ = build(NL_FULL, S, x.shape[1], layers=grp)
        in_maps = [dict(sh, x=np.ascontiguousarray(cur[S * c:S * (c + 1)])) for c in range(NCORES)]
        res = run_bass_kernel_spmd(nc, in_maps, core_ids=list(range(NCORES)))
        cur = np.concatenate([np.asarray(r["out"], dtype=np.float32) for r in res.results], axis=0)
    return cur
```
